# Optimizing a Trainium2 kernel written in Bass

```python
import math
import jax, jax.numpy as jnp
from jax import lax
import numpy as np

D_MODEL = 2048
BATCH = 4
SEQ = 2048
DEPTH = 2
DEC_BATCH = 128
DEC_SEQ = 4
PAST_LEN = 16384
PAGE_SIZE = 128

F32 = jnp.float32
EPS = 1e-6
N_EVEN = (DEPTH + 1) // 2
N_ODD = DEPTH // 2

GLA_H = 4
GLA_DV = D_MODEL // 8
GLA_DK = GLA_DV // 2
GLA_LR = 16
GLA_GATE_NORM = 16.0
GLA_CHUNK = 16
GLA_QK = GLA_H * GLA_DK
GLA_V = GLA_H * GLA_DV

GDN_H = 8
GDN_DK = 128
GDN_DV = (D_MODEL // 2) // GDN_H
GDN_CHUNK = 64
GDN_QK = GDN_H * GDN_DK
GDN_V = GDN_H * GDN_DV
CONV_K = 4
GDN_CONV_W = 2 * GDN_QK + GDN_V

SSD_P = 64
SSD_H = (D_MODEL // 2) // SSD_P
SSD_G = 2
SSD_N = 128
SSD_CHUNK = 64
SSD_W = SSD_H * SSD_P
SSD_CONV_W = SSD_W + 2 * SSD_G * SSD_N

S5_W = D_MODEL // 2
S5_GS = 16
S5_G = S5_W // S5_GS
S5_P = 64

D_FF = 5632
FFN_K = 3

DT_MIN = 1e-3
DT_MAX = 1e-1

AB_SIZES = (GLA_QK, GLA_QK, GLA_V, GLA_LR, GLA_V, GDN_CONV_W, GDN_H, GDN_H, GDN_V)
CD_SIZES = (SSD_W, SSD_CONV_W, SSD_H, S5_W)
IN_AB = sum(AB_SIZES)
IN_CD = sum(CD_SIZES)
OUT_AB = GLA_V + GDN_V
OUT_CD = SSD_W + S5_W

kernel_name = 'hybrid_gla_gdn_ssd_s5_convffn_step'


def _split(t, sizes):
    return jnp.split(t, np.cumsum(sizes)[:-1].tolist(), axis=-1)


def _rms(x, g):
    xf = x.astype(F32)
    y = xf * lax.rsqrt(jnp.mean(xf * xf, -1, keepdims=True) + EPS) * g.astype(F32)
    return y.astype(x.dtype)


def _head_rms(o, g):
    return o * lax.rsqrt(jnp.mean(o * o, -1, keepdims=True) + EPS) * g.astype(F32)


def _l2n(t):
    return t * lax.rsqrt(jnp.sum(t * t, -1, keepdims=True) + EPS)


def _causal_dwconv(x, buf, w, b):
    k = w.shape[0]
    L = x.shape[1]
    xp = jnp.concatenate([buf.astype(x.dtype), x], axis=1)
    y = sum(xp[:, j:j + L] * w[j] for j in range(k)) + b
    return y, xp[:, L:]


def _gla_chunked(q, k, v, log_a, s0):
    bsz, L, H, DK = q.shape
    DV = v.shape[-1]
    C = math.gcd(L, GLA_CHUNK)
    n = L // C
    r = lambda t: t.reshape(bsz, n, C, H, t.shape[-1])
    q, k, v, log_a = r(q * DK ** -0.5), r(k), r(v), r(log_a)
    b = jnp.cumsum(log_a, axis=2)
    causal = jnp.tril(jnp.ones((C, C), bool))
    diff = b[:, :, :, None] - b[:, :, None, :]
    dec = jnp.exp(jnp.where(causal[:, :, None, None], diff, -jnp.inf))
    att = jnp.einsum('bnihd,bnjhd,bnijhd->bnhij', q, k, dec)
    o_intra = jnp.einsum('bnhij,bnjhv->bnihv', att, v)
    b_last = b[:, :, -1]
    q_in = q * jnp.exp(b)
    k_out = k * jnp.exp(b_last[:, :, None] - b)

    def step(s, inp):
        qc, kc, vc, dc = inp
        o = jnp.einsum('bihd,bhdv->bihv', qc, s)
        s = s * dc[..., None] + jnp.einsum('bjhd,bjhv->bhdv', kc, vc)
        return s, o

    sw = lambda t: jnp.moveaxis(t, 1, 0)
    s_fin, o_inter = lax.scan(step, s0, (sw(q_in), sw(k_out), sw(v), sw(jnp.exp(b_last))))
    o = o_intra + jnp.moveaxis(o_inter, 0, 1)
    return o.reshape(bsz, L, H, DV), s_fin


def _gdn_chunked(q, k, v, g, beta, s0):
    bsz, L, H, DK = q.shape
    DV = v.shape[-1]
    C = math.gcd(L, GDN_CHUNK)
    n = L // C
    r4 = lambda t: t.reshape(bsz, n, C, H, t.shape[-1]).transpose(0, 1, 3, 2, 4)
    r3 = lambda t: t.reshape(bsz, n, C, H).transpose(0, 1, 3, 2)
    q, k, v = r4(q * DK ** -0.5), r4(k), r4(v)
    g, beta = r3(g), r3(beta)
    gc = jnp.cumsum(g, axis=-1)
    causal = jnp.tril(jnp.ones((C, C), bool))
    strict = jnp.tril(jnp.ones((C, C), bool), -1)
    dec = jnp.exp(jnp.where(causal, gc[..., :, None] - gc[..., None, :], -jnp.inf))
    kb = k * beta[..., None]
    m = jnp.where(strict, jnp.einsum('bnhid,bnhjd->bnhij', kb, k) * dec, 0.0)
    eye = jnp.eye(C, dtype=F32)
    rhs = jnp.concatenate([v * beta[..., None], kb * jnp.exp(gc)[..., None]], axis=-1)
    sol = lax.linalg.triangular_solve(eye + m, rhs, left_side=True, lower=True, unit_diagonal=True)
    u, w = sol[..., :DV], sol[..., DV:]
    att = jnp.einsum('bnhid,bnhjd->bnhij', q, k) * dec
    q_in = q * jnp.exp(gc)[..., None]
    g_last = gc[..., -1]
    k_out = k * jnp.exp(g_last[..., None] - gc)[..., None]

    def step(s, inp):
        uc, wc, qc, kc, ac, dc = inp
        v_new = uc - jnp.einsum('bhid,bhdv->bhiv', wc, s)
        o = jnp.einsum('bhid,bhdv->bhiv', qc, s) + jnp.einsum('bhij,bhjv->bhiv', ac, v_new)
        s = s * dc[..., None, None] + jnp.einsum('bhjd,bhjv->bhdv', kc, v_new)
        return s, o

    sw = lambda t: jnp.moveaxis(t, 1, 0)
    s_fin, o = lax.scan(step, s0, (sw(u), sw(w), sw(q_in), sw(k_out), sw(att), sw(jnp.exp(g_last))))
    o = jnp.transpose(o, (1, 0, 3, 2, 4))
    return o.reshape(bsz, L, H, DV), s_fin


def _ssd_chunked(x, dt, A, bm, cm, h0):
    bsz, L, H, P = x.shape
    N = bm.shape[-1]
    C = math.gcd(L, SSD_CHUNK)
    n = L // C
    x = x.reshape(bsz, n, C, H, P)
    dt = dt.reshape(bsz, n, C, H)
    bm = bm.reshape(bsz, n, C, H, N)
    cm = cm.reshape(bsz, n, C, H, N)
    acs = jnp.cumsum(dt * A, axis=2)
    causal = jnp.tril(jnp.ones((C, C), bool))
    diff = acs[:, :, :, None] - acs[:, :, None, :]
    dec = jnp.exp(jnp.where(causal[:, :, None], diff, -jnp.inf))
    scores = jnp.einsum('bnihs,bnjhs->bnijh', cm, bm) * dec * dt[:, :, None]
    y_intra = jnp.einsum('bnijh,bnjhp->bnihp', scores, x)
    acs_last = acs[:, :, -1]
    c_in = cm * jnp.exp(acs)[..., None]
    b_out = bm * (jnp.exp(acs_last[:, :, None] - acs) * dt)[..., None]

    def step(h, inp):
        cc, bc, xc, dc = inp
        y = jnp.einsum('bihs,bhps->bihp', cc, h)
        h = h * dc[:, :, None, None] + jnp.einsum('bjhs,bjhp->bhps', bc, xc)
        return h, y

    sw = lambda t: jnp.moveaxis(t, 1, 0)
    h_fin, y_inter = lax.scan(step, h0, (sw(c_in), sw(b_out), sw(x), sw(jnp.exp(acs_last))))
    y = y_intra + jnp.moveaxis(y_inter, 0, 1)
    return y.reshape(bsz, L, H, P), h_fin


def _s5_scan(u, a_re, a_im, b_re, b_im, c_re, c_im, d, log_dt, x0_re, x0_im):
    bsz, L, W = u.shape
    ug = u.reshape(bsz, L, S5_G, S5_GS)
    a_re, a_im = a_re.astype(F32), a_im.astype(F32)
    dt = jnp.exp(log_dt.astype(F32))[:, None]
    mag = jnp.exp(a_re * dt)
    lb_re, lb_im = mag * jnp.cos(a_im * dt), mag * jnp.sin(a_im * dt)
    nr, ni = lb_re - 1.0, lb_im
    den = a_re * a_re + a_im * a_im
    f_re = (nr * a_re + ni * a_im) / den
    f_im = (ni * a_re - nr * a_im) / den
    b_re, b_im = b_re.astype(F32), b_im.astype(F32)
    bb_re = f_re[..., None] * b_re - f_im[..., None] * b_im
    bb_im = f_re[..., None] * b_im + f_im[..., None] * b_re
    bu_re = jnp.einsum('gph,blgh->blgp', bb_re, ug)
    bu_im = jnp.einsum('gph,blgh->blgp', bb_im, ug)
    bu_re = bu_re.at[:, 0].add(lb_re * x0_re - lb_im * x0_im)
    bu_im = bu_im.at[:, 0].add(lb_re * x0_im + lb_im * x0_re)
    ar = jnp.broadcast_to(lb_re, bu_re.shape)
    ai = jnp.broadcast_to(lb_im, bu_im.shape)

    def combine(e1, e2):
        a1r, a1i, b1r, b1i = e1
        a2r, a2i, b2r, b2i = e2
        return (a2r * a1r - a2i * a1i, a2r * a1i + a2i * a1r,
                a2r * b1r - a2i * b1i + b2r, a2r * b1i + a2i * b1r + b2i)

    _, _, xr, xi = lax.associative_scan(combine, (ar, ai, bu_re, bu_im), axis=1)
    y = jnp.einsum('ghp,blgp->blgh', c_re.astype(F32), xr) - jnp.einsum('ghp,blgp->blgh', c_im.astype(F32), xi)
    y = y.reshape(bsz, L, W) + d.astype(F32) * u
    return y, xr[:, -1], xi[:, -1]


def _mixer_ab(h, s_gla, s_gdn, s_conv, w_in, gla_w2, gla_b2, gla_ng, conv_w, conv_b, A_log, dt_bias, gdn_ng, w_out):
    bsz, L, _ = h.shape
    q_a, k_a, v_a, lr_a, r_a, qkv_b, a_b, b_b, g_b = _split(h @ w_in, AB_SIZES)
    heads = lambda t, dd: t.astype(F32).reshape(bsz, L, -1, dd)
    log_a = jax.nn.log_sigmoid(lr_a.astype(F32) @ gla_w2.astype(F32) + gla_b2.astype(F32)) / GLA_GATE_NORM
    o_a, s_gla_new = _gla_chunked(heads(q_a, GLA_DK), heads(k_a, GLA_DK), heads(v_a, GLA_DV),
                                  heads(log_a, GLA_DK), s_gla.astype(F32))
    o_a = _head_rms(o_a, gla_ng) * jax.nn.silu(heads(r_a, GLA_DV))
    qkv_c, s_conv_new = _causal_dwconv(qkv_b, s_conv, conv_w, conv_b)
    q_b, k_b, v_b = _split(jax.nn.silu(qkv_c.astype(F32)), (GDN_QK, GDN_QK, GDN_V))
    g = -jnp.exp(A_log.astype(F32)) * jax.nn.softplus(a_b.astype(F32) + dt_bias.astype(F32))
    beta = jax.nn.sigmoid(b_b.astype(F32))
    o_b, s_gdn_new = _gdn_chunked(_l2n(heads(q_b, GDN_DK)), _l2n(heads(k_b, GDN_DK)), heads(v_b, GDN_DV),
                                  g, beta, s_gdn.astype(F32))
    o_b = _head_rms(o_b, gdn_ng) * jax.nn.silu(heads(g_b, GDN_DV))
    o = jnp.concatenate([o_a.reshape(bsz, L, -1), o_b.reshape(bsz, L, -1)], axis=-1).astype(h.dtype)
    return o @ w_out, s_gla_new, s_gdn_new, s_conv_new


def _mixer_cd(h, s_ssd, s_conv, s_re, s_im, w_in, conv_w, conv_b, A_log, dt_bias, ssd_d, ssd_ng,
              a_re, a_im, b_re, b_im, c_re, c_im, s5_d, log_dt, glu_w, glu_b, w_out):
    bsz, L, _ = h.shape
    z, xbc, dt_raw, u = _split(h @ w_in, CD_SIZES)
    xbc_c, s_conv_new = _causal_dwconv(xbc, s_conv, conv_w, conv_b)
    xs, bm, cm = _split(jax.nn.silu(xbc_c.astype(F32)), (SSD_W, SSD_G * SSD_N, SSD_G * SSD_N))
    rep = SSD_H // SSD_G
    bm = jnp.repeat(bm.reshape(bsz, L, SSD_G, SSD_N), rep, axis=2)
    cm = jnp.repeat(cm.reshape(bsz, L, SSD_G, SSD_N), rep, axis=2)
    xs = xs.reshape(bsz, L, SSD_H, SSD_P)
    dt = jax.nn.softplus(dt_raw.astype(F32) + dt_bias.astype(F32))
    A = -jnp.exp(A_log.astype(F32))
    y_c, s_ssd_new = _ssd_chunked(xs, dt, A, bm, cm, s_ssd.astype(F32))
    y_c = y_c + ssd_d.astype(F32)[:, None] * xs
    gsz = SSD_W // SSD_G
    y_c = y_c.reshape(bsz, L, SSD_G, gsz) * jax.nn.silu(z.astype(F32)).reshape(bsz, L, SSD_G, gsz)
    y_c = _head_rms(y_c, ssd_ng.reshape(SSD_G, gsz)).reshape(bsz, L, SSD_W)
    y_d, s_re_new, s_im_new = _s5_scan(u.astype(F32), a_re, a_im, b_re, b_im, c_re, c_im, s5_d, log_dt,
                                       s_re.astype(F32), s_im.astype(F32))
    z5 = jax.nn.gelu(y_d)
    y_d = z5 * jax.nn.sigmoid(z5 @ glu_w.astype(F32) + glu_b.astype(F32))
    o = jnp.concatenate([y_c, y_d], axis=-1).astype(h.dtype)
    return o @ w_out, s_ssd_new, s_conv_new, s_re_new, s_im_new


def _conv_ffn(h, buf, w_up, conv_w, conv_b, w_down):
    up, new_buf = _causal_dwconv(h @ w_up, buf, conv_w, conv_b)
    a, g = jnp.split(up, 2, axis=-1)
    return (jax.nn.silu(g) * a) @ w_down, new_buf


def _trunk(x, c, state, p):
    s_gla, s_gdn, s_gdnc, s_ssd, s_ssdc, s_re, s_im, s_ffn = state
    names = ('gla', 'gdn', 'gdnc', 'ssd', 'ssdc', 're', 'im', 'ffn')
    new = {k: [] for k in names}
    cond = jax.nn.silu(c.astype(F32)).astype(x.dtype)
    for layer in range(DEPTH):
        mod = (cond @ p['w_ada'][layer] + p['b_ada'][layer])[:, None, :]
        sh1, sc1, g1, sh2, sc2, g2 = jnp.split(mod, 6, axis=-1)
        h = _rms(x, p['g_mix'][layer]) * (1 + sc1) + sh1
        i = layer // 2
        if layer % 2 == 0:
            y, a, b, cv = _mixer_ab(h, s_gla[i], s_gdn[i], s_gdnc[i], p['w_in_ab'][i], p['gla_w2'][i],
                                    p['gla_b2'][i], p['gla_norm_g'][i], p['gdn_conv_w'][i], p['gdn_conv_b'][i],
                                    p['gdn_A_log'][i], p['gdn_dt_bias'][i], p['gdn_norm_g'][i], p['w_out_ab'][i])
            new['gla'].append(a)
            new['gdn'].append(b)
            new['gdnc'].append(cv)
        else:
            y, a, cv, sr, si = _mixer_cd(h, s_ssd[i], s_ssdc[i], s_re[i], s_im[i], p['w_in_cd'][i],
                                         p['ssd_conv_w'][i], p['ssd_conv_b'][i], p['ssd_A_log'][i],
                                         p['ssd_dt_bias'][i], p['ssd_D'][i], p['ssd_norm_g'][i],
                                         p['s5_A_re'][i], p['s5_A_im'][i], p['s5_B_re'][i], p['s5_B_im'][i],
                                         p['s5_C_re'][i], p['s5_C_im'][i], p['s5_D'][i], p['s5_log_dt'][i],
                                         p['s5_glu_w'][i], p['s5_glu_b'][i], p['w_out_cd'][i])
            new['ssd'].append(a)
            new['ssdc'].append(cv)
            new['re'].append(sr)
            new['im'].append(si)
        x = x + g1 * y
        h = _rms(x, p['g_ffn'][layer]) * (1 + sc2) + sh2
        y, fb = _conv_ffn(h, s_ffn[layer], p['w_ffn_up'][layer], p['ffn_conv_w'][layer],
                          p['ffn_conv_b'][layer], p['w_ffn_down'][layer])
        new['ffn'].append(fb)
        x = x + g2 * y
    out = tuple(jnp.stack(new[k]).astype(s.dtype) for k, s in zip(names, state))
    return _rms(x, p['g_final']), out


def setup_inputs(seed: int = 0) -> dict:
    key = jax.random.key(seed)
    cnt = [0]

    def nk():
        cnt[0] += 1
        return jax.random.fold_in(key, cnt[0])

    def nrm(shape, scale=1.0):
        return scale * jax.random.normal(nk(), shape, F32)

    def unif(shape, lo, hi):
        return jax.random.uniform(nk(), shape, F32, lo, hi)

    def dt_bias(shape):
        dt = jnp.exp(unif(shape, math.log(DT_MIN), math.log(DT_MAX)))
        return dt + jnp.log(-jnp.expm1(-dt))

    def gain(shape):
        return 1.0 + nrm(shape, 0.01)

    D = D_MODEL
    a_im = jnp.pi * jnp.arange(S5_P, dtype=F32)
    return {
        'x_prompt': nrm((BATCH, SEQ, D)),
        'x_sample': nrm((DEC_BATCH, DEC_SEQ, D)),
        'c_prompt': nrm((BATCH, D)),
        'c_sample': nrm((DEC_BATCH, D)),
        'state_gla': nrm((N_EVEN, DEC_BATCH, GLA_H, GLA_DK, GLA_DV), 0.1),
        'state_gdn': nrm((N_EVEN, DEC_BATCH, GDN_H, GDN_DK, GDN_DV), 0.1),
        'state_gdn_conv': nrm((N_EVEN, DEC_BATCH, CONV_K - 1, GDN_CONV_W)),
        'state_ssd': nrm((N_ODD, DEC_BATCH, SSD_H, SSD_P, SSD_N), 0.1),
        'state_ssd_conv': nrm((N_ODD, DEC_BATCH, CONV_K - 1, SSD_CONV_W)),
        'state_s5_re': nrm((N_ODD, DEC_BATCH, S5_G, S5_P), 0.1),
        'state_s5_im': nrm((N_ODD, DEC_BATCH, S5_G, S5_P), 0.1),
        'state_ffn_conv': nrm((DEPTH, DEC_BATCH, FFN_K - 1, 2 * D_FF)),
        'w_ada': nrm((DEPTH, D, 6 * D), 0.5 * D ** -0.5),
        'b_ada': nrm((DEPTH, 6 * D), 0.01),
        'g_mix': gain((DEPTH, D)),
        'g_ffn': gain((DEPTH, D)),
        'w_in_ab': nrm((N_EVEN, D, IN_AB), D ** -0.5),
        'gla_w2': nrm((N_EVEN, GLA_LR, GLA_QK), GLA_LR ** -0.5),
        'gla_b2': nrm((N_EVEN, GLA_QK), 0.01),
        'gla_norm_g': gain((N_EVEN, GLA_DV)),
        'gdn_conv_w': nrm((N_EVEN, CONV_K, GDN_CONV_W), CONV_K ** -0.5),
        'gdn_conv_b': nrm((N_EVEN, GDN_CONV_W), 0.01),
        'gdn_A_log': jnp.log(unif((N_EVEN, GDN_H), 1.0, 16.0)),
        'gdn_dt_bias': dt_bias((N_EVEN, GDN_H)),
        'gdn_norm_g': gain((N_EVEN, GDN_DV)),
        'w_out_ab': nrm((N_EVEN, OUT_AB, D), OUT_AB ** -0.5),
        'w_in_cd': nrm((N_ODD, D, IN_CD), D ** -0.5),
        'ssd_conv_w': nrm((N_ODD, CONV_K, SSD_CONV_W), CONV_K ** -0.5),
        'ssd_conv_b': nrm((N_ODD, SSD_CONV_W), 0.01),
        'ssd_A_log': jnp.log(unif((N_ODD, SSD_H), 1.0, 16.0)),
        'ssd_dt_bias': dt_bias((N_ODD, SSD_H)),
        'ssd_D': 1.0 + nrm((N_ODD, SSD_H), 0.1),
        'ssd_norm_g': gain((N_ODD, SSD_W)),
        's5_A_re': -0.5 + nrm((N_ODD, S5_G, S5_P), 0.01),
        's5_A_im': a_im + nrm((N_ODD, S5_G, S5_P), 0.01),
        's5_B_re': nrm((N_ODD, S5_G, S5_P, S5_GS), (2 * S5_GS) ** -0.5),
        's5_B_im': nrm((N_ODD, S5_G, S5_P, S5_GS), (2 * S5_GS) ** -0.5),
        's5_C_re': nrm((N_ODD, S5_G, S5_GS, S5_P), (2 * S5_P) ** -0.5),
        's5_C_im': nrm((N_ODD, S5_G, S5_GS, S5_P), (2 * S5_P) ** -0.5),
        's5_D': nrm((N_ODD, S5_W)),
        's5_log_dt': unif((N_ODD, S5_G), math.log(DT_MIN), math.log(DT_MAX)),
        's5_glu_w': nrm((N_ODD, S5_W, S5_W), S5_W ** -0.5),
        's5_glu_b': nrm((N_ODD, S5_W), 0.01),
        'w_out_cd': nrm((N_ODD, OUT_CD, D), OUT_CD ** -0.5),
        'w_ffn_up': nrm((DEPTH, D, 2 * D_FF), D ** -0.5),
        'ffn_conv_w': nrm((DEPTH, FFN_K, 2 * D_FF), FFN_K ** -0.5),
        'ffn_conv_b': nrm((DEPTH, 2 * D_FF), 0.01),
        'w_ffn_down': nrm((DEPTH, D_FF, D), D_FF ** -0.5),
        'g_final': gain((D,)),
    }


def reference(x_prompt, x_sample, c_prompt, c_sample, state_gla, state_gdn, state_gdn_conv, state_ssd,
              state_ssd_conv, state_s5_re, state_s5_im, state_ffn_conv, w_ada, b_ada, g_mix, g_ffn, w_in_ab,
              gla_w2, gla_b2, gla_norm_g, gdn_conv_w, gdn_conv_b, gdn_A_log, gdn_dt_bias, gdn_norm_g, w_out_ab,
              w_in_cd, ssd_conv_w, ssd_conv_b, ssd_A_log, ssd_dt_bias, ssd_D, ssd_norm_g, s5_A_re, s5_A_im,
              s5_B_re, s5_B_im, s5_C_re, s5_C_im, s5_D, s5_log_dt, s5_glu_w, s5_glu_b, w_out_cd, w_ffn_up,
              ffn_conv_w, ffn_conv_b, w_ffn_down, g_final):
    params = dict(w_ada=w_ada, b_ada=b_ada, g_mix=g_mix, g_ffn=g_ffn, w_in_ab=w_in_ab, gla_w2=gla_w2,
                  gla_b2=gla_b2, gla_norm_g=gla_norm_g, gdn_conv_w=gdn_conv_w, gdn_conv_b=gdn_conv_b,
                  gdn_A_log=gdn_A_log, gdn_dt_bias=gdn_dt_bias, gdn_norm_g=gdn_norm_g, w_out_ab=w_out_ab,
                  w_in_cd=w_in_cd, ssd_conv_w=ssd_conv_w, ssd_conv_b=ssd_conv_b, ssd_A_log=ssd_A_log,
                  ssd_dt_bias=ssd_dt_bias, ssd_D=ssd_D, ssd_norm_g=ssd_norm_g, s5_A_re=s5_A_re, s5_A_im=s5_A_im,
                  s5_B_re=s5_B_re, s5_B_im=s5_B_im, s5_C_re=s5_C_re, s5_C_im=s5_C_im, s5_D=s5_D,
                  s5_log_dt=s5_log_dt, s5_glu_w=s5_glu_w, s5_glu_b=s5_glu_b, w_out_cd=w_out_cd,
                  w_ffn_up=w_ffn_up, ffn_conv_w=ffn_conv_w, ffn_conv_b=ffn_conv_b, w_ffn_down=w_ffn_down,
                  g_final=g_final)
    sample_state = (state_gla, state_gdn, state_gdn_conv, state_ssd, state_ssd_conv, state_s5_re,
                    state_s5_im, state_ffn_conv)
    bp = x_prompt.shape[0]
    prompt_state = tuple(jnp.zeros(s.shape[:1] + (bp,) + s.shape[2:], s.dtype) for s in sample_state)
    y_prompt, (gla_p, gdn_p, gdn_conv_p, ssd_p, ssd_conv_p, s5_re_p, s5_im_p, ffn_conv_p) = _trunk(
        x_prompt, c_prompt, prompt_state, params)
    y_sample, (gla_s, gdn_s, gdn_conv_s, ssd_s, ssd_conv_s, s5_re_s, s5_im_s, ffn_conv_s) = _trunk(
        x_sample, c_sample, sample_state, params)
    return (y_prompt, y_sample, gla_p, gla_s, gdn_p, gdn_s, gdn_conv_p, gdn_conv_s, ssd_p, ssd_s,
            ssd_conv_p, ssd_conv_s, s5_re_p, s5_re_s, s5_im_p, s5_im_s, ffn_conv_p, ffn_conv_s)
```

```python
import os
import numpy as np
from contextlib import ExitStack
import concourse.bass as bass
import concourse.mybir as mybir
from concourse.bass_utils import run_bass_kernel_spmd

F32 = mybir.dt.float32
F32R = mybir.dt.float32r
ALU = mybir.AluOpType
AF = mybir.ActivationFunctionType

NRING = 12
EPOCH = int(os.environ.get("KDEV_EPOCH", "1000000000"))
NEPOCH = 4
NCORES = 8
RUNCORES = int(os.environ.get("KDEV_CORES", "8"))
D = 2048
KC = 16
DFF = 5632
NFF = 88
TP = 256
SEQ = int(os.environ.get("KDEV_SEQ", "2048"))
NPT = SEQ // TP
NSQ = 16
LS = 4
TS = NSQ * LS
EPS = 1e-6
STAGE = int(os.environ.get("KDEV_STAGE", "9"))
SUB = int(os.environ.get("KDEV_SUB", "99"))
CUT = int(os.environ.get("KDEV_CUT", "99"))
CDCUT = int(os.environ.get("KDEV_CDCUT", "99"))
SC = int(os.environ.get("KDEV_SC", "99"))
WS = 256
NWB = 3


PROJ_TAB = {
    "w_in_ab": [(0, 512), (512, 512), (1024, 1024), (2064, 1024), (2048, 16), (3088, 3072), (6176, 1024), (6160, 8), (6168, 8)],
    "w_in_cd": [(0, 1024), (1024, 1536), (2560, 16), (2576, 1024)],
    "w_out_ab": [(0, 2048)],
    "w_out_cd": [(0, 2048)],
    "glu_w": [(0, 1024)],
}


def slab_base(name, col0):
    base = 0
    for (c0, n) in PROJ_TAB[name]:
        if c0 == col0:
            return base
        base += (n + WS - 1) // WS
    raise KeyError((name, col0))


def n_slabs(name):
    return sum((n + WS - 1) // WS for (_, n) in PROJ_TAB[name])


def tile_weight(W, name):
    K = W.shape[0]
    nk = K // 128
    out = []
    for (c0, n) in PROJ_TAB[name]:
        for s0 in range(0, n, WS):
            m = min(WS, n - s0)
            blk = np.zeros((128, nk, WS), np.float32)
            blk[:, :, :m] = W[:, c0 + s0:c0 + s0 + m].reshape(nk, 128, m).transpose(1, 0, 2)
            out.append(blk)
    return np.stack(out)


def tile_cols_all(W):
    K, N = W.shape
    nk = K // 128
    return np.ascontiguousarray(W.reshape(nk, 128, N // WS, WS).transpose(2, 1, 0, 3))


class Buf:
    __slots__ = ("t", "name", "w", "r")

    def __init__(self, t, name):
        self.t = t
        self.name = name
        self.w = None
        self.r = []

    def __getitem__(self, k):
        return self.t[k]


class Prog:
    ENG = ("pe", "act", "dve", "pool", "sp")

    def __init__(self, nc, stack):
        self.nc = nc
        self.stack = stack
        self.ops = {e: [] for e in self.ENG}
        self.cnt = {e: 0 for e in self.ENG if e != "sp"}
        self.sem = {e: [stack.enter_context(nc.semaphore("s_%s%d" % (e, k))) for k in range(NEPOCH)] for e in self.cnt}
        self.dq = {}
        for q in ("sp", "pool"):
            self.dq[q] = dict(
                sems=[stack.enter_context(nc.semaphore("d_%s%d" % (q, i))) for i in range(NRING)], n=0)
        self.nbuf = 0
        self.psl = []
        self.psi = 0

    def sb(self, shape, dtype=F32, name=None):
        self.nbuf += 1
        name = name or "sb%d" % self.nbuf
        t = self.stack.enter_context(self.nc.sbuf_tensor(name, list(shape), dtype))
        return Buf(t, name)

    def ps(self, shape, dtype=F32, name=None):
        self.nbuf += 1
        name = name or "ps%d" % self.nbuf
        t = self.stack.enter_context(self.nc.psum_tensor(name, list(shape), dtype))
        return Buf(t, name)

    def view(self, ap, name):
        self.nbuf += 1
        return Buf(ap, name)

    def barrier(self, bufs):
        toks = [(e, v) for e, v in self.cnt.items() if v > 0]
        for q, st in self.dq.items():
            n = st["n"]
            for ring in range(min(n, NRING)):
                uses = (n - ring + NRING - 1) // NRING
                toks.append(("dma", q, ring, 16 * uses))
        for b in bufs:
            b.w = None
            b.r = list(toks)

    def next_ps(self):
        b = self.psl[self.psi % len(self.psl)]
        self.psi += 1
        return b

    def _deps(self, eng, reads, writes):
        deps = {}
        ddeps = {}

        def add(tok):
            if tok is None:
                return
            if tok[0] == "dma":
                k = (tok[1], tok[2])
                ddeps[k] = max(ddeps.get(k, 0), tok[3])
            else:
                e, idx = tok
                if e == "pe" and eng == "pe":
                    return
                deps[e] = max(deps.get(e, 0), idx)

        for b in reads:
            add(b.w)
        for b in writes:
            add(b.w)
            for t in b.r:
                add(t)
        return deps, ddeps

    def _record(self, tok, reads, writes):
        for b in reads:
            b.r.append(tok)
            if len(b.r) > 48:
                last = {}
                for t in b.r:
                    k = t[:3] if t[0] == "dma" else t[0]
                    if k not in last or t[-1] > last[k][-1]:
                        last[k] = t
                b.r = list(last.values())
        for b in writes:
            b.w = tok
            b.r = []

    def _ew(self, e, v):
        k = (v - 1) // EPOCH
        return (self.sem[e][k], v - k * EPOCH)

    def op(self, eng, fn, reads=(), writes=()):
        deps, ddeps = self._deps(eng, reads, writes)
        self.cnt[eng] += 1
        idx = self.cnt[eng]
        sem = self.sem[eng][(idx - 1) // EPOCH]
        waits = [self._ew(e, v) for e, v in deps.items()]
        waits += [(self.dq[q]["sems"][r], v) for (q, r), v in ddeps.items()]

        def emit(E, fn=fn, waits=waits, sem=sem):
            for s, v in waits:
                E.wait_ge(s, v)
            fn(E).then_inc(sem, 1)

        self.ops[eng].append(emit)
        self._record((eng, idx), reads, writes)

    def dma(self, q, out, in_, reads=(), writes=()):
        deps, ddeps = self._deps(q, reads, writes)
        st = self.dq[q]
        n = st["n"]
        st["n"] += 1
        ring = n % NRING
        val = 16 * (n // NRING + 1)
        sem = st["sems"][ring]
        waits = [self._ew(e, v) for e, v in deps.items()]
        waits += [(self.dq[qq]["sems"][r], v) for (qq, r), v in ddeps.items()]
        if val > 16:
            waits.append((sem, val - 16))

        def emit(E, waits=waits, sem=sem, out=out, in_=in_):
            for s, v in waits:
                E.wait_ge(s, v)
            E.dma_start(out=out, in_=in_).then_inc(sem, 16)

        self.ops[q].append(emit)
        self._record(("dma", q, ring, val), reads, writes)

    def mm(self, ob, o, lb, l, rb, r, start=True, stop=True):
        self.op("pe", lambda E: E.matmul(o, l, r, start=start, stop=stop), reads=[lb, rb], writes=[ob])

    def tr(self, ob, o, ib, i, idb, idap):
        self.op("pe", lambda E: E.transpose(o, i, idap), reads=[ib, idb], writes=[ob])

    def act(self, ob, o, ib, i, func, bias=0.0, scale=1.0, extra=()):
        self.op("act", lambda E: E.activation(o, i, func, bias=bias, scale=scale), reads=[ib] + list(extra), writes=[ob])

    def tt(self, ob, o, ab, a, bb, b, op, eng="dve"):
        self.op(eng, lambda E: E.tensor_tensor(o, a, b, op), reads=[ab, bb], writes=[ob])

    def ts(self, ob, o, ab, a, s1, s2, op0, op1, extra=(), eng="dve"):
        self.op(eng, lambda E: E.tensor_scalar(o, a, s1, s2, op0, op1), reads=[ab] + list(extra), writes=[ob])

    def ts1(self, ob, o, ab, a, s1, op0, extra=(), eng="dve"):
        self.op(eng, lambda E: E.tensor_single_scalar(o, a, s1, op0), reads=[ab] + list(extra), writes=[ob])

    def stt(self, ob, o, ab, a, sc, bb, b, op0, op1, extra=(), eng="dve"):
        self.op(eng, lambda E: E.scalar_tensor_tensor(o, a, sc, b, op0, op1), reads=[ab, bb] + list(extra), writes=[ob])

    def cp(self, ob, o, ib, i, eng="dve"):
        if eng == "act":
            self.op("act", lambda E: E.activation(o, i, AF.Copy), reads=[ib], writes=[ob])
        else:
            self.op(eng, lambda E: E.tensor_copy(o, i), reads=[ib], writes=[ob])

    def memset(self, ob, o, v, eng="dve"):
        self.op(eng, lambda E: E.memset(o, v), writes=[ob])

    def scan(self, ob, o, d0b, d0, d1b, d1, init=0.0):
        self.op("dve", lambda E: E.tensor_tensor_scan(o, d0, d1, init, ALU.mult, ALU.add), reads=[d0b, d1b], writes=[ob])

    def finish(self):
        nc = self.nc
        fin = []
        for q, st in self.dq.items():
            n = st["n"]
            for ring in range(min(n, NRING)):
                uses = (n - ring + NRING - 1) // NRING
                fin.append((st["sems"][ring], 16 * uses))
        fin += [self._ew(e, v) for e, v in self.cnt.items() if v > 0]

        def emit_fin(E, fin=fin):
            for s, v in fin:
                E.wait_ge(s, v)

        if os.environ.get("KDEV_COUNTS"):
            print("COUNTS", dict(self.cnt), {q: st["n"] for q, st in self.dq.items()}, flush=True)
        self.ops["sp"].append(emit_fin)
        ops = self.ops
        with nc.Block() as block:
            @block.sync
            def _(E):
                for f in ops["sp"]:
                    f(E)

            @block.tensor
            def _(E):
                for f in ops["pe"]:
                    f(E)

            @block.scalar
            def _(E):
                for f in ops["act"]:
                    f(E)

            @block.vector
            def _(E):
                for f in ops["dve"]:
                    f(E)

            @block.gpsimd
            def _(E):
                for f in ops["pool"]:
                    f(E)


class Tile:
    def __init__(self, kind, idx):
        self.kind = kind
        self.idx = idx
        if kind == "p":
            self.T, self.nseq, self.L, self.s0, self.C = TP, 1, TP, 0, 64
        else:
            self.T, self.nseq, self.L, self.s0, self.C = TS, NSQ, LS, 1, LS
        self.first = (kind == "p" and idx == 0)
        self.last = (kind == "s") or (idx == NPT - 1)
        self.nch = self.T // self.C


def bc(ap, shape):
    return ap.broadcast_to(list(shape))


def build_program():
    nc = bass.Bass("TRN2", target_bir_lowering=False)

    def din(name, shape):
        return nc.dram_tensor(name, list(shape), F32, kind="ExternalInput").ap()

    def dout(name, shape):
        return nc.dram_tensor(name, list(shape), F32, kind="ExternalOutput").ap()

    NS1 = 1 + NSQ
    I = {}
    for name, shape in [
        ("xp", [D, SEQ]), ("xs", [D, TS]), ("cT", [D, NS1]),
        ("w_ada", [2, 6 * D // WS, 128, KC, WS]), ("b_adaT", [2, 128, 96]), ("g_mixT", [2, 128, KC]), ("g_ffnT", [2, 128, KC]),
        ("g_finT", [128, KC]), ("w_ffn_up", [2, 2 * DFF // WS, 128, KC, WS]), ("w_ffn_down", [2, DFF // 256, 128, 2, D]),
        ("ffn_cw", [2, 128, NFF, 3]), ("ffn_cb", [2, 128, NFF]), ("ffn_st", [2, 128, NFF, NSQ, 2]),
        ("ident", [128, 128]), ("UT", [128, 128]), ("nSU", [128, 128]), ("SL", [128, 128]),
        ("rmask_p", [128, TP]), ("rmask_s", [128, TS]), ("esel", [8, 8, 128]),
        ("w_in_ab", [n_slabs("w_in_ab"), 128, KC, WS]), ("w_out_ab", [n_slabs("w_out_ab"), 128, KC, WS]),
        ("gla_w2", [16, 512]), ("gla_b2T", [128, 4]), ("gla_ngT", [128, 2]), ("gla_st", [NSQ, 128, 4, 256]),
        ("gdn_cw", [128, 24, 4]), ("gdn_cb", [128, 24]), ("gdn_cst", [128, 24, NSQ, 3]),
        ("gdn_Alog", [8, 1]), ("gdn_dtb", [8, 1]), ("gdn_ngT", [128, 1]), ("gdn_st", [NSQ, 128, 8, 128]),
        ("w_in_cd", [n_slabs("w_in_cd"), 128, KC, WS]), ("w_out_cd", [n_slabs("w_out_cd"), 128, KC, WS]), ("ssd_cw", [128, 12, 4]), ("ssd_cb", [128, 12]),
        ("ssd_cst", [128, 12, NSQ, 3]), ("ssd_Alog", [16, 1]), ("ssd_dtb", [16, 1]), ("ssd_Dcol", [128, 8]),
        ("ssd_ngT", [128, 8]), ("ssd_st", [NSQ, 128, 8, 128]),
        ("s5_are", [128, 32]), ("s5_aim", [128, 32]), ("s5_ldt", [128, 32]), ("s5w", [8, 128, 4, 4, 128]),
        ("s5_Dcol", [128, 8]), ("s5_x0", [128, 2, 32, NSQ]), ("glu_w", [n_slabs("glu_w"), 128, 8, WS]), ("glu_bT", [128, 8]),
    ]:
        I[name] = din(name, shape)
    O = {}
    for name, shape in [
        ("yp", [D, SEQ]), ("ys", [D, TS]), ("ffn_st_o", [2, 128, NFF, NS1, 2]),
        ("gla_o", [NS1, 128, 4, 256]), ("gdn_o", [NS1, 128, 8, 128]), ("gdn_cst_o", [128, 24, NS1, 3]),
        ("ssd_o", [NS1, 128, 8, 128]), ("ssd_cst_o", [128, 12, NS1, 3]), ("s5_o", [128, 32, NS1, 2]),
    ]:
        O[name] = dout(name, shape)

    with ExitStack() as stack:
        P = Prog(nc, stack)
        P.psl = [P.ps([128, 512], F32, name="psb%d" % i) for i in range(8)]
        xT = P.sb([128, KC, TP], F32, "xT")
        hT = P.sb([128, KC, TP], F32R, "hT")
        MWt = stack.enter_context(nc.sbuf_tensor("MW", [128, 72, TP], F32))
        yacc = P.view(MWt[:, 0:16, :], "yacc")
        RWt = stack.enter_context(nc.sbuf_tensor("RW", [128, 8, TP], F32R))
        WB = [P.sb([128, KC * WS], F32R, "wb%d" % i) for i in range(NWB)]
        wbi = [0]

        def next_wb():
            b = WB[wbi[0] % NWB]
            wbi[0] += 1
            return b

        def const(name, shape, q="sp"):
            b = P.sb(shape, F32, "c_" + name)
            P.dma(q, b[:], I[name], writes=[b])
            return b

        def U(a, b, name):
            return P.view(MWt[:, a:b, :], name)

        ones = P.sb([128, 128], F32, "ones")
        P.memset(ones, ones[:], 1.0)
        ident = const("ident", [128, 128])
        UT = const("UT", [128, 128])
        nSU = const("nSU", [128, 128])
        SL = const("SL", [128, 128])
        rmask = {"p": const("rmask_p", [128, TP]), "s": const("rmask_s", [128, TS])}
        gfin = const("g_finT", [128, KC])
        modT = [P.view(MWt[:, 4 + 7 * l:11 + 7 * l, :].rearrange("p a t -> p (a t)")[:, 0:96 * NS1].rearrange(
            "p (c n) -> p c n", n=NS1), "modT%d" % l) for l in range(2)]
        modP = [P.sb([128, 96], F32, "modP%d" % l) for l in range(2)]
        modS = P.sb([128, 16, NSQ], F32, "modS")
        modD = nc.dram_tensor("modD", [2, 128, 96, NS1], F32, kind="Internal").ap()
        modDB = Buf(None, "modD")
        ffst = [P.sb([128, NFF, 2], F32, "ffst%d" % l) for l in range(2)]
        ffcw = [P.sb([128, NFF, 3], F32, "ffcw%d" % l) for l in range(2)]
        ffcb = [P.sb([128, NFF], F32, "ffcb%d" % l) for l in range(2)]
        for l in range(2):
            P.memset(ffst[l], ffst[l][:], 0.0)
            P.dma("sp", ffcw[l][:], I["ffn_cw"][l], writes=[ffcw[l]])
            P.dma("sp", ffcb[l][:], I["ffn_cb"][l], writes=[ffcb[l]])
        rstd = P.sb([128, TP], F32, "rstd")
        cvb = U(22, 26, "cvb")
        sgb = U(26, 28, "sgb")
        actT = [P.view(RWt[:, 2 * i:2 * i + 2, :], "actT%d" % i) for i in range(2)]
        cvtmp = P.sb([128, TP], F32, "cvtmp")
        cstage = P.sb([128, TP + 3 * NSQ], F32, "cstage")

        cond0 = P.view(MWt[:, 0:2, :].rearrange("p a t -> p (a t)").rearrange("p (c n) -> p c n", n=32), "cond0")
        condT = P.view(RWt[:, 0:2, :].rearrange("p a t -> p (a t)").rearrange("p (c n) -> p c n", n=32), "condT")
        P.memset(cond0, cond0[:], 0.0)
        P.dma("sp", cond0[:, :, 0:NS1], I["cT"].rearrange("(c p) n -> p c n", p=128), writes=[cond0])
        P.act(condT, condT[:], cond0, cond0[:], AF.Silu)
        for l in range(2):
            badT = P.sb([128, 96], F32, "badT%d" % l)
            gm = P.sb([128, KC], F32, "gmix%d" % l)
            gf = P.sb([128, KC], F32, "gffn%d" % l)
            P.dma("sp", badT[:], I["b_adaT"][l], writes=[badT])
            P.dma("sp", gm[:], I["g_mixT"][l], writes=[gm])
            P.dma("sp", gf[:], I["g_ffnT"][l], writes=[gf])
            nj = WS // 128
            for s in range(6 * D // WS):
                wb = next_wb()
                wbv = wb[:].rearrange("p (c n) -> p c n", n=WS)
                P.dma("pool", wbv, I["w_ada"][l][s], writes=[wb])
                ps = P.next_ps()
                for j in range(nj):
                    for kc in range(KC):
                        P.mm(ps, ps[:, j * 32:j * 32 + 32], wb, wbv[:, kc, j * 128:(j + 1) * 128], condT, condT[:, kc, :],
                             start=(kc == 0), stop=(kc == KC - 1))
                psv = ps[:, 0:32 * nj].rearrange("p (j n) -> p j n", n=32)[:, :, 0:NS1]
                P.tt(modT[l], modT[l][:, s * nj:(s + 1) * nj, :], ps, psv,
                     badT, bc(badT[:, s * nj:(s + 1) * nj].unsqueeze(2), [128, nj, NS1]), ALU.add)
            for (off, g) in ((16, gm), (64, gf)):
                P.stt(modT[l], modT[l][:, off:off + 16, :], modT[l], modT[l][:, off:off + 16, :], 1.0,
                      g, bc(g[:].unsqueeze(2), [128, 16, NS1]), ALU.add, ALU.mult)
            P.cp(modP[l], modP[l][:].unsqueeze(2), modT[l], modT[l][:, :, 0:1])
            P.dma("sp", modD[l], modT[l][:, :, :], reads=[modT[l]], writes=[modDB])

        def v4(ap, tl):
            return ap.rearrange("p c (s l) -> p c s l", l=tl.L)

        def modb(l, off):
            P.dma("sp", modS[:], modD[l][:, off:off + 16, 1:NS1], reads=[modDB], writes=[modS])
            return bc(modS[:].unsqueeze(3), [128, 16, NSQ, LS])

        def rms_stats(tl):
            T = tl.T
            P.act(yacc, yacc[:, :, :T], xT, xT[:, :, :T], AF.Square)
            ps = P.next_ps()
            for c in range(KC):
                P.mm(ps, ps[:, :T], ones, ones[:], yacc, yacc[:, c, :T], start=(c == 0), stop=(c == KC - 1))
            P.act(rstd, rstd[:, :T], ps, ps[:, :T], AF.Ln, bias=EPS, scale=1.0 / D)
            P.act(rstd, rstd[:, :T], rstd, rstd[:, :T], AF.Exp, scale=-0.5)
            P.tt(yacc, yacc[:, :, :T], xT, xT[:, :, :T], rstd, bc(rstd[:, :T].unsqueeze(1), [128, KC, T]), ALU.mult)

        def norm_mod(tl, l, sh, sc):
            T = tl.T
            P.barrier([yacc])
            rms_stats(tl)
            if tl.kind == "p":
                for c in range(KC):
                    P.act(hT, hT[:, c, :T], yacc, yacc[:, c, :T], AF.Identity, bias=modP[l][:, sh + c:sh + c + 1],
                          scale=modP[l][:, sc + c:sc + c + 1], extra=[modP[l]])
            else:
                P.tt(yacc, v4(yacc[:, :, :T], tl), yacc, v4(yacc[:, :, :T], tl), modS, modb(l, sc), ALU.mult)
                P.tt(hT, v4(hT[:, :, :T], tl), yacc, v4(yacc[:, :, :T], tl), modS, modb(l, sh), ALU.add)

        def resid_add(tl, l, goff, src):
            T = tl.T
            if tl.kind == "p":
                for c in range(KC):
                    P.stt(xT, xT[:, c, :T], src, src[:, c, :T], modP[l][:, goff + c:goff + c + 1], xT, xT[:, c, :T],
                          ALU.mult, ALU.add, extra=[modP[l]])
            else:
                P.tt(src, v4(src[:, :, :T], tl), src, v4(src[:, :, :T], tl), modS, modb(l, goff), ALU.mult)
                P.tt(xT, xT[:, :, :T], xT, xT[:, :, :T], src, src[:, :, :T], ALU.add)

        def proj_fm(tl, w_ap, col0, ncols, consume, rhs=None, nk=KC):
            T = tl.T
            rhs = rhs or hT
            wname = w_ap
            sb0 = slab_base(wname, col0)
            for s0 in range(0, ncols, WS):
                n = min(WS, ncols - s0)
                wb = next_wb()
                wbv = wb[:].rearrange("p (c n) -> p c n", n=WS)
                P.dma("pool", wbv[:, 0:nk, :], I[wname][sb0 + s0 // WS], writes=[wb])
                for j in range((n + 127) // 128):
                    m = min(128, n - j * 128)
                    ps = P.next_ps()
                    for kc in range(nk):
                        P.mm(ps, ps[0:m, :T], wb, wbv[:, kc, j * 128:j * 128 + m], rhs, rhs[:, kc, :T],
                             start=(kc == 0), stop=(kc == nk - 1))
                    consume(s0 // 128 + j, m, ps)

        def conv_chunk(tl, ps, cw, cb, ch, K, pst, sin, sout, out_b, out_ap, func):
            T, L, ns = tl.T, tl.L, tl.nseq
            H = K - 1
            sv = cstage[:, 0:ns * (L + H)].rearrange("p (s l) -> p s l", l=L + H)
            if tl.kind == "p":
                P.cp(cstage, sv[:, :, 0:H], pst, pst[:, ch:ch + 1, :])
            else:
                P.dma("sp", sv[:, :, 0:H], sin[:, ch, :, :], writes=[cstage])
            P.act(cstage, sv[:, :, H:H + L], ps, ps[:, :T].rearrange("p (s l) -> p s l", l=L), AF.Copy)
            if tl.kind == "p":
                P.cp(pst, pst[:, ch:ch + 1, :], cstage, sv[:, :, L:L + H])
                if tl.last:
                    P.dma("sp", sout[:, ch, 0:1, :], sv[:, :, L:L + H], reads=[cstage])
            else:
                P.dma("sp", sout[:, ch, 1:NS1, :], sv[:, :, L:L + H], reads=[cstage])
            acc = cvtmp[:, 0:T].rearrange("p (s l) -> p s l", l=L)
            P.ts(cvtmp, acc, cstage, sv[:, :, 0:L], cw[:, ch, 0:1], cb[:, ch:ch + 1], ALU.mult, ALU.add, extra=[cw, cb])
            for k in range(1, K):
                P.stt(cvtmp, acc, cstage, sv[:, :, k:k + L], cw[:, ch, k:k + 1], cvtmp, acc, ALU.mult, ALU.add, extra=[cw])
            ov = out_ap.rearrange("p (s l) -> p s l", l=L)
            if func is None:
                P.cp(out_b, ov, cvtmp, acc)
            else:
                P.act(out_b, ov, cvtmp, acc, func)

        def ffn(tl, l):
            T = tl.T
            P.barrier([cvb, sgb] + actT)
            wup = I["w_ffn_up"][l]
            wdn = I["w_ffn_down"][l]
            for g in range(DFF // 256):
                wba, wbg = next_wb(), next_wb()
                wva = wba[:].rearrange("p (c n) -> p c n", n=WS)
                wvg = wbg[:].rearrange("p (c n) -> p c n", n=WS)
                P.dma("pool", wva, wup[g], writes=[wba])
                P.dma("pool", wvg, wup[DFF // WS + g], writes=[wbg])
                for j in range(4):
                    ch = (g * 2 + j) if j < 2 else (44 + g * 2 + j - 2)
                    wb, wv = (wba, wva) if j < 2 else (wbg, wvg)
                    jj = j % 2
                    ps = P.next_ps()
                    for kc in range(KC):
                        P.mm(ps, ps[:, :T], wb, wv[:, kc, jj * 128:(jj + 1) * 128], hT, hT[:, kc, :T],
                             start=(kc == 0), stop=(kc == KC - 1))
                    conv_chunk(tl, ps, ffcw[l], ffcb[l], ch, 3, ffst[l], I["ffn_st"][l], O["ffn_st_o"][l],
                               cvb, cvb[:, j, :T], None)
                P.act(sgb, sgb[:, :, :T], cvb, cvb[:, 2:4, :T], AF.Silu)
                at = actT[g % 2]
                P.tt(at, at[:, :, :T], sgb, sgb[:, :, :T], cvb, cvb[:, 0:2, :T], ALU.mult)
                wd = next_wb()
                wdv = wd[:, 0:2 * D].rearrange("p (c n) -> p c n", n=D)
                P.dma("pool", wdv, wdn[g], writes=[wd])
                for oc in range(KC):
                    ps = P.next_ps()
                    for kc in range(2):
                        P.mm(ps, ps[:, :T], wd, wdv[:, kc, oc * 128:(oc + 1) * 128], at, at[:, kc, :T],
                             start=(kc == 0), stop=(kc == 1))
                    if g == 0:
                        P.cp(yacc, yacc[:, oc, :T], ps, ps[:, :T], eng="act")
                    else:
                        P.tt(yacc, yacc[:, oc, :T], yacc, yacc[:, oc, :T], ps, ps[:, :T], ALU.add)

        gla_S = [P.sb([128, 4, 256], F32, "glaS0")] * 2
        gdn_S = [P.sb([128, 8, 128], F32, "gdnS0")] * 2
        nb2 = const("gla_b2T", [128, 4])
        P.ts1(nb2, nb2[:], nb2, nb2[:], -1.0, ALU.mult)
        glang = const("gla_ngT", [128, 2])
        gdcw = const("gdn_cw", [128, 24, 4])
        gdcb = const("gdn_cb", [128, 24])
        gdnegA = const("gdn_Alog", [8, 1])
        P.act(gdnegA, gdnegA[:], gdnegA, gdnegA[:], AF.Exp)
        P.ts1(gdnegA, gdnegA[:], gdnegA, gdnegA[:], -1.0, ALU.mult)
        gddtb = const("gdn_dtb", [8, 1])
        gdng = const("gdn_ngT", [128, 1])
        gdcst = P.sb([128, 24, 3], F32, "gdcst")
        P.memset(gdcst, gdcst[:], 0.0)
        lrT = P.view(MWt[0:16, 54, :], "lrT")
        w2sb = P.view(MWt[0:16, 55:57, :].rearrange("p a t -> p (a t)"), "w2sb")
        abT = [P.sb([8, TP], F32, "abT%d" % i) for i in range(5)]
        abT.append(abT[1])
        gtok = P.sb([64, 8], F32, "gtok")
        eglast = P.sb([128, 8], F32, "eglast")
        m_q, m_k, m_v, m_r = U(0, 4, "m_q"), U(4, 8, "m_k"), U(8, 16, "m_v"), U(16, 24, "m_r")
        m_cum, m_o, m_ln = U(24, 28, "m_cum"), U(28, 36, "m_o"), U(36, 40, "m_ln")
        tmpA = [U(40 + i, 41 + i, "tmpA%d" % i) for i in range(7)]
        a_vtk, a_ktk = U(48, 52, "a_vtk"), U(52, 54, "a_ktk")
        GLA_BUFS = [m_q, m_k, m_v, m_r, m_cum, m_o, m_ln, a_vtk, a_ktk] + tmpA
        g_qkv, g_gate, g_ob, g_sq = U(0, 24, "g_qkv"), U(36, 44, "g_gate"), U(44, 52, "g_ob"), U(52, 60, "g_sq")
        g_kb, g_dec, g_A, g_Q = U(52, 54, "g_kb"), U(54, 56, "g_dec"), U(56, 58, "g_A"), U(58, 60, "g_Q")
        g_vb, g_qi, g_kw, g_ko = U(60, 62, "g_vb"), U(62, 64, "g_qi"), U(64, 66, "g_kw"), U(66, 68, "g_ko")
        g_X, g_wk = U(68, 70, "g_X"), U(70, 72, "g_wk")
        g_tokA, g_tokB = U(52, 56, "g_tokA"), U(56, 60, "g_tokB")
        g_esel = U(24, 28, "g_esel")

        def t4(buf, C, nh):
            return buf[:].rearrange("p a t -> p (a t)")[:, 0:nh * C].rearrange("p (h c) -> p h c", c=C)

        def tok(buf, C, n):
            return buf[:].rearrange("p a t -> p (a t)")[0:C, 0:n]

        def gla(tl):
            T, C, nch = tl.T, tl.C, tl.nch
            W = "w_in_ab"
            P.barrier(GLA_BUFS + [lrT, w2sb])
            P.dma("sp", w2sb[:, :], I["gla_w2"], writes=[w2sb])
            proj_fm(tl, W, 0, 512, lambda j, m, ps: P.cp(m_q, m_q[:, j, :T], ps, ps[:, :T], eng="act"))
            proj_fm(tl, W, 512, 512, lambda j, m, ps: P.cp(m_k, m_k[:, j, :T], ps, ps[:, :T], eng="act"))
            proj_fm(tl, W, 1024, 1024, lambda j, m, ps: P.cp(m_v, m_v[:, j, :T], ps, ps[:, :T], eng="act"))
            proj_fm(tl, W, 2064, 1024, lambda j, m, ps: P.act(m_r, m_r[:, j, :T], ps, ps[:, :T], AF.Silu))
            proj_fm(tl, W, 2048, 16, lambda j, m, ps: P.cp(lrT, lrT[0:16, :T], ps, ps[0:16, :T]))
            for h in range(4):
                ps = P.next_ps()
                P.mm(ps, ps[:, :T], w2sb, w2sb[0:16, h * 128:(h + 1) * 128], lrT, lrT[0:16, :T])
                P.act(m_ln, m_ln[:, h, :T], ps, ps[:, :T], AF.Exp, bias=nb2[:, h:h + 1], scale=-1.0, extra=[nb2])
            P.act(m_ln, m_ln[:, :, :T], m_ln, m_ln[:, :, :T], AF.Ln, bias=1.0)
            if SUB < 2:
                return
            rm = rmask[tl.kind]
            for h in range(4):
                P.scan(m_cum, m_cum[:, h, :T], rm, rm[:, 0:T], m_ln, m_ln[:, h, :T])
            eb, QsT, enb, KsT, dl, KoT, attT = [t4(tmpA[i], C, 4) for i in range(7)]
            bt = tmpA
            vtk = tok(a_vtk, C, 1024)
            ktk = tok(a_ktk, C, 512)
            for n in range(nch if SUB >= 3 else 0):
                c0 = n * C
                S = gla_S[n % 2] if tl.kind == "s" else gla_S[0]
                if tl.kind == "s":
                    P.dma("sp", S[:], I["gla_st"][n], writes=[S])
                elif tl.first and n == 0:
                    P.memset(S, S[:], 0.0)
                cs = slice(c0, c0 + C)
                P.act(bt[0], eb, m_cum, m_cum[:, :, cs], AF.Exp, scale=-1.0 / 16)
                P.stt(bt[1], QsT, m_q, m_q[:, :, cs], 128 ** -0.5, bt[0], eb, ALU.mult, ALU.mult)
                P.act(bt[2], enb, m_cum, m_cum[:, :, cs], AF.Exp, scale=1.0 / 16)
                P.tt(bt[3], KsT, m_k, m_k[:, :, cs], bt[2], enb, ALU.mult)
                P.tt(bt[4], dl, m_cum, bc(m_cum[:, :, c0 + C - 1:c0 + C], [128, 4, C]), m_cum, m_cum[:, :, cs], ALU.subtract)
                P.act(bt[4], dl, bt[4], dl, AF.Exp, scale=-1.0 / 16)
                P.tt(bt[5], KoT, m_k, m_k[:, :, cs], bt[4], dl, ALU.mult)
                psa = P.next_ps()
                for h in range(4):
                    P.mm(psa, psa[0:C, h * C:(h + 1) * C], bt[3], KsT[:, h, :], bt[1], QsT[:, h, :])
                P.tt(bt[6], attT[0:C], psa, psa[0:C, 0:4 * C].rearrange("p (h c) -> p h c", c=C),
                     UT, bc(UT[0:C, 0:C].unsqueeze(1), [C, 4, C]), ALU.mult)
                for half in range(2):
                    psv = P.next_ps()
                    for q4 in range(4):
                        P.tr(psv, psv[0:C, q4 * 128:(q4 + 1) * 128], m_v, m_v[:, half * 4 + q4, cs], ident, ident[:])
                    P.cp(a_vtk, vtk[:, half * 512:(half + 1) * 512], psv, psv[0:C, :], eng="act")
                psk = P.next_ps()
                for h in range(4):
                    P.tr(psk, psk[0:C, h * 128:(h + 1) * 128], bt[5], KoT[:, h, :], ident, ident[:])
                P.cp(a_ktk, ktk, psk, psk[0:C, :], eng="act")
                pso = P.next_ps()
                for h in range(4):
                    for vc in range(2):
                        a = h * 2 + vc
                        o_ap = pso[:, a * C:(a + 1) * C]
                        P.mm(pso, o_ap, a_vtk, vtk[:, a * 128:(a + 1) * 128], bt[6], attT[0:C, h, :], start=True, stop=False)
                        P.mm(pso, o_ap, S, S[:, h, vc * 128:(vc + 1) * 128], bt[1], QsT[:, h, :], start=False, stop=True)
                P.cp(m_o, m_o[:, :, cs], pso, pso[:, 0:8 * C].rearrange("p (a c) -> p a c", c=C), eng="act")
                for half in range(2):
                    pss = P.next_ps()
                    for hh in range(2):
                        h = half * 2 + hh
                        P.mm(pss, pss[:, hh * 256:(hh + 1) * 256], a_ktk, ktk[:, h * 128:(h + 1) * 128],
                             a_vtk, vtk[:, h * 256:(h + 1) * 256])
                    for hh in range(2):
                        h = half * 2 + hh
                        P.stt(S, S[:, h, :], S, S[:, h, :], eb[:, h, C - 1:C], pss, pss[:, hh * 256:(hh + 1) * 256],
                              ALU.mult, ALU.add, extra=[bt[0]])
                if tl.kind == "s":
                    P.dma("sp", O["gla_o"][1 + n], S[:], reads=[S])
                elif tl.last and n == nch - 1:
                    P.dma("sp", O["gla_o"][0], S[:], reads=[S])
            if SUB < 4:
                return
            P.act(m_v, m_v[:, :, :T], m_o, m_o[:, :, :T], AF.Square)
            for half in range(2):
                ps = P.next_ps()
                for hh in range(2):
                    h = half * 2 + hh
                    for vc in range(2):
                        P.mm(ps, ps[:, hh * T:(hh + 1) * T], ones, ones[:], m_v, m_v[:, h * 2 + vc, :T],
                             start=(vc == 0), stop=(vc == 1))
                rsv = m_ln[:, half * 2:half * 2 + 2, :T]
                P.act(m_ln, rsv, ps, ps[:, 0:2 * T].rearrange("p (h t) -> p h t", t=T), AF.Ln, bias=EPS, scale=1.0 / 256)
                P.act(m_ln, rsv, m_ln, rsv, AF.Exp, scale=-0.5)
            o4 = m_o[:, :, :T].rearrange("p (h v) t -> p h v t", v=2)
            P.tt(m_o, o4, m_o, o4, m_ln, bc(m_ln[:, :, :T].unsqueeze(2), [128, 4, 2, T]), ALU.mult)
            P.tt(m_o, m_o[:, :, :T], m_o, m_o[:, :, :T], m_r, m_r[:, :, :T], ALU.mult)
            for vc in range(2):
                P.ts1(m_o, o4[:, :, vc, :], m_o, o4[:, :, vc, :], glang[:, vc:vc + 1], ALU.mult, extra=[glang])

        def gdn(tl):
            T, C, nch = tl.T, tl.C, tl.nch
            W = "w_in_ab"
            qkv, gg, ob, sq = g_qkv, g_gate, g_ob, g_sq
            P.barrier([qkv, gg, ob, sq, g_esel])
            esel = g_esel
            eselv = g_esel[:].rearrange("p a t -> p (a t)")[0:8, :].rearrange("p (h m) -> p h m", m=128)
            P.dma("sp", eselv, I["esel"], writes=[g_esel])

            def cons_qkv(j, m, ps):
                conv_chunk(tl, ps, gdcw, gdcb, j, 4, gdcst, I["gdn_cst"], O["gdn_cst_o"], qkv, qkv[:, j, :T], AF.Silu)

            proj_fm(tl, W, 3088, 3072, cons_qkv)
            proj_fm(tl, W, 6176, 1024, lambda j, m, ps: P.act(gg, gg[:, j, :T], ps, ps[:, :T], AF.Silu))
            proj_fm(tl, W, 6160, 8, lambda j, m, ps: P.cp(abT[0], abT[0][0:8, :T], ps, ps[0:8, :T]))
            proj_fm(tl, W, 6168, 8, lambda j, m, ps: P.cp(abT[1], abT[1][0:8, :T], ps, ps[0:8, :T]))
            aT, bT, gcT, egT, ekT, beT = abT
            if SUB < 6:
                return
            P.act(aT, aT[0:8, :T], aT, aT[0:8, :T], AF.Exp, bias=gddtb[0:8, 0:1], extra=[gddtb])
            P.act(aT, aT[0:8, :T], aT, aT[0:8, :T], AF.Ln, bias=1.0)
            P.ts1(aT, aT[0:8, :T], aT, aT[0:8, :T], gdnegA[0:8, 0:1], ALU.mult, extra=[gdnegA])
            P.act(beT, beT[0:8, :T], bT, bT[0:8, :T], AF.Sigmoid)
            rm = rmask[tl.kind]
            P.scan(gcT, gcT[0:8, :T], rm, rm[0:8, 0:T], aT, aT[0:8, :T])
            P.act(egT, egT[0:8, :T], gcT, gcT[0:8, :T], AF.Exp)
            g3 = gcT[0:8, :T].rearrange("p (n c) -> p n c", c=C)
            P.tt(ekT, ekT[0:8, :T].rearrange("p (n c) -> p n c", c=C), gcT, bc(g3[:, :, C - 1:C], [8, nch, C]), gcT, g3,
                 ALU.subtract)
            P.act(ekT, ekT[0:8, :T], ekT, ekT[0:8, :T], AF.Exp)
            for which in range(2):
                src = qkv[:, which * 8:(which + 1) * 8, :T]
                P.act(sq, sq[:, :, :T], qkv, src, AF.Square)
                for pr in range(4):
                    ps = P.next_ps()
                    for hh in range(2):
                        P.mm(ps, ps[:, hh * T:(hh + 1) * T], ones, ones[:], sq, sq[:, pr * 2 + hh, :T])
                    rv = sq[:, pr * 2:pr * 2 + 2, :T]
                    P.act(sq, rv, ps, ps[:, 0:2 * T].rearrange("p (h t) -> p h t", t=T), AF.Ln, bias=EPS)
                    P.act(sq, rv, sq, rv, AF.Exp, scale=-0.5)
                if which == 0:
                    P.stt(qkv, src, qkv, src, 128 ** -0.5, sq, sq[:, :, :T], ALU.mult, ALU.mult)
                else:
                    P.tt(qkv, src, qkv, src, sq, sq[:, :, :T], ALU.mult)
            if SUB < 7:
                return
            tbufs = [g_kb, g_vb, g_qi, g_kw, g_ko, g_dec, g_A, g_Q, g_X, g_wk]
            P.barrier(tbufs)
            kbT, vbT, qiT, kwT, koT, decT, AT, QT, XT, wk = [t4(b, C, 8) for b in tbufs]
            nsteps = {64: 5, 4: 1}[C]
            p3 = lambda ps_: ps_[0:C, 0:8 * C].rearrange("p (h c) -> p h c", c=C)
            pf = lambda ps_: ps_[:, 0:8 * C].rearrange("p (h c) -> p h c", c=C)
            for n in range(nch):
                c0 = n * C
                cs = slice(c0, c0 + C)
                S = gdn_S[n % 2] if tl.kind == "s" else gdn_S[0]
                if tl.kind == "s":
                    P.dma("sp", S[:], I["gdn_st"][n], writes=[S])
                elif tl.first and n == 0:
                    P.memset(S, S[:], 0.0)
                if SUB < 8:
                    break
                if n > 0:
                    P.barrier([g_kb, g_dec, g_A, g_Q])
                qc, kc, vc = qkv[:, 0:8, cs], qkv[:, 8:16, cs], qkv[:, 16:24, cs]

                def bcast_rows(srcb):
                    ps = P.next_ps()
                    for h in range(8):
                        P.mm(ps, ps[:, h * C:(h + 1) * C], esel, eselv[:, h, :], srcb, srcb[0:8, cs])
                    return ps, ps[:, 0:8 * C].rearrange("p (h c) -> p h c", c=C)

                if CUT < 0:
                    continue
                psb, pbv = bcast_rows(beT)
                if CUT < 1:
                    P.cp(g_kb, kbT, psb, pbv)
                    continue
                P.tt(g_kb, kbT, qkv, kc, psb, pbv, ALU.mult)
                P.tt(g_vb, vbT, qkv, vc, psb, pbv, ALU.mult)
                pse, pev = bcast_rows(egT)
                P.tt(g_qi, qiT, qkv, qc, pse, pev, ALU.mult)
                P.tt(g_kw, kwT, g_kb, kbT, pse, pev, ALU.mult)
                P.cp(eglast, eglast[:, :].unsqueeze(2), pse, pev[:, :, C - 1:C])
                psk, pkv = bcast_rows(ekT)
                P.tt(g_ko, koT, qkv, kc, psk, pkv, ALU.mult)
                if CUT < 2:
                    continue
                pst = P.next_ps()
                P.tr(pst, pst[0:C, 0:8], aT, aT[0:8, cs], ident, ident[0:8, 0:8])
                P.cp(gtok, gtok[0:C, :], pst, pst[0:C, 0:8], eng="act")
                P.tt(g_wk, wk[0:C], gtok, bc(gtok[0:C, :].unsqueeze(2), [C, 8, C]), SL, bc(SL[0:C, 0:C].unsqueeze(1), [C, 8, C]),
                     ALU.mult)
                psd = P.next_ps()
                for h in range(8):
                    P.mm(psd, psd[0:C, h * C:(h + 1) * C], g_wk, wk[0:C, h, :], UT, UT[0:C, 0:C])
                P.act(g_dec, decT[0:C], psd, p3(psd), AF.Exp)
                P.tt(g_dec, decT[0:C], g_dec, decT[0:C], UT, bc(UT[0:C, 0:C].unsqueeze(1), [C, 8, C]), ALU.mult)
                if CUT < 3:
                    continue
                psm = P.next_ps()
                for h in range(8):
                    P.mm(psm, psm[0:C, h * C:(h + 1) * C], qkv, kc[:, h, :], g_kb, kbT[:, h, :])
                P.tt(g_A, AT[0:C], psm, p3(psm), g_dec, decT[0:C], ALU.mult)
                P.tt(g_A, AT[0:C], g_A, AT[0:C], nSU, bc(nSU[0:C, 0:C].unsqueeze(1), [C, 8, C]), ALU.mult)
                psa = P.next_ps()
                for h in range(8):
                    P.mm(psa, psa[0:C, h * C:(h + 1) * C], qkv, kc[:, h, :], qkv, qc[:, h, :])
                P.tt(g_wk, wk[0:C], psa, p3(psa), g_dec, decT[0:C], ALU.mult)
                pq = P.next_ps()
                for h in range(8):
                    P.tr(pq, pq[0:C, h * C:(h + 1) * C], g_A, AT[0:C, h, :], ident, ident[0:C, 0:C])
                P.cp(g_Q, QT[0:C], pq, p3(pq), eng="act")
                P.tt(g_X, XT[0:C], g_A, AT[0:C], ident, bc(ident[0:C, 0:C].unsqueeze(1), [C, 8, C]), ALU.add)
                for step in range(nsteps if CUT >= 4 else 0):
                    lastst = (step == nsteps - 1)
                    pqt = P.next_ps()
                    for h in range(8):
                        P.mm(pqt, pqt[0:C, h * C:(h + 1) * C], g_A, AT[0:C, h, :], g_Q, QT[0:C, h, :])
                    if not lastst:
                        pq2 = P.next_ps()
                        for h in range(8):
                            P.mm(pq2, pq2[0:C, h * C:(h + 1) * C], g_Q, QT[0:C, h, :], g_A, AT[0:C, h, :])
                    P.cp(g_Q, QT[0:C], pqt, p3(pqt), eng="act")
                    if not lastst:
                        P.cp(g_A, AT[0:C], pq2, p3(pq2))
                    px = P.next_ps()
                    for h in range(8):
                        P.mm(px, px[0:C, h * C:(h + 1) * C], g_Q, QT[0:C, h, :], g_X, XT[0:C, h, :])
                    P.tt(g_X, XT[0:C], g_X, XT[0:C], px, p3(px), ALU.add)
                if CUT < 5:
                    continue
                P.barrier([g_tokA, g_tokB])
                tA = tok(g_tokA, C, 1024)
                tB = tok(g_tokB, C, 1024)

                def to_tok(srcb, srcv, dstb, dstv):
                    for half in range(2):
                        pt = P.next_ps()
                        for q4 in range(4):
                            P.tr(pt, pt[0:C, q4 * 128:(q4 + 1) * 128], srcb, srcv[:, half * 4 + q4, :], ident, ident[:])
                        P.cp(dstb, dstv[:, half * 512:(half + 1) * 512], pt, pt[0:C, :], eng="act")

                to_tok(g_vb, vbT, g_tokA, tA)
                to_tok(g_kw, kwT, g_tokB, tB)
                pu = P.next_ps()
                pw = P.next_ps()
                for h in range(8):
                    P.mm(pu, pu[:, h * C:(h + 1) * C], g_tokA, tA[:, h * 128:(h + 1) * 128], g_X, XT[0:C, h, :])
                    P.mm(pw, pw[:, h * C:(h + 1) * C], g_tokB, tB[:, h * 128:(h + 1) * 128], g_X, XT[0:C, h, :])
                P.cp(g_vb, vbT, pu, pf(pu), eng="act")
                P.cp(g_kw, kwT, pw, pf(pw))
                if CUT < 6:
                    continue
                pws = P.next_ps()
                for h in range(8):
                    P.mm(pws, pws[:, h * C:(h + 1) * C], S, S[:, h, :], g_kw, kwT[:, h, :])
                P.tt(g_vb, vbT, g_vb, vbT, pws, pf(pws), ALU.subtract)
                to_tok(g_vb, vbT, g_tokA, tA)
                to_tok(g_ko, koT, g_tokB, tB)
                po = P.next_ps()
                for h in range(8):
                    o_ap = po[:, h * C:(h + 1) * C]
                    P.mm(po, o_ap, S, S[:, h, :], g_qi, qiT[:, h, :], start=True, stop=False)
                    P.mm(po, o_ap, g_tokA, tA[:, h * 128:(h + 1) * 128], g_wk, wk[0:C, h, :], start=False, stop=True)
                P.cp(ob, ob[:, :, cs], po, pf(po), eng="act")
                if CUT < 7:
                    continue
                for half in range(2):
                    pss = P.next_ps()
                    for hh in range(4):
                        h = half * 4 + hh
                        P.mm(pss, pss[:, hh * 128:(hh + 1) * 128], g_tokB, tB[:, h * 128:(h + 1) * 128],
                             g_tokA, tA[:, h * 128:(h + 1) * 128])
                    for hh in range(4):
                        h = half * 4 + hh
                        P.stt(S, S[:, h, :], S, S[:, h, :], eglast[:, h:h + 1], pss, pss[:, hh * 128:(hh + 1) * 128],
                              ALU.mult, ALU.add, extra=[eglast])
                if tl.kind == "s":
                    P.dma("sp", O["gdn_o"][1 + n], S[:], reads=[S])
                elif tl.last and n == nch - 1:
                    P.dma("sp", O["gdn_o"][0], S[:], reads=[S])
            if SUB < 9:
                return
            P.barrier([sq])
            P.act(sq, sq[:, :, :T], ob, ob[:, :, :T], AF.Square)
            for pr in range(4):
                ps = P.next_ps()
                for hh in range(2):
                    P.mm(ps, ps[:, hh * T:(hh + 1) * T], ones, ones[:], sq, sq[:, pr * 2 + hh, :T])
                rv = sq[:, pr * 2:pr * 2 + 2, :T]
                P.act(sq, rv, ps, ps[:, 0:2 * T].rearrange("p (h t) -> p h t", t=T), AF.Ln, bias=EPS, scale=1.0 / 128)
                P.act(sq, rv, sq, rv, AF.Exp, scale=-0.5)
            P.tt(ob, ob[:, :, :T], ob, ob[:, :, :T], sq, sq[:, :, :T], ALU.mult)
            P.tt(ob, ob[:, :, :T], ob, ob[:, :, :T], gg, gg[:, :, :T], ALU.mult)
            P.cp(hT, hT[:, 0:8, :T], m_o, m_o[:, :, :T], eng="act")
            P.ts1(hT, hT[:, 8:16, :T], ob, ob[:, :, :T], gdng[:, 0:1], ALU.mult, extra=[gdng])

        def mixer_ab(tl):
            T = tl.T
            gla(tl)
            if SUB >= 5:
                gdn(tl)
            else:
                P.cp(hT, hT[:, 0:8, :T], m_o, m_o[:, :, :T], eng="act")
                P.act(hT, hT[:, 8:16, :T], m_o, m_o[:, :, :T], AF.Copy, scale=0.0)
            P.barrier([yacc])
            proj_fm(tl, "w_out_ab", 0, D, lambda j, m, ps: P.cp(yacc, yacc[:, j, :T], ps, ps[:, :T], eng="act"))
            resid_add(tl, 0, 32, yacc)

        sscw = const("ssd_cw", [128, 12, 4])
        sscb = const("ssd_cb", [128, 12])
        ssnegA = const("ssd_Alog", [16, 1])
        P.act(ssnegA, ssnegA[:], ssnegA, ssnegA[:], AF.Exp)
        P.ts1(ssnegA, ssnegA[:], ssnegA, ssnegA[:], -1.0, ALU.mult)
        ssdtb = const("ssd_dtb", [16, 1])
        ssDcol = const("ssd_Dcol", [128, 8])
        ssng = const("ssd_ngT", [128, 8])
        s5Dcol = const("s5_Dcol", [128, 8])
        glub = const("glu_bT", [128, 8])
        sscst = P.sb([128, 12, 3], F32, "sscst")
        P.memset(sscst, sscst[:], 0.0)
        He = P.sb([128, 8, 128], F32, "He")
        Ho = P.sb([128, 8, 128], F32, "Ho")
        P.memset(He, He[:], 0.0)
        P.memset(Ho, Ho[:], 0.0)
        dtr = [P.sb([16, TP], F32, "dtr%d" % i) for i in range(4)]
        tk = P.sb([64, 48], F32, "tk")
        dcl = P.sb([128, 16], F32, "dcl")
        s5st = P.sb([128, 32, 2], F32, "s5st")
        P.memset(s5st, s5st[:], 0.0)
        s5fre = P.sb([128, 32], F32, "s5fre")
        s5fim = P.sb([128, 32], F32, "s5fim")
        c_z, c_xbc, c_y = U(0, 8, "c_z"), U(8, 20, "c_xbc"), U(28, 36, "c_y")
        c_u = P.view(RWt[:, :, :], "c_u")
        c_sc, c_Ap, c_cin, c_bout = U(36, 40, "c_sc"), U(40, 44, "c_Ap"), U(44, 48, "c_cin"), U(48, 56, "c_bout")
        c_xe, c_xo, c_btk, c_cbm, c_es = U(56, 60, "c_xe"), U(60, 64, "c_xo"), U(64, 65, "c_btk"), U(65, 66, "c_cbm"), U(36, 44, "c_sq")
        c_xm = U(66, 70, "c_xm")
        c_stt = U(48, 56, "c_stt")
        s_tab = [U(36 + 4 * i, 40 + 4 * i, "s_tab%d" % i) for i in range(2)]
        s_bp, s_z = U(44, 46, "s_bp"), U(46, 48, "s_z")
        s_xt = stack.enter_context(nc.sbuf_tensor("s5x", [128, 2, TP], F32R))
        s_x = P.view(s_xt[:], "s_x")
        s_yd = U(0, 8, "s_yd")
        s_z5 = P.view(RWt[:, :, :], "s_z5")
        s_tmp = U(48, 50, "s_tmp")
        s_x0 = U(50, 54, "s_x0")
        s_so = U(54, 58, "s_so")
        CD_BUFS = [c_z, c_xbc, c_y, c_u, c_sc, c_Ap, c_cin, c_bout, c_xe, c_xo, c_btk, c_cbm, c_xm]

        s5tab = nc.dram_tensor("s5tab", [4, 128, 32, TP], F32, kind="Internal").ap()
        s5tabB = Buf(None, "s5tab")

        def s5_setup():
            are = const("s5_are", [128, 32])
            aim = const("s5_aim", [128, 32])
            ldt = const("s5_ldt", [128, 32])
            w = [P.view(MWt[:, 40, i * 32:(i + 1) * 32], "s5w%d" % i) for i in range(8)]
            w += [P.view(MWt[:, 41, i * 32:(i + 1) * 32], "s5w%d" % (8 + i)) for i in range(6)]
            P.barrier(w)
            dtv, ar, th, mag, img, c_, s_, t0, t1, den = w[:10]
            P.act(dtv, dtv[:], ldt, ldt[:], AF.Exp)
            P.tt(ar, ar[:], are, are[:], dtv, dtv[:], ALU.mult)
            P.tt(th, th[:], aim, aim[:], dtv, dtv[:], ALU.mult)
            P.act(mag, mag[:], ar, ar[:], AF.Exp)
            P.act(img, img[:], ar, ar[:], AF.Exp, scale=-1.0)
            P.act(s_, s_[:], th, th[:], AF.Sin, scale=1.0 / 16)
            P.act(t0, t0[:], th, th[:], AF.Sin, scale=1.0 / 32)
            P.tt(t0, t0[:], t0, t0[:], t0, t0[:], ALU.mult)
            P.ts(c_, c_[:], t0, t0[:], -2.0, 1.0, ALU.mult, ALU.add)
            for _ in range(4):
                P.tt(t0, t0[:], c_, c_[:], c_, c_[:], ALU.mult)
                P.tt(t1, t1[:], s_, s_[:], s_, s_[:], ALU.mult)
                P.tt(s_, s_[:], s_, s_[:], c_, c_[:], ALU.mult)
                P.ts1(s_, s_[:], s_, s_[:], 2.0, ALU.mult)
                P.tt(c_, c_[:], t0, t0[:], t1, t1[:], ALU.subtract)
            lre, lim, ire, iim = w[10:14]
            w = w[:10]
            P.tt(lre, lre[:], mag, mag[:], c_, c_[:], ALU.mult)
            P.tt(lim, lim[:], mag, mag[:], s_, s_[:], ALU.mult)
            P.tt(ire, ire[:], img, img[:], c_, c_[:], ALU.mult)
            P.tt(iim, iim[:], img, img[:], s_, s_[:], ALU.mult)
            P.ts1(iim, iim[:], iim, iim[:], -1.0, ALU.mult)
            P.tt(den, den[:], are, are[:], are, are[:], ALU.mult)
            P.tt(t0, t0[:], aim, aim[:], aim, aim[:], ALU.mult)
            P.tt(den, den[:], den, den[:], t0, t0[:], ALU.add)
            P.op("dve", lambda E: E.reciprocal(den[:], den[:]), reads=[den], writes=[den])
            nr = dtv
            P.ts1(nr, nr[:], lre, lre[:], -1.0, ALU.add)
            P.tt(t0, t0[:], nr, nr[:], are, are[:], ALU.mult)
            P.tt(t1, t1[:], lim, lim[:], aim, aim[:], ALU.mult)
            P.tt(t0, t0[:], t0, t0[:], t1, t1[:], ALU.add)
            P.tt(s5fre, s5fre[:], t0, t0[:], den, den[:], ALU.mult)
            P.tt(t0, t0[:], lim, lim[:], are, are[:], ALU.mult)
            P.tt(t1, t1[:], nr, nr[:], aim, aim[:], ALU.mult)
            P.tt(t0, t0[:], t0, t0[:], t1, t1[:], ALU.subtract)
            P.tt(s5fim, s5fim[:], t0, t0[:], den, den[:], ALU.mult)
            Tre, Tim, Tt = U(0, 8, "s5Tre"), U(8, 16, "s5Tim"), U(16, 24, "s5Tt")
            Fre, Fim = U(24, 32, "s5Fre"), U(32, 40, "s5Fim")
            lre, lim, ire, iim = lre, lim, ire, iim
            P.barrier([Tre, Tim, Tt, Fre, Fim])
            for grp in range(4):
                ms = slice(grp * 8, grp * 8 + 8)
                for kind, (bre, bim) in enumerate(((lre, lim), (ire, iim))):
                    P.cp(Tre, Tre[:, :, 0:1], bre, bre[:, ms].unsqueeze(2))
                    P.cp(Tim, Tim[:, :, 0:1], bim, bim[:, ms].unsqueeze(2))
                    n = 1
                    while n < TP:
                        sre = bc(Tre[:, :, n - 1:n], [128, 8, n])
                        sim = bc(Tim[:, :, n - 1:n], [128, 8, n])
                        tv = Tt[:, :, 0:n]
                        P.tt(Tt, tv, Tim, Tim[:, :, 0:n], Tim, sim, ALU.mult)
                        P.tt(Tre, Tre[:, :, n:2 * n], Tre, Tre[:, :, 0:n], Tre, sre, ALU.mult)
                        P.tt(Tre, Tre[:, :, n:2 * n], Tre, Tre[:, :, n:2 * n], Tt, tv, ALU.subtract)
                        P.tt(Tt, tv, Tim, Tim[:, :, 0:n], Tre, sre, ALU.mult)
                        P.tt(Tim, Tim[:, :, n:2 * n], Tre, Tre[:, :, 0:n], Tim, sim, ALU.mult)
                        P.tt(Tim, Tim[:, :, n:2 * n], Tim, Tim[:, :, n:2 * n], Tt, tv, ALU.add)
                        n *= 2
                    if kind == 0:
                        P.dma("sp", s5tab[0][:, ms, :], Tre[:], reads=[Tre], writes=[s5tabB])
                        P.dma("sp", s5tab[1][:, ms, :], Tim[:], reads=[Tim], writes=[s5tabB])
                    else:
                        fre = bc(s5fre[:, ms].unsqueeze(2), [128, 8, TP])
                        fim = bc(s5fim[:, ms].unsqueeze(2), [128, 8, TP])
                        P.tt(Fre, Fre[:], Tre, Tre[:], s5fre, fre, ALU.mult)
                        P.tt(Tt, Tt[:], Tim, Tim[:], s5fim, fim, ALU.mult)
                        P.tt(Fre, Fre[:], Fre, Fre[:], Tt, Tt[:], ALU.subtract)
                        P.tt(Fim, Fim[:], Tre, Tre[:], s5fim, fim, ALU.mult)
                        P.tt(Tt, Tt[:], Tim, Tim[:], s5fre, fre, ALU.mult)
                        P.tt(Fim, Fim[:], Fim, Fim[:], Tt, Tt[:], ALU.add)
                        P.dma("sp", s5tab[2][:, ms, :], Fre[:], reads=[Fre], writes=[s5tabB])
                        P.dma("sp", s5tab[3][:, ms, :], Fim[:], reads=[Fim], writes=[s5tabB])

        def ssd_state_in(n):
            P.barrier([c_stt])
            P.dma("sp", c_stt[:].rearrange("p a t -> p (a t)")[:, 0:1024].rearrange("p (c s) -> p c s", s=128), I["ssd_st"][n],
                  writes=[c_stt])
            sv = c_stt[:].rearrange("p a t -> p (a t)")[:, 0:1024].rearrange("p (c s) -> p c s", s=128)
            for half in range(2):
                ps = P.next_ps()
                for q4 in range(4):
                    P.tr(ps, ps[:, q4 * 128:(q4 + 1) * 128], c_stt, sv[:, half * 4 + q4, :], ident, ident[:])
                pv = ps[:, :].rearrange("p (c q) -> p c q", q=128)
                P.cp(He, He[:, half * 4:half * 4 + 4, 0:64], ps, pv[:, :, 0:64])
                P.cp(Ho, Ho[:, half * 4:half * 4 + 4, 64:128], ps, pv[:, :, 64:128])

        def ssd_state_out(dst):
            P.barrier([c_stt, c_es])
            sv = c_stt[:].rearrange("p a t -> p (a t)")[:, 0:1024].rearrange("p (c s) -> p c s", s=128)
            sm = c_es[:].rearrange("p a t -> p (a t)")[:, 0:1024].rearrange("p (c s) -> p c s", s=128)
            P.tt(c_es, sm, He, He[:], Ho, Ho[:], ALU.add)
            for half in range(2):
                ps = P.next_ps()
                for q4 in range(4):
                    P.tr(ps, ps[:, q4 * 128:(q4 + 1) * 128], c_es, sm[:, half * 4 + q4, :], ident, ident[:])
                P.cp(c_stt, sv[:, half * 4:half * 4 + 4, :], ps, ps[:, :].rearrange("p (c q) -> p c q", q=128), eng="act")
            P.dma("sp", dst, sv, reads=[c_stt])

        def ssd(tl):
            T, C, nch = tl.T, tl.C, tl.nch
            dT, aT, acT, wT = dtr
            P.act(dT, dT[0:16, :T], dT, dT[0:16, :T], AF.Exp, bias=ssdtb[0:16, 0:1], extra=[ssdtb])
            P.act(dT, dT[0:16, :T], dT, dT[0:16, :T], AF.Ln, bias=1.0)
            P.ts1(aT, aT[0:16, :T], dT, dT[0:16, :T], ssnegA[0:16, 0:1], ALU.mult, extra=[ssnegA])
            rm = rmask[tl.kind]
            P.scan(acT, acT[0:16, :T], rm, rm[0:16, 0:T], aT, aT[0:16, :T])
            a3 = acT[0:16, :T].rearrange("p (n c) -> p n c", c=C)
            w3 = wT[0:16, :T].rearrange("p (n c) -> p n c", c=C)
            P.tt(wT, w3, acT, bc(a3[:, :, C - 1:C], [16, nch, C]), acT, a3, ALU.subtract)
            P.act(wT, wT[0:16, :T], wT, wT[0:16, :T], AF.Exp)
            P.tt(wT, wT[0:16, :T], wT, wT[0:16, :T], dT, dT[0:16, :T], ALU.mult)
            P.act(acT, acT[0:16, :T], acT, acT[0:16, :T], AF.Exp)
            scT = t4(c_sc, C, 16)
            Ap = t4(c_Ap, C, 16)
            cin = t4(c_cin, C, 16)
            bout = c_bout[:].rearrange("p a t -> p (a t)")[0:C, :].rearrange("p (h s) -> p h s", s=128)
            xe, xo = tok(c_xe, C, 1024), tok(c_xo, C, 1024)
            btk = tok(c_btk, C, 256)
            cbm = t4(c_cbm, C, 2)
            P.memset(c_xe, xe, 0.0)
            P.memset(c_xo, xo, 0.0)
            p3 = lambda ps_, nh: ps_[0:C, 0:nh * C].rearrange("p (h c) -> p h c", c=C)
            for n in range(nch if SC >= 2 else 0):
                c0 = n * C
                cs = slice(c0, c0 + C)
                if tl.kind == "s" and SC >= 7:
                    ssd_state_in(n)
                    P.barrier([c_bout, c_cin])
                pst = P.next_ps()
                for i, src in enumerate((aT, dT, wT)):
                    P.tr(pst, pst[0:C, i * 16:(i + 1) * 16], src, src[0:16, cs], ident, ident[0:16, 0:16])
                P.cp(tk, tk[0:C, :], pst, pst[0:C, 0:48], eng="act")
                P.tt(c_Ap, Ap[0:C], tk, bc(tk[0:C, 0:16].unsqueeze(2), [C, 16, C]), SL, bc(SL[0:C, 0:C].unsqueeze(1), [C, 16, C]),
                     ALU.mult)
                for half in range(2):
                    psd = P.next_ps()
                    for hh in range(8):
                        P.mm(psd, psd[0:C, hh * C:(hh + 1) * C], c_Ap, Ap[0:C, half * 8 + hh, :], UT, UT[0:C, 0:C])
                    P.act(c_sc, scT[0:C, half * 8:half * 8 + 8, :], psd, p3(psd, 8), AF.Exp)
                pcb = P.next_ps()
                for g in range(2):
                    P.mm(pcb, pcb[0:C, g * C:(g + 1) * C], c_xbc, c_xbc[:, 8 + g, cs], c_xbc, c_xbc[:, 10 + g, cs])
                P.tt(c_cbm, cbm[0:C], pcb, p3(pcb, 2), UT, bc(UT[0:C, 0:C].unsqueeze(1), [C, 2, C]), ALU.mult)
                sc4 = scT[0:C].rearrange("p (g h) c -> p g h c", g=2)
                P.tt(c_sc, sc4, c_sc, sc4, c_cbm, bc(cbm[0:C].unsqueeze(2), [C, 2, 8, C]), ALU.mult)
                P.tt(c_sc, scT[0:C], c_sc, scT[0:C], tk, bc(tk[0:C, 16:32].unsqueeze(2), [C, 16, C]), ALU.mult)
                if SC < 3:
                    continue
                for half in range(2):
                    pt = P.next_ps()
                    for q4 in range(4):
                        P.tr(pt, pt[0:C, q4 * 128:(q4 + 1) * 128], c_xbc, c_xbc[:, half * 4 + q4, cs], ident, ident[:])
                    pv = pt[0:C, :].rearrange("p (c q) -> p c q", q=128)
                    xev = xe[:, half * 512:(half + 1) * 512].rearrange("p (c q) -> p c q", q=128)
                    xov = xo[:, half * 512:(half + 1) * 512].rearrange("p (c q) -> p c q", q=128)
                    P.cp(c_xe, xev[:, :, 0:64], pt, pv[:, :, 0:64])
                    P.cp(c_xo, xov[:, :, 64:128], pt, pv[:, :, 64:128])
                pb = P.next_ps()
                for g in range(2):
                    P.tr(pb, pb[0:C, g * 128:(g + 1) * 128], c_xbc, c_xbc[:, 8 + g, cs], ident, ident[:])
                P.cp(c_btk, btk, pb, pb[0:C, 0:256], eng="act")
                b4 = bout.rearrange("p (g h) s -> p g h s", g=2)
                P.tt(c_bout, b4, c_btk, bc(btk.rearrange("p (g s) -> p g s", g=2).unsqueeze(2), [C, 2, 8, 128]),
                     tk, bc(tk[0:C, 32:48].rearrange("p (g h) -> p g h", g=2).unsqueeze(3), [C, 2, 8, 128]), ALU.mult)
                if SC < 4:
                    continue
                xm = c_xm[:].rearrange("p a t -> p (a t)")[0:16, 0:16 * C].rearrange("p (h c) -> p h c", c=C)
                P.tt(c_xm, xm, acT, bc(acT[0:16, cs].unsqueeze(1), [16, 16, C]),
                     ident, bc(ident[0:16, 0:16].unsqueeze(2), [16, 16, C]), ALU.mult)
                for g in range(2):
                    pse = P.next_ps()
                    for hh in range(8):
                        P.mm(pse, pse[:, hh * C:(hh + 1) * C], ones, ones[0:16, :], c_xm, xm[:, g * 8 + hh, :])
                    pev = pse[:, 0:8 * C].rearrange("p (h c) -> p h c", c=C)
                    P.tt(c_cin, cin[:, g * 8:g * 8 + 8, :], c_xbc, bc(c_xbc[:, 10 + g, cs].unsqueeze(1), [128, 8, C]), pse, pev,
                         ALU.mult)
                    P.cp(dcl, dcl[:, g * 8:g * 8 + 8].unsqueeze(2), pse, pev[:, :, C - 1:C])
                if SC < 5:
                    continue
                py = P.next_ps()
                for c in range(8):
                    o_ap = py[:, c * C:(c + 1) * C]
                    P.mm(py, o_ap, c_xe, xe[:, c * 128:(c + 1) * 128], c_sc, scT[0:C, 2 * c, :], start=True, stop=False)
                    P.mm(py, o_ap, c_xo, xo[:, c * 128:(c + 1) * 128], c_sc, scT[0:C, 2 * c + 1, :], start=False, stop=False)
                    P.mm(py, o_ap, He, He[:, c, :], c_cin, cin[:, 2 * c, :], start=False, stop=False)
                    P.mm(py, o_ap, Ho, Ho[:, c, :], c_cin, cin[:, 2 * c + 1, :], start=False, stop=True)
                P.cp(c_y, c_y[:, :, cs], py, py[:, 0:8 * C].rearrange("p (a c) -> p a c", c=C), eng="act")
                if SC < 6:
                    continue
                for half in range(2):
                    pss = P.next_ps()
                    for cc in range(4):
                        c = half * 4 + cc
                        P.mm(pss, pss[:, cc * 128:cc * 128 + 64], c_bout, bout[:, 2 * c, :], c_xe, xe[:, c * 128:c * 128 + 64])
                        P.mm(pss, pss[:, cc * 128 + 64:(cc + 1) * 128], c_bout, bout[:, 2 * c + 1, :],
                             c_xo, xo[:, c * 128 + 64:(c + 1) * 128])
                    for cc in range(4):
                        c = half * 4 + cc
                        P.stt(He, He[:, c, 0:64], He, He[:, c, 0:64], dcl[:, 2 * c:2 * c + 1], pss, pss[:, cc * 128:cc * 128 + 64],
                              ALU.mult, ALU.add, extra=[dcl])
                        P.stt(Ho, Ho[:, c, 64:128], Ho, Ho[:, c, 64:128], dcl[:, 2 * c + 1:2 * c + 2],
                              pss, pss[:, cc * 128 + 64:(cc + 1) * 128], ALU.mult, ALU.add, extra=[dcl])
                if SC < 7:
                    continue
                if tl.kind == "s":
                    ssd_state_out(O["ssd_o"][1 + n])
                    P.barrier([c_bout, c_cin, c_sc, c_Ap])
                elif tl.last and n == nch - 1:
                    ssd_state_out(O["ssd_o"][0])
            if SC < 8:
                return
            for c in range(8):
                P.stt(c_y, c_y[:, c, :T], c_xbc, c_xbc[:, c, :T], ssDcol[:, c:c + 1], c_y, c_y[:, c, :T], ALU.mult, ALU.add,
                      extra=[ssDcol])
            P.tt(c_y, c_y[:, :, :T], c_y, c_y[:, :, :T], c_z, c_z[:, :, :T], ALU.mult)
            P.barrier([c_es])
            P.act(c_es, c_es[:, :, :T], c_y, c_y[:, :, :T], AF.Square)
            ps = P.next_ps()
            for g in range(2):
                for cc in range(4):
                    P.mm(ps, ps[:, g * T:(g + 1) * T], ones, ones[:], c_es, c_es[:, g * 4 + cc, :T], start=(cc == 0), stop=(cc == 3))
            rsv = c_es[:, 0:2, :T]
            P.act(c_es, rsv, ps, ps[:, 0:2 * T].rearrange("p (g t) -> p g t", t=T), AF.Ln, bias=EPS, scale=1.0 / 512)
            P.act(c_es, rsv, c_es, rsv, AF.Exp, scale=-0.5)
            y4 = c_y[:, :, :T].rearrange("p (g c) t -> p g c t", g=2)
            P.tt(c_y, y4, c_y, y4, c_es, bc(rsv.unsqueeze(2), [128, 2, 4, T]), ALU.mult)
            for c in range(8):
                P.ts1(hT, hT[:, c, :T], c_y, c_y[:, c, :T], ssng[:, c:c + 1], ALU.mult, extra=[ssng])

        def s5(tl):
            T, L, ns = tl.T, tl.L, tl.nseq
            P.barrier(s_tab + [s_bp, s_z, s_x, s_yd, s_tmp, s_x0, s_so])
            x0v = s_x0[:].rearrange("p a t -> p (a t)")[:, 0:2 * 32 * NSQ].rearrange("p (k m s) -> p k m s", k=2, s=NSQ)
            sov = s_so[:].rearrange("p a t -> p (a t)")[:, 0:32 * NSQ * 2].rearrange("p (m s k) -> p m s k", s=NSQ, k=2)
            if tl.kind == "s":
                P.dma("sp", x0v, I["s5_x0"], writes=[s_x0])
            onesrow = bc(ones[:, 0:1], [128, T])
            TL = L
            for c in range(8):
                wb = next_wb()
                wv = wb[:, 0:16 * 128].rearrange("p (k m q) -> p k m q", k=4, q=128)
                P.dma("pool", wv, I["s5w"][c], writes=[wb])
                pyr = P.next_ps()
                pyi = P.next_ps()
                for mm_ in range(4):
                    m = 4 * c + mm_
                    tb = s_tab[m % 2]
                    tv = tb[:].rearrange("p a t -> p (a t)")[:, 0:4 * TL].rearrange("p (k t) -> p k t", k=4)
                    P.dma("sp", tv, s5tab[:, :, m, 0:TL].rearrange("k p t -> p k t"), reads=[s5tabB], writes=[tb])

                    def tab(k):
                        return bc(tv[:, k, :].unsqueeze(1), [128, ns, L])

                    pb = P.next_ps()
                    P.mm(pb, pb[:, 0:T], wb, wv[:, 0, mm_, :], c_u, c_u[:, c, :T])
                    P.mm(pb, pb[:, T:2 * T], wb, wv[:, 1, mm_, :], c_u, c_u[:, c, :T])
                    bur = pb[:, 0:T].rearrange("p (s l) -> p s l", l=L)
                    bui = pb[:, T:2 * T].rearrange("p (s l) -> p s l", l=L)
                    bp = s_bp[:, :, :T].rearrange("p k (s l) -> p k s l", l=L)
                    tm = s_tmp[:, :, :T].rearrange("p k (s l) -> p k s l", l=L)
                    P.tt(s_bp, bp[:, 0], pb, bur, tb, tab(2), ALU.mult)
                    P.tt(s_tmp, tm[:, 0], pb, bui, tb, tab(3), ALU.mult)
                    P.tt(s_bp, bp[:, 0], s_bp, bp[:, 0], s_tmp, tm[:, 0], ALU.subtract)
                    P.tt(s_bp, bp[:, 1], pb, bui, tb, tab(2), ALU.mult)
                    P.tt(s_tmp, tm[:, 1], pb, bur, tb, tab(3), ALU.mult)
                    P.tt(s_bp, bp[:, 1], s_bp, bp[:, 1], s_tmp, tm[:, 1], ALU.add)
                    if tl.kind == "s":
                        for k in range(2):
                            P.tt(s_bp, bp[:, k, :, 0:1], s_bp, bp[:, k, :, 0:1], s_x0, x0v[:, k, m, :].unsqueeze(2), ALU.add)
                        for k in range(2):
                            P.scan(s_z, s_z[:, k, :T], rmask["s"], rmask["s"][:, 0:T], s_bp, s_bp[:, k, :T])
                    else:
                        for k in range(2):
                            P.op("dve", lambda E, k=k, m=m: E.tensor_tensor_scan(
                                s_z[:, k, :T], onesrow, s_bp[:, k, :T], s5st[:, m, k:k + 1], ALU.mult, ALU.add),
                                reads=[ones, s_bp, s5st], writes=[s_z])
                    zv = s_z[:, :, :T].rearrange("p k (s l) -> p k s l", l=L)
                    xv = s_x[:, :, :T].rearrange("p k (s l) -> p k s l", l=L)
                    P.tt(s_tmp, tm[:, 0], s_z, zv[:, 1], tb, tab(1), ALU.mult)
                    P.tt(s_tmp, tm[:, 1], s_z, zv[:, 0], tb, tab(0), ALU.mult)
                    P.tt(s_x, xv[:, 0], s_tmp, tm[:, 1], s_tmp, tm[:, 0], ALU.subtract)
                    P.tt(s_tmp, tm[:, 0], s_z, zv[:, 0], tb, tab(1), ALU.mult)
                    P.tt(s_tmp, tm[:, 1], s_z, zv[:, 1], tb, tab(0), ALU.mult)
                    P.tt(s_x, xv[:, 1], s_tmp, tm[:, 1], s_tmp, tm[:, 0], ALU.add)
                    if tl.kind == "s":
                        P.cp(s_so, sov[:, m].rearrange("p s k -> p k s").unsqueeze(3), s_x, xv[:, :, :, L - 1:L].bitcast(F32))
                    else:
                        P.cp(s5st, s5st[:, m, :].unsqueeze(2), s_x, s_x[:, :, T - 1:T].bitcast(F32))
                    P.mm(pyr, pyr[:, :T], wb, wv[:, 2, mm_, :], s_x, s_x[:, 0, :T], start=(mm_ == 0), stop=(mm_ == 3))
                    P.mm(pyi, pyi[:, :T], wb, wv[:, 3, mm_, :], s_x, s_x[:, 1, :T], start=(mm_ == 0), stop=(mm_ == 3))
                P.cp(s_tmp, s_tmp[:, 0, :T], pyi, pyi[:, :T], eng="act")
                P.tt(s_yd, s_yd[:, c, :T], pyr, pyr[:, :T], s_tmp, s_tmp[:, 0, :T], ALU.subtract)
                P.stt(s_yd, s_yd[:, c, :T], c_u, c_u[:, c, :T].bitcast(F32), s5Dcol[:, c:c + 1], s_yd, s_yd[:, c, :T],
                      ALU.mult, ALU.add, extra=[s5Dcol])
            if tl.kind == "s":
                P.dma("sp", O["s5_o"][:, :, 1:NS1, :], sov, reads=[s_so])
            elif tl.last:
                P.dma("sp", O["s5_o"][:, :, 0:1, :], s5st[:].unsqueeze(2), reads=[s5st])
            P.barrier([s_z5])
            P.act(s_z5, s_z5[:, :, :T], s_yd, s_yd[:, :, :T], AF.Gelu)

            def cons_glu(j, mrows, ps):
                P.act(s_tmp, s_tmp[:, 0, :T], ps, ps[:, :T], AF.Sigmoid, bias=glub[:, j:j + 1], extra=[glub])
                P.tt(hT, hT[:, 8 + j, :T], s_z5, s_z5[:, j, :T].bitcast(F32), s_tmp, s_tmp[:, 0, :T], ALU.mult)

            proj_fm(tl, "glu_w", 0, 1024, cons_glu, rhs=s_z5, nk=8)

        def mixer_cd(tl):
            T = tl.T
            W = "w_in_cd"
            P.barrier(CD_BUFS)
            PJ = int(os.environ.get("KDEV_PJ", "15"))
            if PJ & 1:
                proj_fm(tl, W, 0, 1024, lambda j, m, ps: P.act(c_z, c_z[:, j, :T], ps, ps[:, :T], AF.Silu))
            if PJ & 2:
                proj_fm(tl, W, 1024, 1536, lambda j, m, ps: conv_chunk(
                    tl, ps, sscw, sscb, j, 4, sscst, I["ssd_cst"], O["ssd_cst_o"], c_xbc, c_xbc[:, j, :T], AF.Silu))
            if PJ & 4:
                proj_fm(tl, W, 2560, 16, lambda j, m, ps: P.cp(dtr[0], dtr[0][0:16, :T], ps, ps[0:16, :T]))
            if PJ & 8:
                proj_fm(tl, W, 2576, 1024, lambda j, m, ps: P.cp(c_u, c_u[:, j, :T], ps, ps[:, :T], eng="act"))
            if CDCUT >= 3:
                ssd(tl)
            if CDCUT >= 4:
                s5(tl)
            if CDCUT < 4:
                return
            P.barrier([yacc])
            proj_fm(tl, "w_out_cd", 0, D, lambda j, m, ps: P.cp(yacc, yacc[:, j, :T], ps, ps[:, :T], eng="act"))
            resid_add(tl, 1, 32, yacc)

        def final_out(tl):
            T = tl.T
            P.barrier([yacc])
            rms_stats(tl)
            for c in range(KC):
                P.act(yacc, yacc[:, c, :T], yacc, yacc[:, c, :T], AF.Copy, scale=gfin[:, c:c + 1], extra=[gfin])
            if tl.kind == "p":
                P.dma("sp", ypv[:, :, tl.idx * TP:(tl.idx + 1) * TP], yacc[:, :, :T], reads=[yacc])
            else:
                P.dma("sp", ysv, yacc[:, :, :T], reads=[yacc])

        if STAGE >= 3:
            s5_setup()
        tiles = [Tile("p", i) for i in range(NPT)] + [Tile("s", 0)]
        xpv = I["xp"].rearrange("(c p) t -> p c t", p=128)
        xsv = I["xs"].rearrange("(c p) t -> p c t", p=128)
        ypv = O["yp"].rearrange("(c p) t -> p c t", p=128)
        ysv = O["ys"].rearrange("(c p) t -> p c t", p=128)
        for tl in tiles:
            if tl.kind == "p":
                P.dma("sp", xT[:, :, :tl.T], xpv[:, :, tl.idx * TP:(tl.idx + 1) * TP], writes=[xT])
            else:
                P.dma("sp", xT[:, :, :tl.T], xsv, writes=[xT])
            for l in range(2):
                norm_mod(tl, l, 0, 16)
                if l == 0 and STAGE >= 2:
                    mixer_ab(tl)
                if l == 1 and STAGE >= 3 and CDCUT >= 2:
                    mixer_cd(tl)
                norm_mod(tl, l, 48, 64)
                ffn(tl, l)
                resid_add(tl, l, 80, yacc)
            final_out(tl)
        P.finish()
    return nc


def _fm(v):
    v = np.asarray(v, np.float32)
    return np.ascontiguousarray(v.reshape(-1, 128).T)


def _consts():
    i = np.arange(128)
    c = {}
    c["ident"] = np.eye(128, dtype=np.float32)
    c["UT"] = (i[None, :] >= i[:, None]).astype(np.float32)
    c["nSU"] = -(i[None, :] > i[:, None]).astype(np.float32)
    c["SL"] = (i[:, None] > i[None, :]).astype(np.float32)
    rp = np.ones((128, TP), np.float32)
    rp[:, ::64] = 0.0
    rs = np.ones((128, TS), np.float32)
    rs[:, ::LS] = 0.0
    c["rmask_p"], c["rmask_s"] = rp, rs
    es = np.zeros((8, 8, 128), np.float32)
    for h in range(8):
        es[h, h, :] = 1.0
    c["esel"] = es
    return c


def make_in_maps(inp):
    maps = []
    cst = _consts()
    assert WS == 256
    wt = {}
    wt["w_ada"] = np.stack([tile_cols_all(inp["w_ada"][l]) for l in range(2)])
    wt["w_ffn_up"] = np.stack([tile_cols_all(inp["w_ffn_up"][l]) for l in range(2)])
    wt["w_ffn_down"] = np.ascontiguousarray(inp["w_ffn_down"].reshape(2, DFF // 256, 2, 128, D).transpose(0, 1, 3, 2, 4))
    wt["w_in_ab"] = tile_weight(inp["w_in_ab"][0], "w_in_ab")
    wt["w_out_ab"] = tile_weight(inp["w_out_ab"][0], "w_out_ab")
    wt["w_in_cd"] = tile_weight(inp["w_in_cd"][0], "w_in_cd")
    wt["w_out_cd"] = tile_weight(inp["w_out_cd"][0], "w_out_cd")
    wt["glu_w"] = tile_weight(inp["s5_glu_w"][0], "glu_w")
    Bre, Bim, Cre, Cim = (inp[k][0] for k in ("s5_B_re", "s5_B_im", "s5_C_re", "s5_C_im"))
    cst_s5w = np.zeros((8, 128, 4, 4, 128), np.float32)
    for c in range(8):
        for mm_ in range(4):
            for g2 in range(2):
                gl = 2 * mm_ + g2
                g = 8 * c + gl
                cst_s5w[c, gl * 16:(gl + 1) * 16, 0, mm_, g2 * 64:(g2 + 1) * 64] = Bre[g].T
                cst_s5w[c, gl * 16:(gl + 1) * 16, 1, mm_, g2 * 64:(g2 + 1) * 64] = Bim[g].T
                cst_s5w[c, g2 * 64:(g2 + 1) * 64, 2, mm_, gl * 16:(gl + 1) * 16] = Cre[g].T
                cst_s5w[c, g2 * 64:(g2 + 1) * 64, 3, mm_, gl * 16:(gl + 1) * 16] = Cim[g].T
    for c in range(NCORES):
        b = c // 2
        sl = slice(NSQ * c, NSQ * (c + 1))
        m = dict(cst)
        m["xp"] = inp["x_prompt"][b, :SEQ].T
        m["xs"] = inp["x_sample"][sl].reshape(TS, D).T
        m["cT"] = np.concatenate([inp["c_prompt"][b:b + 1], inp["c_sample"][sl]], 0).T
        m.update(wt)
        m["b_adaT"] = np.stack([_fm(inp["b_ada"][l]) for l in range(2)])
        m["g_mixT"] = np.stack([_fm(inp["g_mix"][l]) for l in range(2)])
        m["g_ffnT"] = np.stack([_fm(inp["g_ffn"][l]) for l in range(2)])
        m["g_finT"] = _fm(inp["g_final"])
        m["ffn_cw"] = inp["ffn_conv_w"].reshape(2, 3, NFF, 128).transpose(0, 3, 2, 1)
        m["ffn_cb"] = inp["ffn_conv_b"].reshape(2, NFF, 128).transpose(0, 2, 1)
        m["ffn_st"] = inp["state_ffn_conv"][:, sl].reshape(2, NSQ, 2, NFF, 128).transpose(0, 4, 3, 1, 2)
        m["gla_w2"] = inp["gla_w2"][0]
        m["gla_b2T"] = _fm(inp["gla_b2"][0])
        m["gla_ngT"] = _fm(inp["gla_norm_g"][0])
        m["gla_st"] = inp["state_gla"][0, sl].transpose(0, 2, 1, 3)
        m["gdn_cw"] = inp["gdn_conv_w"][0].reshape(4, 24, 128).transpose(2, 1, 0)
        m["gdn_cb"] = inp["gdn_conv_b"][0].reshape(24, 128).T
        m["gdn_cst"] = inp["state_gdn_conv"][0, sl].reshape(NSQ, 3, 24, 128).transpose(3, 2, 0, 1)
        m["gdn_Alog"] = inp["gdn_A_log"][0].reshape(8, 1)
        m["gdn_dtb"] = inp["gdn_dt_bias"][0].reshape(8, 1)
        m["gdn_ngT"] = inp["gdn_norm_g"][0].reshape(128, 1)
        m["gdn_st"] = inp["state_gdn"][0, sl].transpose(0, 2, 1, 3)
        m["ssd_cw"] = inp["ssd_conv_w"][0].reshape(4, 12, 128).transpose(2, 1, 0)
        m["ssd_cb"] = inp["ssd_conv_b"][0].reshape(12, 128).T
        m["ssd_cst"] = inp["state_ssd_conv"][0, sl].reshape(NSQ, 3, 12, 128).transpose(3, 2, 0, 1)
        m["ssd_Alog"] = inp["ssd_A_log"][0].reshape(16, 1)
        m["ssd_dtb"] = inp["ssd_dt_bias"][0].reshape(16, 1)
        m["ssd_Dcol"] = np.repeat(inp["ssd_D"][0], 64).reshape(8, 128).T
        m["ssd_ngT"] = _fm(inp["ssd_norm_g"][0])
        m["ssd_st"] = inp["state_ssd"][0, sl].reshape(NSQ, 8, 128, 128).transpose(0, 2, 1, 3)
        modes = lambda a: a.reshape(32, 128).T
        m["s5_are"] = modes(inp["s5_A_re"][0])
        m["s5_aim"] = modes(inp["s5_A_im"][0])
        m["s5_ldt"] = modes(np.repeat(inp["s5_log_dt"][0][:, None], 64, axis=1))
        m["s5w"] = cst_s5w
        m["s5_Dcol"] = _fm(inp["s5_D"][0])
        m["s5_x0"] = np.stack([inp["state_s5_re"][0, sl].reshape(NSQ, 32, 128).transpose(2, 1, 0),
                               inp["state_s5_im"][0, sl].reshape(NSQ, 32, 128).transpose(2, 1, 0)], 1)
        m["glu_bT"] = _fm(inp["s5_glu_b"][0])
        maps.append({k: np.ascontiguousarray(v, dtype=np.float32) for k, v in m.items()})
    return maps


_NC_CACHE = {}


def run_device(inp):
    if "nc" not in _NC_CACHE:
        _NC_CACHE["nc"] = build_program()
    nc = _NC_CACHE["nc"]
    maps = make_in_maps(inp)
    if RUNCORES < NCORES:
        res = run_bass_kernel_spmd(nc, maps[:RUNCORES], core_ids=list(range(RUNCORES)))
        return [res.results[min(c, RUNCORES - 1)] for c in range(NCORES)]
    res = run_bass_kernel_spmd(nc, maps, core_ids=list(range(NCORES)))
    return res.results


def assemble(R):
    B, DB = 4, 128
    y_p = np.stack([R[2 * b]["yp"].T for b in range(B)])
    y_s = np.concatenate([R[c]["ys"].T.reshape(NSQ, LS, D) for c in range(NCORES)], 0)

    def ffn_un(a):
        return a.transpose(0, 3, 4, 2, 1).reshape(2, a.shape[3], 2, 2 * DFF)

    ffn_p = np.concatenate([ffn_un(R[2 * b]["ffn_st_o"][:, :, :, 0:1]) for b in range(B)], 1)
    ffn_s = np.concatenate([ffn_un(R[c]["ffn_st_o"][:, :, :, 1:]) for c in range(NCORES)], 1)

    def st_un(a):
        return a.transpose(0, 2, 1, 3)[None]

    gla_p = np.concatenate([st_un(R[2 * b]["gla_o"][0:1]) for b in range(B)], 1)
    gla_s = np.concatenate([st_un(R[c]["gla_o"][1:]) for c in range(NCORES)], 1)
    gdn_p = np.concatenate([st_un(R[2 * b]["gdn_o"][0:1]) for b in range(B)], 1)
    gdn_s = np.concatenate([st_un(R[c]["gdn_o"][1:]) for c in range(NCORES)], 1)

    def cv_un(a, nchn):
        return a.transpose(2, 3, 1, 0).reshape(1, a.shape[2], 3, nchn * 128)

    gdc_p = np.concatenate([cv_un(R[2 * b]["gdn_cst_o"][:, :, 0:1], 24) for b in range(B)], 1)
    gdc_s = np.concatenate([cv_un(R[c]["gdn_cst_o"][:, :, 1:], 24) for c in range(NCORES)], 1)
    def ssd_un(a):
        return a.transpose(0, 2, 1, 3).reshape(1, a.shape[0], 16, 64, 128)

    ssd_p = np.concatenate([ssd_un(R[2 * b]["ssd_o"][0:1]) for b in range(B)], 1)
    ssd_s = np.concatenate([ssd_un(R[c]["ssd_o"][1:]) for c in range(NCORES)], 1)
    ssc_p = np.concatenate([cv_un(R[2 * b]["ssd_cst_o"][:, :, 0:1], 12) for b in range(B)], 1)
    ssc_s = np.concatenate([cv_un(R[c]["ssd_cst_o"][:, :, 1:], 12) for c in range(NCORES)], 1)

    def s5_un(a, k):
        a = a[:, :, :, k]
        return a.transpose(2, 1, 0).reshape(1, a.shape[2], 64, 64)

    s5 = [np.concatenate([s5_un(R[2 * b]["s5_o"][:, :, 0:1], k) for b in range(B)], 1) for k in range(2)]
    s5s = [np.concatenate([s5_un(R[c]["s5_o"][:, :, 1:], k) for c in range(NCORES)], 1) for k in range(2)]
    out = (y_p, y_s, gla_p, gla_s, gdn_p, gdn_s, gdc_p, gdc_s,
           ssd_p, ssd_s, ssc_p, ssc_s, s5[0], s5s[0], s5[1], s5s[1], ffn_p, ffn_s)
    return tuple(np.ascontiguousarray(o, dtype=np.float32) for o in out)


def kernel(**inputs):
    inp = {k: np.asarray(v) for k, v in inputs.items()}
    R = run_device(inp)
    return assemble(R)
```

```python
import os
import numpy as np
from contextlib import ExitStack
import concourse.bass as bass
import concourse.mybir as mybir
from concourse.bass_utils import run_bass_kernel_spmd

F32 = mybir.dt.float32
F32R = mybir.dt.float32r
ALU = mybir.AluOpType
AF = mybir.ActivationFunctionType

NRING = 12
EPOCH = int(os.environ.get("KDEV_EPOCH", "1000000000"))
NEPOCH = 4
LAZY_PE_INC = bool(int(os.environ.get("KDEV_LAZY", "1")))
SAME_ENGINE_WAITS = bool(int(os.environ.get("KDEV_SEW", "0")))
NCORES = 8
RUNCORES = int(os.environ.get("KDEV_CORES", "8"))
D = 2048
KC = 16
DFF = 5632
NFF = 88
TP = 256
SEQ = int(os.environ.get("KDEV_SEQ", "2048"))
NPT = SEQ // TP
NSQ = 16
LS = 4
TS = NSQ * LS
EPS = 1e-6
STAGE = int(os.environ.get("KDEV_STAGE", "9"))
SUB = int(os.environ.get("KDEV_SUB", "99"))
CUT = int(os.environ.get("KDEV_CUT", "99"))
CDCUT = int(os.environ.get("KDEV_CDCUT", "99"))
SC = int(os.environ.get("KDEV_SC", "99"))
WS = 256
NWB = 3


PROJ_TAB = {
    "w_in_ab": [(0, 512), (512, 512), (1024, 1024), (2064, 1024), (2048, 16), (3088, 3072), (6176, 1024), (6160, 8), (6168, 8)],
    "w_in_cd": [(0, 1024), (1024, 1536), (2560, 16), (2576, 1024)],
    "w_out_ab": [(0, 2048)],
    "w_out_cd": [(0, 2048)],
    "glu_w": [(0, 1024)],
}


def slab_base(name, col0):
    base = 0
    for (c0, n) in PROJ_TAB[name]:
        if c0 == col0:
            return base
        base += (n + WS - 1) // WS
    raise KeyError((name, col0))


def n_slabs(name):
    return sum((n + WS - 1) // WS for (_, n) in PROJ_TAB[name])


def tile_weight(W, name):
    K = W.shape[0]
    nk = K // 128
    out = []
    for (c0, n) in PROJ_TAB[name]:
        for s0 in range(0, n, WS):
            m = min(WS, n - s0)
            blk = np.zeros((128, nk, WS), np.float32)
            blk[:, :, :m] = W[:, c0 + s0:c0 + s0 + m].reshape(nk, 128, m).transpose(1, 0, 2)
            out.append(blk)
    return np.stack(out)


def tile_cols_all(W):
    K, N = W.shape
    nk = K // 128
    return np.ascontiguousarray(W.reshape(nk, 128, N // WS, WS).transpose(2, 1, 0, 3))


class Buf:
    __slots__ = ("t", "name", "w", "r")

    def __init__(self, t, name):
        self.t = t
        self.name = name
        self.w = None
        self.r = []

    def __getitem__(self, k):
        return self.t[k]


class Prog:
    ENG = ("pe", "act", "dve", "pool", "sp")

    def __init__(self, nc, stack):
        self.nc = nc
        self.stack = stack
        self.ops = {e: [] for e in self.ENG}
        self.cnt = {e: 0 for e in self.ENG if e != "sp"}
        self.sem = {e: [stack.enter_context(nc.semaphore("s_%s%d" % (e, k))) for k in range(NEPOCH)] for e in self.cnt}
        self.dq = {}
        for q in ("sp", "pool"):
            self.dq[q] = dict(
                sems=[stack.enter_context(nc.semaphore("d_%s%d" % (q, i))) for i in range(NRING)], n=0)
        self.nbuf = 0
        self.psl = []
        self.psi = 0
        self.sew = True

    def sb(self, shape, dtype=F32, name=None):
        self.nbuf += 1
        name = name or "sb%d" % self.nbuf
        t = self.stack.enter_context(self.nc.sbuf_tensor(name, list(shape), dtype))
        return Buf(t, name)

    def ps(self, shape, dtype=F32, name=None):
        self.nbuf += 1
        name = name or "ps%d" % self.nbuf
        t = self.stack.enter_context(self.nc.psum_tensor(name, list(shape), dtype))
        return Buf(t, name)

    def view(self, ap, name):
        self.nbuf += 1
        return Buf(ap, name)

    def barrier(self, bufs):
        toks = [(e, v) for e, v in self.cnt.items() if v > 0]
        for q, st in self.dq.items():
            n = st["n"]
            for ring in range(min(n, NRING)):
                uses = (n - ring + NRING - 1) // NRING
                toks.append(("dma", q, ring, 16 * uses))
        for b in bufs:
            b.w = None
            b.r = list(toks)

    def next_ps(self):
        b = self.psl[self.psi % len(self.psl)]
        self.psi += 1
        return b

    def _deps(self, eng, reads, writes):
        deps = {}
        ddeps = {}

        def add(tok):
            if tok is None:
                return
            if tok[0] == "dma":
                k = (tok[1], tok[2])
                ddeps[k] = max(ddeps.get(k, 0), tok[3])
            else:
                e, idx = tok
                if e == eng and (e == "pe" or not (SAME_ENGINE_WAITS or self.sew)):
                    return
                deps[e] = max(deps.get(e, 0), idx)

        for b in reads:
            add(b.w)
        for b in writes:
            add(b.w)
            for t in b.r:
                add(t)
        return deps, ddeps

    def _record(self, tok, reads, writes):
        for b in reads:
            b.r.append(tok)
            if len(b.r) > 48:
                last = {}
                for t in b.r:
                    k = t[:3] if t[0] == "dma" else t[0]
                    if k not in last or t[-1] > last[k][-1]:
                        last[k] = t
                b.r = list(last.values())
        for b in writes:
            b.w = tok
            b.r = []

    def _ew(self, e, v):
        k = (v - 1) // EPOCH
        return (self.sem[e][k], v - k * EPOCH)

    def op(self, eng, fn, reads=(), writes=(), inc=True):
        deps, ddeps = self._deps(eng, reads, writes)
        if inc:
            self.cnt[eng] += 1
            idx = self.cnt[eng]
        else:
            idx = self.cnt[eng] + 1
        sem = self.sem[eng][(idx - 1) // EPOCH]
        waits = [self._ew(e, v) for e, v in deps.items()]
        waits += [(self.dq[q]["sems"][r], v) for (q, r), v in ddeps.items()]

        def emit(E, fn=fn, waits=waits, sem=sem, inc=inc):
            for s, v in waits:
                E.wait_ge(s, v)
            ins = fn(E)
            if inc:
                ins.then_inc(sem, 1)

        self.ops[eng].append(emit)
        self._record((eng, idx), reads, writes)

    def dma(self, q, out, in_, reads=(), writes=()):
        deps, ddeps = self._deps(q, reads, writes)
        st = self.dq[q]
        n = st["n"]
        st["n"] += 1
        ring = n % NRING
        val = 16 * (n // NRING + 1)
        sem = st["sems"][ring]
        waits = [self._ew(e, v) for e, v in deps.items()]
        waits += [(self.dq[qq]["sems"][r], v) for (qq, r), v in ddeps.items()]
        if val > 16:
            waits.append((sem, val - 16))

        def emit(E, waits=waits, sem=sem, out=out, in_=in_):
            for s, v in waits:
                E.wait_ge(s, v)
            E.dma_start(out=out, in_=in_).then_inc(sem, 16)

        self.ops[q].append(emit)
        self._record(("dma", q, ring, val), reads, writes)

    def mm(self, ob, o, lb, l, rb, r, start=True, stop=True):
        self.op("pe", lambda E: E.matmul(o, l, r, start=start, stop=stop), reads=[lb, rb], writes=[ob],
                inc=(stop or not LAZY_PE_INC))

    def tr(self, ob, o, ib, i, idb, idap):
        self.op("pe", lambda E: E.transpose(o, i, idap), reads=[ib, idb], writes=[ob])

    def act(self, ob, o, ib, i, func, bias=0.0, scale=1.0, extra=()):
        self.op("act", lambda E: E.activation(o, i, func, bias=bias, scale=scale), reads=[ib] + list(extra), writes=[ob])

    def tt(self, ob, o, ab, a, bb, b, op, eng="dve"):
        self.op(eng, lambda E: E.tensor_tensor(o, a, b, op), reads=[ab, bb], writes=[ob])

    def ts(self, ob, o, ab, a, s1, s2, op0, op1, extra=(), eng="dve"):
        self.op(eng, lambda E: E.tensor_scalar(o, a, s1, s2, op0, op1), reads=[ab] + list(extra), writes=[ob])

    def ts1(self, ob, o, ab, a, s1, op0, extra=(), eng="dve"):
        self.op(eng, lambda E: E.tensor_single_scalar(o, a, s1, op0), reads=[ab] + list(extra), writes=[ob])

    def stt(self, ob, o, ab, a, sc, bb, b, op0, op1, extra=(), eng="dve"):
        self.op(eng, lambda E: E.scalar_tensor_tensor(o, a, sc, b, op0, op1), reads=[ab, bb] + list(extra), writes=[ob])

    def cp(self, ob, o, ib, i, eng="dve"):
        if eng == "act":
            self.op("act", lambda E: E.activation(o, i, AF.Copy), reads=[ib], writes=[ob])
        else:
            self.op(eng, lambda E: E.tensor_copy(o, i), reads=[ib], writes=[ob])

    def memset(self, ob, o, v, eng="dve"):
        self.op(eng, lambda E: E.memset(o, v), writes=[ob])

    def scan(self, ob, o, d0b, d0, d1b, d1, init=0.0):
        self.op("dve", lambda E: E.tensor_tensor_scan(o, d0, d1, init, ALU.mult, ALU.add), reads=[d0b, d1b], writes=[ob])

    def finish(self):
        nc = self.nc
        fin = []
        for q, st in self.dq.items():
            n = st["n"]
            for ring in range(min(n, NRING)):
                uses = (n - ring + NRING - 1) // NRING
                fin.append((st["sems"][ring], 16 * uses))
        fin += [self._ew(e, v) for e, v in self.cnt.items() if v > 0]

        def emit_fin(E, fin=fin):
            for s, v in fin:
                E.wait_ge(s, v)

        if os.environ.get("KDEV_COUNTS"):
            print("COUNTS", dict(self.cnt), {q: st["n"] for q, st in self.dq.items()}, flush=True)
        self.ops["sp"].append(emit_fin)
        ops = self.ops
        with nc.Block() as block:
            @block.sync
            def _(E):
                for f in ops["sp"]:
                    f(E)

            @block.tensor
            def _(E):
                for f in ops["pe"]:
                    f(E)

            @block.scalar
            def _(E):
                for f in ops["act"]:
                    f(E)

            @block.vector
            def _(E):
                for f in ops["dve"]:
                    f(E)

            @block.gpsimd
            def _(E):
                for f in ops["pool"]:
                    f(E)


class Tile:
    def __init__(self, kind, idx):
        self.kind = kind
        self.idx = idx
        if kind == "p":
            self.T, self.nseq, self.L, self.s0, self.C = TP, 1, TP, 0, 64
        else:
            self.T, self.nseq, self.L, self.s0, self.C = TS, NSQ, LS, 1, LS
        self.first = (kind == "p" and idx == 0)
        self.last = (kind == "s") or (idx == NPT - 1)
        self.nch = self.T // self.C


def bc(ap, shape):
    return ap.broadcast_to(list(shape))


def build_program():
    nc = bass.Bass("TRN2", target_bir_lowering=False)

    def din(name, shape):
        return nc.dram_tensor(name, list(shape), F32, kind="ExternalInput").ap()

    def dout(name, shape):
        return nc.dram_tensor(name, list(shape), F32, kind="ExternalOutput").ap()

    NS1 = 1 + NSQ
    I = {}
    for name, shape in [
        ("xp", [D, SEQ]), ("xs", [D, TS]), ("cT", [D, NS1]),
        ("w_ada", [2, 6 * D // WS, 128, KC, WS]), ("b_adaT", [2, 128, 96]), ("g_mixT", [2, 128, KC]), ("g_ffnT", [2, 128, KC]),
        ("g_finT", [128, KC]), ("w_ffn_up", [2, 2 * DFF // WS, 128, KC, WS]), ("w_ffn_down", [2, DFF // 256, 128, 2, D]),
        ("ffn_cw", [2, 128, NFF, 3]), ("ffn_cb", [2, 128, NFF]), ("ffn_st", [2, 128, NFF, NSQ, 2]),
        ("ident", [128, 128]), ("UT", [128, 128]), ("nSU", [128, 128]), ("SL", [128, 128]),
        ("rmask_p", [128, TP]), ("rmask_s", [128, TS]), ("esel", [8, 8, 128]),
        ("w_in_ab", [n_slabs("w_in_ab"), 128, KC, WS]), ("w_out_ab", [n_slabs("w_out_ab"), 128, KC, WS]),
        ("gla_w2", [16, 512]), ("gla_b2T", [128, 4]), ("gla_ngT", [128, 2]), ("gla_st", [NSQ, 128, 4, 256]),
        ("gdn_cw", [128, 24, 4]), ("gdn_cb", [128, 24]), ("gdn_cst", [128, 24, NSQ, 3]),
        ("gdn_Alog", [8, 1]), ("gdn_dtb", [8, 1]), ("gdn_ngT", [128, 1]), ("gdn_st", [NSQ, 128, 8, 128]),
        ("w_in_cd", [n_slabs("w_in_cd"), 128, KC, WS]), ("w_out_cd", [n_slabs("w_out_cd"), 128, KC, WS]), ("ssd_cw", [128, 12, 4]), ("ssd_cb", [128, 12]),
        ("ssd_cst", [128, 12, NSQ, 3]), ("ssd_Alog", [16, 1]), ("ssd_dtb", [16, 1]), ("ssd_Dcol", [128, 8]),
        ("ssd_ngT", [128, 8]), ("ssd_st", [NSQ, 128, 8, 128]),
        ("s5_are", [128, 32]), ("s5_aim", [128, 32]), ("s5_ldt", [128, 32]), ("s5w", [8, 128, 4, 4, 128]),
        ("s5_Dcol", [128, 8]), ("s5_x0", [128, 2, 32, NSQ]), ("glu_w", [n_slabs("glu_w"), 128, 8, WS]), ("glu_bT", [128, 8]),
    ]:
        I[name] = din(name, shape)
    O = {}
    for name, shape in [
        ("yp", [D, SEQ]), ("ys", [D, TS]), ("ffn_st_o", [2, 128, NFF, NS1, 2]),
        ("gla_o", [NS1, 128, 4, 256]), ("gdn_o", [NS1, 128, 8, 128]), ("gdn_cst_o", [128, 24, NS1, 3]),
        ("ssd_o", [NS1, 128, 8, 128]), ("ssd_cst_o", [128, 12, NS1, 3]), ("s5_o", [128, 32, NS1, 2]),
    ]:
        O[name] = dout(name, shape)

    with ExitStack() as stack:
        P = Prog(nc, stack)
        P.psl = [P.ps([128, 512], F32, name="psb%d" % i) for i in range(8)]
        xT = P.sb([128, KC, TP], F32, "xT")
        hT = P.sb([128, KC, TP], F32R, "hT")
        MWt = stack.enter_context(nc.sbuf_tensor("MW", [128, 72, TP], F32))
        yacc = P.view(MWt[:, 0:16, :], "yacc")
        RWt = stack.enter_context(nc.sbuf_tensor("RW", [128, 8, TP], F32R))
        WB = [P.sb([128, KC * WS], F32R, "wb%d" % i) for i in range(NWB)]
        wbi = [0]

        def next_wb():
            b = WB[wbi[0] % NWB]
            wbi[0] += 1
            return b

        def const(name, shape, q="sp"):
            b = P.sb(shape, F32, "c_" + name)
            P.dma(q, b[:], I[name], writes=[b])
            return b

        def U(a, b, name):
            return P.view(MWt[:, a:b, :], name)

        ones = P.sb([128, 128], F32, "ones")
        P.memset(ones, ones[:], 1.0)
        ident = const("ident", [128, 128])
        UT = const("UT", [128, 128])
        nSU = const("nSU", [128, 128])
        SL = const("SL", [128, 128])
        rmask = {"p": const("rmask_p", [128, TP]), "s": const("rmask_s", [128, TS])}
        gfin = const("g_finT", [128, KC])
        modT = [P.view(MWt[:, 4 + 7 * l:11 + 7 * l, :].rearrange("p a t -> p (a t)")[:, 0:96 * NS1].rearrange(
            "p (c n) -> p c n", n=NS1), "modT%d" % l) for l in range(2)]
        modP = [P.sb([128, 96], F32, "modP%d" % l) for l in range(2)]
        modS = P.sb([128, 16, NSQ], F32, "modS")
        modD = nc.dram_tensor("modD", [2, 128, 96, NS1], F32, kind="Internal").ap()
        modDB = Buf(None, "modD")
        ffst = [P.sb([128, NFF, 2], F32, "ffst%d" % l) for l in range(2)]
        ffcw = [P.sb([128, NFF, 3], F32, "ffcw%d" % l) for l in range(2)]
        ffcb = [P.sb([128, NFF], F32, "ffcb%d" % l) for l in range(2)]
        for l in range(2):
            P.memset(ffst[l], ffst[l][:], 0.0)
            P.dma("sp", ffcw[l][:], I["ffn_cw"][l], writes=[ffcw[l]])
            P.dma("sp", ffcb[l][:], I["ffn_cb"][l], writes=[ffcb[l]])
        rstd = P.sb([128, TP], F32, "rstd")
        cvb = U(22, 26, "cvb")
        sgb = U(26, 28, "sgb")
        actT = [P.view(RWt[:, 2 * i:2 * i + 2, :], "actT%d" % i) for i in range(2)]
        cvtmp = P.sb([128, TP], F32, "cvtmp")
        cstage = P.sb([128, TP + 3 * NSQ], F32, "cstage")

        cond0 = P.view(MWt[:, 0:2, :].rearrange("p a t -> p (a t)").rearrange("p (c n) -> p c n", n=32), "cond0")
        condT = P.view(RWt[:, 0:2, :].rearrange("p a t -> p (a t)").rearrange("p (c n) -> p c n", n=32), "condT")
        P.memset(cond0, cond0[:], 0.0)
        P.dma("sp", cond0[:, :, 0:NS1], I["cT"].rearrange("(c p) n -> p c n", p=128), writes=[cond0])
        P.act(condT, condT[:], cond0, cond0[:], AF.Silu)
        for l in range(2):
            badT = P.sb([128, 96], F32, "badT%d" % l)
            gm = P.sb([128, KC], F32, "gmix%d" % l)
            gf = P.sb([128, KC], F32, "gffn%d" % l)
            P.dma("sp", badT[:], I["b_adaT"][l], writes=[badT])
            P.dma("sp", gm[:], I["g_mixT"][l], writes=[gm])
            P.dma("sp", gf[:], I["g_ffnT"][l], writes=[gf])
            nj = WS // 128
            for s in range(6 * D // WS):
                wb = next_wb()
                wbv = wb[:].rearrange("p (c n) -> p c n", n=WS)
                P.dma("pool", wbv, I["w_ada"][l][s], writes=[wb])
                ps = P.next_ps()
                for j in range(nj):
                    for kc in range(KC):
                        P.mm(ps, ps[:, j * 32:j * 32 + 32], wb, wbv[:, kc, j * 128:(j + 1) * 128], condT, condT[:, kc, :],
                             start=(kc == 0), stop=(kc == KC - 1))
                psv = ps[:, 0:32 * nj].rearrange("p (j n) -> p j n", n=32)[:, :, 0:NS1]
                P.tt(modT[l], modT[l][:, s * nj:(s + 1) * nj, :], ps, psv,
                     badT, bc(badT[:, s * nj:(s + 1) * nj].unsqueeze(2), [128, nj, NS1]), ALU.add)
            for (off, g) in ((16, gm), (64, gf)):
                P.stt(modT[l], modT[l][:, off:off + 16, :], modT[l], modT[l][:, off:off + 16, :], 1.0,
                      g, bc(g[:].unsqueeze(2), [128, 16, NS1]), ALU.add, ALU.mult)
            P.cp(modP[l], modP[l][:].unsqueeze(2), modT[l], modT[l][:, :, 0:1])
            P.dma("sp", modD[l], modT[l][:, :, :], reads=[modT[l]], writes=[modDB])

        def v4(ap, tl):
            return ap.rearrange("p c (s l) -> p c s l", l=tl.L)

        def modb(l, off):
            P.dma("sp", modS[:], modD[l][:, off:off + 16, 1:NS1], reads=[modDB], writes=[modS])
            return bc(modS[:].unsqueeze(3), [128, 16, NSQ, LS])

        def rms_stats(tl):
            T = tl.T
            P.act(yacc, yacc[:, :, :T], xT, xT[:, :, :T], AF.Square)
            ps = P.next_ps()
            for c in range(KC):
                P.mm(ps, ps[:, :T], ones, ones[:], yacc, yacc[:, c, :T], start=(c == 0), stop=(c == KC - 1))
            P.act(rstd, rstd[:, :T], ps, ps[:, :T], AF.Ln, bias=EPS, scale=1.0 / D)
            P.act(rstd, rstd[:, :T], rstd, rstd[:, :T], AF.Exp, scale=-0.5)
            P.tt(yacc, yacc[:, :, :T], xT, xT[:, :, :T], rstd, bc(rstd[:, :T].unsqueeze(1), [128, KC, T]), ALU.mult)

        def norm_mod(tl, l, sh, sc):
            T = tl.T
            P.barrier([yacc])
            rms_stats(tl)
            if tl.kind == "p":
                for c in range(KC):
                    P.act(hT, hT[:, c, :T], yacc, yacc[:, c, :T], AF.Identity, bias=modP[l][:, sh + c:sh + c + 1],
                          scale=modP[l][:, sc + c:sc + c + 1], extra=[modP[l]])
            else:
                P.tt(yacc, v4(yacc[:, :, :T], tl), yacc, v4(yacc[:, :, :T], tl), modS, modb(l, sc), ALU.mult)
                P.tt(hT, v4(hT[:, :, :T], tl), yacc, v4(yacc[:, :, :T], tl), modS, modb(l, sh), ALU.add)

        def resid_add(tl, l, goff, src):
            T = tl.T
            if tl.kind == "p":
                for c in range(KC):
                    P.stt(xT, xT[:, c, :T], src, src[:, c, :T], modP[l][:, goff + c:goff + c + 1], xT, xT[:, c, :T],
                          ALU.mult, ALU.add, extra=[modP[l]])
            else:
                P.tt(src, v4(src[:, :, :T], tl), src, v4(src[:, :, :T], tl), modS, modb(l, goff), ALU.mult)
                P.tt(xT, xT[:, :, :T], xT, xT[:, :, :T], src, src[:, :, :T], ALU.add)

        def proj_fm(tl, w_ap, col0, ncols, consume, rhs=None, nk=KC):
            T = tl.T
            rhs = rhs or hT
            wname = w_ap
            sb0 = slab_base(wname, col0)
            for s0 in range(0, ncols, WS):
                n = min(WS, ncols - s0)
                wb = next_wb()
                wbv = wb[:].rearrange("p (c n) -> p c n", n=WS)
                P.dma("pool", wbv[:, 0:nk, :], I[wname][sb0 + s0 // WS], writes=[wb])
                for j in range((n + 127) // 128):
                    m = min(128, n - j * 128)
                    ps = P.next_ps()
                    for kc in range(nk):
                        P.mm(ps, ps[0:m, :T], wb, wbv[:, kc, j * 128:j * 128 + m], rhs, rhs[:, kc, :T],
                             start=(kc == 0), stop=(kc == nk - 1))
                    consume(s0 // 128 + j, m, ps)

        def conv_chunk(tl, ps, cw, cb, ch, K, pst, sin, sout, out_b, out_ap, func):
            T, L, ns = tl.T, tl.L, tl.nseq
            H = K - 1
            sv = cstage[:, 0:ns * (L + H)].rearrange("p (s l) -> p s l", l=L + H)
            if tl.kind == "p":
                P.cp(cstage, sv[:, :, 0:H], pst, pst[:, ch:ch + 1, :])
            else:
                P.dma("sp", sv[:, :, 0:H], sin[:, ch, :, :], writes=[cstage])
            P.act(cstage, sv[:, :, H:H + L], ps, ps[:, :T].rearrange("p (s l) -> p s l", l=L), AF.Copy)
            if tl.kind == "p":
                P.cp(pst, pst[:, ch:ch + 1, :], cstage, sv[:, :, L:L + H])
                if tl.last:
                    P.dma("sp", sout[:, ch, 0:1, :], sv[:, :, L:L + H], reads=[cstage])
            else:
                P.dma("sp", sout[:, ch, 1:NS1, :], sv[:, :, L:L + H], reads=[cstage])
            acc = cvtmp[:, 0:T].rearrange("p (s l) -> p s l", l=L)
            P.ts(cvtmp, acc, cstage, sv[:, :, 0:L], cw[:, ch, 0:1], cb[:, ch:ch + 1], ALU.mult, ALU.add, extra=[cw, cb])
            for k in range(1, K):
                P.stt(cvtmp, acc, cstage, sv[:, :, k:k + L], cw[:, ch, k:k + 1], cvtmp, acc, ALU.mult, ALU.add, extra=[cw])
            ov = out_ap.rearrange("p (s l) -> p s l", l=L)
            if func is None:
                P.cp(out_b, ov, cvtmp, acc)
            else:
                P.act(out_b, ov, cvtmp, acc, func)

        def ffn(tl, l):
            T = tl.T
            P.barrier([cvb, sgb] + actT)
            wup = I["w_ffn_up"][l]
            wdn = I["w_ffn_down"][l]
            for g in range(DFF // 256):
                wba, wbg = next_wb(), next_wb()
                wva = wba[:].rearrange("p (c n) -> p c n", n=WS)
                wvg = wbg[:].rearrange("p (c n) -> p c n", n=WS)
                P.dma("pool", wva, wup[g], writes=[wba])
                P.dma("pool", wvg, wup[DFF // WS + g], writes=[wbg])
                for j in range(4):
                    ch = (g * 2 + j) if j < 2 else (44 + g * 2 + j - 2)
                    wb, wv = (wba, wva) if j < 2 else (wbg, wvg)
                    jj = j % 2
                    ps = P.next_ps()
                    for kc in range(KC):
                        P.mm(ps, ps[:, :T], wb, wv[:, kc, jj * 128:(jj + 1) * 128], hT, hT[:, kc, :T],
                             start=(kc == 0), stop=(kc == KC - 1))
                    conv_chunk(tl, ps, ffcw[l], ffcb[l], ch, 3, ffst[l], I["ffn_st"][l], O["ffn_st_o"][l],
                               cvb, cvb[:, j, :T], None)
                P.act(sgb, sgb[:, :, :T], cvb, cvb[:, 2:4, :T], AF.Silu)
                at = actT[g % 2]
                P.tt(at, at[:, :, :T], sgb, sgb[:, :, :T], cvb, cvb[:, 0:2, :T], ALU.mult)
                wd = next_wb()
                wdv = wd[:, 0:2 * D].rearrange("p (c n) -> p c n", n=D)
                P.dma("pool", wdv, wdn[g], writes=[wd])
                for oc in range(KC):
                    ps = P.next_ps()
                    for kc in range(2):
                        P.mm(ps, ps[:, :T], wd, wdv[:, kc, oc * 128:(oc + 1) * 128], at, at[:, kc, :T],
                             start=(kc == 0), stop=(kc == 1))
                    if g == 0:
                        P.cp(yacc, yacc[:, oc, :T], ps, ps[:, :T], eng="act")
                    else:
                        P.tt(yacc, yacc[:, oc, :T], yacc, yacc[:, oc, :T], ps, ps[:, :T], ALU.add)

        gla_S = [P.sb([128, 4, 256], F32, "glaS0")] * 2
        gdn_S = [P.sb([128, 8, 128], F32, "gdnS0")] * 2
        nb2 = const("gla_b2T", [128, 4])
        P.ts1(nb2, nb2[:], nb2, nb2[:], -1.0, ALU.mult)
        glang = const("gla_ngT", [128, 2])
        gdcw = const("gdn_cw", [128, 24, 4])
        gdcb = const("gdn_cb", [128, 24])
        gdnegA = const("gdn_Alog", [8, 1])
        P.act(gdnegA, gdnegA[:], gdnegA, gdnegA[:], AF.Exp)
        P.ts1(gdnegA, gdnegA[:], gdnegA, gdnegA[:], -1.0, ALU.mult)
        gddtb = const("gdn_dtb", [8, 1])
        gdng = const("gdn_ngT", [128, 1])
        gdcst = P.sb([128, 24, 3], F32, "gdcst")
        P.memset(gdcst, gdcst[:], 0.0)
        lrT = P.view(MWt[0:16, 54, :], "lrT")
        w2sb = P.view(MWt[0:16, 55:57, :].rearrange("p a t -> p (a t)"), "w2sb")
        abT = [P.sb([8, TP], F32, "abT%d" % i) for i in range(5)]
        abT.append(abT[1])
        gtok = P.sb([64, 8], F32, "gtok")
        eglast = P.sb([128, 8], F32, "eglast")
        m_q, m_k, m_v, m_r = U(0, 4, "m_q"), U(4, 8, "m_k"), U(8, 16, "m_v"), U(16, 24, "m_r")
        m_cum, m_o, m_ln = U(24, 28, "m_cum"), U(28, 36, "m_o"), U(36, 40, "m_ln")
        tmpA = [U(40 + i, 41 + i, "tmpA%d" % i) for i in range(7)]
        a_vtk, a_ktk = U(48, 52, "a_vtk"), U(52, 54, "a_ktk")
        GLA_BUFS = [m_q, m_k, m_v, m_r, m_cum, m_o, m_ln, a_vtk, a_ktk] + tmpA
        g_qkv, g_gate, g_ob, g_sq = U(0, 24, "g_qkv"), U(36, 44, "g_gate"), U(44, 52, "g_ob"), U(52, 60, "g_sq")
        g_kb, g_dec, g_A, g_Q = U(52, 54, "g_kb"), U(54, 56, "g_dec"), U(56, 58, "g_A"), U(58, 60, "g_Q")
        g_vb, g_qi, g_kw, g_ko = U(60, 62, "g_vb"), U(62, 64, "g_qi"), U(64, 66, "g_kw"), U(66, 68, "g_ko")
        g_X, g_wk = U(68, 70, "g_X"), U(70, 72, "g_wk")
        g_tokA, g_tokB = U(52, 56, "g_tokA"), U(56, 60, "g_tokB")
        g_esel = U(24, 28, "g_esel")

        def t4(buf, C, nh):
            return buf[:].rearrange("p a t -> p (a t)")[:, 0:nh * C].rearrange("p (h c) -> p h c", c=C)

        def tok(buf, C, n):
            return buf[:].rearrange("p a t -> p (a t)")[0:C, 0:n]

        def gla(tl):
            T, C, nch = tl.T, tl.C, tl.nch
            W = "w_in_ab"
            P.barrier(GLA_BUFS + [lrT, w2sb])
            P.dma("sp", w2sb[:, :], I["gla_w2"], writes=[w2sb])
            proj_fm(tl, W, 0, 512, lambda j, m, ps: P.cp(m_q, m_q[:, j, :T], ps, ps[:, :T], eng="act"))
            proj_fm(tl, W, 512, 512, lambda j, m, ps: P.cp(m_k, m_k[:, j, :T], ps, ps[:, :T], eng="act"))
            proj_fm(tl, W, 1024, 1024, lambda j, m, ps: P.cp(m_v, m_v[:, j, :T], ps, ps[:, :T], eng="act"))
            proj_fm(tl, W, 2064, 1024, lambda j, m, ps: P.act(m_r, m_r[:, j, :T], ps, ps[:, :T], AF.Silu))
            proj_fm(tl, W, 2048, 16, lambda j, m, ps: P.cp(lrT, lrT[0:16, :T], ps, ps[0:16, :T]))
            for h in range(4):
                ps = P.next_ps()
                P.mm(ps, ps[:, :T], w2sb, w2sb[0:16, h * 128:(h + 1) * 128], lrT, lrT[0:16, :T])
                P.act(m_ln, m_ln[:, h, :T], ps, ps[:, :T], AF.Exp, bias=nb2[:, h:h + 1], scale=-1.0, extra=[nb2])
            P.act(m_ln, m_ln[:, :, :T], m_ln, m_ln[:, :, :T], AF.Ln, bias=1.0)
            if SUB < 2:
                return
            rm = rmask[tl.kind]
            for h in range(4):
                P.scan(m_cum, m_cum[:, h, :T], rm, rm[:, 0:T], m_ln, m_ln[:, h, :T])
            eb, QsT, enb, KsT, dl, KoT, attT = [t4(tmpA[i], C, 4) for i in range(7)]
            bt = tmpA
            vtk = tok(a_vtk, C, 1024)
            ktk = tok(a_ktk, C, 512)
            for n in range(nch if SUB >= 3 else 0):
                c0 = n * C
                S = gla_S[n % 2] if tl.kind == "s" else gla_S[0]
                if tl.kind == "s":
                    P.dma("sp", S[:], I["gla_st"][n], writes=[S])
                elif tl.first and n == 0:
                    P.memset(S, S[:], 0.0)
                cs = slice(c0, c0 + C)
                P.act(bt[0], eb, m_cum, m_cum[:, :, cs], AF.Exp, scale=-1.0 / 16)
                P.stt(bt[1], QsT, m_q, m_q[:, :, cs], 128 ** -0.5, bt[0], eb, ALU.mult, ALU.mult)
                P.act(bt[2], enb, m_cum, m_cum[:, :, cs], AF.Exp, scale=1.0 / 16)
                P.tt(bt[3], KsT, m_k, m_k[:, :, cs], bt[2], enb, ALU.mult)
                P.tt(bt[4], dl, m_cum, bc(m_cum[:, :, c0 + C - 1:c0 + C], [128, 4, C]), m_cum, m_cum[:, :, cs], ALU.subtract)
                P.act(bt[4], dl, bt[4], dl, AF.Exp, scale=-1.0 / 16)
                P.tt(bt[5], KoT, m_k, m_k[:, :, cs], bt[4], dl, ALU.mult)
                psa = P.next_ps()
                for h in range(4):
                    P.mm(psa, psa[0:C, h * C:(h + 1) * C], bt[3], KsT[:, h, :], bt[1], QsT[:, h, :])
                P.tt(bt[6], attT[0:C], psa, psa[0:C, 0:4 * C].rearrange("p (h c) -> p h c", c=C),
                     UT, bc(UT[0:C, 0:C].unsqueeze(1), [C, 4, C]), ALU.mult)
                for half in range(2):
                    psv = P.next_ps()
                    for q4 in range(4):
                        P.tr(psv, psv[0:C, q4 * 128:(q4 + 1) * 128], m_v, m_v[:, half * 4 + q4, cs], ident, ident[:])
                    P.cp(a_vtk, vtk[:, half * 512:(half + 1) * 512], psv, psv[0:C, :], eng="act")
                psk = P.next_ps()
                for h in range(4):
                    P.tr(psk, psk[0:C, h * 128:(h + 1) * 128], bt[5], KoT[:, h, :], ident, ident[:])
                P.cp(a_ktk, ktk, psk, psk[0:C, :], eng="act")
                pso = P.next_ps()
                for h in range(4):
                    for vc in range(2):
                        a = h * 2 + vc
                        o_ap = pso[:, a * C:(a + 1) * C]
                        P.mm(pso, o_ap, a_vtk, vtk[:, a * 128:(a + 1) * 128], bt[6], attT[0:C, h, :], start=True, stop=False)
                        P.mm(pso, o_ap, S, S[:, h, vc * 128:(vc + 1) * 128], bt[1], QsT[:, h, :], start=False, stop=True)
                P.cp(m_o, m_o[:, :, cs], pso, pso[:, 0:8 * C].rearrange("p (a c) -> p a c", c=C), eng="act")
                for half in range(2):
                    pss = P.next_ps()
                    for hh in range(2):
                        h = half * 2 + hh
                        P.mm(pss, pss[:, hh * 256:(hh + 1) * 256], a_ktk, ktk[:, h * 128:(h + 1) * 128],
                             a_vtk, vtk[:, h * 256:(h + 1) * 256])
                    for hh in range(2):
                        h = half * 2 + hh
                        P.stt(S, S[:, h, :], S, S[:, h, :], eb[:, h, C - 1:C], pss, pss[:, hh * 256:(hh + 1) * 256],
                              ALU.mult, ALU.add, extra=[bt[0]])
                if tl.kind == "s":
                    P.dma("sp", O["gla_o"][1 + n], S[:], reads=[S])
                elif tl.last and n == nch - 1:
                    P.dma("sp", O["gla_o"][0], S[:], reads=[S])
            if SUB < 4:
                return
            P.act(m_v, m_v[:, :, :T], m_o, m_o[:, :, :T], AF.Square)
            for half in range(2):
                ps = P.next_ps()
                for hh in range(2):
                    h = half * 2 + hh
                    for vc in range(2):
                        P.mm(ps, ps[:, hh * T:(hh + 1) * T], ones, ones[:], m_v, m_v[:, h * 2 + vc, :T],
                             start=(vc == 0), stop=(vc == 1))
                rsv = m_ln[:, half * 2:half * 2 + 2, :T]
                P.act(m_ln, rsv, ps, ps[:, 0:2 * T].rearrange("p (h t) -> p h t", t=T), AF.Ln, bias=EPS, scale=1.0 / 256)
                P.act(m_ln, rsv, m_ln, rsv, AF.Exp, scale=-0.5)
            o4 = m_o[:, :, :T].rearrange("p (h v) t -> p h v t", v=2)
            P.tt(m_o, o4, m_o, o4, m_ln, bc(m_ln[:, :, :T].unsqueeze(2), [128, 4, 2, T]), ALU.mult)
            P.tt(m_o, m_o[:, :, :T], m_o, m_o[:, :, :T], m_r, m_r[:, :, :T], ALU.mult)
            for vc in range(2):
                P.ts1(m_o, o4[:, :, vc, :], m_o, o4[:, :, vc, :], glang[:, vc:vc + 1], ALU.mult, extra=[glang])

        def gdn(tl):
            T, C, nch = tl.T, tl.C, tl.nch
            W = "w_in_ab"
            qkv, gg, ob, sq = g_qkv, g_gate, g_ob, g_sq
            P.barrier([qkv, gg, ob, sq, g_esel])
            esel = g_esel
            eselv = g_esel[:].rearrange("p a t -> p (a t)")[0:8, :].rearrange("p (h m) -> p h m", m=128)
            P.dma("sp", eselv, I["esel"], writes=[g_esel])

            def cons_qkv(j, m, ps):
                conv_chunk(tl, ps, gdcw, gdcb, j, 4, gdcst, I["gdn_cst"], O["gdn_cst_o"], qkv, qkv[:, j, :T], AF.Silu)

            proj_fm(tl, W, 3088, 3072, cons_qkv)
            proj_fm(tl, W, 6176, 1024, lambda j, m, ps: P.act(gg, gg[:, j, :T], ps, ps[:, :T], AF.Silu))
            proj_fm(tl, W, 6160, 8, lambda j, m, ps: P.cp(abT[0], abT[0][0:8, :T], ps, ps[0:8, :T]))
            proj_fm(tl, W, 6168, 8, lambda j, m, ps: P.cp(abT[1], abT[1][0:8, :T], ps, ps[0:8, :T]))
            aT, bT, gcT, egT, ekT, beT = abT
            if SUB < 6:
                return
            P.act(aT, aT[0:8, :T], aT, aT[0:8, :T], AF.Exp, bias=gddtb[0:8, 0:1], extra=[gddtb])
            P.act(aT, aT[0:8, :T], aT, aT[0:8, :T], AF.Ln, bias=1.0)
            P.ts1(aT, aT[0:8, :T], aT, aT[0:8, :T], gdnegA[0:8, 0:1], ALU.mult, extra=[gdnegA])
            P.act(beT, beT[0:8, :T], bT, bT[0:8, :T], AF.Sigmoid)
            rm = rmask[tl.kind]
            P.scan(gcT, gcT[0:8, :T], rm, rm[0:8, 0:T], aT, aT[0:8, :T])
            P.act(egT, egT[0:8, :T], gcT, gcT[0:8, :T], AF.Exp)
            g3 = gcT[0:8, :T].rearrange("p (n c) -> p n c", c=C)
            P.tt(ekT, ekT[0:8, :T].rearrange("p (n c) -> p n c", c=C), gcT, bc(g3[:, :, C - 1:C], [8, nch, C]), gcT, g3,
                 ALU.subtract)
            P.act(ekT, ekT[0:8, :T], ekT, ekT[0:8, :T], AF.Exp)
            for which in range(2):
                src = qkv[:, which * 8:(which + 1) * 8, :T]
                P.act(sq, sq[:, :, :T], qkv, src, AF.Square)
                for pr in range(4):
                    ps = P.next_ps()
                    for hh in range(2):
                        P.mm(ps, ps[:, hh * T:(hh + 1) * T], ones, ones[:], sq, sq[:, pr * 2 + hh, :T])
                    rv = sq[:, pr * 2:pr * 2 + 2, :T]
                    P.act(sq, rv, ps, ps[:, 0:2 * T].rearrange("p (h t) -> p h t", t=T), AF.Ln, bias=EPS)
                    P.act(sq, rv, sq, rv, AF.Exp, scale=-0.5)
                if which == 0:
                    P.stt(qkv, src, qkv, src, 128 ** -0.5, sq, sq[:, :, :T], ALU.mult, ALU.mult)
                else:
                    P.tt(qkv, src, qkv, src, sq, sq[:, :, :T], ALU.mult)
            if SUB < 7:
                return
            tbufs = [g_kb, g_vb, g_qi, g_kw, g_ko, g_dec, g_A, g_Q, g_X, g_wk]
            P.barrier(tbufs)
            kbT, vbT, qiT, kwT, koT, decT, AT, QT, XT, wk = [t4(b, C, 8) for b in tbufs]
            nsteps = {64: 5, 4: 1}[C]
            p3 = lambda ps_: ps_[0:C, 0:8 * C].rearrange("p (h c) -> p h c", c=C)
            pf = lambda ps_: ps_[:, 0:8 * C].rearrange("p (h c) -> p h c", c=C)
            for n in range(nch):
                c0 = n * C
                cs = slice(c0, c0 + C)
                S = gdn_S[n % 2] if tl.kind == "s" else gdn_S[0]
                if tl.kind == "s":
                    P.dma("sp", S[:], I["gdn_st"][n], writes=[S])
                elif tl.first and n == 0:
                    P.memset(S, S[:], 0.0)
                if SUB < 8:
                    break
                if n > 0:
                    P.barrier([g_kb, g_dec, g_A, g_Q])
                qc, kc, vc = qkv[:, 0:8, cs], qkv[:, 8:16, cs], qkv[:, 16:24, cs]

                def bcast_rows(srcb):
                    ps = P.next_ps()
                    for h in range(8):
                        P.mm(ps, ps[:, h * C:(h + 1) * C], esel, eselv[:, h, :], srcb, srcb[0:8, cs])
                    return ps, ps[:, 0:8 * C].rearrange("p (h c) -> p h c", c=C)

                if CUT < 0:
                    continue
                psb, pbv = bcast_rows(beT)
                if CUT < 1:
                    P.cp(g_kb, kbT, psb, pbv)
                    continue
                P.tt(g_kb, kbT, qkv, kc, psb, pbv, ALU.mult)
                P.tt(g_vb, vbT, qkv, vc, psb, pbv, ALU.mult)
                pse, pev = bcast_rows(egT)
                P.tt(g_qi, qiT, qkv, qc, pse, pev, ALU.mult)
                P.tt(g_kw, kwT, g_kb, kbT, pse, pev, ALU.mult)
                P.cp(eglast, eglast[:, :].unsqueeze(2), pse, pev[:, :, C - 1:C])
                psk, pkv = bcast_rows(ekT)
                P.tt(g_ko, koT, qkv, kc, psk, pkv, ALU.mult)
                if CUT < 2:
                    continue
                pst = P.next_ps()
                P.tr(pst, pst[0:C, 0:8], aT, aT[0:8, cs], ident, ident[0:8, 0:8])
                P.cp(gtok, gtok[0:C, :], pst, pst[0:C, 0:8], eng="act")
                P.tt(g_wk, wk[0:C], gtok, bc(gtok[0:C, :].unsqueeze(2), [C, 8, C]), SL, bc(SL[0:C, 0:C].unsqueeze(1), [C, 8, C]),
                     ALU.mult)
                psd = P.next_ps()
                for h in range(8):
                    P.mm(psd, psd[0:C, h * C:(h + 1) * C], g_wk, wk[0:C, h, :], UT, UT[0:C, 0:C])
                P.act(g_dec, decT[0:C], psd, p3(psd), AF.Exp)
                P.tt(g_dec, decT[0:C], g_dec, decT[0:C], UT, bc(UT[0:C, 0:C].unsqueeze(1), [C, 8, C]), ALU.mult)
                if CUT < 3:
                    continue
                psm = P.next_ps()
                for h in range(8):
                    P.mm(psm, psm[0:C, h * C:(h + 1) * C], qkv, kc[:, h, :], g_kb, kbT[:, h, :])
                P.tt(g_A, AT[0:C], psm, p3(psm), g_dec, decT[0:C], ALU.mult)
                P.tt(g_A, AT[0:C], g_A, AT[0:C], nSU, bc(nSU[0:C, 0:C].unsqueeze(1), [C, 8, C]), ALU.mult)
                psa = P.next_ps()
                for h in range(8):
                    P.mm(psa, psa[0:C, h * C:(h + 1) * C], qkv, kc[:, h, :], qkv, qc[:, h, :])
                P.tt(g_wk, wk[0:C], psa, p3(psa), g_dec, decT[0:C], ALU.mult)
                pq = P.next_ps()
                for h in range(8):
                    P.tr(pq, pq[0:C, h * C:(h + 1) * C], g_A, AT[0:C, h, :], ident, ident[0:C, 0:C])
                P.cp(g_Q, QT[0:C], pq, p3(pq), eng="act")
                P.tt(g_X, XT[0:C], g_A, AT[0:C], ident, bc(ident[0:C, 0:C].unsqueeze(1), [C, 8, C]), ALU.add)
                for step in range(nsteps if CUT >= 4 else 0):
                    lastst = (step == nsteps - 1)
                    pqt = P.next_ps()
                    for h in range(8):
                        P.mm(pqt, pqt[0:C, h * C:(h + 1) * C], g_A, AT[0:C, h, :], g_Q, QT[0:C, h, :])
                    if not lastst:
                        pq2 = P.next_ps()
                        for h in range(8):
                            P.mm(pq2, pq2[0:C, h * C:(h + 1) * C], g_Q, QT[0:C, h, :], g_A, AT[0:C, h, :])
                    P.cp(g_Q, QT[0:C], pqt, p3(pqt), eng="act")
                    if not lastst:
                        P.cp(g_A, AT[0:C], pq2, p3(pq2))
                    px = P.next_ps()
                    for h in range(8):
                        P.mm(px, px[0:C, h * C:(h + 1) * C], g_Q, QT[0:C, h, :], g_X, XT[0:C, h, :])
                    P.tt(g_X, XT[0:C], g_X, XT[0:C], px, p3(px), ALU.add)
                if CUT < 5:
                    continue
                P.barrier([g_tokA, g_tokB])
                tA = tok(g_tokA, C, 1024)
                tB = tok(g_tokB, C, 1024)

                def to_tok(srcb, srcv, dstb, dstv):
                    for half in range(2):
                        pt = P.next_ps()
                        for q4 in range(4):
                            P.tr(pt, pt[0:C, q4 * 128:(q4 + 1) * 128], srcb, srcv[:, half * 4 + q4, :], ident, ident[:])
                        P.cp(dstb, dstv[:, half * 512:(half + 1) * 512], pt, pt[0:C, :], eng="act")

                to_tok(g_vb, vbT, g_tokA, tA)
                to_tok(g_kw, kwT, g_tokB, tB)
                pu = P.next_ps()
                pw = P.next_ps()
                for h in range(8):
                    P.mm(pu, pu[:, h * C:(h + 1) * C], g_tokA, tA[:, h * 128:(h + 1) * 128], g_X, XT[0:C, h, :])
                    P.mm(pw, pw[:, h * C:(h + 1) * C], g_tokB, tB[:, h * 128:(h + 1) * 128], g_X, XT[0:C, h, :])
                P.cp(g_vb, vbT, pu, pf(pu), eng="act")
                P.cp(g_kw, kwT, pw, pf(pw))
                if CUT < 6:
                    continue
                pws = P.next_ps()
                for h in range(8):
                    P.mm(pws, pws[:, h * C:(h + 1) * C], S, S[:, h, :], g_kw, kwT[:, h, :])
                P.tt(g_vb, vbT, g_vb, vbT, pws, pf(pws), ALU.subtract)
                to_tok(g_vb, vbT, g_tokA, tA)
                to_tok(g_ko, koT, g_tokB, tB)
                po = P.next_ps()
                for h in range(8):
                    o_ap = po[:, h * C:(h + 1) * C]
                    P.mm(po, o_ap, S, S[:, h, :], g_qi, qiT[:, h, :], start=True, stop=False)
                    P.mm(po, o_ap, g_tokA, tA[:, h * 128:(h + 1) * 128], g_wk, wk[0:C, h, :], start=False, stop=True)
                P.cp(ob, ob[:, :, cs], po, pf(po), eng="act")
                if CUT < 7:
                    continue
                for half in range(2):
                    pss = P.next_ps()
                    for hh in range(4):
                        h = half * 4 + hh
                        P.mm(pss, pss[:, hh * 128:(hh + 1) * 128], g_tokB, tB[:, h * 128:(h + 1) * 128],
                             g_tokA, tA[:, h * 128:(h + 1) * 128])
                    for hh in range(4):
                        h = half * 4 + hh
                        P.stt(S, S[:, h, :], S, S[:, h, :], eglast[:, h:h + 1], pss, pss[:, hh * 128:(hh + 1) * 128],
                              ALU.mult, ALU.add, extra=[eglast])
                if tl.kind == "s":
                    P.dma("sp", O["gdn_o"][1 + n], S[:], reads=[S])
                elif tl.last and n == nch - 1:
                    P.dma("sp", O["gdn_o"][0], S[:], reads=[S])
            if SUB < 9:
                return
            P.barrier([sq])
            P.act(sq, sq[:, :, :T], ob, ob[:, :, :T], AF.Square)
            for pr in range(4):
                ps = P.next_ps()
                for hh in range(2):
                    P.mm(ps, ps[:, hh * T:(hh + 1) * T], ones, ones[:], sq, sq[:, pr * 2 + hh, :T])
                rv = sq[:, pr * 2:pr * 2 + 2, :T]
                P.act(sq, rv, ps, ps[:, 0:2 * T].rearrange("p (h t) -> p h t", t=T), AF.Ln, bias=EPS, scale=1.0 / 128)
                P.act(sq, rv, sq, rv, AF.Exp, scale=-0.5)
            P.tt(ob, ob[:, :, :T], ob, ob[:, :, :T], sq, sq[:, :, :T], ALU.mult)
            P.tt(ob, ob[:, :, :T], ob, ob[:, :, :T], gg, gg[:, :, :T], ALU.mult)
            P.cp(hT, hT[:, 0:8, :T], m_o, m_o[:, :, :T], eng="act")
            P.ts1(hT, hT[:, 8:16, :T], ob, ob[:, :, :T], gdng[:, 0:1], ALU.mult, extra=[gdng])

        def mixer_ab(tl):
            T = tl.T
            gla(tl)
            if SUB >= 5:
                gdn(tl)
            else:
                P.cp(hT, hT[:, 0:8, :T], m_o, m_o[:, :, :T], eng="act")
                P.act(hT, hT[:, 8:16, :T], m_o, m_o[:, :, :T], AF.Copy, scale=0.0)
            P.barrier([yacc])
            proj_fm(tl, "w_out_ab", 0, D, lambda j, m, ps: P.cp(yacc, yacc[:, j, :T], ps, ps[:, :T], eng="act"))
            resid_add(tl, 0, 32, yacc)

        sscw = const("ssd_cw", [128, 12, 4])
        sscb = const("ssd_cb", [128, 12])
        ssnegA = const("ssd_Alog", [16, 1])
        P.act(ssnegA, ssnegA[:], ssnegA, ssnegA[:], AF.Exp)
        P.ts1(ssnegA, ssnegA[:], ssnegA, ssnegA[:], -1.0, ALU.mult)
        ssdtb = const("ssd_dtb", [16, 1])
        ssDcol = const("ssd_Dcol", [128, 8])
        ssng = const("ssd_ngT", [128, 8])
        s5Dcol = const("s5_Dcol", [128, 8])
        glub = const("glu_bT", [128, 8])
        sscst = P.sb([128, 12, 3], F32, "sscst")
        P.memset(sscst, sscst[:], 0.0)
        He = P.sb([128, 8, 128], F32, "He")
        Ho = P.sb([128, 8, 128], F32, "Ho")
        P.memset(He, He[:], 0.0)
        P.memset(Ho, Ho[:], 0.0)
        dtr = [P.sb([16, TP], F32, "dtr%d" % i) for i in range(4)]
        tk = P.sb([64, 48], F32, "tk")
        dcl = P.sb([128, 16], F32, "dcl")
        s5st = P.sb([128, 32, 2], F32, "s5st")
        P.memset(s5st, s5st[:], 0.0)
        s5fre = P.sb([128, 32], F32, "s5fre")
        s5fim = P.sb([128, 32], F32, "s5fim")
        c_z, c_xbc, c_y = U(0, 8, "c_z"), U(8, 20, "c_xbc"), U(28, 36, "c_y")
        c_u = P.view(RWt[:, :, :], "c_u")
        c_sc, c_Ap, c_cin, c_bout = U(36, 40, "c_sc"), U(40, 44, "c_Ap"), U(44, 48, "c_cin"), U(48, 56, "c_bout")
        c_xe, c_xo, c_btk, c_cbm, c_es = U(56, 60, "c_xe"), U(60, 64, "c_xo"), U(64, 65, "c_btk"), U(65, 66, "c_cbm"), U(36, 44, "c_sq")
        c_xm = U(66, 70, "c_xm")
        c_stt = U(48, 56, "c_stt")
        s_tab = [U(36 + 4 * i, 40 + 4 * i, "s_tab%d" % i) for i in range(2)]
        s_bp, s_z = U(44, 46, "s_bp"), U(46, 48, "s_z")
        s_xt = stack.enter_context(nc.sbuf_tensor("s5x", [128, 2, TP], F32R))
        s_x = P.view(s_xt[:], "s_x")
        s_yd = U(0, 8, "s_yd")
        s_z5 = P.view(RWt[:, :, :], "s_z5")
        s_tmp = U(48, 50, "s_tmp")
        s_x0 = U(50, 54, "s_x0")
        s_so = U(54, 58, "s_so")
        CD_BUFS = [c_z, c_xbc, c_y, c_u, c_sc, c_Ap, c_cin, c_bout, c_xe, c_xo, c_btk, c_cbm, c_xm]

        s5tab = nc.dram_tensor("s5tab", [4, 128, 32, TP], F32, kind="Internal").ap()
        s5tabB = Buf(None, "s5tab")

        def s5_setup():
            are = const("s5_are", [128, 32])
            aim = const("s5_aim", [128, 32])
            ldt = const("s5_ldt", [128, 32])
            w = [P.view(MWt[:, 40, i * 32:(i + 1) * 32], "s5w%d" % i) for i in range(8)]
            w += [P.view(MWt[:, 41, i * 32:(i + 1) * 32], "s5w%d" % (8 + i)) for i in range(6)]
            P.barrier(w)
            dtv, ar, th, mag, img, c_, s_, t0, t1, den = w[:10]
            P.act(dtv, dtv[:], ldt, ldt[:], AF.Exp)
            P.tt(ar, ar[:], are, are[:], dtv, dtv[:], ALU.mult)
            P.tt(th, th[:], aim, aim[:], dtv, dtv[:], ALU.mult)
            P.act(mag, mag[:], ar, ar[:], AF.Exp)
            P.act(img, img[:], ar, ar[:], AF.Exp, scale=-1.0)
            P.act(s_, s_[:], th, th[:], AF.Sin, scale=1.0 / 16)
            P.act(t0, t0[:], th, th[:], AF.Sin, scale=1.0 / 32)
            P.tt(t0, t0[:], t0, t0[:], t0, t0[:], ALU.mult)
            P.ts(c_, c_[:], t0, t0[:], -2.0, 1.0, ALU.mult, ALU.add)
            for _ in range(4):
                P.tt(t0, t0[:], c_, c_[:], c_, c_[:], ALU.mult)
                P.tt(t1, t1[:], s_, s_[:], s_, s_[:], ALU.mult)
                P.tt(s_, s_[:], s_, s_[:], c_, c_[:], ALU.mult)
                P.ts1(s_, s_[:], s_, s_[:], 2.0, ALU.mult)
                P.tt(c_, c_[:], t0, t0[:], t1, t1[:], ALU.subtract)
            lre, lim, ire, iim = w[10:14]
            w = w[:10]
            P.tt(lre, lre[:], mag, mag[:], c_, c_[:], ALU.mult)
            P.tt(lim, lim[:], mag, mag[:], s_, s_[:], ALU.mult)
            P.tt(ire, ire[:], img, img[:], c_, c_[:], ALU.mult)
            P.tt(iim, iim[:], img, img[:], s_, s_[:], ALU.mult)
            P.ts1(iim, iim[:], iim, iim[:], -1.0, ALU.mult)
            P.tt(den, den[:], are, are[:], are, are[:], ALU.mult)
            P.tt(t0, t0[:], aim, aim[:], aim, aim[:], ALU.mult)
            P.tt(den, den[:], den, den[:], t0, t0[:], ALU.add)
            P.op("dve", lambda E: E.reciprocal(den[:], den[:]), reads=[den], writes=[den])
            nr = dtv
            P.ts1(nr, nr[:], lre, lre[:], -1.0, ALU.add)
            P.tt(t0, t0[:], nr, nr[:], are, are[:], ALU.mult)
            P.tt(t1, t1[:], lim, lim[:], aim, aim[:], ALU.mult)
            P.tt(t0, t0[:], t0, t0[:], t1, t1[:], ALU.add)
            P.tt(s5fre, s5fre[:], t0, t0[:], den, den[:], ALU.mult)
            P.tt(t0, t0[:], lim, lim[:], are, are[:], ALU.mult)
            P.tt(t1, t1[:], nr, nr[:], aim, aim[:], ALU.mult)
            P.tt(t0, t0[:], t0, t0[:], t1, t1[:], ALU.subtract)
            P.tt(s5fim, s5fim[:], t0, t0[:], den, den[:], ALU.mult)
            Tre, Tim, Tt = U(0, 8, "s5Tre"), U(8, 16, "s5Tim"), U(16, 24, "s5Tt")
            Fre, Fim = U(24, 32, "s5Fre"), U(32, 40, "s5Fim")
            lre, lim, ire, iim = lre, lim, ire, iim
            P.barrier([Tre, Tim, Tt, Fre, Fim])
            for grp in range(4):
                ms = slice(grp * 8, grp * 8 + 8)
                for kind, (bre, bim) in enumerate(((lre, lim), (ire, iim))):
                    P.cp(Tre, Tre[:, :, 0:1], bre, bre[:, ms].unsqueeze(2))
                    P.cp(Tim, Tim[:, :, 0:1], bim, bim[:, ms].unsqueeze(2))
                    n = 1
                    while n < TP:
                        sre = bc(Tre[:, :, n - 1:n], [128, 8, n])
                        sim = bc(Tim[:, :, n - 1:n], [128, 8, n])
                        tv = Tt[:, :, 0:n]
                        P.tt(Tt, tv, Tim, Tim[:, :, 0:n], Tim, sim, ALU.mult)
                        P.tt(Tre, Tre[:, :, n:2 * n], Tre, Tre[:, :, 0:n], Tre, sre, ALU.mult)
                        P.tt(Tre, Tre[:, :, n:2 * n], Tre, Tre[:, :, n:2 * n], Tt, tv, ALU.subtract)
                        P.tt(Tt, tv, Tim, Tim[:, :, 0:n], Tre, sre, ALU.mult)
                        P.tt(Tim, Tim[:, :, n:2 * n], Tre, Tre[:, :, 0:n], Tim, sim, ALU.mult)
                        P.tt(Tim, Tim[:, :, n:2 * n], Tim, Tim[:, :, n:2 * n], Tt, tv, ALU.add)
                        n *= 2
                    if kind == 0:
                        P.dma("sp", s5tab[0][:, ms, :], Tre[:], reads=[Tre], writes=[s5tabB])
                        P.dma("sp", s5tab[1][:, ms, :], Tim[:], reads=[Tim], writes=[s5tabB])
                    else:
                        fre = bc(s5fre[:, ms].unsqueeze(2), [128, 8, TP])
                        fim = bc(s5fim[:, ms].unsqueeze(2), [128, 8, TP])
                        P.tt(Fre, Fre[:], Tre, Tre[:], s5fre, fre, ALU.mult)
                        P.tt(Tt, Tt[:], Tim, Tim[:], s5fim, fim, ALU.mult)
                        P.tt(Fre, Fre[:], Fre, Fre[:], Tt, Tt[:], ALU.subtract)
                        P.tt(Fim, Fim[:], Tre, Tre[:], s5fim, fim, ALU.mult)
                        P.tt(Tt, Tt[:], Tim, Tim[:], s5fre, fre, ALU.mult)
                        P.tt(Fim, Fim[:], Fim, Fim[:], Tt, Tt[:], ALU.add)
                        P.dma("sp", s5tab[2][:, ms, :], Fre[:], reads=[Fre], writes=[s5tabB])
                        P.dma("sp", s5tab[3][:, ms, :], Fim[:], reads=[Fim], writes=[s5tabB])

        def ssd_state_in(n):
            P.barrier([c_stt])
            P.dma("sp", c_stt[:].rearrange("p a t -> p (a t)")[:, 0:1024].rearrange("p (c s) -> p c s", s=128), I["ssd_st"][n],
                  writes=[c_stt])
            sv = c_stt[:].rearrange("p a t -> p (a t)")[:, 0:1024].rearrange("p (c s) -> p c s", s=128)
            for half in range(2):
                ps = P.next_ps()
                for q4 in range(4):
                    P.tr(ps, ps[:, q4 * 128:(q4 + 1) * 128], c_stt, sv[:, half * 4 + q4, :], ident, ident[:])
                pv = ps[:, :].rearrange("p (c q) -> p c q", q=128)
                P.cp(He, He[:, half * 4:half * 4 + 4, 0:64], ps, pv[:, :, 0:64])
                P.cp(Ho, Ho[:, half * 4:half * 4 + 4, 64:128], ps, pv[:, :, 64:128])

        def ssd_state_out(dst):
            P.barrier([c_stt, c_es])
            sv = c_stt[:].rearrange("p a t -> p (a t)")[:, 0:1024].rearrange("p (c s) -> p c s", s=128)
            sm = c_es[:].rearrange("p a t -> p (a t)")[:, 0:1024].rearrange("p (c s) -> p c s", s=128)
            P.tt(c_es, sm, He, He[:], Ho, Ho[:], ALU.add)
            for half in range(2):
                ps = P.next_ps()
                for q4 in range(4):
                    P.tr(ps, ps[:, q4 * 128:(q4 + 1) * 128], c_es, sm[:, half * 4 + q4, :], ident, ident[:])
                P.cp(c_stt, sv[:, half * 4:half * 4 + 4, :], ps, ps[:, :].rearrange("p (c q) -> p c q", q=128), eng="act")
            P.dma("sp", dst, sv, reads=[c_stt])

        def ssd(tl):
            T, C, nch = tl.T, tl.C, tl.nch
            dT, aT, acT, wT = dtr
            P.act(dT, dT[0:16, :T], dT, dT[0:16, :T], AF.Exp, bias=ssdtb[0:16, 0:1], extra=[ssdtb])
            P.act(dT, dT[0:16, :T], dT, dT[0:16, :T], AF.Ln, bias=1.0)
            P.ts1(aT, aT[0:16, :T], dT, dT[0:16, :T], ssnegA[0:16, 0:1], ALU.mult, extra=[ssnegA])
            rm = rmask[tl.kind]
            P.scan(acT, acT[0:16, :T], rm, rm[0:16, 0:T], aT, aT[0:16, :T])
            a3 = acT[0:16, :T].rearrange("p (n c) -> p n c", c=C)
            w3 = wT[0:16, :T].rearrange("p (n c) -> p n c", c=C)
            P.tt(wT, w3, acT, bc(a3[:, :, C - 1:C], [16, nch, C]), acT, a3, ALU.subtract)
            P.act(wT, wT[0:16, :T], wT, wT[0:16, :T], AF.Exp)
            P.tt(wT, wT[0:16, :T], wT, wT[0:16, :T], dT, dT[0:16, :T], ALU.mult)
            P.act(acT, acT[0:16, :T], acT, acT[0:16, :T], AF.Exp)
            scT = t4(c_sc, C, 16)
            Ap = t4(c_Ap, C, 16)
            cin = t4(c_cin, C, 16)
            bout = c_bout[:].rearrange("p a t -> p (a t)")[0:C, :].rearrange("p (h s) -> p h s", s=128)
            xe, xo = tok(c_xe, C, 1024), tok(c_xo, C, 1024)
            btk = tok(c_btk, C, 256)
            cbm = t4(c_cbm, C, 2)
            P.memset(c_xe, xe, 0.0)
            P.memset(c_xo, xo, 0.0)
            p3 = lambda ps_, nh: ps_[0:C, 0:nh * C].rearrange("p (h c) -> p h c", c=C)
            for n in range(nch if SC >= 2 else 0):
                c0 = n * C
                cs = slice(c0, c0 + C)
                if tl.kind == "s" and SC >= 7:
                    ssd_state_in(n)
                    P.barrier([c_bout, c_cin])
                pst = P.next_ps()
                for i, src in enumerate((aT, dT, wT)):
                    P.tr(pst, pst[0:C, i * 16:(i + 1) * 16], src, src[0:16, cs], ident, ident[0:16, 0:16])
                P.cp(tk, tk[0:C, :], pst, pst[0:C, 0:48], eng="act")
                P.tt(c_Ap, Ap[0:C], tk, bc(tk[0:C, 0:16].unsqueeze(2), [C, 16, C]), SL, bc(SL[0:C, 0:C].unsqueeze(1), [C, 16, C]),
                     ALU.mult)
                for half in range(2):
                    psd = P.next_ps()
                    for hh in range(8):
                        P.mm(psd, psd[0:C, hh * C:(hh + 1) * C], c_Ap, Ap[0:C, half * 8 + hh, :], UT, UT[0:C, 0:C])
                    P.act(c_sc, scT[0:C, half * 8:half * 8 + 8, :], psd, p3(psd, 8), AF.Exp)
                pcb = P.next_ps()
                for g in range(2):
                    P.mm(pcb, pcb[0:C, g * C:(g + 1) * C], c_xbc, c_xbc[:, 8 + g, cs], c_xbc, c_xbc[:, 10 + g, cs])
                P.tt(c_cbm, cbm[0:C], pcb, p3(pcb, 2), UT, bc(UT[0:C, 0:C].unsqueeze(1), [C, 2, C]), ALU.mult)
                sc4 = scT[0:C].rearrange("p (g h) c -> p g h c", g=2)
                P.tt(c_sc, sc4, c_sc, sc4, c_cbm, bc(cbm[0:C].unsqueeze(2), [C, 2, 8, C]), ALU.mult)
                P.tt(c_sc, scT[0:C], c_sc, scT[0:C], tk, bc(tk[0:C, 16:32].unsqueeze(2), [C, 16, C]), ALU.mult)
                if SC < 3:
                    continue
                for half in range(2):
                    pt = P.next_ps()
                    for q4 in range(4):
                        P.tr(pt, pt[0:C, q4 * 128:(q4 + 1) * 128], c_xbc, c_xbc[:, half * 4 + q4, cs], ident, ident[:])
                    pv = pt[0:C, :].rearrange("p (c q) -> p c q", q=128)
                    xev = xe[:, half * 512:(half + 1) * 512].rearrange("p (c q) -> p c q", q=128)
                    xov = xo[:, half * 512:(half + 1) * 512].rearrange("p (c q) -> p c q", q=128)
                    P.cp(c_xe, xev[:, :, 0:64], pt, pv[:, :, 0:64])
                    P.cp(c_xo, xov[:, :, 64:128], pt, pv[:, :, 64:128])
                pb = P.next_ps()
                for g in range(2):
                    P.tr(pb, pb[0:C, g * 128:(g + 1) * 128], c_xbc, c_xbc[:, 8 + g, cs], ident, ident[:])
                P.cp(c_btk, btk, pb, pb[0:C, 0:256], eng="act")
                b4 = bout.rearrange("p (g h) s -> p g h s", g=2)
                P.tt(c_bout, b4, c_btk, bc(btk.rearrange("p (g s) -> p g s", g=2).unsqueeze(2), [C, 2, 8, 128]),
                     tk, bc(tk[0:C, 32:48].rearrange("p (g h) -> p g h", g=2).unsqueeze(3), [C, 2, 8, 128]), ALU.mult)
                if SC < 4:
                    continue
                xm = c_xm[:].rearrange("p a t -> p (a t)")[0:16, 0:16 * C].rearrange("p (h c) -> p h c", c=C)
                P.tt(c_xm, xm, acT, bc(acT[0:16, cs].unsqueeze(1), [16, 16, C]),
                     ident, bc(ident[0:16, 0:16].unsqueeze(2), [16, 16, C]), ALU.mult)
                for g in range(2):
                    pse = P.next_ps()
                    for hh in range(8):
                        P.mm(pse, pse[:, hh * C:(hh + 1) * C], ones, ones[0:16, :], c_xm, xm[:, g * 8 + hh, :])
                    pev = pse[:, 0:8 * C].rearrange("p (h c) -> p h c", c=C)
                    P.tt(c_cin, cin[:, g * 8:g * 8 + 8, :], c_xbc, bc(c_xbc[:, 10 + g, cs].unsqueeze(1), [128, 8, C]), pse, pev,
                         ALU.mult)
                    P.cp(dcl, dcl[:, g * 8:g * 8 + 8].unsqueeze(2), pse, pev[:, :, C - 1:C])
                if SC < 5:
                    continue
                py = P.next_ps()
                for c in range(8):
                    o_ap = py[:, c * C:(c + 1) * C]
                    P.mm(py, o_ap, c_xe, xe[:, c * 128:(c + 1) * 128], c_sc, scT[0:C, 2 * c, :], start=True, stop=False)
                    P.mm(py, o_ap, c_xo, xo[:, c * 128:(c + 1) * 128], c_sc, scT[0:C, 2 * c + 1, :], start=False, stop=False)
                    P.mm(py, o_ap, He, He[:, c, :], c_cin, cin[:, 2 * c, :], start=False, stop=False)
                    P.mm(py, o_ap, Ho, Ho[:, c, :], c_cin, cin[:, 2 * c + 1, :], start=False, stop=True)
                P.cp(c_y, c_y[:, :, cs], py, py[:, 0:8 * C].rearrange("p (a c) -> p a c", c=C), eng="act")
                if SC < 6:
                    continue
                for half in range(2):
                    pss = P.next_ps()
                    for cc in range(4):
                        c = half * 4 + cc
                        P.mm(pss, pss[:, cc * 128:cc * 128 + 64], c_bout, bout[:, 2 * c, :], c_xe, xe[:, c * 128:c * 128 + 64])
                        P.mm(pss, pss[:, cc * 128 + 64:(cc + 1) * 128], c_bout, bout[:, 2 * c + 1, :],
                             c_xo, xo[:, c * 128 + 64:(c + 1) * 128])
                    for cc in range(4):
                        c = half * 4 + cc
                        P.stt(He, He[:, c, 0:64], He, He[:, c, 0:64], dcl[:, 2 * c:2 * c + 1], pss, pss[:, cc * 128:cc * 128 + 64],
                              ALU.mult, ALU.add, extra=[dcl])
                        P.stt(Ho, Ho[:, c, 64:128], Ho, Ho[:, c, 64:128], dcl[:, 2 * c + 1:2 * c + 2],
                              pss, pss[:, cc * 128 + 64:(cc + 1) * 128], ALU.mult, ALU.add, extra=[dcl])
                if SC < 7:
                    continue
                if tl.kind == "s":
                    ssd_state_out(O["ssd_o"][1 + n])
                    P.barrier([c_bout, c_cin, c_sc, c_Ap])
                elif tl.last and n == nch - 1:
                    ssd_state_out(O["ssd_o"][0])
            if SC < 8:
                return
            for c in range(8):
                P.stt(c_y, c_y[:, c, :T], c_xbc, c_xbc[:, c, :T], ssDcol[:, c:c + 1], c_y, c_y[:, c, :T], ALU.mult, ALU.add,
                      extra=[ssDcol])
            P.tt(c_y, c_y[:, :, :T], c_y, c_y[:, :, :T], c_z, c_z[:, :, :T], ALU.mult)
            P.barrier([c_es])
            P.act(c_es, c_es[:, :, :T], c_y, c_y[:, :, :T], AF.Square)
            ps = P.next_ps()
            for g in range(2):
                for cc in range(4):
                    P.mm(ps, ps[:, g * T:(g + 1) * T], ones, ones[:], c_es, c_es[:, g * 4 + cc, :T], start=(cc == 0), stop=(cc == 3))
            rsv = c_es[:, 0:2, :T]
            P.act(c_es, rsv, ps, ps[:, 0:2 * T].rearrange("p (g t) -> p g t", t=T), AF.Ln, bias=EPS, scale=1.0 / 512)
            P.act(c_es, rsv, c_es, rsv, AF.Exp, scale=-0.5)
            y4 = c_y[:, :, :T].rearrange("p (g c) t -> p g c t", g=2)
            P.tt(c_y, y4, c_y, y4, c_es, bc(rsv.unsqueeze(2), [128, 2, 4, T]), ALU.mult)
            for c in range(8):
                P.ts1(hT, hT[:, c, :T], c_y, c_y[:, c, :T], ssng[:, c:c + 1], ALU.mult, extra=[ssng])

        def s5(tl):
            s5_body(tl)

        def s5_body(tl):
            T, L, ns = tl.T, tl.L, tl.nseq
            P.barrier(s_tab + [s_bp, s_z, s_x, s_yd, s_tmp, s_x0, s_so])
            x0v = s_x0[:].rearrange("p a t -> p (a t)")[:, 0:2 * 32 * NSQ].rearrange("p (k m s) -> p k m s", k=2, s=NSQ)
            sov = s_so[:].rearrange("p a t -> p (a t)")[:, 0:32 * NSQ * 2].rearrange("p (m s k) -> p m s k", s=NSQ, k=2)
            if tl.kind == "s":
                P.dma("sp", x0v, I["s5_x0"], writes=[s_x0])
            onesrow = bc(ones[:, 0:1], [128, T])
            TL = L
            for c in range(8):
                wb = next_wb()
                wv = wb[:, 0:16 * 128].rearrange("p (k m q) -> p k m q", k=4, q=128)
                P.dma("pool", wv, I["s5w"][c], writes=[wb])
                pyr = P.next_ps()
                pyi = P.next_ps()
                for mm_ in range(4):
                    m = 4 * c + mm_
                    tb = s_tab[m % 2]
                    tv = tb[:].rearrange("p a t -> p (a t)")[:, 0:4 * TL].rearrange("p (k t) -> p k t", k=4)
                    P.dma("sp", tv, s5tab[:, :, m, 0:TL].rearrange("k p t -> p k t"), reads=[s5tabB], writes=[tb])

                    def tab(k):
                        return bc(tv[:, k, :].unsqueeze(1), [128, ns, L])

                    pb = P.next_ps()
                    P.mm(pb, pb[:, 0:T], wb, wv[:, 0, mm_, :], c_u, c_u[:, c, :T])
                    P.mm(pb, pb[:, T:2 * T], wb, wv[:, 1, mm_, :], c_u, c_u[:, c, :T])
                    bur = pb[:, 0:T].rearrange("p (s l) -> p s l", l=L)
                    bui = pb[:, T:2 * T].rearrange("p (s l) -> p s l", l=L)
                    bp = s_bp[:, :, :T].rearrange("p k (s l) -> p k s l", l=L)
                    tm = s_tmp[:, :, :T].rearrange("p k (s l) -> p k s l", l=L)
                    P.tt(s_bp, bp[:, 0], pb, bur, tb, tab(2), ALU.mult)
                    P.tt(s_tmp, tm[:, 0], pb, bui, tb, tab(3), ALU.mult)
                    P.tt(s_bp, bp[:, 0], s_bp, bp[:, 0], s_tmp, tm[:, 0], ALU.subtract)
                    P.tt(s_bp, bp[:, 1], pb, bui, tb, tab(2), ALU.mult)
                    P.tt(s_tmp, tm[:, 1], pb, bur, tb, tab(3), ALU.mult)
                    P.tt(s_bp, bp[:, 1], s_bp, bp[:, 1], s_tmp, tm[:, 1], ALU.add)
                    if tl.kind == "s":
                        for k in range(2):
                            P.tt(s_bp, bp[:, k, :, 0:1], s_bp, bp[:, k, :, 0:1], s_x0, x0v[:, k, m, :].unsqueeze(2), ALU.add)
                        for k in range(2):
                            P.scan(s_z, s_z[:, k, :T], rmask["s"], rmask["s"][:, 0:T], s_bp, s_bp[:, k, :T])
                    else:
                        for k in range(2):
                            P.op("dve", lambda E, k=k, m=m: E.tensor_tensor_scan(
                                s_z[:, k, :T], onesrow, s_bp[:, k, :T], s5st[:, m, k:k + 1], ALU.mult, ALU.add),
                                reads=[ones, s_bp, s5st], writes=[s_z])
                    zv = s_z[:, :, :T].rearrange("p k (s l) -> p k s l", l=L)
                    xv = s_x[:, :, :T].rearrange("p k (s l) -> p k s l", l=L)
                    P.tt(s_tmp, tm[:, 0], s_z, zv[:, 1], tb, tab(1), ALU.mult)
                    P.tt(s_tmp, tm[:, 1], s_z, zv[:, 0], tb, tab(0), ALU.mult)
                    P.tt(s_x, xv[:, 0], s_tmp, tm[:, 1], s_tmp, tm[:, 0], ALU.subtract)
                    P.tt(s_tmp, tm[:, 0], s_z, zv[:, 0], tb, tab(1), ALU.mult)
                    P.tt(s_tmp, tm[:, 1], s_z, zv[:, 1], tb, tab(0), ALU.mult)
                    P.tt(s_x, xv[:, 1], s_tmp, tm[:, 1], s_tmp, tm[:, 0], ALU.add)
                    if tl.kind == "s":
                        P.cp(s_so, sov[:, m].rearrange("p s k -> p k s").unsqueeze(3), s_x, xv[:, :, :, L - 1:L].bitcast(F32))
                    else:
                        P.cp(s5st, s5st[:, m, :].unsqueeze(2), s_x, s_x[:, :, T - 1:T].bitcast(F32))
                    P.mm(pyr, pyr[:, :T], wb, wv[:, 2, mm_, :], s_x, s_x[:, 0, :T], start=(mm_ == 0), stop=(mm_ == 3))
                    P.mm(pyi, pyi[:, :T], wb, wv[:, 3, mm_, :], s_x, s_x[:, 1, :T], start=(mm_ == 0), stop=(mm_ == 3))
                P.cp(s_tmp, s_tmp[:, 0, :T], pyi, pyi[:, :T], eng="act")
                P.tt(s_yd, s_yd[:, c, :T], pyr, pyr[:, :T], s_tmp, s_tmp[:, 0, :T], ALU.subtract)
                P.stt(s_yd, s_yd[:, c, :T], c_u, c_u[:, c, :T].bitcast(F32), s5Dcol[:, c:c + 1], s_yd, s_yd[:, c, :T],
                      ALU.mult, ALU.add, extra=[s5Dcol])
            if tl.kind == "s":
                P.dma("sp", O["s5_o"][:, :, 1:NS1, :], sov, reads=[s_so])
            elif tl.last:
                P.dma("sp", O["s5_o"][:, :, 0:1, :], s5st[:].unsqueeze(2), reads=[s5st])
            P.barrier([s_z5])
            P.act(s_z5, s_z5[:, :, :T], s_yd, s_yd[:, :, :T], AF.Gelu)

            def cons_glu(j, mrows, ps):
                P.act(s_tmp, s_tmp[:, 0, :T], ps, ps[:, :T], AF.Sigmoid, bias=glub[:, j:j + 1], extra=[glub])
                P.tt(hT, hT[:, 8 + j, :T], s_z5, s_z5[:, j, :T].bitcast(F32), s_tmp, s_tmp[:, 0, :T], ALU.mult)

            proj_fm(tl, "glu_w", 0, 1024, cons_glu, rhs=s_z5, nk=8)

        def mixer_cd(tl):
            T = tl.T
            W = "w_in_cd"
            P.barrier(CD_BUFS)
            PJ = int(os.environ.get("KDEV_PJ", "15"))
            if PJ & 1:
                proj_fm(tl, W, 0, 1024, lambda j, m, ps: P.act(c_z, c_z[:, j, :T], ps, ps[:, :T], AF.Silu))
            if PJ & 2:
                proj_fm(tl, W, 1024, 1536, lambda j, m, ps: conv_chunk(
                    tl, ps, sscw, sscb, j, 4, sscst, I["ssd_cst"], O["ssd_cst_o"], c_xbc, c_xbc[:, j, :T], AF.Silu))
            if PJ & 4:
                proj_fm(tl, W, 2560, 16, lambda j, m, ps: P.cp(dtr[0], dtr[0][0:16, :T], ps, ps[0:16, :T]))
            if PJ & 8:
                proj_fm(tl, W, 2576, 1024, lambda j, m, ps: P.cp(c_u, c_u[:, j, :T], ps, ps[:, :T], eng="act"))
            if CDCUT >= 3:
                ssd(tl)
            if CDCUT >= 4:
                s5(tl)
            if CDCUT < 4:
                return
            P.barrier([yacc])
            proj_fm(tl, "w_out_cd", 0, D, lambda j, m, ps: P.cp(yacc, yacc[:, j, :T], ps, ps[:, :T], eng="act"))
            resid_add(tl, 1, 32, yacc)

        def final_out(tl):
            T = tl.T
            P.barrier([yacc])
            rms_stats(tl)
            for c in range(KC):
                P.act(yacc, yacc[:, c, :T], yacc, yacc[:, c, :T], AF.Copy, scale=gfin[:, c:c + 1], extra=[gfin])
            if tl.kind == "p":
                P.dma("sp", ypv[:, :, tl.idx * TP:(tl.idx + 1) * TP], yacc[:, :, :T], reads=[yacc])
            else:
                P.dma("sp", ysv, yacc[:, :, :T], reads=[yacc])

        if STAGE >= 3:
            s5_setup()
        tiles = [Tile("p", i) for i in range(NPT)] + [Tile("s", 0)]
        P.sew = not bool(int(os.environ.get("KDEV_NOSEW", "0")))
        xpv = I["xp"].rearrange("(c p) t -> p c t", p=128)
        xsv = I["xs"].rearrange("(c p) t -> p c t", p=128)
        ypv = O["yp"].rearrange("(c p) t -> p c t", p=128)
        ysv = O["ys"].rearrange("(c p) t -> p c t", p=128)
        for tl in tiles:
            if tl.kind == "p":
                P.dma("sp", xT[:, :, :tl.T], xpv[:, :, tl.idx * TP:(tl.idx + 1) * TP], writes=[xT])
            else:
                P.dma("sp", xT[:, :, :tl.T], xsv, writes=[xT])
            for l in range(2):
                norm_mod(tl, l, 0, 16)
                if l == 0 and STAGE >= 2:
                    mixer_ab(tl)
                if l == 1 and STAGE >= 3 and CDCUT >= 2:
                    mixer_cd(tl)
                norm_mod(tl, l, 48, 64)
                ffn(tl, l)
                resid_add(tl, l, 80, yacc)
            final_out(tl)
        P.finish()
    return nc


def _fm(v):
    v = np.asarray(v, np.float32)
    return np.ascontiguousarray(v.reshape(-1, 128).T)


def _consts():
    i = np.arange(128)
    c = {}
    c["ident"] = np.eye(128, dtype=np.float32)
    c["UT"] = (i[None, :] >= i[:, None]).astype(np.float32)
    c["nSU"] = -(i[None, :] > i[:, None]).astype(np.float32)
    c["SL"] = (i[:, None] > i[None, :]).astype(np.float32)
    rp = np.ones((128, TP), np.float32)
    rp[:, ::64] = 0.0
    rs = np.ones((128, TS), np.float32)
    rs[:, ::LS] = 0.0
    c["rmask_p"], c["rmask_s"] = rp, rs
    es = np.zeros((8, 8, 128), np.float32)
    for h in range(8):
        es[h, h, :] = 1.0
    c["esel"] = es
    return c


def make_in_maps(inp):
    maps = []
    cst = _consts()
    assert WS == 256
    wt = {}
    wt["w_ada"] = np.stack([tile_cols_all(inp["w_ada"][l]) for l in range(2)])
    wt["w_ffn_up"] = np.stack([tile_cols_all(inp["w_ffn_up"][l]) for l in range(2)])
    wt["w_ffn_down"] = np.ascontiguousarray(inp["w_ffn_down"].reshape(2, DFF // 256, 2, 128, D).transpose(0, 1, 3, 2, 4))
    wt["w_in_ab"] = tile_weight(inp["w_in_ab"][0], "w_in_ab")
    wt["w_out_ab"] = tile_weight(inp["w_out_ab"][0], "w_out_ab")
    wt["w_in_cd"] = tile_weight(inp["w_in_cd"][0], "w_in_cd")
    wt["w_out_cd"] = tile_weight(inp["w_out_cd"][0], "w_out_cd")
    wt["glu_w"] = tile_weight(inp["s5_glu_w"][0], "glu_w")
    Bre, Bim, Cre, Cim = (inp[k][0] for k in ("s5_B_re", "s5_B_im", "s5_C_re", "s5_C_im"))
    cst_s5w = np.zeros((8, 128, 4, 4, 128), np.float32)
    for c in range(8):
        for mm_ in range(4):
            for g2 in range(2):
                gl = 2 * mm_ + g2
                g = 8 * c + gl
                cst_s5w[c, gl * 16:(gl + 1) * 16, 0, mm_, g2 * 64:(g2 + 1) * 64] = Bre[g].T
                cst_s5w[c, gl * 16:(gl + 1) * 16, 1, mm_, g2 * 64:(g2 + 1) * 64] = Bim[g].T
                cst_s5w[c, g2 * 64:(g2 + 1) * 64, 2, mm_, gl * 16:(gl + 1) * 16] = Cre[g].T
                cst_s5w[c, g2 * 64:(g2 + 1) * 64, 3, mm_, gl * 16:(gl + 1) * 16] = Cim[g].T
    for c in range(NCORES):
        b = c // 2
        sl = slice(NSQ * c, NSQ * (c + 1))
        m = dict(cst)
        m["xp"] = inp["x_prompt"][b, :SEQ].T
        m["xs"] = inp["x_sample"][sl].reshape(TS, D).T
        m["cT"] = np.concatenate([inp["c_prompt"][b:b + 1], inp["c_sample"][sl]], 0).T
        m.update(wt)
        m["b_adaT"] = np.stack([_fm(inp["b_ada"][l]) for l in range(2)])
        m["g_mixT"] = np.stack([_fm(inp["g_mix"][l]) for l in range(2)])
        m["g_ffnT"] = np.stack([_fm(inp["g_ffn"][l]) for l in range(2)])
        m["g_finT"] = _fm(inp["g_final"])
        m["ffn_cw"] = inp["ffn_conv_w"].reshape(2, 3, NFF, 128).transpose(0, 3, 2, 1)
        m["ffn_cb"] = inp["ffn_conv_b"].reshape(2, NFF, 128).transpose(0, 2, 1)
        m["ffn_st"] = inp["state_ffn_conv"][:, sl].reshape(2, NSQ, 2, NFF, 128).transpose(0, 4, 3, 1, 2)
        m["gla_w2"] = inp["gla_w2"][0]
        m["gla_b2T"] = _fm(inp["gla_b2"][0])
        m["gla_ngT"] = _fm(inp["gla_norm_g"][0])
        m["gla_st"] = inp["state_gla"][0, sl].transpose(0, 2, 1, 3)
        m["gdn_cw"] = inp["gdn_conv_w"][0].reshape(4, 24, 128).transpose(2, 1, 0)
        m["gdn_cb"] = inp["gdn_conv_b"][0].reshape(24, 128).T
        m["gdn_cst"] = inp["state_gdn_conv"][0, sl].reshape(NSQ, 3, 24, 128).transpose(3, 2, 0, 1)
        m["gdn_Alog"] = inp["gdn_A_log"][0].reshape(8, 1)
        m["gdn_dtb"] = inp["gdn_dt_bias"][0].reshape(8, 1)
        m["gdn_ngT"] = inp["gdn_norm_g"][0].reshape(128, 1)
        m["gdn_st"] = inp["state_gdn"][0, sl].transpose(0, 2, 1, 3)
        m["ssd_cw"] = inp["ssd_conv_w"][0].reshape(4, 12, 128).transpose(2, 1, 0)
        m["ssd_cb"] = inp["ssd_conv_b"][0].reshape(12, 128).T
        m["ssd_cst"] = inp["state_ssd_conv"][0, sl].reshape(NSQ, 3, 12, 128).transpose(3, 2, 0, 1)
        m["ssd_Alog"] = inp["ssd_A_log"][0].reshape(16, 1)
        m["ssd_dtb"] = inp["ssd_dt_bias"][0].reshape(16, 1)
        m["ssd_Dcol"] = np.repeat(inp["ssd_D"][0], 64).reshape(8, 128).T
        m["ssd_ngT"] = _fm(inp["ssd_norm_g"][0])
        m["ssd_st"] = inp["state_ssd"][0, sl].reshape(NSQ, 8, 128, 128).transpose(0, 2, 1, 3)
        modes = lambda a: a.reshape(32, 128).T
        m["s5_are"] = modes(inp["s5_A_re"][0])
        m["s5_aim"] = modes(inp["s5_A_im"][0])
        m["s5_ldt"] = modes(np.repeat(inp["s5_log_dt"][0][:, None], 64, axis=1))
        m["s5w"] = cst_s5w
        m["s5_Dcol"] = _fm(inp["s5_D"][0])
        m["s5_x0"] = np.stack([inp["state_s5_re"][0, sl].reshape(NSQ, 32, 128).transpose(2, 1, 0),
                               inp["state_s5_im"][0, sl].reshape(NSQ, 32, 128).transpose(2, 1, 0)], 1)
        m["glu_bT"] = _fm(inp["s5_glu_b"][0])
        maps.append({k: np.ascontiguousarray(v, dtype=np.float32) for k, v in m.items()})
    return maps


_NC_CACHE = {}


def run_device(inp):
    if "nc" not in _NC_CACHE:
        _NC_CACHE["nc"] = build_program()
    nc = _NC_CACHE["nc"]
    maps = make_in_maps(inp)
    if RUNCORES < NCORES:
        res = run_bass_kernel_spmd(nc, maps[:RUNCORES], core_ids=list(range(RUNCORES)))
        return [res.results[min(c, RUNCORES - 1)] for c in range(NCORES)]
    res = run_bass_kernel_spmd(nc, maps, core_ids=list(range(NCORES)))
    return res.results


def assemble(R):
    B, DB = 4, 128
    y_p = np.stack([R[2 * b]["yp"].T for b in range(B)])
    y_s = np.concatenate([R[c]["ys"].T.reshape(NSQ, LS, D) for c in range(NCORES)], 0)

    def ffn_un(a):
        return a.transpose(0, 3, 4, 2, 1).reshape(2, a.shape[3], 2, 2 * DFF)

    ffn_p = np.concatenate([ffn_un(R[2 * b]["ffn_st_o"][:, :, :, 0:1]) for b in range(B)], 1)
    ffn_s = np.concatenate([ffn_un(R[c]["ffn_st_o"][:, :, :, 1:]) for c in range(NCORES)], 1)

    def st_un(a):
        return a.transpose(0, 2, 1, 3)[None]

    gla_p = np.concatenate([st_un(R[2 * b]["gla_o"][0:1]) for b in range(B)], 1)
    gla_s = np.concatenate([st_un(R[c]["gla_o"][1:]) for c in range(NCORES)], 1)
    gdn_p = np.concatenate([st_un(R[2 * b]["gdn_o"][0:1]) for b in range(B)], 1)
    gdn_s = np.concatenate([st_un(R[c]["gdn_o"][1:]) for c in range(NCORES)], 1)

    def cv_un(a, nchn):
        return a.transpose(2, 3, 1, 0).reshape(1, a.shape[2], 3, nchn * 128)

    gdc_p = np.concatenate([cv_un(R[2 * b]["gdn_cst_o"][:, :, 0:1], 24) for b in range(B)], 1)
    gdc_s = np.concatenate([cv_un(R[c]["gdn_cst_o"][:, :, 1:], 24) for c in range(NCORES)], 1)
    def ssd_un(a):
        return a.transpose(0, 2, 1, 3).reshape(1, a.shape[0], 16, 64, 128)

    ssd_p = np.concatenate([ssd_un(R[2 * b]["ssd_o"][0:1]) for b in range(B)], 1)
    ssd_s = np.concatenate([ssd_un(R[c]["ssd_o"][1:]) for c in range(NCORES)], 1)
    ssc_p = np.concatenate([cv_un(R[2 * b]["ssd_cst_o"][:, :, 0:1], 12) for b in range(B)], 1)
    ssc_s = np.concatenate([cv_un(R[c]["ssd_cst_o"][:, :, 1:], 12) for c in range(NCORES)], 1)

    def s5_un(a, k):
        a = a[:, :, :, k]
        return a.transpose(2, 1, 0).reshape(1, a.shape[2], 64, 64)

    s5 = [np.concatenate([s5_un(R[2 * b]["s5_o"][:, :, 0:1], k) for b in range(B)], 1) for k in range(2)]
    s5s = [np.concatenate([s5_un(R[c]["s5_o"][:, :, 1:], k) for c in range(NCORES)], 1) for k in range(2)]
    out = (y_p, y_s, gla_p, gla_s, gdn_p, gdn_s, gdc_p, gdc_s,
           ssd_p, ssd_s, ssc_p, ssc_s, s5[0], s5s[0], s5[1], s5s[1], ffn_p, ffn_s)
    return tuple(np.ascontiguousarray(o, dtype=np.float32) for o in out)


def kernel(**inputs):
    inp = {k: np.asarray(v) for k, v in inputs.items()}
    R = run_device(inp)
    return assemble(R)
```

```python
import os
import numpy as np
from contextlib import ExitStack
import concourse.bass as bass
import concourse.mybir as mybir
from concourse.bass_utils import run_bass_kernel_spmd

F32 = mybir.dt.float32
F32R = mybir.dt.float32r
BF16 = mybir.dt.bfloat16
MMDT = BF16 if int(os.environ.get("KDEV_BF16", "1")) else F32R
ALU = mybir.AluOpType
AF = mybir.ActivationFunctionType

NRING = 12
EPOCH = int(os.environ.get("KDEV_EPOCH", "1000000000"))
NEPOCH = 4
LAZY_PE_INC = bool(int(os.environ.get("KDEV_LAZY", "1")))
SAME_ENGINE_WAITS = bool(int(os.environ.get("KDEV_SEW", "0")))
NCORES = 8
RUNCORES = int(os.environ.get("KDEV_CORES", "8"))
D = 2048
KC = 16
DFF = 5632
NFF = 88
TP = 256
SEQ = int(os.environ.get("KDEV_SEQ", "2048"))
NPT = SEQ // TP
NSQ = 16
LS = 4
TS = NSQ * LS
EPS = 1e-6
STAGE = int(os.environ.get("KDEV_STAGE", "9"))
SUB = int(os.environ.get("KDEV_SUB", "99"))
CUT = int(os.environ.get("KDEV_CUT", "99"))
CDCUT = int(os.environ.get("KDEV_CDCUT", "99"))
SC = int(os.environ.get("KDEV_SC", "99"))
WS = 256
NWB = 3


PROJ_TAB = {
    "w_in_ab": [(0, 512), (512, 512), (1024, 1024), (2064, 1024), (2048, 16), (3088, 3072), (6176, 1024), (6160, 8), (6168, 8)],
    "w_in_cd": [(0, 1024), (1024, 1536), (2560, 16), (2576, 1024)],
    "w_out_ab": [(0, 2048)],
    "w_out_cd": [(0, 2048)],
    "glu_w": [(0, 1024)],
}


def slab_base(name, col0):
    base = 0
    for (c0, n) in PROJ_TAB[name]:
        if c0 == col0:
            return base
        base += (n + WS - 1) // WS
    raise KeyError((name, col0))


def n_slabs(name):
    return sum((n + WS - 1) // WS for (_, n) in PROJ_TAB[name])


def tile_weight(W, name):
    K = W.shape[0]
    nk = K // 128
    out = []
    for (c0, n) in PROJ_TAB[name]:
        for s0 in range(0, n, WS):
            m = min(WS, n - s0)
            blk = np.zeros((128, nk, WS), np.float32)
            blk[:, :, :m] = W[:, c0 + s0:c0 + s0 + m].reshape(nk, 128, m).transpose(1, 0, 2)
            out.append(blk)
    return np.stack(out)


def tile_cols_all(W):
    K, N = W.shape
    nk = K // 128
    return np.ascontiguousarray(W.reshape(nk, 128, N // WS, WS).transpose(2, 1, 0, 3))


class Buf:
    __slots__ = ("t", "name", "w", "r")

    def __init__(self, t, name):
        self.t = t
        self.name = name
        self.w = None
        self.r = []

    def __getitem__(self, k):
        return self.t[k]


class Prog:
    ENG = ("pe", "act", "dve", "pool", "sp")

    def __init__(self, nc, stack):
        self.nc = nc
        self.stack = stack
        self.ops = {e: [] for e in self.ENG}
        self.cnt = {e: 0 for e in self.ENG if e != "sp"}
        self.sem = {e: [stack.enter_context(nc.semaphore("s_%s%d" % (e, k))) for k in range(NEPOCH)] for e in self.cnt}
        self.dq = {}
        for q in ("sp", "pool"):
            self.dq[q] = dict(
                sems=[stack.enter_context(nc.semaphore("d_%s%d" % (q, i))) for i in range(NRING)], n=0)
        self.nbuf = 0
        self.psl = []
        self.psi = 0
        self.sew = True

    def sb(self, shape, dtype=F32, name=None):
        self.nbuf += 1
        name = name or "sb%d" % self.nbuf
        t = self.stack.enter_context(self.nc.sbuf_tensor(name, list(shape), dtype))
        return Buf(t, name)

    def ps(self, shape, dtype=F32, name=None):
        self.nbuf += 1
        name = name or "ps%d" % self.nbuf
        t = self.stack.enter_context(self.nc.psum_tensor(name, list(shape), dtype))
        return Buf(t, name)

    def view(self, ap, name):
        self.nbuf += 1
        return Buf(ap, name)

    def barrier(self, bufs):
        toks = [(e, v) for e, v in self.cnt.items() if v > 0]
        for q, st in self.dq.items():
            n = st["n"]
            for ring in range(min(n, NRING)):
                uses = (n - ring + NRING - 1) // NRING
                toks.append(("dma", q, ring, 16 * uses))
        for b in bufs:
            b.w = None
            b.r = list(toks)

    def next_ps(self):
        b = self.psl[self.psi % len(self.psl)]
        self.psi += 1
        return b

    def _deps(self, eng, reads, writes):
        deps = {}
        ddeps = {}

        def add(tok):
            if tok is None:
                return
            if tok[0] == "dma":
                k = (tok[1], tok[2])
                ddeps[k] = max(ddeps.get(k, 0), tok[3])
            else:
                e, idx = tok
                if e == eng and (e == "pe" or not (SAME_ENGINE_WAITS or self.sew)):
                    return
                deps[e] = max(deps.get(e, 0), idx)

        for b in reads:
            add(b.w)
        for b in writes:
            add(b.w)
            for t in b.r:
                add(t)
        return deps, ddeps

    def _record(self, tok, reads, writes):
        for b in reads:
            b.r.append(tok)
            if len(b.r) > 48:
                last = {}
                for t in b.r:
                    k = t[:3] if t[0] == "dma" else t[0]
                    if k not in last or t[-1] > last[k][-1]:
                        last[k] = t
                b.r = list(last.values())
        for b in writes:
            b.w = tok
            b.r = []

    def _ew(self, e, v):
        k = (v - 1) // EPOCH
        return (self.sem[e][k], v - k * EPOCH)

    def op(self, eng, fn, reads=(), writes=(), inc=True):
        deps, ddeps = self._deps(eng, reads, writes)
        if inc:
            self.cnt[eng] += 1
            idx = self.cnt[eng]
        else:
            idx = self.cnt[eng] + 1
        sem = self.sem[eng][(idx - 1) // EPOCH]
        waits = [self._ew(e, v) for e, v in deps.items()]
        waits += [(self.dq[q]["sems"][r], v) for (q, r), v in ddeps.items()]

        def emit(E, fn=fn, waits=waits, sem=sem, inc=inc):
            for s, v in waits:
                E.wait_ge(s, v)
            ins = fn(E)
            if inc:
                ins.then_inc(sem, 1)

        self.ops[eng].append(emit)
        self._record((eng, idx), reads, writes)

    def dma(self, q, out, in_, reads=(), writes=()):
        deps, ddeps = self._deps(q, reads, writes)
        st = self.dq[q]
        n = st["n"]
        st["n"] += 1
        ring = n % NRING
        val = 16 * (n // NRING + 1)
        sem = st["sems"][ring]
        waits = [self._ew(e, v) for e, v in deps.items()]
        waits += [(self.dq[qq]["sems"][r], v) for (qq, r), v in ddeps.items()]
        if val > 16:
            waits.append((sem, val - 16))

        def emit(E, waits=waits, sem=sem, out=out, in_=in_):
            for s, v in waits:
                E.wait_ge(s, v)
            E.dma_start(out=out, in_=in_).then_inc(sem, 16)

        self.ops[q].append(emit)
        self._record(("dma", q, ring, val), reads, writes)

    def mm(self, ob, o, lb, l, rb, r, start=True, stop=True):
        self.op("pe", lambda E: E.matmul(o, l, r, start=start, stop=stop), reads=[lb, rb], writes=[ob],
                inc=(stop or not LAZY_PE_INC))

    def tr(self, ob, o, ib, i, idb, idap):
        self.op("pe", lambda E: E.transpose(o, i, idap), reads=[ib, idb], writes=[ob])

    def act(self, ob, o, ib, i, func, bias=0.0, scale=1.0, extra=()):
        self.op("act", lambda E: E.activation(o, i, func, bias=bias, scale=scale), reads=[ib] + list(extra), writes=[ob])

    def tt(self, ob, o, ab, a, bb, b, op, eng="dve"):
        self.op(eng, lambda E: E.tensor_tensor(o, a, b, op), reads=[ab, bb], writes=[ob])

    def ts(self, ob, o, ab, a, s1, s2, op0, op1, extra=(), eng="dve"):
        self.op(eng, lambda E: E.tensor_scalar(o, a, s1, s2, op0, op1), reads=[ab] + list(extra), writes=[ob])

    def ts1(self, ob, o, ab, a, s1, op0, extra=(), eng="dve"):
        self.op(eng, lambda E: E.tensor_single_scalar(o, a, s1, op0), reads=[ab] + list(extra), writes=[ob])

    def stt(self, ob, o, ab, a, sc, bb, b, op0, op1, extra=(), eng="dve"):
        self.op(eng, lambda E: E.scalar_tensor_tensor(o, a, sc, b, op0, op1), reads=[ab, bb] + list(extra), writes=[ob])

    def cp(self, ob, o, ib, i, eng="dve"):
        if eng == "act":
            self.op("act", lambda E: E.activation(o, i, AF.Copy), reads=[ib], writes=[ob])
        else:
            self.op(eng, lambda E: E.tensor_copy(o, i), reads=[ib], writes=[ob])

    def memset(self, ob, o, v, eng="dve"):
        self.op(eng, lambda E: E.memset(o, v), writes=[ob])

    def scan(self, ob, o, d0b, d0, d1b, d1, init=0.0):
        self.op("dve", lambda E: E.tensor_tensor_scan(o, d0, d1, init, ALU.mult, ALU.add), reads=[d0b, d1b], writes=[ob])

    def finish(self):
        nc = self.nc
        fin = []
        for q, st in self.dq.items():
            n = st["n"]
            for ring in range(min(n, NRING)):
                uses = (n - ring + NRING - 1) // NRING
                fin.append((st["sems"][ring], 16 * uses))
        fin += [self._ew(e, v) for e, v in self.cnt.items() if v > 0]

        def emit_fin(E, fin=fin):
            for s, v in fin:
                E.wait_ge(s, v)

        if os.environ.get("KDEV_COUNTS"):
            print("COUNTS", dict(self.cnt), {q: st["n"] for q, st in self.dq.items()}, flush=True)
        self.ops["sp"].append(emit_fin)
        ops = self.ops
        with nc.Block() as block:
            @block.sync
            def _(E):
                for f in ops["sp"]:
                    f(E)

            @block.tensor
            def _(E):
                for f in ops["pe"]:
                    f(E)

            @block.scalar
            def _(E):
                for f in ops["act"]:
                    f(E)

            @block.vector
            def _(E):
                for f in ops["dve"]:
                    f(E)

            @block.gpsimd
            def _(E):
                for f in ops["pool"]:
                    f(E)


class Tile:
    def __init__(self, kind, idx):
        self.kind = kind
        self.idx = idx
        if kind == "p":
            self.T, self.nseq, self.L, self.s0, self.C = TP, 1, TP, 0, 64
        else:
            self.T, self.nseq, self.L, self.s0, self.C = TS, NSQ, LS, 1, LS
        self.first = (kind == "p" and idx == 0)
        self.last = (kind == "s") or (idx == NPT - 1)
        self.nch = self.T // self.C


def bc(ap, shape):
    return ap.broadcast_to(list(shape))


def build_program():
    nc = bass.Bass("TRN2", target_bir_lowering=False)

    def din(name, shape):
        return nc.dram_tensor(name, list(shape), F32, kind="ExternalInput").ap()

    def dout(name, shape):
        return nc.dram_tensor(name, list(shape), F32, kind="ExternalOutput").ap()

    NS1 = 1 + NSQ
    I = {}
    for name, shape in [
        ("xp", [D, SEQ]), ("xs", [D, TS]), ("cT", [D, NS1]),
        ("w_ada", [2, 6 * D // WS, 128, KC, WS]), ("b_adaT", [2, 128, 96]), ("g_mixT", [2, 128, KC]), ("g_ffnT", [2, 128, KC]),
        ("g_finT", [128, KC]), ("w_ffn_up", [2, 2 * DFF // WS, 128, KC, WS]), ("w_ffn_down", [2, DFF // 256, 128, 2, D]),
        ("ffn_cw", [2, 128, NFF, 3]), ("ffn_cb", [2, 128, NFF]), ("ffn_st", [2, 128, NFF, NSQ, 2]),
        ("ident", [128, 128]), ("UT", [128, 128]), ("nSU", [128, 128]), ("SL", [128, 128]),
        ("rmask_p", [128, TP]), ("rmask_s", [128, TS]), ("esel", [8, 8, 128]),
        ("w_in_ab", [n_slabs("w_in_ab"), 128, KC, WS]), ("w_out_ab", [n_slabs("w_out_ab"), 128, KC, WS]),
        ("gla_w2", [16, 512]), ("gla_b2T", [128, 4]), ("gla_ngT", [128, 2]), ("gla_st", [NSQ, 128, 4, 256]),
        ("gdn_cw", [128, 24, 4]), ("gdn_cb", [128, 24]), ("gdn_cst", [128, 24, NSQ, 3]),
        ("gdn_Alog", [8, 1]), ("gdn_dtb", [8, 1]), ("gdn_ngT", [128, 1]), ("gdn_st", [NSQ, 128, 8, 128]),
        ("w_in_cd", [n_slabs("w_in_cd"), 128, KC, WS]), ("w_out_cd", [n_slabs("w_out_cd"), 128, KC, WS]), ("ssd_cw", [128, 12, 4]), ("ssd_cb", [128, 12]),
        ("ssd_cst", [128, 12, NSQ, 3]), ("ssd_Alog", [16, 1]), ("ssd_dtb", [16, 1]), ("ssd_Dcol", [128, 8]),
        ("ssd_ngT", [128, 8]), ("ssd_st", [NSQ, 128, 8, 128]),
        ("s5_are", [128, 32]), ("s5_aim", [128, 32]), ("s5_ldt", [128, 32]), ("s5w", [8, 128, 4, 4, 128]),
        ("s5_Dcol", [128, 8]), ("s5_x0", [128, 2, 32, NSQ]), ("glu_w", [n_slabs("glu_w"), 128, 8, WS]), ("glu_bT", [128, 8]),
    ]:
        I[name] = din(name, shape)
    O = {}
    for name, shape in [
        ("yp", [D, SEQ]), ("ys", [D, TS]), ("ffn_st_o", [2, 128, NFF, NS1, 2]),
        ("gla_o", [NS1, 128, 4, 256]), ("gdn_o", [NS1, 128, 8, 128]), ("gdn_cst_o", [128, 24, NS1, 3]),
        ("ssd_o", [NS1, 128, 8, 128]), ("ssd_cst_o", [128, 12, NS1, 3]), ("s5_o", [128, 32, NS1, 2]),
    ]:
        O[name] = dout(name, shape)

    with ExitStack() as stack:
        P = Prog(nc, stack)
        P.psl = [P.ps([128, 512], F32, name="psb%d" % i) for i in range(8)]
        xT = P.sb([128, KC, TP], F32, "xT")
        hT = P.sb([128, KC, TP], MMDT, "hT")
        MWt = stack.enter_context(nc.sbuf_tensor("MW", [128, 72, TP], F32))
        yacc = P.view(MWt[:, 0:16, :], "yacc")
        RWt = stack.enter_context(nc.sbuf_tensor("RW", [128, 8, TP], F32R))
        RBt = stack.enter_context(nc.sbuf_tensor("RB", [128, 8, TP], MMDT))
        WB = [P.sb([128, KC * WS], MMDT, "wb%d" % i) for i in range(NWB)]
        s5wb = P.sb([128, 16 * 128], F32R, "s5wb")
        wbi = [0]

        def next_wb():
            b = WB[wbi[0] % NWB]
            wbi[0] += 1
            return b

        def const(name, shape, q="sp"):
            b = P.sb(shape, F32, "c_" + name)
            P.dma(q, b[:], I[name], writes=[b])
            return b

        def U(a, b, name):
            return P.view(MWt[:, a:b, :], name)

        ones = P.sb([128, 128], F32, "ones")
        P.memset(ones, ones[:], 1.0)
        ident = const("ident", [128, 128])
        UT = const("UT", [128, 128])
        nSU = const("nSU", [128, 128])
        SL = const("SL", [128, 128])
        rmask = {"p": const("rmask_p", [128, TP]), "s": const("rmask_s", [128, TS])}
        gfin = const("g_finT", [128, KC])
        modT = [P.view(MWt[:, 4 + 7 * l:11 + 7 * l, :].rearrange("p a t -> p (a t)")[:, 0:96 * NS1].rearrange(
            "p (c n) -> p c n", n=NS1), "modT%d" % l) for l in range(2)]
        modP = [P.sb([128, 96], F32, "modP%d" % l) for l in range(2)]
        modS = P.sb([128, 16, NSQ], F32, "modS")
        modD = nc.dram_tensor("modD", [2, 128, 96, NS1], F32, kind="Internal").ap()
        modDB = Buf(None, "modD")
        ffst = [P.sb([128, NFF, 2], F32, "ffst%d" % l) for l in range(2)]
        ffcw = [P.sb([128, NFF, 3], F32, "ffcw%d" % l) for l in range(2)]
        ffcb = [P.sb([128, NFF], F32, "ffcb%d" % l) for l in range(2)]
        for l in range(2):
            P.memset(ffst[l], ffst[l][:], 0.0)
            P.dma("sp", ffcw[l][:], I["ffn_cw"][l], writes=[ffcw[l]])
            P.dma("sp", ffcb[l][:], I["ffn_cb"][l], writes=[ffcb[l]])
        rstd = P.sb([128, TP], F32, "rstd")
        cvb = U(22, 26, "cvb")
        sgb = U(26, 28, "sgb")
        actT = [P.view(RBt[:, 2 * i:2 * i + 2, :], "actT%d" % i) for i in range(2)]
        cvtmp = P.sb([128, TP], F32, "cvtmp")
        cstage = P.sb([128, TP + 3 * NSQ], F32, "cstage")

        cond0 = P.view(MWt[:, 0:2, :].rearrange("p a t -> p (a t)").rearrange("p (c n) -> p c n", n=32), "cond0")
        condT = P.view(RBt[:, 0:2, :].rearrange("p a t -> p (a t)").rearrange("p (c n) -> p c n", n=32), "condT")
        P.memset(cond0, cond0[:], 0.0)
        P.dma("sp", cond0[:, :, 0:NS1], I["cT"].rearrange("(c p) n -> p c n", p=128), writes=[cond0])
        P.act(condT, condT[:], cond0, cond0[:], AF.Silu)
        for l in range(2):
            badT = P.sb([128, 96], F32, "badT%d" % l)
            gm = P.sb([128, KC], F32, "gmix%d" % l)
            gf = P.sb([128, KC], F32, "gffn%d" % l)
            P.dma("sp", badT[:], I["b_adaT"][l], writes=[badT])
            P.dma("sp", gm[:], I["g_mixT"][l], writes=[gm])
            P.dma("sp", gf[:], I["g_ffnT"][l], writes=[gf])
            nj = WS // 128
            for s in range(6 * D // WS):
                wb = next_wb()
                wbv = wb[:].rearrange("p (c n) -> p c n", n=WS)
                P.dma("pool", wbv, I["w_ada"][l][s], writes=[wb])
                ps = P.next_ps()
                for j in range(nj):
                    for kc in range(KC):
                        P.mm(ps, ps[:, j * 32:j * 32 + 32], wb, wbv[:, kc, j * 128:(j + 1) * 128], condT, condT[:, kc, :],
                             start=(kc == 0), stop=(kc == KC - 1))
                psv = ps[:, 0:32 * nj].rearrange("p (j n) -> p j n", n=32)[:, :, 0:NS1]
                P.tt(modT[l], modT[l][:, s * nj:(s + 1) * nj, :], ps, psv,
                     badT, bc(badT[:, s * nj:(s + 1) * nj].unsqueeze(2), [128, nj, NS1]), ALU.add)
            for (off, g) in ((16, gm), (64, gf)):
                P.stt(modT[l], modT[l][:, off:off + 16, :], modT[l], modT[l][:, off:off + 16, :], 1.0,
                      g, bc(g[:].unsqueeze(2), [128, 16, NS1]), ALU.add, ALU.mult)
            P.cp(modP[l], modP[l][:].unsqueeze(2), modT[l], modT[l][:, :, 0:1])
            P.dma("sp", modD[l], modT[l][:, :, :], reads=[modT[l]], writes=[modDB])

        def v4(ap, tl):
            return ap.rearrange("p c (s l) -> p c s l", l=tl.L)

        def modb(l, off):
            P.dma("sp", modS[:], modD[l][:, off:off + 16, 1:NS1], reads=[modDB], writes=[modS])
            return bc(modS[:].unsqueeze(3), [128, 16, NSQ, LS])

        def rms_stats(tl):
            T = tl.T
            P.act(yacc, yacc[:, :, :T], xT, xT[:, :, :T], AF.Square)
            ps = P.next_ps()
            for c in range(KC):
                P.mm(ps, ps[:, :T], ones, ones[:], yacc, yacc[:, c, :T], start=(c == 0), stop=(c == KC - 1))
            P.act(rstd, rstd[:, :T], ps, ps[:, :T], AF.Ln, bias=EPS, scale=1.0 / D)
            P.act(rstd, rstd[:, :T], rstd, rstd[:, :T], AF.Exp, scale=-0.5)
            P.tt(yacc, yacc[:, :, :T], xT, xT[:, :, :T], rstd, bc(rstd[:, :T].unsqueeze(1), [128, KC, T]), ALU.mult)

        def norm_mod(tl, l, sh, sc):
            T = tl.T
            P.barrier([yacc])
            rms_stats(tl)
            if tl.kind == "p":
                for c in range(KC):
                    P.act(hT, hT[:, c, :T], yacc, yacc[:, c, :T], AF.Identity, bias=modP[l][:, sh + c:sh + c + 1],
                          scale=modP[l][:, sc + c:sc + c + 1], extra=[modP[l]])
            else:
                P.tt(yacc, v4(yacc[:, :, :T], tl), yacc, v4(yacc[:, :, :T], tl), modS, modb(l, sc), ALU.mult)
                P.tt(hT, v4(hT[:, :, :T], tl), yacc, v4(yacc[:, :, :T], tl), modS, modb(l, sh), ALU.add)

        def resid_add(tl, l, goff, src):
            T = tl.T
            if tl.kind == "p":
                for c in range(KC):
                    P.stt(xT, xT[:, c, :T], src, src[:, c, :T], modP[l][:, goff + c:goff + c + 1], xT, xT[:, c, :T],
                          ALU.mult, ALU.add, extra=[modP[l]])
            else:
                P.tt(src, v4(src[:, :, :T], tl), src, v4(src[:, :, :T], tl), modS, modb(l, goff), ALU.mult)
                P.tt(xT, xT[:, :, :T], xT, xT[:, :, :T], src, src[:, :, :T], ALU.add)

        def proj_fm(tl, w_ap, col0, ncols, consume, rhs=None, nk=KC):
            T = tl.T
            rhs = rhs or hT
            wname = w_ap
            sb0 = slab_base(wname, col0)
            for s0 in range(0, ncols, WS):
                n = min(WS, ncols - s0)
                wb = next_wb()
                wbv = wb[:].rearrange("p (c n) -> p c n", n=WS)
                P.dma("pool", wbv[:, 0:nk, :], I[wname][sb0 + s0 // WS], writes=[wb])
                for j in range((n + 127) // 128):
                    m = min(128, n - j * 128)
                    ps = P.next_ps()
                    for kc in range(nk):
                        P.mm(ps, ps[0:m, :T], wb, wbv[:, kc, j * 128:j * 128 + m], rhs, rhs[:, kc, :T],
                             start=(kc == 0), stop=(kc == nk - 1))
                    consume(s0 // 128 + j, m, ps)

        def conv_chunk(tl, ps, cw, cb, ch, K, pst, sin, sout, out_b, out_ap, func):
            T, L, ns = tl.T, tl.L, tl.nseq
            H = K - 1
            sv = cstage[:, 0:ns * (L + H)].rearrange("p (s l) -> p s l", l=L + H)
            if tl.kind == "p":
                P.cp(cstage, sv[:, :, 0:H], pst, pst[:, ch:ch + 1, :])
            else:
                P.dma("sp", sv[:, :, 0:H], sin[:, ch, :, :], writes=[cstage])
            P.act(cstage, sv[:, :, H:H + L], ps, ps[:, :T].rearrange("p (s l) -> p s l", l=L), AF.Copy)
            if tl.kind == "p":
                P.cp(pst, pst[:, ch:ch + 1, :], cstage, sv[:, :, L:L + H])
                if tl.last:
                    P.dma("sp", sout[:, ch, 0:1, :], sv[:, :, L:L + H], reads=[cstage])
            else:
                P.dma("sp", sout[:, ch, 1:NS1, :], sv[:, :, L:L + H], reads=[cstage])
            acc = cvtmp[:, 0:T].rearrange("p (s l) -> p s l", l=L)
            P.ts(cvtmp, acc, cstage, sv[:, :, 0:L], cw[:, ch, 0:1], cb[:, ch:ch + 1], ALU.mult, ALU.add, extra=[cw, cb])
            for k in range(1, K):
                P.stt(cvtmp, acc, cstage, sv[:, :, k:k + L], cw[:, ch, k:k + 1], cvtmp, acc, ALU.mult, ALU.add, extra=[cw])
            ov = out_ap.rearrange("p (s l) -> p s l", l=L)
            if func is None:
                P.cp(out_b, ov, cvtmp, acc)
            else:
                P.act(out_b, ov, cvtmp, acc, func)

        def ffn(tl, l):
            T = tl.T
            P.barrier([cvb, sgb] + actT)
            wup = I["w_ffn_up"][l]
            wdn = I["w_ffn_down"][l]
            for g in range(DFF // 256):
                wba, wbg = next_wb(), next_wb()
                wva = wba[:].rearrange("p (c n) -> p c n", n=WS)
                wvg = wbg[:].rearrange("p (c n) -> p c n", n=WS)
                P.dma("pool", wva, wup[g], writes=[wba])
                P.dma("pool", wvg, wup[DFF // WS + g], writes=[wbg])
                for j in range(4):
                    ch = (g * 2 + j) if j < 2 else (44 + g * 2 + j - 2)
                    wb, wv = (wba, wva) if j < 2 else (wbg, wvg)
                    jj = j % 2
                    ps = P.next_ps()
                    for kc in range(KC):
                        P.mm(ps, ps[:, :T], wb, wv[:, kc, jj * 128:(jj + 1) * 128], hT, hT[:, kc, :T],
                             start=(kc == 0), stop=(kc == KC - 1))
                    conv_chunk(tl, ps, ffcw[l], ffcb[l], ch, 3, ffst[l], I["ffn_st"][l], O["ffn_st_o"][l],
                               cvb, cvb[:, j, :T], None)
                P.act(sgb, sgb[:, :, :T], cvb, cvb[:, 2:4, :T], AF.Silu)
                at = actT[g % 2]
                P.tt(at, at[:, :, :T], sgb, sgb[:, :, :T], cvb, cvb[:, 0:2, :T], ALU.mult)
                wd = next_wb()
                wdv = wd[:, 0:2 * D].rearrange("p (c n) -> p c n", n=D)
                P.dma("pool", wdv, wdn[g], writes=[wd])
                for oc in range(KC):
                    ps = P.next_ps()
                    for kc in range(2):
                        P.mm(ps, ps[:, :T], wd, wdv[:, kc, oc * 128:(oc + 1) * 128], at, at[:, kc, :T],
                             start=(kc == 0), stop=(kc == 1))
                    if g == 0:
                        P.cp(yacc, yacc[:, oc, :T], ps, ps[:, :T], eng="act")
                    else:
                        P.tt(yacc, yacc[:, oc, :T], yacc, yacc[:, oc, :T], ps, ps[:, :T], ALU.add)

        gla_S = [P.sb([128, 4, 256], F32, "glaS0")] * 2
        gdn_S = [P.sb([128, 8, 128], F32, "gdnS0")] * 2
        nb2 = const("gla_b2T", [128, 4])
        P.ts1(nb2, nb2[:], nb2, nb2[:], -1.0, ALU.mult)
        glang = const("gla_ngT", [128, 2])
        gdcw = const("gdn_cw", [128, 24, 4])
        gdcb = const("gdn_cb", [128, 24])
        gdnegA = const("gdn_Alog", [8, 1])
        P.act(gdnegA, gdnegA[:], gdnegA, gdnegA[:], AF.Exp)
        P.ts1(gdnegA, gdnegA[:], gdnegA, gdnegA[:], -1.0, ALU.mult)
        gddtb = const("gdn_dtb", [8, 1])
        gdng = const("gdn_ngT", [128, 1])
        gdcst = P.sb([128, 24, 3], F32, "gdcst")
        P.memset(gdcst, gdcst[:], 0.0)
        lrT = P.view(MWt[0:16, 54, :], "lrT")
        w2sb = P.view(MWt[0:16, 55:57, :].rearrange("p a t -> p (a t)"), "w2sb")
        abT = [P.sb([8, TP], F32, "abT%d" % i) for i in range(5)]
        abT.append(abT[1])
        gtok = P.sb([64, 8], F32, "gtok")
        eglast = P.sb([128, 8], F32, "eglast")
        m_q, m_k, m_v, m_r = U(0, 4, "m_q"), U(4, 8, "m_k"), U(8, 16, "m_v"), U(16, 24, "m_r")
        m_cum, m_o, m_ln = U(24, 28, "m_cum"), U(28, 36, "m_o"), U(36, 40, "m_ln")
        tmpA = [U(40 + i, 41 + i, "tmpA%d" % i) for i in range(7)]
        a_vtk, a_ktk = U(48, 52, "a_vtk"), U(52, 54, "a_ktk")
        GLA_BUFS = [m_q, m_k, m_v, m_r, m_cum, m_o, m_ln, a_vtk, a_ktk] + tmpA
        g_qkv, g_gate, g_ob, g_sq = U(0, 24, "g_qkv"), U(36, 44, "g_gate"), U(44, 52, "g_ob"), U(52, 60, "g_sq")
        g_kb, g_dec, g_A, g_Q = U(52, 54, "g_kb"), U(54, 56, "g_dec"), U(56, 58, "g_A"), U(58, 60, "g_Q")
        g_vb, g_qi, g_kw, g_ko = U(60, 62, "g_vb"), U(62, 64, "g_qi"), U(64, 66, "g_kw"), U(66, 68, "g_ko")
        g_X, g_wk = U(68, 70, "g_X"), U(70, 72, "g_wk")
        g_tokA, g_tokB = U(52, 56, "g_tokA"), U(56, 60, "g_tokB")
        g_esel = U(24, 28, "g_esel")

        def t4(buf, C, nh):
            return buf[:].rearrange("p a t -> p (a t)")[:, 0:nh * C].rearrange("p (h c) -> p h c", c=C)

        def tok(buf, C, n):
            return buf[:].rearrange("p a t -> p (a t)")[0:C, 0:n]

        def gla(tl):
            T, C, nch = tl.T, tl.C, tl.nch
            W = "w_in_ab"
            P.barrier(GLA_BUFS + [lrT, w2sb])
            P.dma("sp", w2sb[:, :], I["gla_w2"], writes=[w2sb])
            proj_fm(tl, W, 0, 512, lambda j, m, ps: P.cp(m_q, m_q[:, j, :T], ps, ps[:, :T], eng="act"))
            proj_fm(tl, W, 512, 512, lambda j, m, ps: P.cp(m_k, m_k[:, j, :T], ps, ps[:, :T], eng="act"))
            proj_fm(tl, W, 1024, 1024, lambda j, m, ps: P.cp(m_v, m_v[:, j, :T], ps, ps[:, :T], eng="act"))
            proj_fm(tl, W, 2064, 1024, lambda j, m, ps: P.act(m_r, m_r[:, j, :T], ps, ps[:, :T], AF.Silu))
            proj_fm(tl, W, 2048, 16, lambda j, m, ps: P.cp(lrT, lrT[0:16, :T], ps, ps[0:16, :T]))
            for h in range(4):
                ps = P.next_ps()
                P.mm(ps, ps[:, :T], w2sb, w2sb[0:16, h * 128:(h + 1) * 128], lrT, lrT[0:16, :T])
                P.act(m_ln, m_ln[:, h, :T], ps, ps[:, :T], AF.Exp, bias=nb2[:, h:h + 1], scale=-1.0, extra=[nb2])
            P.act(m_ln, m_ln[:, :, :T], m_ln, m_ln[:, :, :T], AF.Ln, bias=1.0)
            if SUB < 2:
                return
            rm = rmask[tl.kind]
            for h in range(4):
                P.scan(m_cum, m_cum[:, h, :T], rm, rm[:, 0:T], m_ln, m_ln[:, h, :T])
            eb, QsT, enb, KsT, dl, KoT, attT = [t4(tmpA[i], C, 4) for i in range(7)]
            bt = tmpA
            vtk = tok(a_vtk, C, 1024)
            ktk = tok(a_ktk, C, 512)
            for n in range(nch if SUB >= 3 else 0):
                c0 = n * C
                S = gla_S[n % 2] if tl.kind == "s" else gla_S[0]
                if tl.kind == "s":
                    P.dma("sp", S[:], I["gla_st"][n], writes=[S])
                elif tl.first and n == 0:
                    P.memset(S, S[:], 0.0)
                cs = slice(c0, c0 + C)
                P.act(bt[0], eb, m_cum, m_cum[:, :, cs], AF.Exp, scale=-1.0 / 16)
                P.stt(bt[1], QsT, m_q, m_q[:, :, cs], 128 ** -0.5, bt[0], eb, ALU.mult, ALU.mult)
                P.act(bt[2], enb, m_cum, m_cum[:, :, cs], AF.Exp, scale=1.0 / 16)
                P.tt(bt[3], KsT, m_k, m_k[:, :, cs], bt[2], enb, ALU.mult)
                P.tt(bt[4], dl, m_cum, bc(m_cum[:, :, c0 + C - 1:c0 + C], [128, 4, C]), m_cum, m_cum[:, :, cs], ALU.subtract)
                P.act(bt[4], dl, bt[4], dl, AF.Exp, scale=-1.0 / 16)
                P.tt(bt[5], KoT, m_k, m_k[:, :, cs], bt[4], dl, ALU.mult)
                psa = P.next_ps()
                for h in range(4):
                    P.mm(psa, psa[0:C, h * C:(h + 1) * C], bt[3], KsT[:, h, :], bt[1], QsT[:, h, :])
                P.tt(bt[6], attT[0:C], psa, psa[0:C, 0:4 * C].rearrange("p (h c) -> p h c", c=C),
                     UT, bc(UT[0:C, 0:C].unsqueeze(1), [C, 4, C]), ALU.mult)
                for half in range(2):
                    psv = P.next_ps()
                    for q4 in range(4):
                        P.tr(psv, psv[0:C, q4 * 128:(q4 + 1) * 128], m_v, m_v[:, half * 4 + q4, cs], ident, ident[:])
                    P.cp(a_vtk, vtk[:, half * 512:(half + 1) * 512], psv, psv[0:C, :], eng="act")
                psk = P.next_ps()
                for h in range(4):
                    P.tr(psk, psk[0:C, h * 128:(h + 1) * 128], bt[5], KoT[:, h, :], ident, ident[:])
                P.cp(a_ktk, ktk, psk, psk[0:C, :], eng="act")
                pso = P.next_ps()
                for h in range(4):
                    for vc in range(2):
                        a = h * 2 + vc
                        o_ap = pso[:, a * C:(a + 1) * C]
                        P.mm(pso, o_ap, a_vtk, vtk[:, a * 128:(a + 1) * 128], bt[6], attT[0:C, h, :], start=True, stop=False)
                        P.mm(pso, o_ap, S, S[:, h, vc * 128:(vc + 1) * 128], bt[1], QsT[:, h, :], start=False, stop=True)
                P.cp(m_o, m_o[:, :, cs], pso, pso[:, 0:8 * C].rearrange("p (a c) -> p a c", c=C), eng="act")
                for half in range(2):
                    pss = P.next_ps()
                    for hh in range(2):
                        h = half * 2 + hh
                        P.mm(pss, pss[:, hh * 256:(hh + 1) * 256], a_ktk, ktk[:, h * 128:(h + 1) * 128],
                             a_vtk, vtk[:, h * 256:(h + 1) * 256])
                    for hh in range(2):
                        h = half * 2 + hh
                        P.stt(S, S[:, h, :], S, S[:, h, :], eb[:, h, C - 1:C], pss, pss[:, hh * 256:(hh + 1) * 256],
                              ALU.mult, ALU.add, extra=[bt[0]])
                if tl.kind == "s":
                    P.dma("sp", O["gla_o"][1 + n], S[:], reads=[S])
                elif tl.last and n == nch - 1:
                    P.dma("sp", O["gla_o"][0], S[:], reads=[S])
            if SUB < 4:
                return
            P.act(m_v, m_v[:, :, :T], m_o, m_o[:, :, :T], AF.Square)
            for half in range(2):
                ps = P.next_ps()
                for hh in range(2):
                    h = half * 2 + hh
                    for vc in range(2):
                        P.mm(ps, ps[:, hh * T:(hh + 1) * T], ones, ones[:], m_v, m_v[:, h * 2 + vc, :T],
                             start=(vc == 0), stop=(vc == 1))
                rsv = m_ln[:, half * 2:half * 2 + 2, :T]
                P.act(m_ln, rsv, ps, ps[:, 0:2 * T].rearrange("p (h t) -> p h t", t=T), AF.Ln, bias=EPS, scale=1.0 / 256)
                P.act(m_ln, rsv, m_ln, rsv, AF.Exp, scale=-0.5)
            o4 = m_o[:, :, :T].rearrange("p (h v) t -> p h v t", v=2)
            P.tt(m_o, o4, m_o, o4, m_ln, bc(m_ln[:, :, :T].unsqueeze(2), [128, 4, 2, T]), ALU.mult)
            P.tt(m_o, m_o[:, :, :T], m_o, m_o[:, :, :T], m_r, m_r[:, :, :T], ALU.mult)
            for vc in range(2):
                P.ts1(m_o, o4[:, :, vc, :], m_o, o4[:, :, vc, :], glang[:, vc:vc + 1], ALU.mult, extra=[glang])

        def gdn(tl):
            T, C, nch = tl.T, tl.C, tl.nch
            W = "w_in_ab"
            qkv, gg, ob, sq = g_qkv, g_gate, g_ob, g_sq
            P.barrier([qkv, gg, ob, sq, g_esel])
            esel = g_esel
            eselv = g_esel[:].rearrange("p a t -> p (a t)")[0:8, :].rearrange("p (h m) -> p h m", m=128)
            P.dma("sp", eselv, I["esel"], writes=[g_esel])

            def cons_qkv(j, m, ps):
                conv_chunk(tl, ps, gdcw, gdcb, j, 4, gdcst, I["gdn_cst"], O["gdn_cst_o"], qkv, qkv[:, j, :T], AF.Silu)

            proj_fm(tl, W, 3088, 3072, cons_qkv)
            proj_fm(tl, W, 6176, 1024, lambda j, m, ps: P.act(gg, gg[:, j, :T], ps, ps[:, :T], AF.Silu))
            proj_fm(tl, W, 6160, 8, lambda j, m, ps: P.cp(abT[0], abT[0][0:8, :T], ps, ps[0:8, :T]))
            proj_fm(tl, W, 6168, 8, lambda j, m, ps: P.cp(abT[1], abT[1][0:8, :T], ps, ps[0:8, :T]))
            aT, bT, gcT, egT, ekT, beT = abT
            if SUB < 6:
                return
            P.act(aT, aT[0:8, :T], aT, aT[0:8, :T], AF.Exp, bias=gddtb[0:8, 0:1], extra=[gddtb])
            P.act(aT, aT[0:8, :T], aT, aT[0:8, :T], AF.Ln, bias=1.0)
            P.ts1(aT, aT[0:8, :T], aT, aT[0:8, :T], gdnegA[0:8, 0:1], ALU.mult, extra=[gdnegA])
            P.act(beT, beT[0:8, :T], bT, bT[0:8, :T], AF.Sigmoid)
            rm = rmask[tl.kind]
            P.scan(gcT, gcT[0:8, :T], rm, rm[0:8, 0:T], aT, aT[0:8, :T])
            P.act(egT, egT[0:8, :T], gcT, gcT[0:8, :T], AF.Exp)
            g3 = gcT[0:8, :T].rearrange("p (n c) -> p n c", c=C)
            P.tt(ekT, ekT[0:8, :T].rearrange("p (n c) -> p n c", c=C), gcT, bc(g3[:, :, C - 1:C], [8, nch, C]), gcT, g3,
                 ALU.subtract)
            P.act(ekT, ekT[0:8, :T], ekT, ekT[0:8, :T], AF.Exp)
            for which in range(2):
                src = qkv[:, which * 8:(which + 1) * 8, :T]
                P.act(sq, sq[:, :, :T], qkv, src, AF.Square)
                for pr in range(4):
                    ps = P.next_ps()
                    for hh in range(2):
                        P.mm(ps, ps[:, hh * T:(hh + 1) * T], ones, ones[:], sq, sq[:, pr * 2 + hh, :T])
                    rv = sq[:, pr * 2:pr * 2 + 2, :T]
                    P.act(sq, rv, ps, ps[:, 0:2 * T].rearrange("p (h t) -> p h t", t=T), AF.Ln, bias=EPS)
                    P.act(sq, rv, sq, rv, AF.Exp, scale=-0.5)
                if which == 0:
                    P.stt(qkv, src, qkv, src, 128 ** -0.5, sq, sq[:, :, :T], ALU.mult, ALU.mult)
                else:
                    P.tt(qkv, src, qkv, src, sq, sq[:, :, :T], ALU.mult)
            if SUB < 7:
                return
            tbufs = [g_kb, g_vb, g_qi, g_kw, g_ko, g_dec, g_A, g_Q, g_X, g_wk]
            P.barrier(tbufs)
            kbT, vbT, qiT, kwT, koT, decT, AT, QT, XT, wk = [t4(b, C, 8) for b in tbufs]
            nsteps = {64: 5, 4: 1}[C]
            p3 = lambda ps_: ps_[0:C, 0:8 * C].rearrange("p (h c) -> p h c", c=C)
            pf = lambda ps_: ps_[:, 0:8 * C].rearrange("p (h c) -> p h c", c=C)
            for n in range(nch):
                c0 = n * C
                cs = slice(c0, c0 + C)
                S = gdn_S[n % 2] if tl.kind == "s" else gdn_S[0]
                if tl.kind == "s":
                    P.dma("sp", S[:], I["gdn_st"][n], writes=[S])
                elif tl.first and n == 0:
                    P.memset(S, S[:], 0.0)
                if SUB < 8:
                    break
                if n > 0:
                    P.barrier([g_kb, g_dec, g_A, g_Q])
                qc, kc, vc = qkv[:, 0:8, cs], qkv[:, 8:16, cs], qkv[:, 16:24, cs]

                def bcast_rows(srcb):
                    ps = P.next_ps()
                    for h in range(8):
                        P.mm(ps, ps[:, h * C:(h + 1) * C], esel, eselv[:, h, :], srcb, srcb[0:8, cs])
                    return ps, ps[:, 0:8 * C].rearrange("p (h c) -> p h c", c=C)

                if CUT < 0:
                    continue
                psb, pbv = bcast_rows(beT)
                if CUT < 1:
                    P.cp(g_kb, kbT, psb, pbv)
                    continue
                P.tt(g_kb, kbT, qkv, kc, psb, pbv, ALU.mult)
                P.tt(g_vb, vbT, qkv, vc, psb, pbv, ALU.mult)
                pse, pev = bcast_rows(egT)
                P.tt(g_qi, qiT, qkv, qc, pse, pev, ALU.mult)
                P.tt(g_kw, kwT, g_kb, kbT, pse, pev, ALU.mult)
                P.cp(eglast, eglast[:, :].unsqueeze(2), pse, pev[:, :, C - 1:C])
                psk, pkv = bcast_rows(ekT)
                P.tt(g_ko, koT, qkv, kc, psk, pkv, ALU.mult)
                if CUT < 2:
                    continue
                pst = P.next_ps()
                P.tr(pst, pst[0:C, 0:8], aT, aT[0:8, cs], ident, ident[0:8, 0:8])
                P.cp(gtok, gtok[0:C, :], pst, pst[0:C, 0:8], eng="act")
                P.tt(g_wk, wk[0:C], gtok, bc(gtok[0:C, :].unsqueeze(2), [C, 8, C]), SL, bc(SL[0:C, 0:C].unsqueeze(1), [C, 8, C]),
                     ALU.mult)
                psd = P.next_ps()
                for h in range(8):
                    P.mm(psd, psd[0:C, h * C:(h + 1) * C], g_wk, wk[0:C, h, :], UT, UT[0:C, 0:C])
                P.act(g_dec, decT[0:C], psd, p3(psd), AF.Exp)
                P.tt(g_dec, decT[0:C], g_dec, decT[0:C], UT, bc(UT[0:C, 0:C].unsqueeze(1), [C, 8, C]), ALU.mult)
                if CUT < 3:
                    continue
                psm = P.next_ps()
                for h in range(8):
                    P.mm(psm, psm[0:C, h * C:(h + 1) * C], qkv, kc[:, h, :], g_kb, kbT[:, h, :])
                P.tt(g_A, AT[0:C], psm, p3(psm), g_dec, decT[0:C], ALU.mult)
                P.tt(g_A, AT[0:C], g_A, AT[0:C], nSU, bc(nSU[0:C, 0:C].unsqueeze(1), [C, 8, C]), ALU.mult)
                psa = P.next_ps()
                for h in range(8):
                    P.mm(psa, psa[0:C, h * C:(h + 1) * C], qkv, kc[:, h, :], qkv, qc[:, h, :])
                P.tt(g_wk, wk[0:C], psa, p3(psa), g_dec, decT[0:C], ALU.mult)
                pq = P.next_ps()
                for h in range(8):
                    P.tr(pq, pq[0:C, h * C:(h + 1) * C], g_A, AT[0:C, h, :], ident, ident[0:C, 0:C])
                P.cp(g_Q, QT[0:C], pq, p3(pq), eng="act")
                P.tt(g_X, XT[0:C], g_A, AT[0:C], ident, bc(ident[0:C, 0:C].unsqueeze(1), [C, 8, C]), ALU.add)
                for step in range(nsteps if CUT >= 4 else 0):
                    lastst = (step == nsteps - 1)
                    pqt = P.next_ps()
                    for h in range(8):
                        P.mm(pqt, pqt[0:C, h * C:(h + 1) * C], g_A, AT[0:C, h, :], g_Q, QT[0:C, h, :])
                    if not lastst:
                        pq2 = P.next_ps()
                        for h in range(8):
                            P.mm(pq2, pq2[0:C, h * C:(h + 1) * C], g_Q, QT[0:C, h, :], g_A, AT[0:C, h, :])
                    P.cp(g_Q, QT[0:C], pqt, p3(pqt), eng="act")
                    if not lastst:
                        P.cp(g_A, AT[0:C], pq2, p3(pq2))
                    px = P.next_ps()
                    for h in range(8):
                        P.mm(px, px[0:C, h * C:(h + 1) * C], g_Q, QT[0:C, h, :], g_X, XT[0:C, h, :])
                    P.tt(g_X, XT[0:C], g_X, XT[0:C], px, p3(px), ALU.add)
                if CUT < 5:
                    continue
                P.barrier([g_tokA, g_tokB])
                tA = tok(g_tokA, C, 1024)
                tB = tok(g_tokB, C, 1024)

                def to_tok(srcb, srcv, dstb, dstv):
                    for half in range(2):
                        pt = P.next_ps()
                        for q4 in range(4):
                            P.tr(pt, pt[0:C, q4 * 128:(q4 + 1) * 128], srcb, srcv[:, half * 4 + q4, :], ident, ident[:])
                        P.cp(dstb, dstv[:, half * 512:(half + 1) * 512], pt, pt[0:C, :], eng="act")

                to_tok(g_vb, vbT, g_tokA, tA)
                to_tok(g_kw, kwT, g_tokB, tB)
                pu = P.next_ps()
                pw = P.next_ps()
                for h in range(8):
                    P.mm(pu, pu[:, h * C:(h + 1) * C], g_tokA, tA[:, h * 128:(h + 1) * 128], g_X, XT[0:C, h, :])
                    P.mm(pw, pw[:, h * C:(h + 1) * C], g_tokB, tB[:, h * 128:(h + 1) * 128], g_X, XT[0:C, h, :])
                P.cp(g_vb, vbT, pu, pf(pu), eng="act")
                P.cp(g_kw, kwT, pw, pf(pw))
                if CUT < 6:
                    continue
                pws = P.next_ps()
                for h in range(8):
                    P.mm(pws, pws[:, h * C:(h + 1) * C], S, S[:, h, :], g_kw, kwT[:, h, :])
                P.tt(g_vb, vbT, g_vb, vbT, pws, pf(pws), ALU.subtract)
                to_tok(g_vb, vbT, g_tokA, tA)
                to_tok(g_ko, koT, g_tokB, tB)
                po = P.next_ps()
                for h in range(8):
                    o_ap = po[:, h * C:(h + 1) * C]
                    P.mm(po, o_ap, S, S[:, h, :], g_qi, qiT[:, h, :], start=True, stop=False)
                    P.mm(po, o_ap, g_tokA, tA[:, h * 128:(h + 1) * 128], g_wk, wk[0:C, h, :], start=False, stop=True)
                P.cp(ob, ob[:, :, cs], po, pf(po), eng="act")
                if CUT < 7:
                    continue
                for half in range(2):
                    pss = P.next_ps()
                    for hh in range(4):
                        h = half * 4 + hh
                        P.mm(pss, pss[:, hh * 128:(hh + 1) * 128], g_tokB, tB[:, h * 128:(h + 1) * 128],
                             g_tokA, tA[:, h * 128:(h + 1) * 128])
                    for hh in range(4):
                        h = half * 4 + hh
                        P.stt(S, S[:, h, :], S, S[:, h, :], eglast[:, h:h + 1], pss, pss[:, hh * 128:(hh + 1) * 128],
                              ALU.mult, ALU.add, extra=[eglast])
                if tl.kind == "s":
                    P.dma("sp", O["gdn_o"][1 + n], S[:], reads=[S])
                elif tl.last and n == nch - 1:
                    P.dma("sp", O["gdn_o"][0], S[:], reads=[S])
            if SUB < 9:
                return
            P.barrier([sq])
            P.act(sq, sq[:, :, :T], ob, ob[:, :, :T], AF.Square)
            for pr in range(4):
                ps = P.next_ps()
                for hh in range(2):
                    P.mm(ps, ps[:, hh * T:(hh + 1) * T], ones, ones[:], sq, sq[:, pr * 2 + hh, :T])
                rv = sq[:, pr * 2:pr * 2 + 2, :T]
                P.act(sq, rv, ps, ps[:, 0:2 * T].rearrange("p (h t) -> p h t", t=T), AF.Ln, bias=EPS, scale=1.0 / 128)
                P.act(sq, rv, sq, rv, AF.Exp, scale=-0.5)
            P.tt(ob, ob[:, :, :T], ob, ob[:, :, :T], sq, sq[:, :, :T], ALU.mult)
            P.tt(ob, ob[:, :, :T], ob, ob[:, :, :T], gg, gg[:, :, :T], ALU.mult)
            P.cp(hT, hT[:, 0:8, :T], m_o, m_o[:, :, :T], eng="act")
            P.ts1(hT, hT[:, 8:16, :T], ob, ob[:, :, :T], gdng[:, 0:1], ALU.mult, extra=[gdng])

        def mixer_ab(tl):
            T = tl.T
            gla(tl)
            if SUB >= 5:
                gdn(tl)
            else:
                P.cp(hT, hT[:, 0:8, :T], m_o, m_o[:, :, :T], eng="act")
                P.act(hT, hT[:, 8:16, :T], m_o, m_o[:, :, :T], AF.Copy, scale=0.0)
            P.barrier([yacc])
            proj_fm(tl, "w_out_ab", 0, D, lambda j, m, ps: P.cp(yacc, yacc[:, j, :T], ps, ps[:, :T], eng="act"))
            resid_add(tl, 0, 32, yacc)

        sscw = const("ssd_cw", [128, 12, 4])
        sscb = const("ssd_cb", [128, 12])
        ssnegA = const("ssd_Alog", [16, 1])
        P.act(ssnegA, ssnegA[:], ssnegA, ssnegA[:], AF.Exp)
        P.ts1(ssnegA, ssnegA[:], ssnegA, ssnegA[:], -1.0, ALU.mult)
        ssdtb = const("ssd_dtb", [16, 1])
        ssDcol = const("ssd_Dcol", [128, 8])
        ssng = const("ssd_ngT", [128, 8])
        s5Dcol = const("s5_Dcol", [128, 8])
        glub = const("glu_bT", [128, 8])
        sscst = P.sb([128, 12, 3], F32, "sscst")
        P.memset(sscst, sscst[:], 0.0)
        He = P.sb([128, 8, 128], F32, "He")
        Ho = P.sb([128, 8, 128], F32, "Ho")
        P.memset(He, He[:], 0.0)
        P.memset(Ho, Ho[:], 0.0)
        dtr = [P.sb([16, TP], F32, "dtr%d" % i) for i in range(4)]
        tk = P.sb([64, 48], F32, "tk")
        dcl = P.sb([128, 16], F32, "dcl")
        s5st = P.sb([128, 32, 2], F32, "s5st")
        P.memset(s5st, s5st[:], 0.0)
        s5fre = P.sb([128, 32], F32, "s5fre")
        s5fim = P.sb([128, 32], F32, "s5fim")
        c_z, c_xbc, c_y = U(0, 8, "c_z"), U(8, 20, "c_xbc"), U(28, 36, "c_y")
        c_u = P.view(RWt[:, :, :], "c_u")
        c_sc, c_Ap, c_cin, c_bout = U(36, 40, "c_sc"), U(40, 44, "c_Ap"), U(44, 48, "c_cin"), U(48, 56, "c_bout")
        c_xe, c_xo, c_btk, c_cbm, c_es = U(56, 60, "c_xe"), U(60, 64, "c_xo"), U(64, 65, "c_btk"), U(65, 66, "c_cbm"), U(36, 44, "c_sq")
        c_xm = U(66, 70, "c_xm")
        c_stt = U(48, 56, "c_stt")
        s_tab = [U(36 + 4 * i, 40 + 4 * i, "s_tab%d" % i) for i in range(2)]
        s_bp, s_z = U(44, 46, "s_bp"), U(46, 48, "s_z")
        s_xt = stack.enter_context(nc.sbuf_tensor("s5x", [128, 2, TP], F32R))
        s_x = P.view(s_xt[:], "s_x")
        s_yd = U(0, 8, "s_yd")
        s_z5 = P.view(RBt[:, :, :], "s_z5")
        s_tmp = U(48, 50, "s_tmp")
        s_x0 = U(50, 54, "s_x0")
        s_so = U(54, 58, "s_so")
        CD_BUFS = [c_z, c_xbc, c_y, c_u, c_sc, c_Ap, c_cin, c_bout, c_xe, c_xo, c_btk, c_cbm, c_xm]

        s5tab = nc.dram_tensor("s5tab", [4, 128, 32, TP], F32, kind="Internal").ap()
        s5tabB = Buf(None, "s5tab")

        def s5_setup():
            are = const("s5_are", [128, 32])
            aim = const("s5_aim", [128, 32])
            ldt = const("s5_ldt", [128, 32])
            w = [P.view(MWt[:, 40, i * 32:(i + 1) * 32], "s5w%d" % i) for i in range(8)]
            w += [P.view(MWt[:, 41, i * 32:(i + 1) * 32], "s5w%d" % (8 + i)) for i in range(6)]
            P.barrier(w)
            dtv, ar, th, mag, img, c_, s_, t0, t1, den = w[:10]
            P.act(dtv, dtv[:], ldt, ldt[:], AF.Exp)
            P.tt(ar, ar[:], are, are[:], dtv, dtv[:], ALU.mult)
            P.tt(th, th[:], aim, aim[:], dtv, dtv[:], ALU.mult)
            P.act(mag, mag[:], ar, ar[:], AF.Exp)
            P.act(img, img[:], ar, ar[:], AF.Exp, scale=-1.0)
            P.act(s_, s_[:], th, th[:], AF.Sin, scale=1.0 / 16)
            P.act(t0, t0[:], th, th[:], AF.Sin, scale=1.0 / 32)
            P.tt(t0, t0[:], t0, t0[:], t0, t0[:], ALU.mult)
            P.ts(c_, c_[:], t0, t0[:], -2.0, 1.0, ALU.mult, ALU.add)
            for _ in range(4):
                P.tt(t0, t0[:], c_, c_[:], c_, c_[:], ALU.mult)
                P.tt(t1, t1[:], s_, s_[:], s_, s_[:], ALU.mult)
                P.tt(s_, s_[:], s_, s_[:], c_, c_[:], ALU.mult)
                P.ts1(s_, s_[:], s_, s_[:], 2.0, ALU.mult)
                P.tt(c_, c_[:], t0, t0[:], t1, t1[:], ALU.subtract)
            lre, lim, ire, iim = w[10:14]
            w = w[:10]
            P.tt(lre, lre[:], mag, mag[:], c_, c_[:], ALU.mult)
            P.tt(lim, lim[:], mag, mag[:], s_, s_[:], ALU.mult)
            P.tt(ire, ire[:], img, img[:], c_, c_[:], ALU.mult)
            P.tt(iim, iim[:], img, img[:], s_, s_[:], ALU.mult)
            P.ts1(iim, iim[:], iim, iim[:], -1.0, ALU.mult)
            P.tt(den, den[:], are, are[:], are, are[:], ALU.mult)
            P.tt(t0, t0[:], aim, aim[:], aim, aim[:], ALU.mult)
            P.tt(den, den[:], den, den[:], t0, t0[:], ALU.add)
            P.op("dve", lambda E: E.reciprocal(den[:], den[:]), reads=[den], writes=[den])
            nr = dtv
            P.ts1(nr, nr[:], lre, lre[:], -1.0, ALU.add)
            P.tt(t0, t0[:], nr, nr[:], are, are[:], ALU.mult)
            P.tt(t1, t1[:], lim, lim[:], aim, aim[:], ALU.mult)
            P.tt(t0, t0[:], t0, t0[:], t1, t1[:], ALU.add)
            P.tt(s5fre, s5fre[:], t0, t0[:], den, den[:], ALU.mult)
            P.tt(t0, t0[:], lim, lim[:], are, are[:], ALU.mult)
            P.tt(t1, t1[:], nr, nr[:], aim, aim[:], ALU.mult)
            P.tt(t0, t0[:], t0, t0[:], t1, t1[:], ALU.subtract)
            P.tt(s5fim, s5fim[:], t0, t0[:], den, den[:], ALU.mult)
            Tre, Tim, Tt = U(0, 8, "s5Tre"), U(8, 16, "s5Tim"), U(16, 24, "s5Tt")
            Fre, Fim = U(24, 32, "s5Fre"), U(32, 40, "s5Fim")
            lre, lim, ire, iim = lre, lim, ire, iim
            P.barrier([Tre, Tim, Tt, Fre, Fim])
            for grp in range(4):
                ms = slice(grp * 8, grp * 8 + 8)
                for kind, (bre, bim) in enumerate(((lre, lim), (ire, iim))):
                    P.cp(Tre, Tre[:, :, 0:1], bre, bre[:, ms].unsqueeze(2))
                    P.cp(Tim, Tim[:, :, 0:1], bim, bim[:, ms].unsqueeze(2))
                    n = 1
                    while n < TP:
                        sre = bc(Tre[:, :, n - 1:n], [128, 8, n])
                        sim = bc(Tim[:, :, n - 1:n], [128, 8, n])
                        tv = Tt[:, :, 0:n]
                        P.tt(Tt, tv, Tim, Tim[:, :, 0:n], Tim, sim, ALU.mult)
                        P.tt(Tre, Tre[:, :, n:2 * n], Tre, Tre[:, :, 0:n], Tre, sre, ALU.mult)
                        P.tt(Tre, Tre[:, :, n:2 * n], Tre, Tre[:, :, n:2 * n], Tt, tv, ALU.subtract)
                        P.tt(Tt, tv, Tim, Tim[:, :, 0:n], Tre, sre, ALU.mult)
                        P.tt(Tim, Tim[:, :, n:2 * n], Tre, Tre[:, :, 0:n], Tim, sim, ALU.mult)
                        P.tt(Tim, Tim[:, :, n:2 * n], Tim, Tim[:, :, n:2 * n], Tt, tv, ALU.add)
                        n *= 2
                    if kind == 0:
                        P.dma("sp", s5tab[0][:, ms, :], Tre[:], reads=[Tre], writes=[s5tabB])
                        P.dma("sp", s5tab[1][:, ms, :], Tim[:], reads=[Tim], writes=[s5tabB])
                    else:
                        fre = bc(s5fre[:, ms].unsqueeze(2), [128, 8, TP])
                        fim = bc(s5fim[:, ms].unsqueeze(2), [128, 8, TP])
                        P.tt(Fre, Fre[:], Tre, Tre[:], s5fre, fre, ALU.mult)
                        P.tt(Tt, Tt[:], Tim, Tim[:], s5fim, fim, ALU.mult)
                        P.tt(Fre, Fre[:], Fre, Fre[:], Tt, Tt[:], ALU.subtract)
                        P.tt(Fim, Fim[:], Tre, Tre[:], s5fim, fim, ALU.mult)
                        P.tt(Tt, Tt[:], Tim, Tim[:], s5fre, fre, ALU.mult)
                        P.tt(Fim, Fim[:], Fim, Fim[:], Tt, Tt[:], ALU.add)
                        P.dma("sp", s5tab[2][:, ms, :], Fre[:], reads=[Fre], writes=[s5tabB])
                        P.dma("sp", s5tab[3][:, ms, :], Fim[:], reads=[Fim], writes=[s5tabB])

        def ssd_state_in(n):
            P.barrier([c_stt])
            P.dma("sp", c_stt[:].rearrange("p a t -> p (a t)")[:, 0:1024].rearrange("p (c s) -> p c s", s=128), I["ssd_st"][n],
                  writes=[c_stt])
            sv = c_stt[:].rearrange("p a t -> p (a t)")[:, 0:1024].rearrange("p (c s) -> p c s", s=128)
            for half in range(2):
                ps = P.next_ps()
                for q4 in range(4):
                    P.tr(ps, ps[:, q4 * 128:(q4 + 1) * 128], c_stt, sv[:, half * 4 + q4, :], ident, ident[:])
                pv = ps[:, :].rearrange("p (c q) -> p c q", q=128)
                P.cp(He, He[:, half * 4:half * 4 + 4, 0:64], ps, pv[:, :, 0:64])
                P.cp(Ho, Ho[:, half * 4:half * 4 + 4, 64:128], ps, pv[:, :, 64:128])

        def ssd_state_out(dst):
            P.barrier([c_stt, c_es])
            sv = c_stt[:].rearrange("p a t -> p (a t)")[:, 0:1024].rearrange("p (c s) -> p c s", s=128)
            sm = c_es[:].rearrange("p a t -> p (a t)")[:, 0:1024].rearrange("p (c s) -> p c s", s=128)
            P.tt(c_es, sm, He, He[:], Ho, Ho[:], ALU.add)
            for half in range(2):
                ps = P.next_ps()
                for q4 in range(4):
                    P.tr(ps, ps[:, q4 * 128:(q4 + 1) * 128], c_es, sm[:, half * 4 + q4, :], ident, ident[:])
                P.cp(c_stt, sv[:, half * 4:half * 4 + 4, :], ps, ps[:, :].rearrange("p (c q) -> p c q", q=128), eng="act")
            P.dma("sp", dst, sv, reads=[c_stt])

        def ssd(tl):
            T, C, nch = tl.T, tl.C, tl.nch
            dT, aT, acT, wT = dtr
            P.act(dT, dT[0:16, :T], dT, dT[0:16, :T], AF.Exp, bias=ssdtb[0:16, 0:1], extra=[ssdtb])
            P.act(dT, dT[0:16, :T], dT, dT[0:16, :T], AF.Ln, bias=1.0)
            P.ts1(aT, aT[0:16, :T], dT, dT[0:16, :T], ssnegA[0:16, 0:1], ALU.mult, extra=[ssnegA])
            rm = rmask[tl.kind]
            P.scan(acT, acT[0:16, :T], rm, rm[0:16, 0:T], aT, aT[0:16, :T])
            a3 = acT[0:16, :T].rearrange("p (n c) -> p n c", c=C)
            w3 = wT[0:16, :T].rearrange("p (n c) -> p n c", c=C)
            P.tt(wT, w3, acT, bc(a3[:, :, C - 1:C], [16, nch, C]), acT, a3, ALU.subtract)
            P.act(wT, wT[0:16, :T], wT, wT[0:16, :T], AF.Exp)
            P.tt(wT, wT[0:16, :T], wT, wT[0:16, :T], dT, dT[0:16, :T], ALU.mult)
            P.act(acT, acT[0:16, :T], acT, acT[0:16, :T], AF.Exp)
            scT = t4(c_sc, C, 16)
            Ap = t4(c_Ap, C, 16)
            cin = t4(c_cin, C, 16)
            bout = c_bout[:].rearrange("p a t -> p (a t)")[0:C, :].rearrange("p (h s) -> p h s", s=128)
            xe, xo = tok(c_xe, C, 1024), tok(c_xo, C, 1024)
            btk = tok(c_btk, C, 256)
            cbm = t4(c_cbm, C, 2)
            P.memset(c_xe, xe, 0.0)
            P.memset(c_xo, xo, 0.0)
            p3 = lambda ps_, nh: ps_[0:C, 0:nh * C].rearrange("p (h c) -> p h c", c=C)
            for n in range(nch if SC >= 2 else 0):
                c0 = n * C
                cs = slice(c0, c0 + C)
                if tl.kind == "s" and SC >= 7:
                    ssd_state_in(n)
                    P.barrier([c_bout, c_cin])
                pst = P.next_ps()
                for i, src in enumerate((aT, dT, wT)):
                    P.tr(pst, pst[0:C, i * 16:(i + 1) * 16], src, src[0:16, cs], ident, ident[0:16, 0:16])
                P.cp(tk, tk[0:C, :], pst, pst[0:C, 0:48], eng="act")
                P.tt(c_Ap, Ap[0:C], tk, bc(tk[0:C, 0:16].unsqueeze(2), [C, 16, C]), SL, bc(SL[0:C, 0:C].unsqueeze(1), [C, 16, C]),
                     ALU.mult)
                for half in range(2):
                    psd = P.next_ps()
                    for hh in range(8):
                        P.mm(psd, psd[0:C, hh * C:(hh + 1) * C], c_Ap, Ap[0:C, half * 8 + hh, :], UT, UT[0:C, 0:C])
                    P.act(c_sc, scT[0:C, half * 8:half * 8 + 8, :], psd, p3(psd, 8), AF.Exp)
                pcb = P.next_ps()
                for g in range(2):
                    P.mm(pcb, pcb[0:C, g * C:(g + 1) * C], c_xbc, c_xbc[:, 8 + g, cs], c_xbc, c_xbc[:, 10 + g, cs])
                P.tt(c_cbm, cbm[0:C], pcb, p3(pcb, 2), UT, bc(UT[0:C, 0:C].unsqueeze(1), [C, 2, C]), ALU.mult)
                sc4 = scT[0:C].rearrange("p (g h) c -> p g h c", g=2)
                P.tt(c_sc, sc4, c_sc, sc4, c_cbm, bc(cbm[0:C].unsqueeze(2), [C, 2, 8, C]), ALU.mult)
                P.tt(c_sc, scT[0:C], c_sc, scT[0:C], tk, bc(tk[0:C, 16:32].unsqueeze(2), [C, 16, C]), ALU.mult)
                if SC < 3:
                    continue
                for half in range(2):
                    pt = P.next_ps()
                    for q4 in range(4):
                        P.tr(pt, pt[0:C, q4 * 128:(q4 + 1) * 128], c_xbc, c_xbc[:, half * 4 + q4, cs], ident, ident[:])
                    pv = pt[0:C, :].rearrange("p (c q) -> p c q", q=128)
                    xev = xe[:, half * 512:(half + 1) * 512].rearrange("p (c q) -> p c q", q=128)
                    xov = xo[:, half * 512:(half + 1) * 512].rearrange("p (c q) -> p c q", q=128)
                    P.cp(c_xe, xev[:, :, 0:64], pt, pv[:, :, 0:64])
                    P.cp(c_xo, xov[:, :, 64:128], pt, pv[:, :, 64:128])
                pb = P.next_ps()
                for g in range(2):
                    P.tr(pb, pb[0:C, g * 128:(g + 1) * 128], c_xbc, c_xbc[:, 8 + g, cs], ident, ident[:])
                P.cp(c_btk, btk, pb, pb[0:C, 0:256], eng="act")
                b4 = bout.rearrange("p (g h) s -> p g h s", g=2)
                P.tt(c_bout, b4, c_btk, bc(btk.rearrange("p (g s) -> p g s", g=2).unsqueeze(2), [C, 2, 8, 128]),
                     tk, bc(tk[0:C, 32:48].rearrange("p (g h) -> p g h", g=2).unsqueeze(3), [C, 2, 8, 128]), ALU.mult)
                if SC < 4:
                    continue
                xm = c_xm[:].rearrange("p a t -> p (a t)")[0:16, 0:16 * C].rearrange("p (h c) -> p h c", c=C)
                P.tt(c_xm, xm, acT, bc(acT[0:16, cs].unsqueeze(1), [16, 16, C]),
                     ident, bc(ident[0:16, 0:16].unsqueeze(2), [16, 16, C]), ALU.mult)
                for g in range(2):
                    pse = P.next_ps()
                    for hh in range(8):
                        P.mm(pse, pse[:, hh * C:(hh + 1) * C], ones, ones[0:16, :], c_xm, xm[:, g * 8 + hh, :])
                    pev = pse[:, 0:8 * C].rearrange("p (h c) -> p h c", c=C)
                    P.tt(c_cin, cin[:, g * 8:g * 8 + 8, :], c_xbc, bc(c_xbc[:, 10 + g, cs].unsqueeze(1), [128, 8, C]), pse, pev,
                         ALU.mult)
                    P.cp(dcl, dcl[:, g * 8:g * 8 + 8].unsqueeze(2), pse, pev[:, :, C - 1:C])
                if SC < 5:
                    continue
                py = P.next_ps()
                for c in range(8):
                    o_ap = py[:, c * C:(c + 1) * C]
                    P.mm(py, o_ap, c_xe, xe[:, c * 128:(c + 1) * 128], c_sc, scT[0:C, 2 * c, :], start=True, stop=False)
                    P.mm(py, o_ap, c_xo, xo[:, c * 128:(c + 1) * 128], c_sc, scT[0:C, 2 * c + 1, :], start=False, stop=False)
                    P.mm(py, o_ap, He, He[:, c, :], c_cin, cin[:, 2 * c, :], start=False, stop=False)
                    P.mm(py, o_ap, Ho, Ho[:, c, :], c_cin, cin[:, 2 * c + 1, :], start=False, stop=True)
                P.cp(c_y, c_y[:, :, cs], py, py[:, 0:8 * C].rearrange("p (a c) -> p a c", c=C), eng="act")
                if SC < 6:
                    continue
                for half in range(2):
                    pss = P.next_ps()
                    for cc in range(4):
                        c = half * 4 + cc
                        P.mm(pss, pss[:, cc * 128:cc * 128 + 64], c_bout, bout[:, 2 * c, :], c_xe, xe[:, c * 128:c * 128 + 64])
                        P.mm(pss, pss[:, cc * 128 + 64:(cc + 1) * 128], c_bout, bout[:, 2 * c + 1, :],
                             c_xo, xo[:, c * 128 + 64:(c + 1) * 128])
                    for cc in range(4):
                        c = half * 4 + cc
                        P.stt(He, He[:, c, 0:64], He, He[:, c, 0:64], dcl[:, 2 * c:2 * c + 1], pss, pss[:, cc * 128:cc * 128 + 64],
                              ALU.mult, ALU.add, extra=[dcl])
                        P.stt(Ho, Ho[:, c, 64:128], Ho, Ho[:, c, 64:128], dcl[:, 2 * c + 1:2 * c + 2],
                              pss, pss[:, cc * 128 + 64:(cc + 1) * 128], ALU.mult, ALU.add, extra=[dcl])
                if SC < 7:
                    continue
                if tl.kind == "s":
                    ssd_state_out(O["ssd_o"][1 + n])
                    P.barrier([c_bout, c_cin, c_sc, c_Ap])
                elif tl.last and n == nch - 1:
                    ssd_state_out(O["ssd_o"][0])
            if SC < 8:
                return
            for c in range(8):
                P.stt(c_y, c_y[:, c, :T], c_xbc, c_xbc[:, c, :T], ssDcol[:, c:c + 1], c_y, c_y[:, c, :T], ALU.mult, ALU.add,
                      extra=[ssDcol])
            P.tt(c_y, c_y[:, :, :T], c_y, c_y[:, :, :T], c_z, c_z[:, :, :T], ALU.mult)
            P.barrier([c_es])
            P.act(c_es, c_es[:, :, :T], c_y, c_y[:, :, :T], AF.Square)
            ps = P.next_ps()
            for g in range(2):
                for cc in range(4):
                    P.mm(ps, ps[:, g * T:(g + 1) * T], ones, ones[:], c_es, c_es[:, g * 4 + cc, :T], start=(cc == 0), stop=(cc == 3))
            rsv = c_es[:, 0:2, :T]
            P.act(c_es, rsv, ps, ps[:, 0:2 * T].rearrange("p (g t) -> p g t", t=T), AF.Ln, bias=EPS, scale=1.0 / 512)
            P.act(c_es, rsv, c_es, rsv, AF.Exp, scale=-0.5)
            y4 = c_y[:, :, :T].rearrange("p (g c) t -> p g c t", g=2)
            P.tt(c_y, y4, c_y, y4, c_es, bc(rsv.unsqueeze(2), [128, 2, 4, T]), ALU.mult)
            for c in range(8):
                P.ts1(hT, hT[:, c, :T], c_y, c_y[:, c, :T], ssng[:, c:c + 1], ALU.mult, extra=[ssng])

        def s5(tl):
            s5_body(tl)

        def s5_body(tl):
            T, L, ns = tl.T, tl.L, tl.nseq
            P.barrier(s_tab + [s_bp, s_z, s_x, s_yd, s_tmp, s_x0, s_so])
            x0v = s_x0[:].rearrange("p a t -> p (a t)")[:, 0:2 * 32 * NSQ].rearrange("p (k m s) -> p k m s", k=2, s=NSQ)
            sov = s_so[:].rearrange("p a t -> p (a t)")[:, 0:32 * NSQ * 2].rearrange("p (m s k) -> p m s k", s=NSQ, k=2)
            if tl.kind == "s":
                P.dma("sp", x0v, I["s5_x0"], writes=[s_x0])
            onesrow = bc(ones[:, 0:1], [128, T])
            TL = L
            for c in range(8):
                wb = s5wb
                wv = wb[:, 0:16 * 128].rearrange("p (k m q) -> p k m q", k=4, q=128)
                P.dma("pool", wv, I["s5w"][c], writes=[wb])
                pyr = P.next_ps()
                pyi = P.next_ps()
                for mm_ in range(4):
                    m = 4 * c + mm_
                    tb = s_tab[m % 2]
                    tv = tb[:].rearrange("p a t -> p (a t)")[:, 0:4 * TL].rearrange("p (k t) -> p k t", k=4)
                    P.dma("sp", tv, s5tab[:, :, m, 0:TL].rearrange("k p t -> p k t"), reads=[s5tabB], writes=[tb])

                    def tab(k):
                        return bc(tv[:, k, :].unsqueeze(1), [128, ns, L])

                    pb = P.next_ps()
                    P.mm(pb, pb[:, 0:T], wb, wv[:, 0, mm_, :], c_u, c_u[:, c, :T])
                    P.mm(pb, pb[:, T:2 * T], wb, wv[:, 1, mm_, :], c_u, c_u[:, c, :T])
                    bur = pb[:, 0:T].rearrange("p (s l) -> p s l", l=L)
                    bui = pb[:, T:2 * T].rearrange("p (s l) -> p s l", l=L)
                    bp = s_bp[:, :, :T].rearrange("p k (s l) -> p k s l", l=L)
                    tm = s_tmp[:, :, :T].rearrange("p k (s l) -> p k s l", l=L)
                    P.tt(s_bp, bp[:, 0], pb, bur, tb, tab(2), ALU.mult)
                    P.tt(s_tmp, tm[:, 0], pb, bui, tb, tab(3), ALU.mult)
                    P.tt(s_bp, bp[:, 0], s_bp, bp[:, 0], s_tmp, tm[:, 0], ALU.subtract)
                    P.tt(s_bp, bp[:, 1], pb, bui, tb, tab(2), ALU.mult)
                    P.tt(s_tmp, tm[:, 1], pb, bur, tb, tab(3), ALU.mult)
                    P.tt(s_bp, bp[:, 1], s_bp, bp[:, 1], s_tmp, tm[:, 1], ALU.add)
                    if tl.kind == "s":
                        for k in range(2):
                            P.tt(s_bp, bp[:, k, :, 0:1], s_bp, bp[:, k, :, 0:1], s_x0, x0v[:, k, m, :].unsqueeze(2), ALU.add)
                        for k in range(2):
                            P.scan(s_z, s_z[:, k, :T], rmask["s"], rmask["s"][:, 0:T], s_bp, s_bp[:, k, :T])
                    else:
                        for k in range(2):
                            P.op("dve", lambda E, k=k, m=m: E.tensor_tensor_scan(
                                s_z[:, k, :T], onesrow, s_bp[:, k, :T], s5st[:, m, k:k + 1], ALU.mult, ALU.add),
                                reads=[ones, s_bp, s5st], writes=[s_z])
                    zv = s_z[:, :, :T].rearrange("p k (s l) -> p k s l", l=L)
                    xv = s_x[:, :, :T].rearrange("p k (s l) -> p k s l", l=L)
                    P.tt(s_tmp, tm[:, 0], s_z, zv[:, 1], tb, tab(1), ALU.mult)
                    P.tt(s_tmp, tm[:, 1], s_z, zv[:, 0], tb, tab(0), ALU.mult)
                    P.tt(s_x, xv[:, 0], s_tmp, tm[:, 1], s_tmp, tm[:, 0], ALU.subtract)
                    P.tt(s_tmp, tm[:, 0], s_z, zv[:, 0], tb, tab(1), ALU.mult)
                    P.tt(s_tmp, tm[:, 1], s_z, zv[:, 1], tb, tab(0), ALU.mult)
                    P.tt(s_x, xv[:, 1], s_tmp, tm[:, 1], s_tmp, tm[:, 0], ALU.add)
                    if tl.kind == "s":
                        P.cp(s_so, sov[:, m].rearrange("p s k -> p k s").unsqueeze(3), s_x, xv[:, :, :, L - 1:L].bitcast(F32))
                    else:
                        P.cp(s5st, s5st[:, m, :].unsqueeze(2), s_x, s_x[:, :, T - 1:T].bitcast(F32))
                    P.mm(pyr, pyr[:, :T], wb, wv[:, 2, mm_, :], s_x, s_x[:, 0, :T], start=(mm_ == 0), stop=(mm_ == 3))
                    P.mm(pyi, pyi[:, :T], wb, wv[:, 3, mm_, :], s_x, s_x[:, 1, :T], start=(mm_ == 0), stop=(mm_ == 3))
                P.cp(s_tmp, s_tmp[:, 0, :T], pyi, pyi[:, :T], eng="act")
                P.tt(s_yd, s_yd[:, c, :T], pyr, pyr[:, :T], s_tmp, s_tmp[:, 0, :T], ALU.subtract)
                P.stt(s_yd, s_yd[:, c, :T], c_u, c_u[:, c, :T].bitcast(F32), s5Dcol[:, c:c + 1], s_yd, s_yd[:, c, :T],
                      ALU.mult, ALU.add, extra=[s5Dcol])
            if tl.kind == "s":
                P.dma("sp", O["s5_o"][:, :, 1:NS1, :], sov, reads=[s_so])
            elif tl.last:
                P.dma("sp", O["s5_o"][:, :, 0:1, :], s5st[:].unsqueeze(2), reads=[s5st])
            P.barrier([s_z5])
            P.act(s_z5, s_z5[:, :, :T], s_yd, s_yd[:, :, :T], AF.Gelu)

            def cons_glu(j, mrows, ps):
                P.act(s_tmp, s_tmp[:, 0, :T], ps, ps[:, :T], AF.Sigmoid, bias=glub[:, j:j + 1], extra=[glub])
                P.tt(hT, hT[:, 8 + j, :T], s_z5, s_z5[:, j, :T], s_tmp, s_tmp[:, 0, :T], ALU.mult)

            proj_fm(tl, "glu_w", 0, 1024, cons_glu, rhs=s_z5, nk=8)

        def mixer_cd(tl):
            T = tl.T
            W = "w_in_cd"
            P.barrier(CD_BUFS)
            PJ = int(os.environ.get("KDEV_PJ", "15"))
            if PJ & 1:
                proj_fm(tl, W, 0, 1024, lambda j, m, ps: P.act(c_z, c_z[:, j, :T], ps, ps[:, :T], AF.Silu))
            if PJ & 2:
                proj_fm(tl, W, 1024, 1536, lambda j, m, ps: conv_chunk(
                    tl, ps, sscw, sscb, j, 4, sscst, I["ssd_cst"], O["ssd_cst_o"], c_xbc, c_xbc[:, j, :T], AF.Silu))
            if PJ & 4:
                proj_fm(tl, W, 2560, 16, lambda j, m, ps: P.cp(dtr[0], dtr[0][0:16, :T], ps, ps[0:16, :T]))
            if PJ & 8:
                proj_fm(tl, W, 2576, 1024, lambda j, m, ps: P.cp(c_u, c_u[:, j, :T], ps, ps[:, :T], eng="act"))
            if CDCUT >= 3:
                ssd(tl)
            if CDCUT >= 4:
                s5(tl)
            if CDCUT < 4:
                return
            P.barrier([yacc])
            proj_fm(tl, "w_out_cd", 0, D, lambda j, m, ps: P.cp(yacc, yacc[:, j, :T], ps, ps[:, :T], eng="act"))
            resid_add(tl, 1, 32, yacc)

        def final_out(tl):
            T = tl.T
            P.barrier([yacc])
            rms_stats(tl)
            for c in range(KC):
                P.act(yacc, yacc[:, c, :T], yacc, yacc[:, c, :T], AF.Copy, scale=gfin[:, c:c + 1], extra=[gfin])
            if tl.kind == "p":
                P.dma("sp", ypv[:, :, tl.idx * TP:(tl.idx + 1) * TP], yacc[:, :, :T], reads=[yacc])
            else:
                P.dma("sp", ysv, yacc[:, :, :T], reads=[yacc])

        if STAGE >= 3:
            s5_setup()
        tiles = [Tile("p", i) for i in range(NPT)] + [Tile("s", 0)]
        P.sew = not bool(int(os.environ.get("KDEV_NOSEW", "0")))
        xpv = I["xp"].rearrange("(c p) t -> p c t", p=128)
        xsv = I["xs"].rearrange("(c p) t -> p c t", p=128)
        ypv = O["yp"].rearrange("(c p) t -> p c t", p=128)
        ysv = O["ys"].rearrange("(c p) t -> p c t", p=128)
        for tl in tiles:
            if tl.kind == "p":
                P.dma("sp", xT[:, :, :tl.T], xpv[:, :, tl.idx * TP:(tl.idx + 1) * TP], writes=[xT])
            else:
                P.dma("sp", xT[:, :, :tl.T], xsv, writes=[xT])
            for l in range(2):
                norm_mod(tl, l, 0, 16)
                if l == 0 and STAGE >= 2:
                    mixer_ab(tl)
                if l == 1 and STAGE >= 3 and CDCUT >= 2:
                    mixer_cd(tl)
                norm_mod(tl, l, 48, 64)
                ffn(tl, l)
                resid_add(tl, l, 80, yacc)
            final_out(tl)
        P.finish()
    return nc


def _fm(v):
    v = np.asarray(v, np.float32)
    return np.ascontiguousarray(v.reshape(-1, 128).T)


def _consts():
    i = np.arange(128)
    c = {}
    c["ident"] = np.eye(128, dtype=np.float32)
    c["UT"] = (i[None, :] >= i[:, None]).astype(np.float32)
    c["nSU"] = -(i[None, :] > i[:, None]).astype(np.float32)
    c["SL"] = (i[:, None] > i[None, :]).astype(np.float32)
    rp = np.ones((128, TP), np.float32)
    rp[:, ::64] = 0.0
    rs = np.ones((128, TS), np.float32)
    rs[:, ::LS] = 0.0
    c["rmask_p"], c["rmask_s"] = rp, rs
    es = np.zeros((8, 8, 128), np.float32)
    for h in range(8):
        es[h, h, :] = 1.0
    c["esel"] = es
    return c


def make_in_maps(inp):
    maps = []
    cst = _consts()
    assert WS == 256
    wt = {}
    wt["w_ada"] = np.stack([tile_cols_all(inp["w_ada"][l]) for l in range(2)])
    wt["w_ffn_up"] = np.stack([tile_cols_all(inp["w_ffn_up"][l]) for l in range(2)])
    wt["w_ffn_down"] = np.ascontiguousarray(inp["w_ffn_down"].reshape(2, DFF // 256, 2, 128, D).transpose(0, 1, 3, 2, 4))
    wt["w_in_ab"] = tile_weight(inp["w_in_ab"][0], "w_in_ab")
    wt["w_out_ab"] = tile_weight(inp["w_out_ab"][0], "w_out_ab")
    wt["w_in_cd"] = tile_weight(inp["w_in_cd"][0], "w_in_cd")
    wt["w_out_cd"] = tile_weight(inp["w_out_cd"][0], "w_out_cd")
    wt["glu_w"] = tile_weight(inp["s5_glu_w"][0], "glu_w")
    Bre, Bim, Cre, Cim = (inp[k][0] for k in ("s5_B_re", "s5_B_im", "s5_C_re", "s5_C_im"))
    cst_s5w = np.zeros((8, 128, 4, 4, 128), np.float32)
    for c in range(8):
        for mm_ in range(4):
            for g2 in range(2):
                gl = 2 * mm_ + g2
                g = 8 * c + gl
                cst_s5w[c, gl * 16:(gl + 1) * 16, 0, mm_, g2 * 64:(g2 + 1) * 64] = Bre[g].T
                cst_s5w[c, gl * 16:(gl + 1) * 16, 1, mm_, g2 * 64:(g2 + 1) * 64] = Bim[g].T
                cst_s5w[c, g2 * 64:(g2 + 1) * 64, 2, mm_, gl * 16:(gl + 1) * 16] = Cre[g].T
                cst_s5w[c, g2 * 64:(g2 + 1) * 64, 3, mm_, gl * 16:(gl + 1) * 16] = Cim[g].T
    for c in range(NCORES):
        b = c // 2
        sl = slice(NSQ * c, NSQ * (c + 1))
        m = dict(cst)
        m["xp"] = inp["x_prompt"][b, :SEQ].T
        m["xs"] = inp["x_sample"][sl].reshape(TS, D).T
        m["cT"] = np.concatenate([inp["c_prompt"][b:b + 1], inp["c_sample"][sl]], 0).T
        m.update(wt)
        m["b_adaT"] = np.stack([_fm(inp["b_ada"][l]) for l in range(2)])
        m["g_mixT"] = np.stack([_fm(inp["g_mix"][l]) for l in range(2)])
        m["g_ffnT"] = np.stack([_fm(inp["g_ffn"][l]) for l in range(2)])
        m["g_finT"] = _fm(inp["g_final"])
        m["ffn_cw"] = inp["ffn_conv_w"].reshape(2, 3, NFF, 128).transpose(0, 3, 2, 1)
        m["ffn_cb"] = inp["ffn_conv_b"].reshape(2, NFF, 128).transpose(0, 2, 1)
        m["ffn_st"] = inp["state_ffn_conv"][:, sl].reshape(2, NSQ, 2, NFF, 128).transpose(0, 4, 3, 1, 2)
        m["gla_w2"] = inp["gla_w2"][0]
        m["gla_b2T"] = _fm(inp["gla_b2"][0])
        m["gla_ngT"] = _fm(inp["gla_norm_g"][0])
        m["gla_st"] = inp["state_gla"][0, sl].transpose(0, 2, 1, 3)
        m["gdn_cw"] = inp["gdn_conv_w"][0].reshape(4, 24, 128).transpose(2, 1, 0)
        m["gdn_cb"] = inp["gdn_conv_b"][0].reshape(24, 128).T
        m["gdn_cst"] = inp["state_gdn_conv"][0, sl].reshape(NSQ, 3, 24, 128).transpose(3, 2, 0, 1)
        m["gdn_Alog"] = inp["gdn_A_log"][0].reshape(8, 1)
        m["gdn_dtb"] = inp["gdn_dt_bias"][0].reshape(8, 1)
        m["gdn_ngT"] = inp["gdn_norm_g"][0].reshape(128, 1)
        m["gdn_st"] = inp["state_gdn"][0, sl].transpose(0, 2, 1, 3)
        m["ssd_cw"] = inp["ssd_conv_w"][0].reshape(4, 12, 128).transpose(2, 1, 0)
        m["ssd_cb"] = inp["ssd_conv_b"][0].reshape(12, 128).T
        m["ssd_cst"] = inp["state_ssd_conv"][0, sl].reshape(NSQ, 3, 12, 128).transpose(3, 2, 0, 1)
        m["ssd_Alog"] = inp["ssd_A_log"][0].reshape(16, 1)
        m["ssd_dtb"] = inp["ssd_dt_bias"][0].reshape(16, 1)
        m["ssd_Dcol"] = np.repeat(inp["ssd_D"][0], 64).reshape(8, 128).T
        m["ssd_ngT"] = _fm(inp["ssd_norm_g"][0])
        m["ssd_st"] = inp["state_ssd"][0, sl].reshape(NSQ, 8, 128, 128).transpose(0, 2, 1, 3)
        modes = lambda a: a.reshape(32, 128).T
        m["s5_are"] = modes(inp["s5_A_re"][0])
        m["s5_aim"] = modes(inp["s5_A_im"][0])
        m["s5_ldt"] = modes(np.repeat(inp["s5_log_dt"][0][:, None], 64, axis=1))
        m["s5w"] = cst_s5w
        m["s5_Dcol"] = _fm(inp["s5_D"][0])
        m["s5_x0"] = np.stack([inp["state_s5_re"][0, sl].reshape(NSQ, 32, 128).transpose(2, 1, 0),
                               inp["state_s5_im"][0, sl].reshape(NSQ, 32, 128).transpose(2, 1, 0)], 1)
        m["glu_bT"] = _fm(inp["s5_glu_b"][0])
        maps.append({k: np.ascontiguousarray(v, dtype=np.float32) for k, v in m.items()})
    return maps


_NC_CACHE = {}


def run_device(inp):
    if "nc" not in _NC_CACHE:
        _NC_CACHE["nc"] = build_program()
    nc = _NC_CACHE["nc"]
    maps = make_in_maps(inp)
    if RUNCORES < NCORES:
        res = run_bass_kernel_spmd(nc, maps[:RUNCORES], core_ids=list(range(RUNCORES)))
        return [res.results[min(c, RUNCORES - 1)] for c in range(NCORES)]
    res = run_bass_kernel_spmd(nc, maps, core_ids=list(range(NCORES)))
    return res.results


def assemble(R):
    B, DB = 4, 128
    y_p = np.stack([R[2 * b]["yp"].T for b in range(B)])
    y_s = np.concatenate([R[c]["ys"].T.reshape(NSQ, LS, D) for c in range(NCORES)], 0)

    def ffn_un(a):
        return a.transpose(0, 3, 4, 2, 1).reshape(2, a.shape[3], 2, 2 * DFF)

    ffn_p = np.concatenate([ffn_un(R[2 * b]["ffn_st_o"][:, :, :, 0:1]) for b in range(B)], 1)
    ffn_s = np.concatenate([ffn_un(R[c]["ffn_st_o"][:, :, :, 1:]) for c in range(NCORES)], 1)

    def st_un(a):
        return a.transpose(0, 2, 1, 3)[None]

    gla_p = np.concatenate([st_un(R[2 * b]["gla_o"][0:1]) for b in range(B)], 1)
    gla_s = np.concatenate([st_un(R[c]["gla_o"][1:]) for c in range(NCORES)], 1)
    gdn_p = np.concatenate([st_un(R[2 * b]["gdn_o"][0:1]) for b in range(B)], 1)
    gdn_s = np.concatenate([st_un(R[c]["gdn_o"][1:]) for c in range(NCORES)], 1)

    def cv_un(a, nchn):
        return a.transpose(2, 3, 1, 0).reshape(1, a.shape[2], 3, nchn * 128)

    gdc_p = np.concatenate([cv_un(R[2 * b]["gdn_cst_o"][:, :, 0:1], 24) for b in range(B)], 1)
    gdc_s = np.concatenate([cv_un(R[c]["gdn_cst_o"][:, :, 1:], 24) for c in range(NCORES)], 1)
    def ssd_un(a):
        return a.transpose(0, 2, 1, 3).reshape(1, a.shape[0], 16, 64, 128)

    ssd_p = np.concatenate([ssd_un(R[2 * b]["ssd_o"][0:1]) for b in range(B)], 1)
    ssd_s = np.concatenate([ssd_un(R[c]["ssd_o"][1:]) for c in range(NCORES)], 1)
    ssc_p = np.concatenate([cv_un(R[2 * b]["ssd_cst_o"][:, :, 0:1], 12) for b in range(B)], 1)
    ssc_s = np.concatenate([cv_un(R[c]["ssd_cst_o"][:, :, 1:], 12) for c in range(NCORES)], 1)

    def s5_un(a, k):
        a = a[:, :, :, k]
        return a.transpose(2, 1, 0).reshape(1, a.shape[2], 64, 64)

    s5 = [np.concatenate([s5_un(R[2 * b]["s5_o"][:, :, 0:1], k) for b in range(B)], 1) for k in range(2)]
    s5s = [np.concatenate([s5_un(R[c]["s5_o"][:, :, 1:], k) for c in range(NCORES)], 1) for k in range(2)]
    out = (y_p, y_s, gla_p, gla_s, gdn_p, gdn_s, gdc_p, gdc_s,
           ssd_p, ssd_s, ssc_p, ssc_s, s5[0], s5s[0], s5[1], s5s[1], ffn_p, ffn_s)
    return tuple(np.ascontiguousarray(o, dtype=np.float32) for o in out)


def kernel(**inputs):
    inp = {k: np.asarray(v) for k, v in inputs.items()}
    R = run_device(inp)
    return assemble(R)
```

```python
import os
import numpy as np
from contextlib import ExitStack
import concourse.bass as bass
import concourse.mybir as mybir
from concourse.bass_utils import run_bass_kernel_spmd

F32 = mybir.dt.float32
F32R = mybir.dt.float32r
BF16 = mybir.dt.bfloat16
MMDT = BF16 if int(os.environ.get("KDEV_BF16", "1")) else F32R
ALU = mybir.AluOpType
AF = mybir.ActivationFunctionType

NRING = 12
EPOCH = int(os.environ.get("KDEV_EPOCH", "1000000000"))
NEPOCH = 4
LAZY_PE_INC = bool(int(os.environ.get("KDEV_LAZY", "1")))
SAME_ENGINE_WAITS = bool(int(os.environ.get("KDEV_SEW", "0")))
NCORES = 8
RUNCORES = int(os.environ.get("KDEV_CORES", "8"))
D = 2048
KC = 16
DFF = 5632
NFF = 88
TP = 256
SEQ = int(os.environ.get("KDEV_SEQ", "2048"))
NPT = SEQ // TP
NSQ = 16
LS = 4
TS = NSQ * LS
EPS = 1e-6
STAGE = int(os.environ.get("KDEV_STAGE", "9"))
SUB = int(os.environ.get("KDEV_SUB", "99"))
CUT = int(os.environ.get("KDEV_CUT", "99"))
CDCUT = int(os.environ.get("KDEV_CDCUT", "99"))
SC = int(os.environ.get("KDEV_SC", "99"))
WS = 256
NWB = int(os.environ.get("KDEV_NWB", "6"))


PROJ_TAB = {
    "w_in_ab": [(0, 512), (512, 512), (1024, 1024), (2064, 1024), (2048, 16), (3088, 3072), (6176, 1024), (6160, 8), (6168, 8)],
    "w_in_cd": [(0, 1024), (1024, 1536), (2560, 16), (2576, 1024)],
    "w_out_ab": [(0, 2048)],
    "w_out_cd": [(0, 2048)],
    "glu_w": [(0, 1024)],
}


def slab_base(name, col0):
    base = 0
    for (c0, n) in PROJ_TAB[name]:
        if c0 == col0:
            return base
        base += (n + WS - 1) // WS
    raise KeyError((name, col0))


def n_slabs(name):
    return sum((n + WS - 1) // WS for (_, n) in PROJ_TAB[name])


def tile_weight(W, name):
    K = W.shape[0]
    nk = K // 128
    out = []
    for (c0, n) in PROJ_TAB[name]:
        for s0 in range(0, n, WS):
            m = min(WS, n - s0)
            blk = np.zeros((128, nk, WS), np.float32)
            blk[:, :, :m] = W[:, c0 + s0:c0 + s0 + m].reshape(nk, 128, m).transpose(1, 0, 2)
            out.append(blk)
    return np.stack(out)


def tile_cols_all(W):
    K, N = W.shape
    nk = K // 128
    return np.ascontiguousarray(W.reshape(nk, 128, N // WS, WS).transpose(2, 1, 0, 3))


class Buf:
    __slots__ = ("t", "name", "w", "r")

    def __init__(self, t, name):
        self.t = t
        self.name = name
        self.w = None
        self.r = []

    def __getitem__(self, k):
        return self.t[k]


class Prog:
    ENG = ("pe", "act", "dve", "pool", "sp")

    def __init__(self, nc, stack):
        self.nc = nc
        self.stack = stack
        self.ops = {e: [] for e in self.ENG}
        self.cnt = {e: 0 for e in self.ENG if e != "sp"}
        self.sem = {e: [stack.enter_context(nc.semaphore("s_%s%d" % (e, k))) for k in range(NEPOCH)] for e in self.cnt}
        self.dq = {}
        for q in ("sp", "pool"):
            self.dq[q] = dict(
                sems=[stack.enter_context(nc.semaphore("d_%s%d" % (q, i))) for i in range(NRING)], n=0)
        self.nbuf = 0
        self.psl = []
        self.psi = 0
        self.sew = True

    def sb(self, shape, dtype=F32, name=None):
        self.nbuf += 1
        name = name or "sb%d" % self.nbuf
        t = self.stack.enter_context(self.nc.sbuf_tensor(name, list(shape), dtype))
        return Buf(t, name)

    def ps(self, shape, dtype=F32, name=None):
        self.nbuf += 1
        name = name or "ps%d" % self.nbuf
        t = self.stack.enter_context(self.nc.psum_tensor(name, list(shape), dtype))
        return Buf(t, name)

    def view(self, ap, name):
        self.nbuf += 1
        return Buf(ap, name)

    def barrier(self, bufs):
        toks = [(e, v) for e, v in self.cnt.items() if v > 0]
        for q, st in self.dq.items():
            n = st["n"]
            for ring in range(min(n, NRING)):
                uses = (n - ring + NRING - 1) // NRING
                toks.append(("dma", q, ring, 16 * uses))
        for b in bufs:
            b.w = None
            b.r = list(toks)

    def next_ps(self):
        b = self.psl[self.psi % len(self.psl)]
        self.psi += 1
        return b

    def _deps(self, eng, reads, writes):
        deps = {}
        ddeps = {}

        def add(tok):
            if tok is None:
                return
            if tok[0] == "dma":
                k = (tok[1], tok[2])
                ddeps[k] = max(ddeps.get(k, 0), tok[3])
            else:
                e, idx = tok
                if e == eng and (e == "pe" or not (SAME_ENGINE_WAITS or self.sew)):
                    return
                deps[e] = max(deps.get(e, 0), idx)

        for b in reads:
            add(b.w)
        for b in writes:
            add(b.w)
            for t in b.r:
                add(t)
        return deps, ddeps

    def _record(self, tok, reads, writes):
        for b in reads:
            b.r.append(tok)
            if len(b.r) > 48:
                last = {}
                for t in b.r:
                    k = t[:3] if t[0] == "dma" else t[0]
                    if k not in last or t[-1] > last[k][-1]:
                        last[k] = t
                b.r = list(last.values())
        for b in writes:
            b.w = tok
            b.r = []

    def _ew(self, e, v):
        k = (v - 1) // EPOCH
        return (self.sem[e][k], v - k * EPOCH)

    def op(self, eng, fn, reads=(), writes=(), inc=True):
        deps, ddeps = self._deps(eng, reads, writes)
        if inc:
            self.cnt[eng] += 1
            idx = self.cnt[eng]
        else:
            idx = self.cnt[eng] + 1
        sem = self.sem[eng][(idx - 1) // EPOCH]
        waits = [self._ew(e, v) for e, v in deps.items()]
        waits += [(self.dq[q]["sems"][r], v) for (q, r), v in ddeps.items()]

        def emit(E, fn=fn, waits=waits, sem=sem, inc=inc):
            for s, v in waits:
                E.wait_ge(s, v)
            ins = fn(E)
            if inc:
                ins.then_inc(sem, 1)

        self.ops[eng].append(emit)
        self._record((eng, idx), reads, writes)

    def dma(self, q, out, in_, reads=(), writes=()):
        deps, ddeps = self._deps(q, reads, writes)
        st = self.dq[q]
        n = st["n"]
        st["n"] += 1
        ring = n % NRING
        val = 16 * (n // NRING + 1)
        sem = st["sems"][ring]
        waits = [self._ew(e, v) for e, v in deps.items()]
        waits += [(self.dq[qq]["sems"][r], v) for (qq, r), v in ddeps.items()]
        if val > 16:
            waits.append((sem, val - 16))

        def emit(E, waits=waits, sem=sem, out=out, in_=in_):
            for s, v in waits:
                E.wait_ge(s, v)
            E.dma_start(out=out, in_=in_).then_inc(sem, 16)

        self.ops[q].append(emit)
        self._record(("dma", q, ring, val), reads, writes)

    def mm(self, ob, o, lb, l, rb, r, start=True, stop=True):
        self.op("pe", lambda E: E.matmul(o, l, r, start=start, stop=stop), reads=[lb, rb], writes=[ob],
                inc=(stop or not LAZY_PE_INC))

    def tr(self, ob, o, ib, i, idb, idap):
        self.op("pe", lambda E: E.transpose(o, i, idap), reads=[ib, idb], writes=[ob])

    def act(self, ob, o, ib, i, func, bias=0.0, scale=1.0, extra=()):
        self.op("act", lambda E: E.activation(o, i, func, bias=bias, scale=scale), reads=[ib] + list(extra), writes=[ob])

    def tt(self, ob, o, ab, a, bb, b, op, eng="dve"):
        self.op(eng, lambda E: E.tensor_tensor(o, a, b, op), reads=[ab, bb], writes=[ob])

    def ts(self, ob, o, ab, a, s1, s2, op0, op1, extra=(), eng="dve"):
        self.op(eng, lambda E: E.tensor_scalar(o, a, s1, s2, op0, op1), reads=[ab] + list(extra), writes=[ob])

    def ts1(self, ob, o, ab, a, s1, op0, extra=(), eng="dve"):
        self.op(eng, lambda E: E.tensor_single_scalar(o, a, s1, op0), reads=[ab] + list(extra), writes=[ob])

    def stt(self, ob, o, ab, a, sc, bb, b, op0, op1, extra=(), eng="dve"):
        self.op(eng, lambda E: E.scalar_tensor_tensor(o, a, sc, b, op0, op1), reads=[ab, bb] + list(extra), writes=[ob])

    def cp(self, ob, o, ib, i, eng="dve"):
        if eng == "act":
            self.op("act", lambda E: E.activation(o, i, AF.Copy), reads=[ib], writes=[ob])
        else:
            self.op(eng, lambda E: E.tensor_copy(o, i), reads=[ib], writes=[ob])

    def memset(self, ob, o, v, eng="dve"):
        self.op(eng, lambda E: E.memset(o, v), writes=[ob])

    def scan(self, ob, o, d0b, d0, d1b, d1, init=0.0):
        self.op("dve", lambda E: E.tensor_tensor_scan(o, d0, d1, init, ALU.mult, ALU.add), reads=[d0b, d1b], writes=[ob])

    def finish(self):
        nc = self.nc
        fin = []
        for q, st in self.dq.items():
            n = st["n"]
            for ring in range(min(n, NRING)):
                uses = (n - ring + NRING - 1) // NRING
                fin.append((st["sems"][ring], 16 * uses))
        fin += [self._ew(e, v) for e, v in self.cnt.items() if v > 0]

        def emit_fin(E, fin=fin):
            for s, v in fin:
                E.wait_ge(s, v)

        if os.environ.get("KDEV_COUNTS"):
            print("COUNTS", dict(self.cnt), {q: st["n"] for q, st in self.dq.items()}, flush=True)
        self.ops["sp"].append(emit_fin)
        ops = self.ops
        with nc.Block() as block:
            @block.sync
            def _(E):
                for f in ops["sp"]:
                    f(E)

            @block.tensor
            def _(E):
                for f in ops["pe"]:
                    f(E)

            @block.scalar
            def _(E):
                for f in ops["act"]:
                    f(E)

            @block.vector
            def _(E):
                for f in ops["dve"]:
                    f(E)

            @block.gpsimd
            def _(E):
                for f in ops["pool"]:
                    f(E)


class Tile:
    def __init__(self, kind, idx):
        self.kind = kind
        self.idx = idx
        if kind == "p":
            self.T, self.nseq, self.L, self.s0, self.C = TP, 1, TP, 0, 64
        else:
            self.T, self.nseq, self.L, self.s0, self.C = TS, NSQ, LS, 1, LS
        self.first = (kind == "p" and idx == 0)
        self.last = (kind == "s") or (idx == NPT - 1)
        self.nch = self.T // self.C


def bc(ap, shape):
    return ap.broadcast_to(list(shape))


def build_program():
    nc = bass.Bass("TRN2", target_bir_lowering=False)

    def din(name, shape):
        return nc.dram_tensor(name, list(shape), F32, kind="ExternalInput").ap()

    def dout(name, shape):
        return nc.dram_tensor(name, list(shape), F32, kind="ExternalOutput").ap()

    NS1 = 1 + NSQ
    I = {}
    for name, shape in [
        ("xp", [D, SEQ]), ("xs", [D, TS]), ("cT", [D, NS1]),
        ("w_ada", [2, 6 * D // WS, 128, KC, WS]), ("b_adaT", [2, 128, 96]), ("g_mixT", [2, 128, KC]), ("g_ffnT", [2, 128, KC]),
        ("g_finT", [128, KC]), ("w_ffn_up", [2, 2 * DFF // WS, 128, KC, WS]), ("w_ffn_down", [2, DFF // 256, 128, 2, D]),
        ("ffn_cw", [2, 128, NFF, 3]), ("ffn_cb", [2, 128, NFF]), ("ffn_st", [2, 128, NFF, NSQ, 2]),
        ("ident", [128, 128]), ("UT", [128, 128]), ("nSU", [128, 128]), ("SL", [128, 128]),
        ("rmask_p", [128, TP]), ("rmask_s", [128, TS]), ("esel", [8, 8, 128]),
        ("w_in_ab", [n_slabs("w_in_ab"), 128, KC, WS]), ("w_out_ab", [n_slabs("w_out_ab"), 128, KC, WS]),
        ("gla_w2", [16, 512]), ("gla_b2T", [128, 4]), ("gla_ngT", [128, 2]), ("gla_st", [NSQ, 128, 4, 256]),
        ("gdn_cw", [128, 24, 4]), ("gdn_cb", [128, 24]), ("gdn_cst", [128, 24, NSQ, 3]),
        ("gdn_Alog", [8, 1]), ("gdn_dtb", [8, 1]), ("gdn_ngT", [128, 1]), ("gdn_st", [NSQ, 128, 8, 128]),
        ("w_in_cd", [n_slabs("w_in_cd"), 128, KC, WS]), ("w_out_cd", [n_slabs("w_out_cd"), 128, KC, WS]), ("ssd_cw", [128, 12, 4]), ("ssd_cb", [128, 12]),
        ("ssd_cst", [128, 12, NSQ, 3]), ("ssd_Alog", [16, 1]), ("ssd_dtb", [16, 1]), ("ssd_Dcol", [128, 8]),
        ("ssd_ngT", [128, 8]), ("ssd_st", [NSQ, 128, 8, 128]),
        ("s5_are", [128, 32]), ("s5_aim", [128, 32]), ("s5_ldt", [128, 32]), ("s5w", [8, 128, 4, 4, 128]),
        ("s5_Dcol", [128, 8]), ("s5_x0", [128, 2, 32, NSQ]), ("glu_w", [n_slabs("glu_w"), 128, 8, WS]), ("glu_bT", [128, 8]),
    ]:
        I[name] = din(name, shape)
    O = {}
    for name, shape in [
        ("yp", [D, SEQ]), ("ys", [D, TS]), ("ffn_st_o", [2, 128, NFF, NS1, 2]),
        ("gla_o", [NS1, 128, 4, 256]), ("gdn_o", [NS1, 128, 8, 128]), ("gdn_cst_o", [128, 24, NS1, 3]),
        ("ssd_o", [NS1, 128, 8, 128]), ("ssd_cst_o", [128, 12, NS1, 3]), ("s5_o", [128, 32, NS1, 2]),
    ]:
        O[name] = dout(name, shape)

    with ExitStack() as stack:
        P = Prog(nc, stack)
        P.psl = [P.ps([128, 512], F32, name="psb%d" % i) for i in range(8)]
        xT = P.sb([128, KC, TP], F32, "xT")
        hT = P.sb([128, KC, TP], MMDT, "hT")
        MWt = stack.enter_context(nc.sbuf_tensor("MW", [128, 72, TP], F32))
        yacc = P.view(MWt[:, 0:16, :], "yacc")
        RWt = stack.enter_context(nc.sbuf_tensor("RW", [128, 8, TP], F32R))
        RBt = stack.enter_context(nc.sbuf_tensor("RB", [128, 8, TP], MMDT))
        WB = [P.sb([128, KC * WS], MMDT, "wb%d" % i) for i in range(NWB)]
        s5wb = P.sb([128, 16 * 128], F32R, "s5wb")
        wbi = [0]

        def next_wb():
            b = WB[wbi[0] % NWB]
            wbi[0] += 1
            return b

        def const(name, shape, q="sp"):
            b = P.sb(shape, F32, "c_" + name)
            P.dma(q, b[:], I[name], writes=[b])
            return b

        def U(a, b, name):
            return P.view(MWt[:, a:b, :], name)

        ones = P.sb([128, 128], F32, "ones")
        P.memset(ones, ones[:], 1.0)
        ident = const("ident", [128, 128])
        UT = const("UT", [128, 128])
        nSU = const("nSU", [128, 128])
        SL = const("SL", [128, 128])
        rmask = {"p": const("rmask_p", [128, TP]), "s": const("rmask_s", [128, TS])}
        gfin = const("g_finT", [128, KC])
        modT = [P.view(MWt[:, 4 + 7 * l:11 + 7 * l, :].rearrange("p a t -> p (a t)")[:, 0:96 * NS1].rearrange(
            "p (c n) -> p c n", n=NS1), "modT%d" % l) for l in range(2)]
        modP = [P.sb([128, 96], F32, "modP%d" % l) for l in range(2)]
        modS = P.sb([128, 16, NSQ], F32, "modS")
        modD = nc.dram_tensor("modD", [2, 128, 96, NS1], F32, kind="Internal").ap()
        modDB = Buf(None, "modD")
        ffst = [P.sb([128, NFF, 2], F32, "ffst%d" % l) for l in range(2)]
        ffcw = [P.sb([128, NFF, 3], F32, "ffcw%d" % l) for l in range(2)]
        ffcb = [P.sb([128, NFF], F32, "ffcb%d" % l) for l in range(2)]
        for l in range(2):
            P.memset(ffst[l], ffst[l][:], 0.0)
            P.dma("sp", ffcw[l][:], I["ffn_cw"][l], writes=[ffcw[l]])
            P.dma("sp", ffcb[l][:], I["ffn_cb"][l], writes=[ffcb[l]])
        rstd = P.sb([128, TP], F32, "rstd")
        cvb = U(22, 26, "cvb")
        sgb = U(26, 28, "sgb")
        actT = [P.view(RBt[:, 2 * i:2 * i + 2, :], "actT%d" % i) for i in range(2)]
        cvtmp = P.sb([128, TP], F32, "cvtmp")
        cstage = P.sb([128, TP + 3 * NSQ], F32, "cstage")

        cond0 = P.view(MWt[:, 0:2, :].rearrange("p a t -> p (a t)").rearrange("p (c n) -> p c n", n=32), "cond0")
        condT = P.view(RBt[:, 0:2, :].rearrange("p a t -> p (a t)").rearrange("p (c n) -> p c n", n=32), "condT")
        P.memset(cond0, cond0[:], 0.0)
        P.dma("sp", cond0[:, :, 0:NS1], I["cT"].rearrange("(c p) n -> p c n", p=128), writes=[cond0])
        P.act(condT, condT[:], cond0, cond0[:], AF.Silu)
        for l in range(2):
            badT = P.sb([128, 96], F32, "badT%d" % l)
            gm = P.sb([128, KC], F32, "gmix%d" % l)
            gf = P.sb([128, KC], F32, "gffn%d" % l)
            P.dma("sp", badT[:], I["b_adaT"][l], writes=[badT])
            P.dma("sp", gm[:], I["g_mixT"][l], writes=[gm])
            P.dma("sp", gf[:], I["g_ffnT"][l], writes=[gf])
            nj = WS // 128
            for s in range(6 * D // WS):
                wb = next_wb()
                wbv = wb[:].rearrange("p (c n) -> p c n", n=WS)
                P.dma("pool", wbv, I["w_ada"][l][s], writes=[wb])
                ps = P.next_ps()
                for j in range(nj):
                    for kc in range(KC):
                        P.mm(ps, ps[:, j * 32:j * 32 + 32], wb, wbv[:, kc, j * 128:(j + 1) * 128], condT, condT[:, kc, :],
                             start=(kc == 0), stop=(kc == KC - 1))
                psv = ps[:, 0:32 * nj].rearrange("p (j n) -> p j n", n=32)[:, :, 0:NS1]
                P.tt(modT[l], modT[l][:, s * nj:(s + 1) * nj, :], ps, psv,
                     badT, bc(badT[:, s * nj:(s + 1) * nj].unsqueeze(2), [128, nj, NS1]), ALU.add)
            for (off, g) in ((16, gm), (64, gf)):
                P.stt(modT[l], modT[l][:, off:off + 16, :], modT[l], modT[l][:, off:off + 16, :], 1.0,
                      g, bc(g[:].unsqueeze(2), [128, 16, NS1]), ALU.add, ALU.mult)
            P.cp(modP[l], modP[l][:].unsqueeze(2), modT[l], modT[l][:, :, 0:1])
            P.dma("sp", modD[l], modT[l][:, :, :], reads=[modT[l]], writes=[modDB])

        def v4(ap, tl):
            return ap.rearrange("p c (s l) -> p c s l", l=tl.L)

        def modb(l, off):
            P.dma("sp", modS[:], modD[l][:, off:off + 16, 1:NS1], reads=[modDB], writes=[modS])
            return bc(modS[:].unsqueeze(3), [128, 16, NSQ, LS])

        def rms_stats(tl):
            T = tl.T
            P.act(yacc, yacc[:, :, :T], xT, xT[:, :, :T], AF.Square)
            ps = P.next_ps()
            for c in range(KC):
                P.mm(ps, ps[:, :T], ones, ones[:], yacc, yacc[:, c, :T], start=(c == 0), stop=(c == KC - 1))
            P.act(rstd, rstd[:, :T], ps, ps[:, :T], AF.Ln, bias=EPS, scale=1.0 / D)
            P.act(rstd, rstd[:, :T], rstd, rstd[:, :T], AF.Exp, scale=-0.5)
            P.tt(yacc, yacc[:, :, :T], xT, xT[:, :, :T], rstd, bc(rstd[:, :T].unsqueeze(1), [128, KC, T]), ALU.mult)

        def norm_mod(tl, l, sh, sc):
            T = tl.T
            P.barrier([yacc])
            rms_stats(tl)
            if tl.kind == "p":
                for c in range(KC):
                    P.act(hT, hT[:, c, :T], yacc, yacc[:, c, :T], AF.Identity, bias=modP[l][:, sh + c:sh + c + 1],
                          scale=modP[l][:, sc + c:sc + c + 1], extra=[modP[l]])
            else:
                P.tt(yacc, v4(yacc[:, :, :T], tl), yacc, v4(yacc[:, :, :T], tl), modS, modb(l, sc), ALU.mult)
                P.tt(hT, v4(hT[:, :, :T], tl), yacc, v4(yacc[:, :, :T], tl), modS, modb(l, sh), ALU.add)

        def resid_add(tl, l, goff, src):
            T = tl.T
            if tl.kind == "p":
                for c in range(KC):
                    P.stt(xT, xT[:, c, :T], src, src[:, c, :T], modP[l][:, goff + c:goff + c + 1], xT, xT[:, c, :T],
                          ALU.mult, ALU.add, extra=[modP[l]])
            else:
                P.tt(src, v4(src[:, :, :T], tl), src, v4(src[:, :, :T], tl), modS, modb(l, goff), ALU.mult)
                P.tt(xT, xT[:, :, :T], xT, xT[:, :, :T], src, src[:, :, :T], ALU.add)

        def proj_fm(tl, w_ap, col0, ncols, consume, rhs=None, nk=KC):
            T = tl.T
            rhs = rhs or hT
            wname = w_ap
            sb0 = slab_base(wname, col0)
            for s0 in range(0, ncols, WS):
                n = min(WS, ncols - s0)
                wb = next_wb()
                wbv = wb[:].rearrange("p (c n) -> p c n", n=WS)
                P.dma("pool", wbv[:, 0:nk, :], I[wname][sb0 + s0 // WS], writes=[wb])
                for j in range((n + 127) // 128):
                    m = min(128, n - j * 128)
                    ps = P.next_ps()
                    for kc in range(nk):
                        P.mm(ps, ps[0:m, :T], wb, wbv[:, kc, j * 128:j * 128 + m], rhs, rhs[:, kc, :T],
                             start=(kc == 0), stop=(kc == nk - 1))
                    consume(s0 // 128 + j, m, ps)

        def conv_chunk(tl, ps, cw, cb, ch, K, pst, sin, sout, out_b, out_ap, func):
            T, L, ns = tl.T, tl.L, tl.nseq
            H = K - 1
            sv = cstage[:, 0:ns * (L + H)].rearrange("p (s l) -> p s l", l=L + H)
            if tl.kind == "p":
                P.cp(cstage, sv[:, :, 0:H], pst, pst[:, ch:ch + 1, :])
            else:
                P.dma("sp", sv[:, :, 0:H], sin[:, ch, :, :], writes=[cstage])
            P.act(cstage, sv[:, :, H:H + L], ps, ps[:, :T].rearrange("p (s l) -> p s l", l=L), AF.Copy)
            if tl.kind == "p":
                P.cp(pst, pst[:, ch:ch + 1, :], cstage, sv[:, :, L:L + H])
                if tl.last:
                    P.dma("sp", sout[:, ch, 0:1, :], sv[:, :, L:L + H], reads=[cstage])
            else:
                P.dma("sp", sout[:, ch, 1:NS1, :], sv[:, :, L:L + H], reads=[cstage])
            acc = cvtmp[:, 0:T].rearrange("p (s l) -> p s l", l=L)
            P.ts(cvtmp, acc, cstage, sv[:, :, 0:L], cw[:, ch, 0:1], cb[:, ch:ch + 1], ALU.mult, ALU.add, extra=[cw, cb])
            for k in range(1, K):
                P.stt(cvtmp, acc, cstage, sv[:, :, k:k + L], cw[:, ch, k:k + 1], cvtmp, acc, ALU.mult, ALU.add, extra=[cw])
            ov = out_ap.rearrange("p (s l) -> p s l", l=L)
            if func is None:
                P.cp(out_b, ov, cvtmp, acc)
            else:
                P.act(out_b, ov, cvtmp, acc, func)

        def ffn(tl, l):
            T = tl.T
            P.barrier([cvb, sgb] + actT)
            wup = I["w_ffn_up"][l]
            wdn = I["w_ffn_down"][l]
            for g in range(DFF // 256):
                wba, wbg = next_wb(), next_wb()
                wva = wba[:].rearrange("p (c n) -> p c n", n=WS)
                wvg = wbg[:].rearrange("p (c n) -> p c n", n=WS)
                P.dma("pool", wva, wup[g], writes=[wba])
                P.dma("pool", wvg, wup[DFF // WS + g], writes=[wbg])
                for j in range(4):
                    ch = (g * 2 + j) if j < 2 else (44 + g * 2 + j - 2)
                    wb, wv = (wba, wva) if j < 2 else (wbg, wvg)
                    jj = j % 2
                    ps = P.next_ps()
                    for kc in range(KC):
                        P.mm(ps, ps[:, :T], wb, wv[:, kc, jj * 128:(jj + 1) * 128], hT, hT[:, kc, :T],
                             start=(kc == 0), stop=(kc == KC - 1))
                    conv_chunk(tl, ps, ffcw[l], ffcb[l], ch, 3, ffst[l], I["ffn_st"][l], O["ffn_st_o"][l],
                               cvb, cvb[:, j, :T], None)
                P.act(sgb, sgb[:, :, :T], cvb, cvb[:, 2:4, :T], AF.Silu)
                at = actT[g % 2]
                P.tt(at, at[:, :, :T], sgb, sgb[:, :, :T], cvb, cvb[:, 0:2, :T], ALU.mult)
                wd = next_wb()
                wdv = wd[:, 0:2 * D].rearrange("p (c n) -> p c n", n=D)
                P.dma("pool", wdv, wdn[g], writes=[wd])
                for oc in range(KC):
                    ps = P.next_ps()
                    for kc in range(2):
                        P.mm(ps, ps[:, :T], wd, wdv[:, kc, oc * 128:(oc + 1) * 128], at, at[:, kc, :T],
                             start=(kc == 0), stop=(kc == 1))
                    if g == 0:
                        P.cp(yacc, yacc[:, oc, :T], ps, ps[:, :T], eng="act")
                    else:
                        P.tt(yacc, yacc[:, oc, :T], yacc, yacc[:, oc, :T], ps, ps[:, :T], ALU.add)

        gla_S = [P.sb([128, 4, 256], F32, "glaS0")] * 2
        gdn_S = [P.sb([128, 8, 128], F32, "gdnS0")] * 2
        nb2 = const("gla_b2T", [128, 4])
        P.ts1(nb2, nb2[:], nb2, nb2[:], -1.0, ALU.mult)
        glang = const("gla_ngT", [128, 2])
        gdcw = const("gdn_cw", [128, 24, 4])
        gdcb = const("gdn_cb", [128, 24])
        gdnegA = const("gdn_Alog", [8, 1])
        P.act(gdnegA, gdnegA[:], gdnegA, gdnegA[:], AF.Exp)
        P.ts1(gdnegA, gdnegA[:], gdnegA, gdnegA[:], -1.0, ALU.mult)
        gddtb = const("gdn_dtb", [8, 1])
        gdng = const("gdn_ngT", [128, 1])
        gdcst = P.sb([128, 24, 3], F32, "gdcst")
        P.memset(gdcst, gdcst[:], 0.0)
        lrT = P.view(MWt[0:16, 54, :], "lrT")
        w2sb = P.view(MWt[0:16, 55:57, :].rearrange("p a t -> p (a t)"), "w2sb")
        abT = [P.sb([8, TP], F32, "abT%d" % i) for i in range(5)]
        abT.append(abT[1])
        gtok = P.sb([64, 8], F32, "gtok")
        eglast = P.sb([128, 8], F32, "eglast")
        m_q, m_k, m_v, m_r = U(0, 4, "m_q"), U(4, 8, "m_k"), U(8, 16, "m_v"), U(16, 24, "m_r")
        m_cum, m_o, m_ln = U(24, 28, "m_cum"), U(28, 36, "m_o"), U(36, 40, "m_ln")
        tmpA = [U(40 + i, 41 + i, "tmpA%d" % i) for i in range(7)]
        a_vtk, a_ktk = U(48, 52, "a_vtk"), U(52, 54, "a_ktk")
        GLA_BUFS = [m_q, m_k, m_v, m_r, m_cum, m_o, m_ln, a_vtk, a_ktk] + tmpA
        g_qkv, g_gate, g_ob, g_sq = U(0, 24, "g_qkv"), U(36, 44, "g_gate"), U(44, 52, "g_ob"), U(52, 60, "g_sq")
        g_kb, g_dec, g_A, g_Q = U(52, 54, "g_kb"), U(54, 56, "g_dec"), U(56, 58, "g_A"), U(58, 60, "g_Q")
        g_vb, g_qi, g_kw, g_ko = U(60, 62, "g_vb"), U(62, 64, "g_qi"), U(64, 66, "g_kw"), U(66, 68, "g_ko")
        g_X, g_wk = U(68, 70, "g_X"), U(70, 72, "g_wk")
        g_tokA, g_tokB = U(52, 56, "g_tokA"), U(56, 60, "g_tokB")
        g_esel = U(24, 28, "g_esel")

        def t4(buf, C, nh):
            return buf[:].rearrange("p a t -> p (a t)")[:, 0:nh * C].rearrange("p (h c) -> p h c", c=C)

        def tok(buf, C, n):
            return buf[:].rearrange("p a t -> p (a t)")[0:C, 0:n]

        def gla(tl):
            T, C, nch = tl.T, tl.C, tl.nch
            W = "w_in_ab"
            P.barrier(GLA_BUFS + [lrT, w2sb])
            P.dma("sp", w2sb[:, :], I["gla_w2"], writes=[w2sb])
            proj_fm(tl, W, 0, 512, lambda j, m, ps: P.cp(m_q, m_q[:, j, :T], ps, ps[:, :T], eng="act"))
            proj_fm(tl, W, 512, 512, lambda j, m, ps: P.cp(m_k, m_k[:, j, :T], ps, ps[:, :T], eng="act"))
            proj_fm(tl, W, 1024, 1024, lambda j, m, ps: P.cp(m_v, m_v[:, j, :T], ps, ps[:, :T], eng="act"))
            proj_fm(tl, W, 2064, 1024, lambda j, m, ps: P.act(m_r, m_r[:, j, :T], ps, ps[:, :T], AF.Silu))
            proj_fm(tl, W, 2048, 16, lambda j, m, ps: P.cp(lrT, lrT[0:16, :T], ps, ps[0:16, :T]))
            for h in range(4):
                ps = P.next_ps()
                P.mm(ps, ps[:, :T], w2sb, w2sb[0:16, h * 128:(h + 1) * 128], lrT, lrT[0:16, :T])
                P.act(m_ln, m_ln[:, h, :T], ps, ps[:, :T], AF.Exp, bias=nb2[:, h:h + 1], scale=-1.0, extra=[nb2])
            P.act(m_ln, m_ln[:, :, :T], m_ln, m_ln[:, :, :T], AF.Ln, bias=1.0)
            if SUB < 2:
                return
            rm = rmask[tl.kind]
            for h in range(4):
                P.scan(m_cum, m_cum[:, h, :T], rm, rm[:, 0:T], m_ln, m_ln[:, h, :T])
            eb, QsT, enb, KsT, dl, KoT, attT = [t4(tmpA[i], C, 4) for i in range(7)]
            bt = tmpA
            vtk = tok(a_vtk, C, 1024)
            ktk = tok(a_ktk, C, 512)
            for n in range(nch if SUB >= 3 else 0):
                c0 = n * C
                S = gla_S[n % 2] if tl.kind == "s" else gla_S[0]
                if tl.kind == "s":
                    P.dma("sp", S[:], I["gla_st"][n], writes=[S])
                elif tl.first and n == 0:
                    P.memset(S, S[:], 0.0)
                cs = slice(c0, c0 + C)
                P.act(bt[0], eb, m_cum, m_cum[:, :, cs], AF.Exp, scale=-1.0 / 16)
                P.stt(bt[1], QsT, m_q, m_q[:, :, cs], 128 ** -0.5, bt[0], eb, ALU.mult, ALU.mult)
                P.act(bt[2], enb, m_cum, m_cum[:, :, cs], AF.Exp, scale=1.0 / 16)
                P.tt(bt[3], KsT, m_k, m_k[:, :, cs], bt[2], enb, ALU.mult)
                P.tt(bt[4], dl, m_cum, bc(m_cum[:, :, c0 + C - 1:c0 + C], [128, 4, C]), m_cum, m_cum[:, :, cs], ALU.subtract)
                P.act(bt[4], dl, bt[4], dl, AF.Exp, scale=-1.0 / 16)
                P.tt(bt[5], KoT, m_k, m_k[:, :, cs], bt[4], dl, ALU.mult)
                psa = P.next_ps()
                for h in range(4):
                    P.mm(psa, psa[0:C, h * C:(h + 1) * C], bt[3], KsT[:, h, :], bt[1], QsT[:, h, :])
                P.tt(bt[6], attT[0:C], psa, psa[0:C, 0:4 * C].rearrange("p (h c) -> p h c", c=C),
                     UT, bc(UT[0:C, 0:C].unsqueeze(1), [C, 4, C]), ALU.mult)
                for half in range(2):
                    psv = P.next_ps()
                    for q4 in range(4):
                        P.tr(psv, psv[0:C, q4 * 128:(q4 + 1) * 128], m_v, m_v[:, half * 4 + q4, cs], ident, ident[:])
                    P.cp(a_vtk, vtk[:, half * 512:(half + 1) * 512], psv, psv[0:C, :], eng="act")
                psk = P.next_ps()
                for h in range(4):
                    P.tr(psk, psk[0:C, h * 128:(h + 1) * 128], bt[5], KoT[:, h, :], ident, ident[:])
                P.cp(a_ktk, ktk, psk, psk[0:C, :], eng="act")
                pso = P.next_ps()
                for h in range(4):
                    for vc in range(2):
                        a = h * 2 + vc
                        o_ap = pso[:, a * C:(a + 1) * C]
                        P.mm(pso, o_ap, a_vtk, vtk[:, a * 128:(a + 1) * 128], bt[6], attT[0:C, h, :], start=True, stop=False)
                        P.mm(pso, o_ap, S, S[:, h, vc * 128:(vc + 1) * 128], bt[1], QsT[:, h, :], start=False, stop=True)
                P.cp(m_o, m_o[:, :, cs], pso, pso[:, 0:8 * C].rearrange("p (a c) -> p a c", c=C), eng="act")
                for half in range(2):
                    pss = P.next_ps()
                    for hh in range(2):
                        h = half * 2 + hh
                        P.mm(pss, pss[:, hh * 256:(hh + 1) * 256], a_ktk, ktk[:, h * 128:(h + 1) * 128],
                             a_vtk, vtk[:, h * 256:(h + 1) * 256])
                    for hh in range(2):
                        h = half * 2 + hh
                        P.stt(S, S[:, h, :], S, S[:, h, :], eb[:, h, C - 1:C], pss, pss[:, hh * 256:(hh + 1) * 256],
                              ALU.mult, ALU.add, extra=[bt[0]])
                if tl.kind == "s":
                    P.dma("sp", O["gla_o"][1 + n], S[:], reads=[S])
                elif tl.last and n == nch - 1:
                    P.dma("sp", O["gla_o"][0], S[:], reads=[S])
            if SUB < 4:
                return
            P.act(m_v, m_v[:, :, :T], m_o, m_o[:, :, :T], AF.Square)
            for half in range(2):
                ps = P.next_ps()
                for hh in range(2):
                    h = half * 2 + hh
                    for vc in range(2):
                        P.mm(ps, ps[:, hh * T:(hh + 1) * T], ones, ones[:], m_v, m_v[:, h * 2 + vc, :T],
                             start=(vc == 0), stop=(vc == 1))
                rsv = m_ln[:, half * 2:half * 2 + 2, :T]
                P.act(m_ln, rsv, ps, ps[:, 0:2 * T].rearrange("p (h t) -> p h t", t=T), AF.Ln, bias=EPS, scale=1.0 / 256)
                P.act(m_ln, rsv, m_ln, rsv, AF.Exp, scale=-0.5)
            o4 = m_o[:, :, :T].rearrange("p (h v) t -> p h v t", v=2)
            P.tt(m_o, o4, m_o, o4, m_ln, bc(m_ln[:, :, :T].unsqueeze(2), [128, 4, 2, T]), ALU.mult)
            P.tt(m_o, m_o[:, :, :T], m_o, m_o[:, :, :T], m_r, m_r[:, :, :T], ALU.mult)
            for vc in range(2):
                P.ts1(m_o, o4[:, :, vc, :], m_o, o4[:, :, vc, :], glang[:, vc:vc + 1], ALU.mult, extra=[glang])

        def gdn(tl):
            T, C, nch = tl.T, tl.C, tl.nch
            W = "w_in_ab"
            qkv, gg, ob, sq = g_qkv, g_gate, g_ob, g_sq
            P.barrier([qkv, gg, ob, sq, g_esel])
            esel = g_esel
            eselv = g_esel[:].rearrange("p a t -> p (a t)")[0:8, :].rearrange("p (h m) -> p h m", m=128)
            P.dma("sp", eselv, I["esel"], writes=[g_esel])

            def cons_qkv(j, m, ps):
                conv_chunk(tl, ps, gdcw, gdcb, j, 4, gdcst, I["gdn_cst"], O["gdn_cst_o"], qkv, qkv[:, j, :T], AF.Silu)

            proj_fm(tl, W, 3088, 3072, cons_qkv)
            proj_fm(tl, W, 6176, 1024, lambda j, m, ps: P.act(gg, gg[:, j, :T], ps, ps[:, :T], AF.Silu))
            proj_fm(tl, W, 6160, 8, lambda j, m, ps: P.cp(abT[0], abT[0][0:8, :T], ps, ps[0:8, :T]))
            proj_fm(tl, W, 6168, 8, lambda j, m, ps: P.cp(abT[1], abT[1][0:8, :T], ps, ps[0:8, :T]))
            aT, bT, gcT, egT, ekT, beT = abT
            if SUB < 6:
                return
            P.act(aT, aT[0:8, :T], aT, aT[0:8, :T], AF.Exp, bias=gddtb[0:8, 0:1], extra=[gddtb])
            P.act(aT, aT[0:8, :T], aT, aT[0:8, :T], AF.Ln, bias=1.0)
            P.ts1(aT, aT[0:8, :T], aT, aT[0:8, :T], gdnegA[0:8, 0:1], ALU.mult, extra=[gdnegA])
            P.act(beT, beT[0:8, :T], bT, bT[0:8, :T], AF.Sigmoid)
            rm = rmask[tl.kind]
            P.scan(gcT, gcT[0:8, :T], rm, rm[0:8, 0:T], aT, aT[0:8, :T])
            P.act(egT, egT[0:8, :T], gcT, gcT[0:8, :T], AF.Exp)
            g3 = gcT[0:8, :T].rearrange("p (n c) -> p n c", c=C)
            P.tt(ekT, ekT[0:8, :T].rearrange("p (n c) -> p n c", c=C), gcT, bc(g3[:, :, C - 1:C], [8, nch, C]), gcT, g3,
                 ALU.subtract)
            P.act(ekT, ekT[0:8, :T], ekT, ekT[0:8, :T], AF.Exp)
            for which in range(2):
                src = qkv[:, which * 8:(which + 1) * 8, :T]
                P.act(sq, sq[:, :, :T], qkv, src, AF.Square)
                for pr in range(4):
                    ps = P.next_ps()
                    for hh in range(2):
                        P.mm(ps, ps[:, hh * T:(hh + 1) * T], ones, ones[:], sq, sq[:, pr * 2 + hh, :T])
                    rv = sq[:, pr * 2:pr * 2 + 2, :T]
                    P.act(sq, rv, ps, ps[:, 0:2 * T].rearrange("p (h t) -> p h t", t=T), AF.Ln, bias=EPS)
                    P.act(sq, rv, sq, rv, AF.Exp, scale=-0.5)
                if which == 0:
                    P.stt(qkv, src, qkv, src, 128 ** -0.5, sq, sq[:, :, :T], ALU.mult, ALU.mult)
                else:
                    P.tt(qkv, src, qkv, src, sq, sq[:, :, :T], ALU.mult)
            if SUB < 7:
                return
            tbufs = [g_kb, g_vb, g_qi, g_kw, g_ko, g_dec, g_A, g_Q, g_X, g_wk]
            P.barrier(tbufs)
            kbT, vbT, qiT, kwT, koT, decT, AT, QT, XT, wk = [t4(b, C, 8) for b in tbufs]
            nsteps = {64: 5, 4: 1}[C]
            p3 = lambda ps_: ps_[0:C, 0:8 * C].rearrange("p (h c) -> p h c", c=C)
            pf = lambda ps_: ps_[:, 0:8 * C].rearrange("p (h c) -> p h c", c=C)
            for n in range(nch):
                c0 = n * C
                cs = slice(c0, c0 + C)
                S = gdn_S[n % 2] if tl.kind == "s" else gdn_S[0]
                if tl.kind == "s":
                    P.dma("sp", S[:], I["gdn_st"][n], writes=[S])
                elif tl.first and n == 0:
                    P.memset(S, S[:], 0.0)
                if SUB < 8:
                    break
                if n > 0:
                    P.barrier([g_kb, g_dec, g_A, g_Q])
                qc, kc, vc = qkv[:, 0:8, cs], qkv[:, 8:16, cs], qkv[:, 16:24, cs]

                def bcast_rows(srcb):
                    ps = P.next_ps()
                    for h in range(8):
                        P.mm(ps, ps[:, h * C:(h + 1) * C], esel, eselv[:, h, :], srcb, srcb[0:8, cs])
                    return ps, ps[:, 0:8 * C].rearrange("p (h c) -> p h c", c=C)

                if CUT < 0:
                    continue
                psb, pbv = bcast_rows(beT)
                if CUT < 1:
                    P.cp(g_kb, kbT, psb, pbv)
                    continue
                P.tt(g_kb, kbT, qkv, kc, psb, pbv, ALU.mult)
                P.tt(g_vb, vbT, qkv, vc, psb, pbv, ALU.mult)
                pse, pev = bcast_rows(egT)
                P.tt(g_qi, qiT, qkv, qc, pse, pev, ALU.mult)
                P.tt(g_kw, kwT, g_kb, kbT, pse, pev, ALU.mult)
                P.cp(eglast, eglast[:, :].unsqueeze(2), pse, pev[:, :, C - 1:C])
                psk, pkv = bcast_rows(ekT)
                P.tt(g_ko, koT, qkv, kc, psk, pkv, ALU.mult)
                if CUT < 2:
                    continue
                pst = P.next_ps()
                P.tr(pst, pst[0:C, 0:8], aT, aT[0:8, cs], ident, ident[0:8, 0:8])
                P.cp(gtok, gtok[0:C, :], pst, pst[0:C, 0:8], eng="act")
                P.tt(g_wk, wk[0:C], gtok, bc(gtok[0:C, :].unsqueeze(2), [C, 8, C]), SL, bc(SL[0:C, 0:C].unsqueeze(1), [C, 8, C]),
                     ALU.mult)
                psd = P.next_ps()
                for h in range(8):
                    P.mm(psd, psd[0:C, h * C:(h + 1) * C], g_wk, wk[0:C, h, :], UT, UT[0:C, 0:C])
                P.act(g_dec, decT[0:C], psd, p3(psd), AF.Exp)
                P.tt(g_dec, decT[0:C], g_dec, decT[0:C], UT, bc(UT[0:C, 0:C].unsqueeze(1), [C, 8, C]), ALU.mult)
                if CUT < 3:
                    continue
                psm = P.next_ps()
                for h in range(8):
                    P.mm(psm, psm[0:C, h * C:(h + 1) * C], qkv, kc[:, h, :], g_kb, kbT[:, h, :])
                P.tt(g_A, AT[0:C], psm, p3(psm), g_dec, decT[0:C], ALU.mult)
                P.tt(g_A, AT[0:C], g_A, AT[0:C], nSU, bc(nSU[0:C, 0:C].unsqueeze(1), [C, 8, C]), ALU.mult)
                psa = P.next_ps()
                for h in range(8):
                    P.mm(psa, psa[0:C, h * C:(h + 1) * C], qkv, kc[:, h, :], qkv, qc[:, h, :])
                P.tt(g_wk, wk[0:C], psa, p3(psa), g_dec, decT[0:C], ALU.mult)
                pq = P.next_ps()
                for h in range(8):
                    P.tr(pq, pq[0:C, h * C:(h + 1) * C], g_A, AT[0:C, h, :], ident, ident[0:C, 0:C])
                P.cp(g_Q, QT[0:C], pq, p3(pq), eng="act")
                P.tt(g_X, XT[0:C], g_A, AT[0:C], ident, bc(ident[0:C, 0:C].unsqueeze(1), [C, 8, C]), ALU.add)
                for step in range(nsteps if CUT >= 4 else 0):
                    lastst = (step == nsteps - 1)
                    pqt = P.next_ps()
                    for h in range(8):
                        P.mm(pqt, pqt[0:C, h * C:(h + 1) * C], g_A, AT[0:C, h, :], g_Q, QT[0:C, h, :])
                    if not lastst:
                        pq2 = P.next_ps()
                        for h in range(8):
                            P.mm(pq2, pq2[0:C, h * C:(h + 1) * C], g_Q, QT[0:C, h, :], g_A, AT[0:C, h, :])
                    P.cp(g_Q, QT[0:C], pqt, p3(pqt), eng="act")
                    if not lastst:
                        P.cp(g_A, AT[0:C], pq2, p3(pq2))
                    px = P.next_ps()
                    for h in range(8):
                        P.mm(px, px[0:C, h * C:(h + 1) * C], g_Q, QT[0:C, h, :], g_X, XT[0:C, h, :])
                    P.tt(g_X, XT[0:C], g_X, XT[0:C], px, p3(px), ALU.add)
                if CUT < 5:
                    continue
                P.barrier([g_tokA, g_tokB])
                tA = tok(g_tokA, C, 1024)
                tB = tok(g_tokB, C, 1024)

                def to_tok(srcb, srcv, dstb, dstv):
                    for half in range(2):
                        pt = P.next_ps()
                        for q4 in range(4):
                            P.tr(pt, pt[0:C, q4 * 128:(q4 + 1) * 128], srcb, srcv[:, half * 4 + q4, :], ident, ident[:])
                        P.cp(dstb, dstv[:, half * 512:(half + 1) * 512], pt, pt[0:C, :], eng="act")

                to_tok(g_vb, vbT, g_tokA, tA)
                to_tok(g_kw, kwT, g_tokB, tB)
                pu = P.next_ps()
                pw = P.next_ps()
                for h in range(8):
                    P.mm(pu, pu[:, h * C:(h + 1) * C], g_tokA, tA[:, h * 128:(h + 1) * 128], g_X, XT[0:C, h, :])
                    P.mm(pw, pw[:, h * C:(h + 1) * C], g_tokB, tB[:, h * 128:(h + 1) * 128], g_X, XT[0:C, h, :])
                P.cp(g_vb, vbT, pu, pf(pu), eng="act")
                P.cp(g_kw, kwT, pw, pf(pw))
                if CUT < 6:
                    continue
                pws = P.next_ps()
                for h in range(8):
                    P.mm(pws, pws[:, h * C:(h + 1) * C], S, S[:, h, :], g_kw, kwT[:, h, :])
                P.tt(g_vb, vbT, g_vb, vbT, pws, pf(pws), ALU.subtract)
                to_tok(g_vb, vbT, g_tokA, tA)
                to_tok(g_ko, koT, g_tokB, tB)
                po = P.next_ps()
                for h in range(8):
                    o_ap = po[:, h * C:(h + 1) * C]
                    P.mm(po, o_ap, S, S[:, h, :], g_qi, qiT[:, h, :], start=True, stop=False)
                    P.mm(po, o_ap, g_tokA, tA[:, h * 128:(h + 1) * 128], g_wk, wk[0:C, h, :], start=False, stop=True)
                P.cp(ob, ob[:, :, cs], po, pf(po), eng="act")
                if CUT < 7:
                    continue
                for half in range(2):
                    pss = P.next_ps()
                    for hh in range(4):
                        h = half * 4 + hh
                        P.mm(pss, pss[:, hh * 128:(hh + 1) * 128], g_tokB, tB[:, h * 128:(h + 1) * 128],
                             g_tokA, tA[:, h * 128:(h + 1) * 128])
                    for hh in range(4):
                        h = half * 4 + hh
                        P.stt(S, S[:, h, :], S, S[:, h, :], eglast[:, h:h + 1], pss, pss[:, hh * 128:(hh + 1) * 128],
                              ALU.mult, ALU.add, extra=[eglast])
                if tl.kind == "s":
                    P.dma("sp", O["gdn_o"][1 + n], S[:], reads=[S])
                elif tl.last and n == nch - 1:
                    P.dma("sp", O["gdn_o"][0], S[:], reads=[S])
            if SUB < 9:
                return
            P.barrier([sq])
            P.act(sq, sq[:, :, :T], ob, ob[:, :, :T], AF.Square)
            for pr in range(4):
                ps = P.next_ps()
                for hh in range(2):
                    P.mm(ps, ps[:, hh * T:(hh + 1) * T], ones, ones[:], sq, sq[:, pr * 2 + hh, :T])
                rv = sq[:, pr * 2:pr * 2 + 2, :T]
                P.act(sq, rv, ps, ps[:, 0:2 * T].rearrange("p (h t) -> p h t", t=T), AF.Ln, bias=EPS, scale=1.0 / 128)
                P.act(sq, rv, sq, rv, AF.Exp, scale=-0.5)
            P.tt(ob, ob[:, :, :T], ob, ob[:, :, :T], sq, sq[:, :, :T], ALU.mult)
            P.tt(ob, ob[:, :, :T], ob, ob[:, :, :T], gg, gg[:, :, :T], ALU.mult)
            P.cp(hT, hT[:, 0:8, :T], m_o, m_o[:, :, :T], eng="act")
            P.ts1(hT, hT[:, 8:16, :T], ob, ob[:, :, :T], gdng[:, 0:1], ALU.mult, extra=[gdng])

        def mixer_ab(tl):
            T = tl.T
            gla(tl)
            if SUB >= 5:
                gdn(tl)
            else:
                P.cp(hT, hT[:, 0:8, :T], m_o, m_o[:, :, :T], eng="act")
                P.act(hT, hT[:, 8:16, :T], m_o, m_o[:, :, :T], AF.Copy, scale=0.0)
            P.barrier([yacc])
            proj_fm(tl, "w_out_ab", 0, D, lambda j, m, ps: P.cp(yacc, yacc[:, j, :T], ps, ps[:, :T], eng="act"))
            resid_add(tl, 0, 32, yacc)

        sscw = const("ssd_cw", [128, 12, 4])
        sscb = const("ssd_cb", [128, 12])
        ssnegA = const("ssd_Alog", [16, 1])
        P.act(ssnegA, ssnegA[:], ssnegA, ssnegA[:], AF.Exp)
        P.ts1(ssnegA, ssnegA[:], ssnegA, ssnegA[:], -1.0, ALU.mult)
        ssdtb = const("ssd_dtb", [16, 1])
        ssDcol = const("ssd_Dcol", [128, 8])
        ssng = const("ssd_ngT", [128, 8])
        s5Dcol = const("s5_Dcol", [128, 8])
        glub = const("glu_bT", [128, 8])
        sscst = P.sb([128, 12, 3], F32, "sscst")
        P.memset(sscst, sscst[:], 0.0)
        He = P.sb([128, 8, 128], F32, "He")
        Ho = P.sb([128, 8, 128], F32, "Ho")
        P.memset(He, He[:], 0.0)
        P.memset(Ho, Ho[:], 0.0)
        dtr = [P.sb([16, TP], F32, "dtr%d" % i) for i in range(4)]
        tk = P.sb([64, 48], F32, "tk")
        dcl = P.sb([128, 16], F32, "dcl")
        s5st = P.sb([128, 32, 2], F32, "s5st")
        P.memset(s5st, s5st[:], 0.0)
        s5fre = P.sb([128, 32], F32, "s5fre")
        s5fim = P.sb([128, 32], F32, "s5fim")
        c_z, c_xbc, c_y = U(0, 8, "c_z"), U(8, 20, "c_xbc"), U(28, 36, "c_y")
        c_u = P.view(RWt[:, :, :], "c_u")
        c_sc, c_Ap, c_cin, c_bout = U(36, 40, "c_sc"), U(40, 44, "c_Ap"), U(44, 48, "c_cin"), U(48, 56, "c_bout")
        c_xe, c_xo, c_btk, c_cbm, c_es = U(56, 60, "c_xe"), U(60, 64, "c_xo"), U(64, 65, "c_btk"), U(65, 66, "c_cbm"), U(36, 44, "c_sq")
        c_xm = U(66, 70, "c_xm")
        c_stt = U(48, 56, "c_stt")
        s_tab = [U(36 + 4 * i, 40 + 4 * i, "s_tab%d" % i) for i in range(2)]
        s_bp, s_z = U(44, 46, "s_bp"), U(46, 48, "s_z")
        s_xt = stack.enter_context(nc.sbuf_tensor("s5x", [128, 2, TP], F32R))
        s_x = P.view(s_xt[:], "s_x")
        s_yd = U(0, 8, "s_yd")
        s_z5 = P.view(RBt[:, :, :], "s_z5")
        s_tmp = U(48, 50, "s_tmp")
        s_x0 = U(50, 54, "s_x0")
        s_so = U(54, 58, "s_so")
        CD_BUFS = [c_z, c_xbc, c_y, c_u, c_sc, c_Ap, c_cin, c_bout, c_xe, c_xo, c_btk, c_cbm, c_xm]

        s5tab = nc.dram_tensor("s5tab", [4, 128, 32, TP], F32, kind="Internal").ap()
        s5tabB = Buf(None, "s5tab")

        def s5_setup():
            are = const("s5_are", [128, 32])
            aim = const("s5_aim", [128, 32])
            ldt = const("s5_ldt", [128, 32])
            w = [P.view(MWt[:, 40, i * 32:(i + 1) * 32], "s5w%d" % i) for i in range(8)]
            w += [P.view(MWt[:, 41, i * 32:(i + 1) * 32], "s5w%d" % (8 + i)) for i in range(6)]
            P.barrier(w)
            dtv, ar, th, mag, img, c_, s_, t0, t1, den = w[:10]
            P.act(dtv, dtv[:], ldt, ldt[:], AF.Exp)
            P.tt(ar, ar[:], are, are[:], dtv, dtv[:], ALU.mult)
            P.tt(th, th[:], aim, aim[:], dtv, dtv[:], ALU.mult)
            P.act(mag, mag[:], ar, ar[:], AF.Exp)
            P.act(img, img[:], ar, ar[:], AF.Exp, scale=-1.0)
            P.act(s_, s_[:], th, th[:], AF.Sin, scale=1.0 / 16)
            P.act(t0, t0[:], th, th[:], AF.Sin, scale=1.0 / 32)
            P.tt(t0, t0[:], t0, t0[:], t0, t0[:], ALU.mult)
            P.ts(c_, c_[:], t0, t0[:], -2.0, 1.0, ALU.mult, ALU.add)
            for _ in range(4):
                P.tt(t0, t0[:], c_, c_[:], c_, c_[:], ALU.mult)
                P.tt(t1, t1[:], s_, s_[:], s_, s_[:], ALU.mult)
                P.tt(s_, s_[:], s_, s_[:], c_, c_[:], ALU.mult)
                P.ts1(s_, s_[:], s_, s_[:], 2.0, ALU.mult)
                P.tt(c_, c_[:], t0, t0[:], t1, t1[:], ALU.subtract)
            lre, lim, ire, iim = w[10:14]
            w = w[:10]
            P.tt(lre, lre[:], mag, mag[:], c_, c_[:], ALU.mult)
            P.tt(lim, lim[:], mag, mag[:], s_, s_[:], ALU.mult)
            P.tt(ire, ire[:], img, img[:], c_, c_[:], ALU.mult)
            P.tt(iim, iim[:], img, img[:], s_, s_[:], ALU.mult)
            P.ts1(iim, iim[:], iim, iim[:], -1.0, ALU.mult)
            P.tt(den, den[:], are, are[:], are, are[:], ALU.mult)
            P.tt(t0, t0[:], aim, aim[:], aim, aim[:], ALU.mult)
            P.tt(den, den[:], den, den[:], t0, t0[:], ALU.add)
            P.op("dve", lambda E: E.reciprocal(den[:], den[:]), reads=[den], writes=[den])
            nr = dtv
            P.ts1(nr, nr[:], lre, lre[:], -1.0, ALU.add)
            P.tt(t0, t0[:], nr, nr[:], are, are[:], ALU.mult)
            P.tt(t1, t1[:], lim, lim[:], aim, aim[:], ALU.mult)
            P.tt(t0, t0[:], t0, t0[:], t1, t1[:], ALU.add)
            P.tt(s5fre, s5fre[:], t0, t0[:], den, den[:], ALU.mult)
            P.tt(t0, t0[:], lim, lim[:], are, are[:], ALU.mult)
            P.tt(t1, t1[:], nr, nr[:], aim, aim[:], ALU.mult)
            P.tt(t0, t0[:], t0, t0[:], t1, t1[:], ALU.subtract)
            P.tt(s5fim, s5fim[:], t0, t0[:], den, den[:], ALU.mult)
            Tre, Tim, Tt = U(0, 8, "s5Tre"), U(8, 16, "s5Tim"), U(16, 24, "s5Tt")
            Fre, Fim = U(24, 32, "s5Fre"), U(32, 40, "s5Fim")
            lre, lim, ire, iim = lre, lim, ire, iim
            P.barrier([Tre, Tim, Tt, Fre, Fim])
            for grp in range(4):
                ms = slice(grp * 8, grp * 8 + 8)
                for kind, (bre, bim) in enumerate(((lre, lim), (ire, iim))):
                    P.cp(Tre, Tre[:, :, 0:1], bre, bre[:, ms].unsqueeze(2))
                    P.cp(Tim, Tim[:, :, 0:1], bim, bim[:, ms].unsqueeze(2))
                    n = 1
                    while n < TP:
                        sre = bc(Tre[:, :, n - 1:n], [128, 8, n])
                        sim = bc(Tim[:, :, n - 1:n], [128, 8, n])
                        tv = Tt[:, :, 0:n]
                        P.tt(Tt, tv, Tim, Tim[:, :, 0:n], Tim, sim, ALU.mult)
                        P.tt(Tre, Tre[:, :, n:2 * n], Tre, Tre[:, :, 0:n], Tre, sre, ALU.mult)
                        P.tt(Tre, Tre[:, :, n:2 * n], Tre, Tre[:, :, n:2 * n], Tt, tv, ALU.subtract)
                        P.tt(Tt, tv, Tim, Tim[:, :, 0:n], Tre, sre, ALU.mult)
                        P.tt(Tim, Tim[:, :, n:2 * n], Tre, Tre[:, :, 0:n], Tim, sim, ALU.mult)
                        P.tt(Tim, Tim[:, :, n:2 * n], Tim, Tim[:, :, n:2 * n], Tt, tv, ALU.add)
                        n *= 2
                    if kind == 0:
                        P.dma("sp", s5tab[0][:, ms, :], Tre[:], reads=[Tre], writes=[s5tabB])
                        P.dma("sp", s5tab[1][:, ms, :], Tim[:], reads=[Tim], writes=[s5tabB])
                    else:
                        fre = bc(s5fre[:, ms].unsqueeze(2), [128, 8, TP])
                        fim = bc(s5fim[:, ms].unsqueeze(2), [128, 8, TP])
                        P.tt(Fre, Fre[:], Tre, Tre[:], s5fre, fre, ALU.mult)
                        P.tt(Tt, Tt[:], Tim, Tim[:], s5fim, fim, ALU.mult)
                        P.tt(Fre, Fre[:], Fre, Fre[:], Tt, Tt[:], ALU.subtract)
                        P.tt(Fim, Fim[:], Tre, Tre[:], s5fim, fim, ALU.mult)
                        P.tt(Tt, Tt[:], Tim, Tim[:], s5fre, fre, ALU.mult)
                        P.tt(Fim, Fim[:], Fim, Fim[:], Tt, Tt[:], ALU.add)
                        P.dma("sp", s5tab[2][:, ms, :], Fre[:], reads=[Fre], writes=[s5tabB])
                        P.dma("sp", s5tab[3][:, ms, :], Fim[:], reads=[Fim], writes=[s5tabB])

        def ssd_state_in(n):
            P.barrier([c_stt])
            P.dma("sp", c_stt[:].rearrange("p a t -> p (a t)")[:, 0:1024].rearrange("p (c s) -> p c s", s=128), I["ssd_st"][n],
                  writes=[c_stt])
            sv = c_stt[:].rearrange("p a t -> p (a t)")[:, 0:1024].rearrange("p (c s) -> p c s", s=128)
            for half in range(2):
                ps = P.next_ps()
                for q4 in range(4):
                    P.tr(ps, ps[:, q4 * 128:(q4 + 1) * 128], c_stt, sv[:, half * 4 + q4, :], ident, ident[:])
                pv = ps[:, :].rearrange("p (c q) -> p c q", q=128)
                P.cp(He, He[:, half * 4:half * 4 + 4, 0:64], ps, pv[:, :, 0:64])
                P.cp(Ho, Ho[:, half * 4:half * 4 + 4, 64:128], ps, pv[:, :, 64:128])

        def ssd_state_out(dst):
            P.barrier([c_stt, c_es])
            sv = c_stt[:].rearrange("p a t -> p (a t)")[:, 0:1024].rearrange("p (c s) -> p c s", s=128)
            sm = c_es[:].rearrange("p a t -> p (a t)")[:, 0:1024].rearrange("p (c s) -> p c s", s=128)
            P.tt(c_es, sm, He, He[:], Ho, Ho[:], ALU.add)
            for half in range(2):
                ps = P.next_ps()
                for q4 in range(4):
                    P.tr(ps, ps[:, q4 * 128:(q4 + 1) * 128], c_es, sm[:, half * 4 + q4, :], ident, ident[:])
                P.cp(c_stt, sv[:, half * 4:half * 4 + 4, :], ps, ps[:, :].rearrange("p (c q) -> p c q", q=128), eng="act")
            P.dma("sp", dst, sv, reads=[c_stt])

        def ssd(tl):
            T, C, nch = tl.T, tl.C, tl.nch
            dT, aT, acT, wT = dtr
            P.act(dT, dT[0:16, :T], dT, dT[0:16, :T], AF.Exp, bias=ssdtb[0:16, 0:1], extra=[ssdtb])
            P.act(dT, dT[0:16, :T], dT, dT[0:16, :T], AF.Ln, bias=1.0)
            P.ts1(aT, aT[0:16, :T], dT, dT[0:16, :T], ssnegA[0:16, 0:1], ALU.mult, extra=[ssnegA])
            rm = rmask[tl.kind]
            P.scan(acT, acT[0:16, :T], rm, rm[0:16, 0:T], aT, aT[0:16, :T])
            a3 = acT[0:16, :T].rearrange("p (n c) -> p n c", c=C)
            w3 = wT[0:16, :T].rearrange("p (n c) -> p n c", c=C)
            P.tt(wT, w3, acT, bc(a3[:, :, C - 1:C], [16, nch, C]), acT, a3, ALU.subtract)
            P.act(wT, wT[0:16, :T], wT, wT[0:16, :T], AF.Exp)
            P.tt(wT, wT[0:16, :T], wT, wT[0:16, :T], dT, dT[0:16, :T], ALU.mult)
            P.act(acT, acT[0:16, :T], acT, acT[0:16, :T], AF.Exp)
            scT = t4(c_sc, C, 16)
            Ap = t4(c_Ap, C, 16)
            cin = t4(c_cin, C, 16)
            bout = c_bout[:].rearrange("p a t -> p (a t)")[0:C, :].rearrange("p (h s) -> p h s", s=128)
            xe, xo = tok(c_xe, C, 1024), tok(c_xo, C, 1024)
            btk = tok(c_btk, C, 256)
            cbm = t4(c_cbm, C, 2)
            P.memset(c_xe, xe, 0.0)
            P.memset(c_xo, xo, 0.0)
            p3 = lambda ps_, nh: ps_[0:C, 0:nh * C].rearrange("p (h c) -> p h c", c=C)
            for n in range(nch if SC >= 2 else 0):
                c0 = n * C
                cs = slice(c0, c0 + C)
                if tl.kind == "s" and SC >= 7:
                    ssd_state_in(n)
                    P.barrier([c_bout, c_cin])
                pst = P.next_ps()
                for i, src in enumerate((aT, dT, wT)):
                    P.tr(pst, pst[0:C, i * 16:(i + 1) * 16], src, src[0:16, cs], ident, ident[0:16, 0:16])
                P.cp(tk, tk[0:C, :], pst, pst[0:C, 0:48], eng="act")
                P.tt(c_Ap, Ap[0:C], tk, bc(tk[0:C, 0:16].unsqueeze(2), [C, 16, C]), SL, bc(SL[0:C, 0:C].unsqueeze(1), [C, 16, C]),
                     ALU.mult)
                for half in range(2):
                    psd = P.next_ps()
                    for hh in range(8):
                        P.mm(psd, psd[0:C, hh * C:(hh + 1) * C], c_Ap, Ap[0:C, half * 8 + hh, :], UT, UT[0:C, 0:C])
                    P.act(c_sc, scT[0:C, half * 8:half * 8 + 8, :], psd, p3(psd, 8), AF.Exp)
                pcb = P.next_ps()
                for g in range(2):
                    P.mm(pcb, pcb[0:C, g * C:(g + 1) * C], c_xbc, c_xbc[:, 8 + g, cs], c_xbc, c_xbc[:, 10 + g, cs])
                P.tt(c_cbm, cbm[0:C], pcb, p3(pcb, 2), UT, bc(UT[0:C, 0:C].unsqueeze(1), [C, 2, C]), ALU.mult)
                sc4 = scT[0:C].rearrange("p (g h) c -> p g h c", g=2)
                P.tt(c_sc, sc4, c_sc, sc4, c_cbm, bc(cbm[0:C].unsqueeze(2), [C, 2, 8, C]), ALU.mult)
                P.tt(c_sc, scT[0:C], c_sc, scT[0:C], tk, bc(tk[0:C, 16:32].unsqueeze(2), [C, 16, C]), ALU.mult)
                if SC < 3:
                    continue
                for half in range(2):
                    pt = P.next_ps()
                    for q4 in range(4):
                        P.tr(pt, pt[0:C, q4 * 128:(q4 + 1) * 128], c_xbc, c_xbc[:, half * 4 + q4, cs], ident, ident[:])
                    pv = pt[0:C, :].rearrange("p (c q) -> p c q", q=128)
                    xev = xe[:, half * 512:(half + 1) * 512].rearrange("p (c q) -> p c q", q=128)
                    xov = xo[:, half * 512:(half + 1) * 512].rearrange("p (c q) -> p c q", q=128)
                    P.cp(c_xe, xev[:, :, 0:64], pt, pv[:, :, 0:64])
                    P.cp(c_xo, xov[:, :, 64:128], pt, pv[:, :, 64:128])
                pb = P.next_ps()
                for g in range(2):
                    P.tr(pb, pb[0:C, g * 128:(g + 1) * 128], c_xbc, c_xbc[:, 8 + g, cs], ident, ident[:])
                P.cp(c_btk, btk, pb, pb[0:C, 0:256], eng="act")
                b4 = bout.rearrange("p (g h) s -> p g h s", g=2)
                P.tt(c_bout, b4, c_btk, bc(btk.rearrange("p (g s) -> p g s", g=2).unsqueeze(2), [C, 2, 8, 128]),
                     tk, bc(tk[0:C, 32:48].rearrange("p (g h) -> p g h", g=2).unsqueeze(3), [C, 2, 8, 128]), ALU.mult)
                if SC < 4:
                    continue
                xm = c_xm[:].rearrange("p a t -> p (a t)")[0:16, 0:16 * C].rearrange("p (h c) -> p h c", c=C)
                P.tt(c_xm, xm, acT, bc(acT[0:16, cs].unsqueeze(1), [16, 16, C]),
                     ident, bc(ident[0:16, 0:16].unsqueeze(2), [16, 16, C]), ALU.mult)
                for g in range(2):
                    pse = P.next_ps()
                    for hh in range(8):
                        P.mm(pse, pse[:, hh * C:(hh + 1) * C], ones, ones[0:16, :], c_xm, xm[:, g * 8 + hh, :])
                    pev = pse[:, 0:8 * C].rearrange("p (h c) -> p h c", c=C)
                    P.tt(c_cin, cin[:, g * 8:g * 8 + 8, :], c_xbc, bc(c_xbc[:, 10 + g, cs].unsqueeze(1), [128, 8, C]), pse, pev,
                         ALU.mult)
                    P.cp(dcl, dcl[:, g * 8:g * 8 + 8].unsqueeze(2), pse, pev[:, :, C - 1:C])
                if SC < 5:
                    continue
                py = P.next_ps()
                for c in range(8):
                    o_ap = py[:, c * C:(c + 1) * C]
                    P.mm(py, o_ap, c_xe, xe[:, c * 128:(c + 1) * 128], c_sc, scT[0:C, 2 * c, :], start=True, stop=False)
                    P.mm(py, o_ap, c_xo, xo[:, c * 128:(c + 1) * 128], c_sc, scT[0:C, 2 * c + 1, :], start=False, stop=False)
                    P.mm(py, o_ap, He, He[:, c, :], c_cin, cin[:, 2 * c, :], start=False, stop=False)
                    P.mm(py, o_ap, Ho, Ho[:, c, :], c_cin, cin[:, 2 * c + 1, :], start=False, stop=True)
                P.cp(c_y, c_y[:, :, cs], py, py[:, 0:8 * C].rearrange("p (a c) -> p a c", c=C), eng="act")
                if SC < 6:
                    continue
                for half in range(2):
                    pss = P.next_ps()
                    for cc in range(4):
                        c = half * 4 + cc
                        P.mm(pss, pss[:, cc * 128:cc * 128 + 64], c_bout, bout[:, 2 * c, :], c_xe, xe[:, c * 128:c * 128 + 64])
                        P.mm(pss, pss[:, cc * 128 + 64:(cc + 1) * 128], c_bout, bout[:, 2 * c + 1, :],
                             c_xo, xo[:, c * 128 + 64:(c + 1) * 128])
                    for cc in range(4):
                        c = half * 4 + cc
                        P.stt(He, He[:, c, 0:64], He, He[:, c, 0:64], dcl[:, 2 * c:2 * c + 1], pss, pss[:, cc * 128:cc * 128 + 64],
                              ALU.mult, ALU.add, extra=[dcl])
                        P.stt(Ho, Ho[:, c, 64:128], Ho, Ho[:, c, 64:128], dcl[:, 2 * c + 1:2 * c + 2],
                              pss, pss[:, cc * 128 + 64:(cc + 1) * 128], ALU.mult, ALU.add, extra=[dcl])
                if SC < 7:
                    continue
                if tl.kind == "s":
                    ssd_state_out(O["ssd_o"][1 + n])
                    P.barrier([c_bout, c_cin, c_sc, c_Ap])
                elif tl.last and n == nch - 1:
                    ssd_state_out(O["ssd_o"][0])
            if SC < 8:
                return
            for c in range(8):
                P.stt(c_y, c_y[:, c, :T], c_xbc, c_xbc[:, c, :T], ssDcol[:, c:c + 1], c_y, c_y[:, c, :T], ALU.mult, ALU.add,
                      extra=[ssDcol])
            P.tt(c_y, c_y[:, :, :T], c_y, c_y[:, :, :T], c_z, c_z[:, :, :T], ALU.mult)
            P.barrier([c_es])
            P.act(c_es, c_es[:, :, :T], c_y, c_y[:, :, :T], AF.Square)
            ps = P.next_ps()
            for g in range(2):
                for cc in range(4):
                    P.mm(ps, ps[:, g * T:(g + 1) * T], ones, ones[:], c_es, c_es[:, g * 4 + cc, :T], start=(cc == 0), stop=(cc == 3))
            rsv = c_es[:, 0:2, :T]
            P.act(c_es, rsv, ps, ps[:, 0:2 * T].rearrange("p (g t) -> p g t", t=T), AF.Ln, bias=EPS, scale=1.0 / 512)
            P.act(c_es, rsv, c_es, rsv, AF.Exp, scale=-0.5)
            y4 = c_y[:, :, :T].rearrange("p (g c) t -> p g c t", g=2)
            P.tt(c_y, y4, c_y, y4, c_es, bc(rsv.unsqueeze(2), [128, 2, 4, T]), ALU.mult)
            for c in range(8):
                P.ts1(hT, hT[:, c, :T], c_y, c_y[:, c, :T], ssng[:, c:c + 1], ALU.mult, extra=[ssng])

        def s5(tl):
            s5_body(tl)

        def s5_body(tl):
            T, L, ns = tl.T, tl.L, tl.nseq
            P.barrier(s_tab + [s_bp, s_z, s_x, s_yd, s_tmp, s_x0, s_so])
            x0v = s_x0[:].rearrange("p a t -> p (a t)")[:, 0:2 * 32 * NSQ].rearrange("p (k m s) -> p k m s", k=2, s=NSQ)
            sov = s_so[:].rearrange("p a t -> p (a t)")[:, 0:32 * NSQ * 2].rearrange("p (m s k) -> p m s k", s=NSQ, k=2)
            if tl.kind == "s":
                P.dma("sp", x0v, I["s5_x0"], writes=[s_x0])
            onesrow = bc(ones[:, 0:1], [128, T])
            TL = L
            for c in range(8):
                wb = s5wb
                wv = wb[:, 0:16 * 128].rearrange("p (k m q) -> p k m q", k=4, q=128)
                P.dma("pool", wv, I["s5w"][c], writes=[wb])
                pyr = P.next_ps()
                pyi = P.next_ps()
                for mm_ in range(4):
                    m = 4 * c + mm_
                    tb = s_tab[m % 2]
                    tv = tb[:].rearrange("p a t -> p (a t)")[:, 0:4 * TL].rearrange("p (k t) -> p k t", k=4)
                    P.dma("sp", tv, s5tab[:, :, m, 0:TL].rearrange("k p t -> p k t"), reads=[s5tabB], writes=[tb])

                    def tab(k):
                        return bc(tv[:, k, :].unsqueeze(1), [128, ns, L])

                    pb = P.next_ps()
                    P.mm(pb, pb[:, 0:T], wb, wv[:, 0, mm_, :], c_u, c_u[:, c, :T])
                    P.mm(pb, pb[:, T:2 * T], wb, wv[:, 1, mm_, :], c_u, c_u[:, c, :T])
                    bur = pb[:, 0:T].rearrange("p (s l) -> p s l", l=L)
                    bui = pb[:, T:2 * T].rearrange("p (s l) -> p s l", l=L)
                    bp = s_bp[:, :, :T].rearrange("p k (s l) -> p k s l", l=L)
                    tm = s_tmp[:, :, :T].rearrange("p k (s l) -> p k s l", l=L)
                    P.tt(s_bp, bp[:, 0], pb, bur, tb, tab(2), ALU.mult)
                    P.tt(s_tmp, tm[:, 0], pb, bui, tb, tab(3), ALU.mult)
                    P.tt(s_bp, bp[:, 0], s_bp, bp[:, 0], s_tmp, tm[:, 0], ALU.subtract)
                    P.tt(s_bp, bp[:, 1], pb, bui, tb, tab(2), ALU.mult)
                    P.tt(s_tmp, tm[:, 1], pb, bur, tb, tab(3), ALU.mult)
                    P.tt(s_bp, bp[:, 1], s_bp, bp[:, 1], s_tmp, tm[:, 1], ALU.add)
                    if tl.kind == "s":
                        for k in range(2):
                            P.tt(s_bp, bp[:, k, :, 0:1], s_bp, bp[:, k, :, 0:1], s_x0, x0v[:, k, m, :].unsqueeze(2), ALU.add)
                        for k in range(2):
                            P.scan(s_z, s_z[:, k, :T], rmask["s"], rmask["s"][:, 0:T], s_bp, s_bp[:, k, :T])
                    else:
                        for k in range(2):
                            P.op("dve", lambda E, k=k, m=m: E.tensor_tensor_scan(
                                s_z[:, k, :T], onesrow, s_bp[:, k, :T], s5st[:, m, k:k + 1], ALU.mult, ALU.add),
                                reads=[ones, s_bp, s5st], writes=[s_z])
                    zv = s_z[:, :, :T].rearrange("p k (s l) -> p k s l", l=L)
                    xv = s_x[:, :, :T].rearrange("p k (s l) -> p k s l", l=L)
                    P.tt(s_tmp, tm[:, 0], s_z, zv[:, 1], tb, tab(1), ALU.mult)
                    P.tt(s_tmp, tm[:, 1], s_z, zv[:, 0], tb, tab(0), ALU.mult)
                    P.tt(s_x, xv[:, 0], s_tmp, tm[:, 1], s_tmp, tm[:, 0], ALU.subtract)
                    P.tt(s_tmp, tm[:, 0], s_z, zv[:, 0], tb, tab(1), ALU.mult)
                    P.tt(s_tmp, tm[:, 1], s_z, zv[:, 1], tb, tab(0), ALU.mult)
                    P.tt(s_x, xv[:, 1], s_tmp, tm[:, 1], s_tmp, tm[:, 0], ALU.add)
                    if tl.kind == "s":
                        P.cp(s_so, sov[:, m].rearrange("p s k -> p k s").unsqueeze(3), s_x, xv[:, :, :, L - 1:L].bitcast(F32))
                    else:
                        P.cp(s5st, s5st[:, m, :].unsqueeze(2), s_x, s_x[:, :, T - 1:T].bitcast(F32))
                    P.mm(pyr, pyr[:, :T], wb, wv[:, 2, mm_, :], s_x, s_x[:, 0, :T], start=(mm_ == 0), stop=(mm_ == 3))
                    P.mm(pyi, pyi[:, :T], wb, wv[:, 3, mm_, :], s_x, s_x[:, 1, :T], start=(mm_ == 0), stop=(mm_ == 3))
                P.cp(s_tmp, s_tmp[:, 0, :T], pyi, pyi[:, :T], eng="act")
                P.tt(s_yd, s_yd[:, c, :T], pyr, pyr[:, :T], s_tmp, s_tmp[:, 0, :T], ALU.subtract)
                P.stt(s_yd, s_yd[:, c, :T], c_u, c_u[:, c, :T].bitcast(F32), s5Dcol[:, c:c + 1], s_yd, s_yd[:, c, :T],
                      ALU.mult, ALU.add, extra=[s5Dcol])
            if tl.kind == "s":
                P.dma("sp", O["s5_o"][:, :, 1:NS1, :], sov, reads=[s_so])
            elif tl.last:
                P.dma("sp", O["s5_o"][:, :, 0:1, :], s5st[:].unsqueeze(2), reads=[s5st])
            P.barrier([s_z5])
            P.act(s_z5, s_z5[:, :, :T], s_yd, s_yd[:, :, :T], AF.Gelu)

            def cons_glu(j, mrows, ps):
                P.act(s_tmp, s_tmp[:, 0, :T], ps, ps[:, :T], AF.Sigmoid, bias=glub[:, j:j + 1], extra=[glub])
                P.tt(hT, hT[:, 8 + j, :T], s_z5, s_z5[:, j, :T], s_tmp, s_tmp[:, 0, :T], ALU.mult)

            proj_fm(tl, "glu_w", 0, 1024, cons_glu, rhs=s_z5, nk=8)

        def mixer_cd(tl):
            T = tl.T
            W = "w_in_cd"
            P.barrier(CD_BUFS)
            PJ = int(os.environ.get("KDEV_PJ", "15"))
            if PJ & 1:
                proj_fm(tl, W, 0, 1024, lambda j, m, ps: P.act(c_z, c_z[:, j, :T], ps, ps[:, :T], AF.Silu))
            if PJ & 2:
                proj_fm(tl, W, 1024, 1536, lambda j, m, ps: conv_chunk(
                    tl, ps, sscw, sscb, j, 4, sscst, I["ssd_cst"], O["ssd_cst_o"], c_xbc, c_xbc[:, j, :T], AF.Silu))
            if PJ & 4:
                proj_fm(tl, W, 2560, 16, lambda j, m, ps: P.cp(dtr[0], dtr[0][0:16, :T], ps, ps[0:16, :T]))
            if PJ & 8:
                proj_fm(tl, W, 2576, 1024, lambda j, m, ps: P.cp(c_u, c_u[:, j, :T], ps, ps[:, :T], eng="act"))
            if CDCUT >= 3:
                ssd(tl)
            if CDCUT >= 4:
                s5(tl)
            if CDCUT < 4:
                return
            P.barrier([yacc])
            proj_fm(tl, "w_out_cd", 0, D, lambda j, m, ps: P.cp(yacc, yacc[:, j, :T], ps, ps[:, :T], eng="act"))
            resid_add(tl, 1, 32, yacc)

        def final_out(tl):
            T = tl.T
            P.barrier([yacc])
            rms_stats(tl)
            for c in range(KC):
                P.act(yacc, yacc[:, c, :T], yacc, yacc[:, c, :T], AF.Copy, scale=gfin[:, c:c + 1], extra=[gfin])
            if tl.kind == "p":
                P.dma("sp", ypv[:, :, tl.idx * TP:(tl.idx + 1) * TP], yacc[:, :, :T], reads=[yacc])
            else:
                P.dma("sp", ysv, yacc[:, :, :T], reads=[yacc])

        if STAGE >= 3:
            s5_setup()
        tiles = [Tile("p", i) for i in range(NPT)] + [Tile("s", 0)]
        P.sew = not bool(int(os.environ.get("KDEV_NOSEW", "0")))
        xpv = I["xp"].rearrange("(c p) t -> p c t", p=128)
        xsv = I["xs"].rearrange("(c p) t -> p c t", p=128)
        ypv = O["yp"].rearrange("(c p) t -> p c t", p=128)
        ysv = O["ys"].rearrange("(c p) t -> p c t", p=128)
        for tl in tiles:
            if tl.kind == "p":
                P.dma("sp", xT[:, :, :tl.T], xpv[:, :, tl.idx * TP:(tl.idx + 1) * TP], writes=[xT])
            else:
                P.dma("sp", xT[:, :, :tl.T], xsv, writes=[xT])
            for l in range(2):
                norm_mod(tl, l, 0, 16)
                if l == 0 and STAGE >= 2:
                    mixer_ab(tl)
                if l == 1 and STAGE >= 3 and CDCUT >= 2:
                    mixer_cd(tl)
                norm_mod(tl, l, 48, 64)
                ffn(tl, l)
                resid_add(tl, l, 80, yacc)
            final_out(tl)
        P.finish()
    return nc


def _fm(v):
    v = np.asarray(v, np.float32)
    return np.ascontiguousarray(v.reshape(-1, 128).T)


def _consts():
    i = np.arange(128)
    c = {}
    c["ident"] = np.eye(128, dtype=np.float32)
    c["UT"] = (i[None, :] >= i[:, None]).astype(np.float32)
    c["nSU"] = -(i[None, :] > i[:, None]).astype(np.float32)
    c["SL"] = (i[:, None] > i[None, :]).astype(np.float32)
    rp = np.ones((128, TP), np.float32)
    rp[:, ::64] = 0.0
    rs = np.ones((128, TS), np.float32)
    rs[:, ::LS] = 0.0
    c["rmask_p"], c["rmask_s"] = rp, rs
    es = np.zeros((8, 8, 128), np.float32)
    for h in range(8):
        es[h, h, :] = 1.0
    c["esel"] = es
    return c


def make_in_maps(inp):
    maps = []
    cst = _consts()
    assert WS == 256
    wt = {}
    wt["w_ada"] = np.stack([tile_cols_all(inp["w_ada"][l]) for l in range(2)])
    wt["w_ffn_up"] = np.stack([tile_cols_all(inp["w_ffn_up"][l]) for l in range(2)])
    wt["w_ffn_down"] = np.ascontiguousarray(inp["w_ffn_down"].reshape(2, DFF // 256, 2, 128, D).transpose(0, 1, 3, 2, 4))
    wt["w_in_ab"] = tile_weight(inp["w_in_ab"][0], "w_in_ab")
    wt["w_out_ab"] = tile_weight(inp["w_out_ab"][0], "w_out_ab")
    wt["w_in_cd"] = tile_weight(inp["w_in_cd"][0], "w_in_cd")
    wt["w_out_cd"] = tile_weight(inp["w_out_cd"][0], "w_out_cd")
    wt["glu_w"] = tile_weight(inp["s5_glu_w"][0], "glu_w")
    Bre, Bim, Cre, Cim = (inp[k][0] for k in ("s5_B_re", "s5_B_im", "s5_C_re", "s5_C_im"))
    cst_s5w = np.zeros((8, 128, 4, 4, 128), np.float32)
    for c in range(8):
        for mm_ in range(4):
            for g2 in range(2):
                gl = 2 * mm_ + g2
                g = 8 * c + gl
                cst_s5w[c, gl * 16:(gl + 1) * 16, 0, mm_, g2 * 64:(g2 + 1) * 64] = Bre[g].T
                cst_s5w[c, gl * 16:(gl + 1) * 16, 1, mm_, g2 * 64:(g2 + 1) * 64] = Bim[g].T
                cst_s5w[c, g2 * 64:(g2 + 1) * 64, 2, mm_, gl * 16:(gl + 1) * 16] = Cre[g].T
                cst_s5w[c, g2 * 64:(g2 + 1) * 64, 3, mm_, gl * 16:(gl + 1) * 16] = Cim[g].T
    for c in range(NCORES):
        b = c // 2
        sl = slice(NSQ * c, NSQ * (c + 1))
        m = dict(cst)
        m["xp"] = inp["x_prompt"][b, :SEQ].T
        m["xs"] = inp["x_sample"][sl].reshape(TS, D).T
        m["cT"] = np.concatenate([inp["c_prompt"][b:b + 1], inp["c_sample"][sl]], 0).T
        m.update(wt)
        m["b_adaT"] = np.stack([_fm(inp["b_ada"][l]) for l in range(2)])
        m["g_mixT"] = np.stack([_fm(inp["g_mix"][l]) for l in range(2)])
        m["g_ffnT"] = np.stack([_fm(inp["g_ffn"][l]) for l in range(2)])
        m["g_finT"] = _fm(inp["g_final"])
        m["ffn_cw"] = inp["ffn_conv_w"].reshape(2, 3, NFF, 128).transpose(0, 3, 2, 1)
        m["ffn_cb"] = inp["ffn_conv_b"].reshape(2, NFF, 128).transpose(0, 2, 1)
        m["ffn_st"] = inp["state_ffn_conv"][:, sl].reshape(2, NSQ, 2, NFF, 128).transpose(0, 4, 3, 1, 2)
        m["gla_w2"] = inp["gla_w2"][0]
        m["gla_b2T"] = _fm(inp["gla_b2"][0])
        m["gla_ngT"] = _fm(inp["gla_norm_g"][0])
        m["gla_st"] = inp["state_gla"][0, sl].transpose(0, 2, 1, 3)
        m["gdn_cw"] = inp["gdn_conv_w"][0].reshape(4, 24, 128).transpose(2, 1, 0)
        m["gdn_cb"] = inp["gdn_conv_b"][0].reshape(24, 128).T
        m["gdn_cst"] = inp["state_gdn_conv"][0, sl].reshape(NSQ, 3, 24, 128).transpose(3, 2, 0, 1)
        m["gdn_Alog"] = inp["gdn_A_log"][0].reshape(8, 1)
        m["gdn_dtb"] = inp["gdn_dt_bias"][0].reshape(8, 1)
        m["gdn_ngT"] = inp["gdn_norm_g"][0].reshape(128, 1)
        m["gdn_st"] = inp["state_gdn"][0, sl].transpose(0, 2, 1, 3)
        m["ssd_cw"] = inp["ssd_conv_w"][0].reshape(4, 12, 128).transpose(2, 1, 0)
        m["ssd_cb"] = inp["ssd_conv_b"][0].reshape(12, 128).T
        m["ssd_cst"] = inp["state_ssd_conv"][0, sl].reshape(NSQ, 3, 12, 128).transpose(3, 2, 0, 1)
        m["ssd_Alog"] = inp["ssd_A_log"][0].reshape(16, 1)
        m["ssd_dtb"] = inp["ssd_dt_bias"][0].reshape(16, 1)
        m["ssd_Dcol"] = np.repeat(inp["ssd_D"][0], 64).reshape(8, 128).T
        m["ssd_ngT"] = _fm(inp["ssd_norm_g"][0])
        m["ssd_st"] = inp["state_ssd"][0, sl].reshape(NSQ, 8, 128, 128).transpose(0, 2, 1, 3)
        modes = lambda a: a.reshape(32, 128).T
        m["s5_are"] = modes(inp["s5_A_re"][0])
        m["s5_aim"] = modes(inp["s5_A_im"][0])
        m["s5_ldt"] = modes(np.repeat(inp["s5_log_dt"][0][:, None], 64, axis=1))
        m["s5w"] = cst_s5w
        m["s5_Dcol"] = _fm(inp["s5_D"][0])
        m["s5_x0"] = np.stack([inp["state_s5_re"][0, sl].reshape(NSQ, 32, 128).transpose(2, 1, 0),
                               inp["state_s5_im"][0, sl].reshape(NSQ, 32, 128).transpose(2, 1, 0)], 1)
        m["glu_bT"] = _fm(inp["s5_glu_b"][0])
        maps.append({k: np.ascontiguousarray(v, dtype=np.float32) for k, v in m.items()})
    return maps


_NC_CACHE = {}


def run_device(inp):
    if "nc" not in _NC_CACHE:
        _NC_CACHE["nc"] = build_program()
    nc = _NC_CACHE["nc"]
    maps = make_in_maps(inp)
    if RUNCORES < NCORES:
        res = run_bass_kernel_spmd(nc, maps[:RUNCORES], core_ids=list(range(RUNCORES)))
        return [res.results[min(c, RUNCORES - 1)] for c in range(NCORES)]
    res = run_bass_kernel_spmd(nc, maps, core_ids=list(range(NCORES)))
    return res.results


def assemble(R):
    B, DB = 4, 128
    y_p = np.stack([R[2 * b]["yp"].T for b in range(B)])
    y_s = np.concatenate([R[c]["ys"].T.reshape(NSQ, LS, D) for c in range(NCORES)], 0)

    def ffn_un(a):
        return a.transpose(0, 3, 4, 2, 1).reshape(2, a.shape[3], 2, 2 * DFF)

    ffn_p = np.concatenate([ffn_un(R[2 * b]["ffn_st_o"][:, :, :, 0:1]) for b in range(B)], 1)
    ffn_s = np.concatenate([ffn_un(R[c]["ffn_st_o"][:, :, :, 1:]) for c in range(NCORES)], 1)

    def st_un(a):
        return a.transpose(0, 2, 1, 3)[None]

    gla_p = np.concatenate([st_un(R[2 * b]["gla_o"][0:1]) for b in range(B)], 1)
    gla_s = np.concatenate([st_un(R[c]["gla_o"][1:]) for c in range(NCORES)], 1)
    gdn_p = np.concatenate([st_un(R[2 * b]["gdn_o"][0:1]) for b in range(B)], 1)
    gdn_s = np.concatenate([st_un(R[c]["gdn_o"][1:]) for c in range(NCORES)], 1)

    def cv_un(a, nchn):
        return a.transpose(2, 3, 1, 0).reshape(1, a.shape[2], 3, nchn * 128)

    gdc_p = np.concatenate([cv_un(R[2 * b]["gdn_cst_o"][:, :, 0:1], 24) for b in range(B)], 1)
    gdc_s = np.concatenate([cv_un(R[c]["gdn_cst_o"][:, :, 1:], 24) for c in range(NCORES)], 1)
    def ssd_un(a):
        return a.transpose(0, 2, 1, 3).reshape(1, a.shape[0], 16, 64, 128)

    ssd_p = np.concatenate([ssd_un(R[2 * b]["ssd_o"][0:1]) for b in range(B)], 1)
    ssd_s = np.concatenate([ssd_un(R[c]["ssd_o"][1:]) for c in range(NCORES)], 1)
    ssc_p = np.concatenate([cv_un(R[2 * b]["ssd_cst_o"][:, :, 0:1], 12) for b in range(B)], 1)
    ssc_s = np.concatenate([cv_un(R[c]["ssd_cst_o"][:, :, 1:], 12) for c in range(NCORES)], 1)

    def s5_un(a, k):
        a = a[:, :, :, k]
        return a.transpose(2, 1, 0).reshape(1, a.shape[2], 64, 64)

    s5 = [np.concatenate([s5_un(R[2 * b]["s5_o"][:, :, 0:1], k) for b in range(B)], 1) for k in range(2)]
    s5s = [np.concatenate([s5_un(R[c]["s5_o"][:, :, 1:], k) for c in range(NCORES)], 1) for k in range(2)]
    out = (y_p, y_s, gla_p, gla_s, gdn_p, gdn_s, gdc_p, gdc_s,
           ssd_p, ssd_s, ssc_p, ssc_s, s5[0], s5s[0], s5[1], s5s[1], ffn_p, ffn_s)
    return tuple(np.ascontiguousarray(o, dtype=np.float32) for o in out)


def kernel(**inputs):
    inp = {k: np.asarray(v) for k, v in inputs.items()}
    R = run_device(inp)
    return assemble(R)
```

```python
import os
import numpy as np
from contextlib import ExitStack
import concourse.bass as bass
import concourse.mybir as mybir
from concourse.bass_utils import run_bass_kernel_spmd

F32 = mybir.dt.float32
F32R = mybir.dt.float32r
ALU = mybir.AluOpType
AF = mybir.ActivationFunctionType

NRING = 12
EPOCH = int(os.environ.get("KDEV_EPOCH", "1000000000"))
NEPOCH = 4
NCORES = 8
RUNCORES = int(os.environ.get("KDEV_CORES", "8"))
D = 2048
KC = 16
DFF = 5632
NFF = 88
TP = 256
SEQ = int(os.environ.get("KDEV_SEQ", "2048"))
NPT = SEQ // TP
NSQ = 16
LS = 4
TS = NSQ * LS
EPS = 1e-6
STAGE = int(os.environ.get("KDEV_STAGE", "9"))
SUB = int(os.environ.get("KDEV_SUB", "99"))
CUT = int(os.environ.get("KDEV_CUT", "99"))
CDCUT = int(os.environ.get("KDEV_CDCUT", "99"))
SC = int(os.environ.get("KDEV_SC", "99"))
WS = 256
NWB = 3


PROJ_TAB = {
    "w_in_ab": [(0, 512), (512, 512), (1024, 1024), (2064, 1024), (2048, 16), (3088, 3072), (6176, 1024), (6160, 8), (6168, 8)],
    "w_in_cd": [(0, 1024), (1024, 1536), (2560, 16), (2576, 1024)],
    "w_out_ab": [(0, 2048)],
    "w_out_cd": [(0, 2048)],
    "glu_w": [(0, 1024)],
}


def slab_base(name, col0):
    base = 0
    for (c0, n) in PROJ_TAB[name]:
        if c0 == col0:
            return base
        base += (n + WS - 1) // WS
    raise KeyError((name, col0))


def n_slabs(name):
    return sum((n + WS - 1) // WS for (_, n) in PROJ_TAB[name])


def tile_weight(W, name):
    K = W.shape[0]
    nk = K // 128
    out = []
    for (c0, n) in PROJ_TAB[name]:
        for s0 in range(0, n, WS):
            m = min(WS, n - s0)
            blk = np.zeros((128, nk, WS), np.float32)
            blk[:, :, :m] = W[:, c0 + s0:c0 + s0 + m].reshape(nk, 128, m).transpose(1, 0, 2)
            out.append(blk)
    return np.stack(out)


def tile_cols_all(W):
    K, N = W.shape
    nk = K // 128
    return np.ascontiguousarray(W.reshape(nk, 128, N // WS, WS).transpose(2, 1, 0, 3))


class Buf:
    __slots__ = ("t", "name", "w", "r")

    def __init__(self, t, name):
        self.t = t
        self.name = name
        self.w = None
        self.r = []

    def __getitem__(self, k):
        return self.t[k]


class Prog:
    ENG = ("pe", "act", "dve", "pool", "sp")

    def __init__(self, nc, stack):
        self.nc = nc
        self.stack = stack
        self.ops = {e: [] for e in self.ENG}
        self.cnt = {e: 0 for e in self.ENG if e != "sp"}
        self.sem = {e: [stack.enter_context(nc.semaphore("s_%s%d" % (e, k))) for k in range(NEPOCH)] for e in self.cnt}
        self.dq = {}
        for q in ("sp", "pool"):
            self.dq[q] = dict(
                sems=[stack.enter_context(nc.semaphore("d_%s%d" % (q, i))) for i in range(NRING)], n=0)
        self.nbuf = 0
        self.psl = []
        self.psi = 0
        self.waited = {e: {} for e in self.ENG}

    def sb(self, shape, dtype=F32, name=None):
        self.nbuf += 1
        name = name or "sb%d" % self.nbuf
        t = self.stack.enter_context(self.nc.sbuf_tensor(name, list(shape), dtype))
        return Buf(t, name)

    def ps(self, shape, dtype=F32, name=None):
        self.nbuf += 1
        name = name or "ps%d" % self.nbuf
        t = self.stack.enter_context(self.nc.psum_tensor(name, list(shape), dtype))
        return Buf(t, name)

    def view(self, ap, name):
        self.nbuf += 1
        return Buf(ap, name)

    def barrier(self, bufs):
        toks = [(e, v) for e, v in self.cnt.items() if v > 0]
        for q, st in self.dq.items():
            n = st["n"]
            for ring in range(min(n, NRING)):
                uses = (n - ring + NRING - 1) // NRING
                toks.append(("dma", q, ring, 16 * uses))
        for b in bufs:
            b.w = None
            b.r = list(toks)

    def next_ps(self):
        b = self.psl[self.psi % len(self.psl)]
        self.psi += 1
        return b

    def _dedupe(self, eng, waits):
        seen = self.waited[eng]
        out = []
        for s, v in waits:
            k = id(s)
            if seen.get(k, 0) >= v:
                continue
            seen[k] = v
            out.append((s, v))
        return out

    def _deps(self, eng, reads, writes):
        deps = {}
        ddeps = {}

        def add(tok):
            if tok is None:
                return
            if tok[0] == "dma":
                k = (tok[1], tok[2])
                ddeps[k] = max(ddeps.get(k, 0), tok[3])
            else:
                e, idx = tok
                if e == "pe" and eng == "pe":
                    return
                deps[e] = max(deps.get(e, 0), idx)

        for b in reads:
            add(b.w)
        for b in writes:
            add(b.w)
            for t in b.r:
                add(t)
        return deps, ddeps

    def _record(self, tok, reads, writes):
        for b in reads:
            b.r.append(tok)
            if len(b.r) > 48:
                last = {}
                for t in b.r:
                    k = t[:3] if t[0] == "dma" else t[0]
                    if k not in last or t[-1] > last[k][-1]:
                        last[k] = t
                b.r = list(last.values())
        for b in writes:
            b.w = tok
            b.r = []

    def _ew(self, e, v):
        k = (v - 1) // EPOCH
        return (self.sem[e][k], v - k * EPOCH)

    def op(self, eng, fn, reads=(), writes=()):
        deps, ddeps = self._deps(eng, reads, writes)
        self.cnt[eng] += 1
        idx = self.cnt[eng]
        sem = self.sem[eng][(idx - 1) // EPOCH]
        waits = [self._ew(e, v) for e, v in deps.items()]
        waits += [(self.dq[q]["sems"][r], v) for (q, r), v in ddeps.items()]
        waits = self._dedupe(eng, waits)

        def emit(E, fn=fn, waits=waits, sem=sem):
            for s, v in waits:
                E.wait_ge(s, v)
            fn(E).then_inc(sem, 1)

        self.ops[eng].append(emit)
        self._record((eng, idx), reads, writes)

    def dma(self, q, out, in_, reads=(), writes=()):
        deps, ddeps = self._deps(q, reads, writes)
        st = self.dq[q]
        n = st["n"]
        st["n"] += 1
        ring = n % NRING
        val = 16 * (n // NRING + 1)
        sem = st["sems"][ring]
        waits = [self._ew(e, v) for e, v in deps.items()]
        waits += [(self.dq[qq]["sems"][r], v) for (qq, r), v in ddeps.items()]
        if val > 16:
            waits.append((sem, val - 16))
        waits = self._dedupe(q, waits)

        def emit(E, waits=waits, sem=sem, out=out, in_=in_):
            for s, v in waits:
                E.wait_ge(s, v)
            E.dma_start(out=out, in_=in_).then_inc(sem, 16)

        self.ops[q].append(emit)
        self._record(("dma", q, ring, val), reads, writes)

    def mm(self, ob, o, lb, l, rb, r, start=True, stop=True):
        self.op("pe", lambda E: E.matmul(o, l, r, start=start, stop=stop), reads=[lb, rb], writes=[ob])

    def tr(self, ob, o, ib, i, idb, idap):
        self.op("pe", lambda E: E.transpose(o, i, idap), reads=[ib, idb], writes=[ob])

    def act(self, ob, o, ib, i, func, bias=0.0, scale=1.0, extra=()):
        self.op("act", lambda E: E.activation(o, i, func, bias=bias, scale=scale), reads=[ib] + list(extra), writes=[ob])

    def tt(self, ob, o, ab, a, bb, b, op, eng="dve"):
        self.op(eng, lambda E: E.tensor_tensor(o, a, b, op), reads=[ab, bb], writes=[ob])

    def ts(self, ob, o, ab, a, s1, s2, op0, op1, extra=(), eng="dve"):
        self.op(eng, lambda E: E.tensor_scalar(o, a, s1, s2, op0, op1), reads=[ab] + list(extra), writes=[ob])

    def ts1(self, ob, o, ab, a, s1, op0, extra=(), eng="dve"):
        self.op(eng, lambda E: E.tensor_single_scalar(o, a, s1, op0), reads=[ab] + list(extra), writes=[ob])

    def stt(self, ob, o, ab, a, sc, bb, b, op0, op1, extra=(), eng="dve"):
        self.op(eng, lambda E: E.scalar_tensor_tensor(o, a, sc, b, op0, op1), reads=[ab, bb] + list(extra), writes=[ob])

    def cp(self, ob, o, ib, i, eng="dve"):
        if eng == "act":
            self.op("act", lambda E: E.activation(o, i, AF.Copy), reads=[ib], writes=[ob])
        else:
            self.op(eng, lambda E: E.tensor_copy(o, i), reads=[ib], writes=[ob])

    def memset(self, ob, o, v, eng="dve"):
        self.op(eng, lambda E: E.memset(o, v), writes=[ob])

    def scan(self, ob, o, d0b, d0, d1b, d1, init=0.0):
        self.op("dve", lambda E: E.tensor_tensor_scan(o, d0, d1, init, ALU.mult, ALU.add), reads=[d0b, d1b], writes=[ob])

    def finish(self):
        nc = self.nc
        fin = []
        for q, st in self.dq.items():
            n = st["n"]
            for ring in range(min(n, NRING)):
                uses = (n - ring + NRING - 1) // NRING
                fin.append((st["sems"][ring], 16 * uses))
        fin += [self._ew(e, v) for e, v in self.cnt.items() if v > 0]

        def emit_fin(E, fin=fin):
            for s, v in fin:
                E.wait_ge(s, v)

        if os.environ.get("KDEV_COUNTS"):
            print("COUNTS", dict(self.cnt), {q: st["n"] for q, st in self.dq.items()}, flush=True)
        self.ops["sp"].append(emit_fin)
        ops = self.ops
        with nc.Block() as block:
            @block.sync
            def _(E):
                for f in ops["sp"]:
                    f(E)

            @block.tensor
            def _(E):
                for f in ops["pe"]:
                    f(E)

            @block.scalar
            def _(E):
                for f in ops["act"]:
                    f(E)

            @block.vector
            def _(E):
                for f in ops["dve"]:
                    f(E)

            @block.gpsimd
            def _(E):
                for f in ops["pool"]:
                    f(E)


class Tile:
    def __init__(self, kind, idx):
        self.kind = kind
        self.idx = idx
        if kind == "p":
            self.T, self.nseq, self.L, self.s0, self.C = TP, 1, TP, 0, 64
        else:
            self.T, self.nseq, self.L, self.s0, self.C = TS, NSQ, LS, 1, LS
        self.first = (kind == "p" and idx == 0)
        self.last = (kind == "s") or (idx == NPT - 1)
        self.nch = self.T // self.C


def bc(ap, shape):
    return ap.broadcast_to(list(shape))


def build_program():
    nc = bass.Bass("TRN2", target_bir_lowering=False)

    def din(name, shape):
        return nc.dram_tensor(name, list(shape), F32, kind="ExternalInput").ap()

    def dout(name, shape):
        return nc.dram_tensor(name, list(shape), F32, kind="ExternalOutput").ap()

    NS1 = 1 + NSQ
    I = {}
    for name, shape in [
        ("xp", [D, SEQ]), ("xs", [D, TS]), ("cT", [D, NS1]),
        ("w_ada", [2, 6 * D // WS, 128, KC, WS]), ("b_adaT", [2, 128, 96]), ("g_mixT", [2, 128, KC]), ("g_ffnT", [2, 128, KC]),
        ("g_finT", [128, KC]), ("w_ffn_up", [2, 2 * DFF // WS, 128, KC, WS]), ("w_ffn_down", [2, DFF // 256, 128, 2, D]),
        ("ffn_cw", [2, 128, NFF, 3]), ("ffn_cb", [2, 128, NFF]), ("ffn_st", [2, 128, NFF, NSQ, 2]),
        ("ident", [128, 128]), ("UT", [128, 128]), ("nSU", [128, 128]), ("SL", [128, 128]),
        ("rmask_p", [128, TP]), ("rmask_s", [128, TS]), ("esel", [8, 8, 128]),
        ("w_in_ab", [n_slabs("w_in_ab"), 128, KC, WS]), ("w_out_ab", [n_slabs("w_out_ab"), 128, KC, WS]),
        ("gla_w2", [16, 512]), ("gla_b2T", [128, 4]), ("gla_ngT", [128, 2]), ("gla_st", [NSQ, 128, 4, 256]),
        ("gdn_cw", [128, 24, 4]), ("gdn_cb", [128, 24]), ("gdn_cst", [128, 24, NSQ, 3]),
        ("gdn_Alog", [8, 1]), ("gdn_dtb", [8, 1]), ("gdn_ngT", [128, 1]), ("gdn_st", [NSQ, 128, 8, 128]),
        ("w_in_cd", [n_slabs("w_in_cd"), 128, KC, WS]), ("w_out_cd", [n_slabs("w_out_cd"), 128, KC, WS]), ("ssd_cw", [128, 12, 4]), ("ssd_cb", [128, 12]),
        ("ssd_cst", [128, 12, NSQ, 3]), ("ssd_Alog", [16, 1]), ("ssd_dtb", [16, 1]), ("ssd_Dcol", [128, 8]),
        ("ssd_ngT", [128, 8]), ("ssd_st", [NSQ, 128, 8, 128]),
        ("s5_are", [128, 32]), ("s5_aim", [128, 32]), ("s5_ldt", [128, 32]), ("s5w", [8, 128, 4, 4, 128]),
        ("s5_Dcol", [128, 8]), ("s5_x0", [128, 2, 32, NSQ]), ("glu_w", [n_slabs("glu_w"), 128, 8, WS]), ("glu_bT", [128, 8]),
    ]:
        I[name] = din(name, shape)
    O = {}
    for name, shape in [
        ("yp", [D, SEQ]), ("ys", [D, TS]), ("ffn_st_o", [2, 128, NFF, NS1, 2]),
        ("gla_o", [NS1, 128, 4, 256]), ("gdn_o", [NS1, 128, 8, 128]), ("gdn_cst_o", [128, 24, NS1, 3]),
        ("ssd_o", [NS1, 128, 8, 128]), ("ssd_cst_o", [128, 12, NS1, 3]), ("s5_o", [128, 32, NS1, 2]),
    ]:
        O[name] = dout(name, shape)

    with ExitStack() as stack:
        P = Prog(nc, stack)
        P.psl = [P.ps([128, 512], F32, name="psb%d" % i) for i in range(8)]
        xT = P.sb([128, KC, TP], F32, "xT")
        hT = P.sb([128, KC, TP], F32R, "hT")
        MWt = stack.enter_context(nc.sbuf_tensor("MW", [128, 72, TP], F32))
        yacc = P.view(MWt[:, 0:16, :], "yacc")
        RWt = stack.enter_context(nc.sbuf_tensor("RW", [128, 8, TP], F32R))
        WB = [P.sb([128, KC * WS], F32R, "wb%d" % i) for i in range(NWB)]
        wbi = [0]

        def next_wb():
            b = WB[wbi[0] % NWB]
            wbi[0] += 1
            return b

        def const(name, shape, q="sp"):
            b = P.sb(shape, F32, "c_" + name)
            P.dma(q, b[:], I[name], writes=[b])
            return b

        def U(a, b, name):
            return P.view(MWt[:, a:b, :], name)

        ones = P.sb([128, 128], F32, "ones")
        P.memset(ones, ones[:], 1.0)
        ident = const("ident", [128, 128])
        UT = const("UT", [128, 128])
        nSU = const("nSU", [128, 128])
        SL = const("SL", [128, 128])
        rmask = {"p": const("rmask_p", [128, TP]), "s": const("rmask_s", [128, TS])}
        gfin = const("g_finT", [128, KC])
        modT = [P.view(MWt[:, 4 + 7 * l:11 + 7 * l, :].rearrange("p a t -> p (a t)")[:, 0:96 * NS1].rearrange(
            "p (c n) -> p c n", n=NS1), "modT%d" % l) for l in range(2)]
        modP = [P.sb([128, 96], F32, "modP%d" % l) for l in range(2)]
        modS = P.sb([128, 16, NSQ], F32, "modS")
        modD = nc.dram_tensor("modD", [2, 128, 96, NS1], F32, kind="Internal").ap()
        modDB = Buf(None, "modD")
        ffst = [P.sb([128, NFF, 2], F32, "ffst%d" % l) for l in range(2)]
        ffcw = [P.sb([128, NFF, 3], F32, "ffcw%d" % l) for l in range(2)]
        ffcb = [P.sb([128, NFF], F32, "ffcb%d" % l) for l in range(2)]
        for l in range(2):
            P.memset(ffst[l], ffst[l][:], 0.0)
            P.dma("sp", ffcw[l][:], I["ffn_cw"][l], writes=[ffcw[l]])
            P.dma("sp", ffcb[l][:], I["ffn_cb"][l], writes=[ffcb[l]])
        rstd = P.sb([128, TP], F32, "rstd")
        cvb = U(22, 26, "cvb")
        sgb = U(26, 28, "sgb")
        actT = [P.view(RWt[:, 2 * i:2 * i + 2, :], "actT%d" % i) for i in range(2)]
        cvtmp = P.sb([128, TP], F32, "cvtmp")
        cstage = P.sb([128, TP + 3 * NSQ], F32, "cstage")

        cond0 = P.view(MWt[:, 0:2, :].rearrange("p a t -> p (a t)").rearrange("p (c n) -> p c n", n=32), "cond0")
        condT = P.view(RWt[:, 0:2, :].rearrange("p a t -> p (a t)").rearrange("p (c n) -> p c n", n=32), "condT")
        P.memset(cond0, cond0[:], 0.0)
        P.dma("sp", cond0[:, :, 0:NS1], I["cT"].rearrange("(c p) n -> p c n", p=128), writes=[cond0])
        P.act(condT, condT[:], cond0, cond0[:], AF.Silu)
        for l in range(2):
            badT = P.sb([128, 96], F32, "badT%d" % l)
            gm = P.sb([128, KC], F32, "gmix%d" % l)
            gf = P.sb([128, KC], F32, "gffn%d" % l)
            P.dma("sp", badT[:], I["b_adaT"][l], writes=[badT])
            P.dma("sp", gm[:], I["g_mixT"][l], writes=[gm])
            P.dma("sp", gf[:], I["g_ffnT"][l], writes=[gf])
            nj = WS // 128
            for s in range(6 * D // WS):
                wb = next_wb()
                wbv = wb[:].rearrange("p (c n) -> p c n", n=WS)
                P.dma("pool", wbv, I["w_ada"][l][s], writes=[wb])
                ps = P.next_ps()
                for j in range(nj):
                    for kc in range(KC):
                        P.mm(ps, ps[:, j * 32:j * 32 + 32], wb, wbv[:, kc, j * 128:(j + 1) * 128], condT, condT[:, kc, :],
                             start=(kc == 0), stop=(kc == KC - 1))
                psv = ps[:, 0:32 * nj].rearrange("p (j n) -> p j n", n=32)[:, :, 0:NS1]
                P.tt(modT[l], modT[l][:, s * nj:(s + 1) * nj, :], ps, psv,
                     badT, bc(badT[:, s * nj:(s + 1) * nj].unsqueeze(2), [128, nj, NS1]), ALU.add)
            for (off, g) in ((16, gm), (64, gf)):
                P.stt(modT[l], modT[l][:, off:off + 16, :], modT[l], modT[l][:, off:off + 16, :], 1.0,
                      g, bc(g[:].unsqueeze(2), [128, 16, NS1]), ALU.add, ALU.mult)
            P.cp(modP[l], modP[l][:].unsqueeze(2), modT[l], modT[l][:, :, 0:1])
            P.dma("sp", modD[l], modT[l][:, :, :], reads=[modT[l]], writes=[modDB])

        def v4(ap, tl):
            return ap.rearrange("p c (s l) -> p c s l", l=tl.L)

        def modb(l, off):
            P.dma("sp", modS[:], modD[l][:, off:off + 16, 1:NS1], reads=[modDB], writes=[modS])
            return bc(modS[:].unsqueeze(3), [128, 16, NSQ, LS])

        def rms_stats(tl):
            T = tl.T
            P.act(yacc, yacc[:, :, :T], xT, xT[:, :, :T], AF.Square)
            ps = P.next_ps()
            for c in range(KC):
                P.mm(ps, ps[:, :T], ones, ones[:], yacc, yacc[:, c, :T], start=(c == 0), stop=(c == KC - 1))
            P.act(rstd, rstd[:, :T], ps, ps[:, :T], AF.Ln, bias=EPS, scale=1.0 / D)
            P.act(rstd, rstd[:, :T], rstd, rstd[:, :T], AF.Exp, scale=-0.5)
            P.tt(yacc, yacc[:, :, :T], xT, xT[:, :, :T], rstd, bc(rstd[:, :T].unsqueeze(1), [128, KC, T]), ALU.mult)

        def norm_mod(tl, l, sh, sc):
            T = tl.T
            P.barrier([yacc])
            rms_stats(tl)
            if tl.kind == "p":
                for c in range(KC):
                    P.act(hT, hT[:, c, :T], yacc, yacc[:, c, :T], AF.Identity, bias=modP[l][:, sh + c:sh + c + 1],
                          scale=modP[l][:, sc + c:sc + c + 1], extra=[modP[l]])
            else:
                P.tt(yacc, v4(yacc[:, :, :T], tl), yacc, v4(yacc[:, :, :T], tl), modS, modb(l, sc), ALU.mult)
                P.tt(hT, v4(hT[:, :, :T], tl), yacc, v4(yacc[:, :, :T], tl), modS, modb(l, sh), ALU.add)

        def resid_add(tl, l, goff, src):
            T = tl.T
            if tl.kind == "p":
                for c in range(KC):
                    P.stt(xT, xT[:, c, :T], src, src[:, c, :T], modP[l][:, goff + c:goff + c + 1], xT, xT[:, c, :T],
                          ALU.mult, ALU.add, extra=[modP[l]])
            else:
                P.tt(src, v4(src[:, :, :T], tl), src, v4(src[:, :, :T], tl), modS, modb(l, goff), ALU.mult)
                P.tt(xT, xT[:, :, :T], xT, xT[:, :, :T], src, src[:, :, :T], ALU.add)

        def proj_fm(tl, w_ap, col0, ncols, consume, rhs=None, nk=KC):
            T = tl.T
            rhs = rhs or hT
            wname = w_ap
            sb0 = slab_base(wname, col0)
            for s0 in range(0, ncols, WS):
                n = min(WS, ncols - s0)
                wb = next_wb()
                wbv = wb[:].rearrange("p (c n) -> p c n", n=WS)
                P.dma("pool", wbv[:, 0:nk, :], I[wname][sb0 + s0 // WS], writes=[wb])
                for j in range((n + 127) // 128):
                    m = min(128, n - j * 128)
                    ps = P.next_ps()
                    for kc in range(nk):
                        P.mm(ps, ps[0:m, :T], wb, wbv[:, kc, j * 128:j * 128 + m], rhs, rhs[:, kc, :T],
                             start=(kc == 0), stop=(kc == nk - 1))
                    consume(s0 // 128 + j, m, ps)

        def conv_chunk(tl, ps, cw, cb, ch, K, pst, sin, sout, out_b, out_ap, func):
            T, L, ns = tl.T, tl.L, tl.nseq
            H = K - 1
            sv = cstage[:, 0:ns * (L + H)].rearrange("p (s l) -> p s l", l=L + H)
            if tl.kind == "p":
                P.cp(cstage, sv[:, :, 0:H], pst, pst[:, ch:ch + 1, :])
            else:
                P.dma("sp", sv[:, :, 0:H], sin[:, ch, :, :], writes=[cstage])
            P.act(cstage, sv[:, :, H:H + L], ps, ps[:, :T].rearrange("p (s l) -> p s l", l=L), AF.Copy)
            if tl.kind == "p":
                P.cp(pst, pst[:, ch:ch + 1, :], cstage, sv[:, :, L:L + H])
                if tl.last:
                    P.dma("sp", sout[:, ch, 0:1, :], sv[:, :, L:L + H], reads=[cstage])
            else:
                P.dma("sp", sout[:, ch, 1:NS1, :], sv[:, :, L:L + H], reads=[cstage])
            acc = cvtmp[:, 0:T].rearrange("p (s l) -> p s l", l=L)
            P.ts(cvtmp, acc, cstage, sv[:, :, 0:L], cw[:, ch, 0:1], cb[:, ch:ch + 1], ALU.mult, ALU.add, extra=[cw, cb])
            for k in range(1, K):
                P.stt(cvtmp, acc, cstage, sv[:, :, k:k + L], cw[:, ch, k:k + 1], cvtmp, acc, ALU.mult, ALU.add, extra=[cw])
            ov = out_ap.rearrange("p (s l) -> p s l", l=L)
            if func is None:
                P.cp(out_b, ov, cvtmp, acc)
            else:
                P.act(out_b, ov, cvtmp, acc, func)

        def ffn(tl, l):
            T = tl.T
            P.barrier([cvb, sgb] + actT)
            wup = I["w_ffn_up"][l]
            wdn = I["w_ffn_down"][l]
            for g in range(DFF // 256):
                wba, wbg = next_wb(), next_wb()
                wva = wba[:].rearrange("p (c n) -> p c n", n=WS)
                wvg = wbg[:].rearrange("p (c n) -> p c n", n=WS)
                P.dma("pool", wva, wup[g], writes=[wba])
                P.dma("pool", wvg, wup[DFF // WS + g], writes=[wbg])
                for j in range(4):
                    ch = (g * 2 + j) if j < 2 else (44 + g * 2 + j - 2)
                    wb, wv = (wba, wva) if j < 2 else (wbg, wvg)
                    jj = j % 2
                    ps = P.next_ps()
                    for kc in range(KC):
                        P.mm(ps, ps[:, :T], wb, wv[:, kc, jj * 128:(jj + 1) * 128], hT, hT[:, kc, :T],
                             start=(kc == 0), stop=(kc == KC - 1))
                    conv_chunk(tl, ps, ffcw[l], ffcb[l], ch, 3, ffst[l], I["ffn_st"][l], O["ffn_st_o"][l],
                               cvb, cvb[:, j, :T], None)
                P.act(sgb, sgb[:, :, :T], cvb, cvb[:, 2:4, :T], AF.Silu)
                at = actT[g % 2]
                P.tt(at, at[:, :, :T], sgb, sgb[:, :, :T], cvb, cvb[:, 0:2, :T], ALU.mult)
                wd = next_wb()
                wdv = wd[:, 0:2 * D].rearrange("p (c n) -> p c n", n=D)
                P.dma("pool", wdv, wdn[g], writes=[wd])
                for oc in range(KC):
                    ps = P.next_ps()
                    for kc in range(2):
                        P.mm(ps, ps[:, :T], wd, wdv[:, kc, oc * 128:(oc + 1) * 128], at, at[:, kc, :T],
                             start=(kc == 0), stop=(kc == 1))
                    if g == 0:
                        P.cp(yacc, yacc[:, oc, :T], ps, ps[:, :T], eng="act")
                    else:
                        P.tt(yacc, yacc[:, oc, :T], yacc, yacc[:, oc, :T], ps, ps[:, :T], ALU.add)

        gla_S = [P.sb([128, 4, 256], F32, "glaS0")] * 2
        gdn_S = [P.sb([128, 8, 128], F32, "gdnS0")] * 2
        nb2 = const("gla_b2T", [128, 4])
        P.ts1(nb2, nb2[:], nb2, nb2[:], -1.0, ALU.mult)
        glang = const("gla_ngT", [128, 2])
        gdcw = const("gdn_cw", [128, 24, 4])
        gdcb = const("gdn_cb", [128, 24])
        gdnegA = const("gdn_Alog", [8, 1])
        P.act(gdnegA, gdnegA[:], gdnegA, gdnegA[:], AF.Exp)
        P.ts1(gdnegA, gdnegA[:], gdnegA, gdnegA[:], -1.0, ALU.mult)
        gddtb = const("gdn_dtb", [8, 1])
        gdng = const("gdn_ngT", [128, 1])
        gdcst = P.sb([128, 24, 3], F32, "gdcst")
        P.memset(gdcst, gdcst[:], 0.0)
        lrT = P.view(MWt[0:16, 54, :], "lrT")
        w2sb = P.view(MWt[0:16, 55:57, :].rearrange("p a t -> p (a t)"), "w2sb")
        abT = [P.sb([8, TP], F32, "abT%d" % i) for i in range(5)]
        abT.append(abT[1])
        gtok = P.sb([64, 8], F32, "gtok")
        eglast = P.sb([128, 8], F32, "eglast")
        m_q, m_k, m_v, m_r = U(0, 4, "m_q"), U(4, 8, "m_k"), U(8, 16, "m_v"), U(16, 24, "m_r")
        m_cum, m_o, m_ln = U(24, 28, "m_cum"), U(28, 36, "m_o"), U(36, 40, "m_ln")
        tmpA = [U(40 + i, 41 + i, "tmpA%d" % i) for i in range(7)]
        a_vtk, a_ktk = U(48, 52, "a_vtk"), U(52, 54, "a_ktk")
        GLA_BUFS = [m_q, m_k, m_v, m_r, m_cum, m_o, m_ln, a_vtk, a_ktk] + tmpA
        g_qkv, g_gate, g_ob, g_sq = U(0, 24, "g_qkv"), U(36, 44, "g_gate"), U(44, 52, "g_ob"), U(52, 60, "g_sq")
        g_kb, g_dec, g_A, g_Q = U(52, 54, "g_kb"), U(54, 56, "g_dec"), U(56, 58, "g_A"), U(58, 60, "g_Q")
        g_vb, g_qi, g_kw, g_ko = U(60, 62, "g_vb"), U(62, 64, "g_qi"), U(64, 66, "g_kw"), U(66, 68, "g_ko")
        g_X, g_wk = U(68, 70, "g_X"), U(70, 72, "g_wk")
        g_tokA, g_tokB = U(52, 56, "g_tokA"), U(56, 60, "g_tokB")
        g_esel = U(24, 28, "g_esel")

        def t4(buf, C, nh):
            return buf[:].rearrange("p a t -> p (a t)")[:, 0:nh * C].rearrange("p (h c) -> p h c", c=C)

        def tok(buf, C, n):
            return buf[:].rearrange("p a t -> p (a t)")[0:C, 0:n]

        def gla(tl):
            T, C, nch = tl.T, tl.C, tl.nch
            W = "w_in_ab"
            P.barrier(GLA_BUFS + [lrT, w2sb])
            P.dma("sp", w2sb[:, :], I["gla_w2"], writes=[w2sb])
            proj_fm(tl, W, 0, 512, lambda j, m, ps: P.cp(m_q, m_q[:, j, :T], ps, ps[:, :T], eng="act"))
            proj_fm(tl, W, 512, 512, lambda j, m, ps: P.cp(m_k, m_k[:, j, :T], ps, ps[:, :T], eng="act"))
            proj_fm(tl, W, 1024, 1024, lambda j, m, ps: P.cp(m_v, m_v[:, j, :T], ps, ps[:, :T], eng="act"))
            proj_fm(tl, W, 2064, 1024, lambda j, m, ps: P.act(m_r, m_r[:, j, :T], ps, ps[:, :T], AF.Silu))
            proj_fm(tl, W, 2048, 16, lambda j, m, ps: P.cp(lrT, lrT[0:16, :T], ps, ps[0:16, :T]))
            for h in range(4):
                ps = P.next_ps()
                P.mm(ps, ps[:, :T], w2sb, w2sb[0:16, h * 128:(h + 1) * 128], lrT, lrT[0:16, :T])
                P.act(m_ln, m_ln[:, h, :T], ps, ps[:, :T], AF.Exp, bias=nb2[:, h:h + 1], scale=-1.0, extra=[nb2])
            P.act(m_ln, m_ln[:, :, :T], m_ln, m_ln[:, :, :T], AF.Ln, bias=1.0)
            if SUB < 2:
                return
            rm = rmask[tl.kind]
            for h in range(4):
                P.scan(m_cum, m_cum[:, h, :T], rm, rm[:, 0:T], m_ln, m_ln[:, h, :T])
            eb, QsT, enb, KsT, dl, KoT, attT = [t4(tmpA[i], C, 4) for i in range(7)]
            bt = tmpA
            vtk = tok(a_vtk, C, 1024)
            ktk = tok(a_ktk, C, 512)
            for n in range(nch if SUB >= 3 else 0):
                c0 = n * C
                S = gla_S[n % 2] if tl.kind == "s" else gla_S[0]
                if tl.kind == "s":
                    P.dma("sp", S[:], I["gla_st"][n], writes=[S])
                elif tl.first and n == 0:
                    P.memset(S, S[:], 0.0)
                cs = slice(c0, c0 + C)
                P.act(bt[0], eb, m_cum, m_cum[:, :, cs], AF.Exp, scale=-1.0 / 16)
                P.stt(bt[1], QsT, m_q, m_q[:, :, cs], 128 ** -0.5, bt[0], eb, ALU.mult, ALU.mult)
                P.act(bt[2], enb, m_cum, m_cum[:, :, cs], AF.Exp, scale=1.0 / 16)
                P.tt(bt[3], KsT, m_k, m_k[:, :, cs], bt[2], enb, ALU.mult)
                P.tt(bt[4], dl, m_cum, bc(m_cum[:, :, c0 + C - 1:c0 + C], [128, 4, C]), m_cum, m_cum[:, :, cs], ALU.subtract)
                P.act(bt[4], dl, bt[4], dl, AF.Exp, scale=-1.0 / 16)
                P.tt(bt[5], KoT, m_k, m_k[:, :, cs], bt[4], dl, ALU.mult)
                psa = P.next_ps()
                for h in range(4):
                    P.mm(psa, psa[0:C, h * C:(h + 1) * C], bt[3], KsT[:, h, :], bt[1], QsT[:, h, :])
                P.tt(bt[6], attT[0:C], psa, psa[0:C, 0:4 * C].rearrange("p (h c) -> p h c", c=C),
                     UT, bc(UT[0:C, 0:C].unsqueeze(1), [C, 4, C]), ALU.mult)
                for half in range(2):
                    psv = P.next_ps()
                    for q4 in range(4):
                        P.tr(psv, psv[0:C, q4 * 128:(q4 + 1) * 128], m_v, m_v[:, half * 4 + q4, cs], ident, ident[:])
                    P.cp(a_vtk, vtk[:, half * 512:(half + 1) * 512], psv, psv[0:C, :], eng="act")
                psk = P.next_ps()
                for h in range(4):
                    P.tr(psk, psk[0:C, h * 128:(h + 1) * 128], bt[5], KoT[:, h, :], ident, ident[:])
                P.cp(a_ktk, ktk, psk, psk[0:C, :], eng="act")
                pso = P.next_ps()
                for h in range(4):
                    for vc in range(2):
                        a = h * 2 + vc
                        o_ap = pso[:, a * C:(a + 1) * C]
                        P.mm(pso, o_ap, a_vtk, vtk[:, a * 128:(a + 1) * 128], bt[6], attT[0:C, h, :], start=True, stop=False)
                        P.mm(pso, o_ap, S, S[:, h, vc * 128:(vc + 1) * 128], bt[1], QsT[:, h, :], start=False, stop=True)
                P.cp(m_o, m_o[:, :, cs], pso, pso[:, 0:8 * C].rearrange("p (a c) -> p a c", c=C), eng="act")
                for half in range(2):
                    pss = P.next_ps()
                    for hh in range(2):
                        h = half * 2 + hh
                        P.mm(pss, pss[:, hh * 256:(hh + 1) * 256], a_ktk, ktk[:, h * 128:(h + 1) * 128],
                             a_vtk, vtk[:, h * 256:(h + 1) * 256])
                    for hh in range(2):
                        h = half * 2 + hh
                        P.stt(S, S[:, h, :], S, S[:, h, :], eb[:, h, C - 1:C], pss, pss[:, hh * 256:(hh + 1) * 256],
                              ALU.mult, ALU.add, extra=[bt[0]])
                if tl.kind == "s":
                    P.dma("sp", O["gla_o"][1 + n], S[:], reads=[S])
                elif tl.last and n == nch - 1:
                    P.dma("sp", O["gla_o"][0], S[:], reads=[S])
            if SUB < 4:
                return
            P.act(m_v, m_v[:, :, :T], m_o, m_o[:, :, :T], AF.Square)
            for half in range(2):
                ps = P.next_ps()
                for hh in range(2):
                    h = half * 2 + hh
                    for vc in range(2):
                        P.mm(ps, ps[:, hh * T:(hh + 1) * T], ones, ones[:], m_v, m_v[:, h * 2 + vc, :T],
                             start=(vc == 0), stop=(vc == 1))
                rsv = m_ln[:, half * 2:half * 2 + 2, :T]
                P.act(m_ln, rsv, ps, ps[:, 0:2 * T].rearrange("p (h t) -> p h t", t=T), AF.Ln, bias=EPS, scale=1.0 / 256)
                P.act(m_ln, rsv, m_ln, rsv, AF.Exp, scale=-0.5)
            o4 = m_o[:, :, :T].rearrange("p (h v) t -> p h v t", v=2)
            P.tt(m_o, o4, m_o, o4, m_ln, bc(m_ln[:, :, :T].unsqueeze(2), [128, 4, 2, T]), ALU.mult)
            P.tt(m_o, m_o[:, :, :T], m_o, m_o[:, :, :T], m_r, m_r[:, :, :T], ALU.mult)
            for vc in range(2):
                P.ts1(m_o, o4[:, :, vc, :], m_o, o4[:, :, vc, :], glang[:, vc:vc + 1], ALU.mult, extra=[glang])

        def gdn(tl):
            T, C, nch = tl.T, tl.C, tl.nch
            W = "w_in_ab"
            qkv, gg, ob, sq = g_qkv, g_gate, g_ob, g_sq
            P.barrier([qkv, gg, ob, sq, g_esel])
            esel = g_esel
            eselv = g_esel[:].rearrange("p a t -> p (a t)")[0:8, :].rearrange("p (h m) -> p h m", m=128)
            P.dma("sp", eselv, I["esel"], writes=[g_esel])

            def cons_qkv(j, m, ps):
                conv_chunk(tl, ps, gdcw, gdcb, j, 4, gdcst, I["gdn_cst"], O["gdn_cst_o"], qkv, qkv[:, j, :T], AF.Silu)

            proj_fm(tl, W, 3088, 3072, cons_qkv)
            proj_fm(tl, W, 6176, 1024, lambda j, m, ps: P.act(gg, gg[:, j, :T], ps, ps[:, :T], AF.Silu))
            proj_fm(tl, W, 6160, 8, lambda j, m, ps: P.cp(abT[0], abT[0][0:8, :T], ps, ps[0:8, :T]))
            proj_fm(tl, W, 6168, 8, lambda j, m, ps: P.cp(abT[1], abT[1][0:8, :T], ps, ps[0:8, :T]))
            aT, bT, gcT, egT, ekT, beT = abT
            if SUB < 6:
                return
            P.act(aT, aT[0:8, :T], aT, aT[0:8, :T], AF.Exp, bias=gddtb[0:8, 0:1], extra=[gddtb])
            P.act(aT, aT[0:8, :T], aT, aT[0:8, :T], AF.Ln, bias=1.0)
            P.ts1(aT, aT[0:8, :T], aT, aT[0:8, :T], gdnegA[0:8, 0:1], ALU.mult, extra=[gdnegA])
            P.act(beT, beT[0:8, :T], bT, bT[0:8, :T], AF.Sigmoid)
            rm = rmask[tl.kind]
            P.scan(gcT, gcT[0:8, :T], rm, rm[0:8, 0:T], aT, aT[0:8, :T])
            P.act(egT, egT[0:8, :T], gcT, gcT[0:8, :T], AF.Exp)
            g3 = gcT[0:8, :T].rearrange("p (n c) -> p n c", c=C)
            P.tt(ekT, ekT[0:8, :T].rearrange("p (n c) -> p n c", c=C), gcT, bc(g3[:, :, C - 1:C], [8, nch, C]), gcT, g3,
                 ALU.subtract)
            P.act(ekT, ekT[0:8, :T], ekT, ekT[0:8, :T], AF.Exp)
            for which in range(2):
                src = qkv[:, which * 8:(which + 1) * 8, :T]
                P.act(sq, sq[:, :, :T], qkv, src, AF.Square)
                for pr in range(4):
                    ps = P.next_ps()
                    for hh in range(2):
                        P.mm(ps, ps[:, hh * T:(hh + 1) * T], ones, ones[:], sq, sq[:, pr * 2 + hh, :T])
                    rv = sq[:, pr * 2:pr * 2 + 2, :T]
                    P.act(sq, rv, ps, ps[:, 0:2 * T].rearrange("p (h t) -> p h t", t=T), AF.Ln, bias=EPS)
                    P.act(sq, rv, sq, rv, AF.Exp, scale=-0.5)
                if which == 0:
                    P.stt(qkv, src, qkv, src, 128 ** -0.5, sq, sq[:, :, :T], ALU.mult, ALU.mult)
                else:
                    P.tt(qkv, src, qkv, src, sq, sq[:, :, :T], ALU.mult)
            if SUB < 7:
                return
            tbufs = [g_kb, g_vb, g_qi, g_kw, g_ko, g_dec, g_A, g_Q, g_X, g_wk]
            P.barrier(tbufs)
            kbT, vbT, qiT, kwT, koT, decT, AT, QT, XT, wk = [t4(b, C, 8) for b in tbufs]
            nsteps = {64: 5, 4: 1}[C]
            p3 = lambda ps_: ps_[0:C, 0:8 * C].rearrange("p (h c) -> p h c", c=C)
            pf = lambda ps_: ps_[:, 0:8 * C].rearrange("p (h c) -> p h c", c=C)
            for n in range(nch):
                c0 = n * C
                cs = slice(c0, c0 + C)
                S = gdn_S[n % 2] if tl.kind == "s" else gdn_S[0]
                if tl.kind == "s":
                    P.dma("sp", S[:], I["gdn_st"][n], writes=[S])
                elif tl.first and n == 0:
                    P.memset(S, S[:], 0.0)
                if SUB < 8:
                    break
                if n > 0:
                    P.barrier([g_kb, g_dec, g_A, g_Q])
                qc, kc, vc = qkv[:, 0:8, cs], qkv[:, 8:16, cs], qkv[:, 16:24, cs]

                def bcast_rows(srcb):
                    ps = P.next_ps()
                    for h in range(8):
                        P.mm(ps, ps[:, h * C:(h + 1) * C], esel, eselv[:, h, :], srcb, srcb[0:8, cs])
                    return ps, ps[:, 0:8 * C].rearrange("p (h c) -> p h c", c=C)

                if CUT < 0:
                    continue
                psb, pbv = bcast_rows(beT)
                if CUT < 1:
                    P.cp(g_kb, kbT, psb, pbv)
                    continue
                P.tt(g_kb, kbT, qkv, kc, psb, pbv, ALU.mult)
                P.tt(g_vb, vbT, qkv, vc, psb, pbv, ALU.mult)
                pse, pev = bcast_rows(egT)
                P.tt(g_qi, qiT, qkv, qc, pse, pev, ALU.mult)
                P.tt(g_kw, kwT, g_kb, kbT, pse, pev, ALU.mult)
                P.cp(eglast, eglast[:, :].unsqueeze(2), pse, pev[:, :, C - 1:C])
                psk, pkv = bcast_rows(ekT)
                P.tt(g_ko, koT, qkv, kc, psk, pkv, ALU.mult)
                if CUT < 2:
                    continue
                pst = P.next_ps()
                P.tr(pst, pst[0:C, 0:8], aT, aT[0:8, cs], ident, ident[0:8, 0:8])
                P.cp(gtok, gtok[0:C, :], pst, pst[0:C, 0:8], eng="act")
                P.tt(g_wk, wk[0:C], gtok, bc(gtok[0:C, :].unsqueeze(2), [C, 8, C]), SL, bc(SL[0:C, 0:C].unsqueeze(1), [C, 8, C]),
                     ALU.mult)
                psd = P.next_ps()
                for h in range(8):
                    P.mm(psd, psd[0:C, h * C:(h + 1) * C], g_wk, wk[0:C, h, :], UT, UT[0:C, 0:C])
                P.act(g_dec, decT[0:C], psd, p3(psd), AF.Exp)
                P.tt(g_dec, decT[0:C], g_dec, decT[0:C], UT, bc(UT[0:C, 0:C].unsqueeze(1), [C, 8, C]), ALU.mult)
                if CUT < 3:
                    continue
                psm = P.next_ps()
                for h in range(8):
                    P.mm(psm, psm[0:C, h * C:(h + 1) * C], qkv, kc[:, h, :], g_kb, kbT[:, h, :])
                P.tt(g_A, AT[0:C], psm, p3(psm), g_dec, decT[0:C], ALU.mult)
                P.tt(g_A, AT[0:C], g_A, AT[0:C], nSU, bc(nSU[0:C, 0:C].unsqueeze(1), [C, 8, C]), ALU.mult)
                psa = P.next_ps()
                for h in range(8):
                    P.mm(psa, psa[0:C, h * C:(h + 1) * C], qkv, kc[:, h, :], qkv, qc[:, h, :])
                P.tt(g_wk, wk[0:C], psa, p3(psa), g_dec, decT[0:C], ALU.mult)
                pq = P.next_ps()
                for h in range(8):
                    P.tr(pq, pq[0:C, h * C:(h + 1) * C], g_A, AT[0:C, h, :], ident, ident[0:C, 0:C])
                P.cp(g_Q, QT[0:C], pq, p3(pq), eng="act")
                P.tt(g_X, XT[0:C], g_A, AT[0:C], ident, bc(ident[0:C, 0:C].unsqueeze(1), [C, 8, C]), ALU.add)
                for step in range(nsteps if CUT >= 4 else 0):
                    lastst = (step == nsteps - 1)
                    pqt = P.next_ps()
                    for h in range(8):
                        P.mm(pqt, pqt[0:C, h * C:(h + 1) * C], g_A, AT[0:C, h, :], g_Q, QT[0:C, h, :])
                    if not lastst:
                        pq2 = P.next_ps()
                        for h in range(8):
                            P.mm(pq2, pq2[0:C, h * C:(h + 1) * C], g_Q, QT[0:C, h, :], g_A, AT[0:C, h, :])
                    P.cp(g_Q, QT[0:C], pqt, p3(pqt), eng="act")
                    if not lastst:
                        P.cp(g_A, AT[0:C], pq2, p3(pq2))
                    px = P.next_ps()
                    for h in range(8):
                        P.mm(px, px[0:C, h * C:(h + 1) * C], g_Q, QT[0:C, h, :], g_X, XT[0:C, h, :])
                    P.tt(g_X, XT[0:C], g_X, XT[0:C], px, p3(px), ALU.add)
                if CUT < 5:
                    continue
                P.barrier([g_tokA, g_tokB])
                tA = tok(g_tokA, C, 1024)
                tB = tok(g_tokB, C, 1024)

                def to_tok(srcb, srcv, dstb, dstv):
                    for half in range(2):
                        pt = P.next_ps()
                        for q4 in range(4):
                            P.tr(pt, pt[0:C, q4 * 128:(q4 + 1) * 128], srcb, srcv[:, half * 4 + q4, :], ident, ident[:])
                        P.cp(dstb, dstv[:, half * 512:(half + 1) * 512], pt, pt[0:C, :], eng="act")

                to_tok(g_vb, vbT, g_tokA, tA)
                to_tok(g_kw, kwT, g_tokB, tB)
                pu = P.next_ps()
                pw = P.next_ps()
                for h in range(8):
                    P.mm(pu, pu[:, h * C:(h + 1) * C], g_tokA, tA[:, h * 128:(h + 1) * 128], g_X, XT[0:C, h, :])
                    P.mm(pw, pw[:, h * C:(h + 1) * C], g_tokB, tB[:, h * 128:(h + 1) * 128], g_X, XT[0:C, h, :])
                P.cp(g_vb, vbT, pu, pf(pu), eng="act")
                P.cp(g_kw, kwT, pw, pf(pw))
                if CUT < 6:
                    continue
                pws = P.next_ps()
                for h in range(8):
                    P.mm(pws, pws[:, h * C:(h + 1) * C], S, S[:, h, :], g_kw, kwT[:, h, :])
                P.tt(g_vb, vbT, g_vb, vbT, pws, pf(pws), ALU.subtract)
                to_tok(g_vb, vbT, g_tokA, tA)
                to_tok(g_ko, koT, g_tokB, tB)
                po = P.next_ps()
                for h in range(8):
                    o_ap = po[:, h * C:(h + 1) * C]
                    P.mm(po, o_ap, S, S[:, h, :], g_qi, qiT[:, h, :], start=True, stop=False)
                    P.mm(po, o_ap, g_tokA, tA[:, h * 128:(h + 1) * 128], g_wk, wk[0:C, h, :], start=False, stop=True)
                P.cp(ob, ob[:, :, cs], po, pf(po), eng="act")
                if CUT < 7:
                    continue
                for half in range(2):
                    pss = P.next_ps()
                    for hh in range(4):
                        h = half * 4 + hh
                        P.mm(pss, pss[:, hh * 128:(hh + 1) * 128], g_tokB, tB[:, h * 128:(h + 1) * 128],
                             g_tokA, tA[:, h * 128:(h + 1) * 128])
                    for hh in range(4):
                        h = half * 4 + hh
                        P.stt(S, S[:, h, :], S, S[:, h, :], eglast[:, h:h + 1], pss, pss[:, hh * 128:(hh + 1) * 128],
                              ALU.mult, ALU.add, extra=[eglast])
                if tl.kind == "s":
                    P.dma("sp", O["gdn_o"][1 + n], S[:], reads=[S])
                elif tl.last and n == nch - 1:
                    P.dma("sp", O["gdn_o"][0], S[:], reads=[S])
            if SUB < 9:
                return
            P.barrier([sq])
            P.act(sq, sq[:, :, :T], ob, ob[:, :, :T], AF.Square)
            for pr in range(4):
                ps = P.next_ps()
                for hh in range(2):
                    P.mm(ps, ps[:, hh * T:(hh + 1) * T], ones, ones[:], sq, sq[:, pr * 2 + hh, :T])
                rv = sq[:, pr * 2:pr * 2 + 2, :T]
                P.act(sq, rv, ps, ps[:, 0:2 * T].rearrange("p (h t) -> p h t", t=T), AF.Ln, bias=EPS, scale=1.0 / 128)
                P.act(sq, rv, sq, rv, AF.Exp, scale=-0.5)
            P.tt(ob, ob[:, :, :T], ob, ob[:, :, :T], sq, sq[:, :, :T], ALU.mult)
            P.tt(ob, ob[:, :, :T], ob, ob[:, :, :T], gg, gg[:, :, :T], ALU.mult)
            P.cp(hT, hT[:, 0:8, :T], m_o, m_o[:, :, :T], eng="act")
            P.ts1(hT, hT[:, 8:16, :T], ob, ob[:, :, :T], gdng[:, 0:1], ALU.mult, extra=[gdng])

        def mixer_ab(tl):
            T = tl.T
            gla(tl)
            if SUB >= 5:
                gdn(tl)
            else:
                P.cp(hT, hT[:, 0:8, :T], m_o, m_o[:, :, :T], eng="act")
                P.act(hT, hT[:, 8:16, :T], m_o, m_o[:, :, :T], AF.Copy, scale=0.0)
            P.barrier([yacc])
            proj_fm(tl, "w_out_ab", 0, D, lambda j, m, ps: P.cp(yacc, yacc[:, j, :T], ps, ps[:, :T], eng="act"))
            resid_add(tl, 0, 32, yacc)

        sscw = const("ssd_cw", [128, 12, 4])
        sscb = const("ssd_cb", [128, 12])
        ssnegA = const("ssd_Alog", [16, 1])
        P.act(ssnegA, ssnegA[:], ssnegA, ssnegA[:], AF.Exp)
        P.ts1(ssnegA, ssnegA[:], ssnegA, ssnegA[:], -1.0, ALU.mult)
        ssdtb = const("ssd_dtb", [16, 1])
        ssDcol = const("ssd_Dcol", [128, 8])
        ssng = const("ssd_ngT", [128, 8])
        s5Dcol = const("s5_Dcol", [128, 8])
        glub = const("glu_bT", [128, 8])
        sscst = P.sb([128, 12, 3], F32, "sscst")
        P.memset(sscst, sscst[:], 0.0)
        He = P.sb([128, 8, 128], F32, "He")
        Ho = P.sb([128, 8, 128], F32, "Ho")
        P.memset(He, He[:], 0.0)
        P.memset(Ho, Ho[:], 0.0)
        dtr = [P.sb([16, TP], F32, "dtr%d" % i) for i in range(4)]
        tk = P.sb([64, 48], F32, "tk")
        dcl = P.sb([128, 16], F32, "dcl")
        s5st = P.sb([128, 32, 2], F32, "s5st")
        P.memset(s5st, s5st[:], 0.0)
        s5fre = P.sb([128, 32], F32, "s5fre")
        s5fim = P.sb([128, 32], F32, "s5fim")
        c_z, c_xbc, c_y = U(0, 8, "c_z"), U(8, 20, "c_xbc"), U(28, 36, "c_y")
        c_u = P.view(RWt[:, :, :], "c_u")
        c_sc, c_Ap, c_cin, c_bout = U(36, 40, "c_sc"), U(40, 44, "c_Ap"), U(44, 48, "c_cin"), U(48, 56, "c_bout")
        c_xe, c_xo, c_btk, c_cbm, c_es = U(56, 60, "c_xe"), U(60, 64, "c_xo"), U(64, 65, "c_btk"), U(65, 66, "c_cbm"), U(36, 44, "c_sq")
        c_xm = U(66, 70, "c_xm")
        c_stt = U(48, 56, "c_stt")
        s_tab = [U(36 + 4 * i, 40 + 4 * i, "s_tab%d" % i) for i in range(2)]
        s_bp, s_z = U(44, 46, "s_bp"), U(46, 48, "s_z")
        s_xt = stack.enter_context(nc.sbuf_tensor("s5x", [128, 2, TP], F32R))
        s_x = P.view(s_xt[:], "s_x")
        s_yd = U(0, 8, "s_yd")
        s_z5 = P.view(RWt[:, :, :], "s_z5")
        s_tmp = U(48, 50, "s_tmp")
        s_x0 = U(50, 54, "s_x0")
        s_so = U(54, 58, "s_so")
        CD_BUFS = [c_z, c_xbc, c_y, c_u, c_sc, c_Ap, c_cin, c_bout, c_xe, c_xo, c_btk, c_cbm, c_xm]

        s5tab = nc.dram_tensor("s5tab", [4, 128, 32, TP], F32, kind="Internal").ap()
        s5tabB = Buf(None, "s5tab")

        def s5_setup():
            are = const("s5_are", [128, 32])
            aim = const("s5_aim", [128, 32])
            ldt = const("s5_ldt", [128, 32])
            w = [P.view(MWt[:, 40, i * 32:(i + 1) * 32], "s5w%d" % i) for i in range(8)]
            w += [P.view(MWt[:, 41, i * 32:(i + 1) * 32], "s5w%d" % (8 + i)) for i in range(6)]
            P.barrier(w)
            dtv, ar, th, mag, img, c_, s_, t0, t1, den = w[:10]
            P.act(dtv, dtv[:], ldt, ldt[:], AF.Exp)
            P.tt(ar, ar[:], are, are[:], dtv, dtv[:], ALU.mult)
            P.tt(th, th[:], aim, aim[:], dtv, dtv[:], ALU.mult)
            P.act(mag, mag[:], ar, ar[:], AF.Exp)
            P.act(img, img[:], ar, ar[:], AF.Exp, scale=-1.0)
            P.act(s_, s_[:], th, th[:], AF.Sin, scale=1.0 / 16)
            P.act(t0, t0[:], th, th[:], AF.Sin, scale=1.0 / 32)
            P.tt(t0, t0[:], t0, t0[:], t0, t0[:], ALU.mult)
            P.ts(c_, c_[:], t0, t0[:], -2.0, 1.0, ALU.mult, ALU.add)
            for _ in range(4):
                P.tt(t0, t0[:], c_, c_[:], c_, c_[:], ALU.mult)
                P.tt(t1, t1[:], s_, s_[:], s_, s_[:], ALU.mult)
                P.tt(s_, s_[:], s_, s_[:], c_, c_[:], ALU.mult)
                P.ts1(s_, s_[:], s_, s_[:], 2.0, ALU.mult)
                P.tt(c_, c_[:], t0, t0[:], t1, t1[:], ALU.subtract)
            lre, lim, ire, iim = w[10:14]
            w = w[:10]
            P.tt(lre, lre[:], mag, mag[:], c_, c_[:], ALU.mult)
            P.tt(lim, lim[:], mag, mag[:], s_, s_[:], ALU.mult)
            P.tt(ire, ire[:], img, img[:], c_, c_[:], ALU.mult)
            P.tt(iim, iim[:], img, img[:], s_, s_[:], ALU.mult)
            P.ts1(iim, iim[:], iim, iim[:], -1.0, ALU.mult)
            P.tt(den, den[:], are, are[:], are, are[:], ALU.mult)
            P.tt(t0, t0[:], aim, aim[:], aim, aim[:], ALU.mult)
            P.tt(den, den[:], den, den[:], t0, t0[:], ALU.add)
            P.op("dve", lambda E: E.reciprocal(den[:], den[:]), reads=[den], writes=[den])
            nr = dtv
            P.ts1(nr, nr[:], lre, lre[:], -1.0, ALU.add)
            P.tt(t0, t0[:], nr, nr[:], are, are[:], ALU.mult)
            P.tt(t1, t1[:], lim, lim[:], aim, aim[:], ALU.mult)
            P.tt(t0, t0[:], t0, t0[:], t1, t1[:], ALU.add)
            P.tt(s5fre, s5fre[:], t0, t0[:], den, den[:], ALU.mult)
            P.tt(t0, t0[:], lim, lim[:], are, are[:], ALU.mult)
            P.tt(t1, t1[:], nr, nr[:], aim, aim[:], ALU.mult)
            P.tt(t0, t0[:], t0, t0[:], t1, t1[:], ALU.subtract)
            P.tt(s5fim, s5fim[:], t0, t0[:], den, den[:], ALU.mult)
            Tre, Tim, Tt = U(0, 8, "s5Tre"), U(8, 16, "s5Tim"), U(16, 24, "s5Tt")
            Fre, Fim = U(24, 32, "s5Fre"), U(32, 40, "s5Fim")
            lre, lim, ire, iim = lre, lim, ire, iim
            P.barrier([Tre, Tim, Tt, Fre, Fim])
            for grp in range(4):
                ms = slice(grp * 8, grp * 8 + 8)
                for kind, (bre, bim) in enumerate(((lre, lim), (ire, iim))):
                    P.cp(Tre, Tre[:, :, 0:1], bre, bre[:, ms].unsqueeze(2))
                    P.cp(Tim, Tim[:, :, 0:1], bim, bim[:, ms].unsqueeze(2))
                    n = 1
                    while n < TP:
                        sre = bc(Tre[:, :, n - 1:n], [128, 8, n])
                        sim = bc(Tim[:, :, n - 1:n], [128, 8, n])
                        tv = Tt[:, :, 0:n]
                        P.tt(Tt, tv, Tim, Tim[:, :, 0:n], Tim, sim, ALU.mult)
                        P.tt(Tre, Tre[:, :, n:2 * n], Tre, Tre[:, :, 0:n], Tre, sre, ALU.mult)
                        P.tt(Tre, Tre[:, :, n:2 * n], Tre, Tre[:, :, n:2 * n], Tt, tv, ALU.subtract)
                        P.tt(Tt, tv, Tim, Tim[:, :, 0:n], Tre, sre, ALU.mult)
                        P.tt(Tim, Tim[:, :, n:2 * n], Tre, Tre[:, :, 0:n], Tim, sim, ALU.mult)
                        P.tt(Tim, Tim[:, :, n:2 * n], Tim, Tim[:, :, n:2 * n], Tt, tv, ALU.add)
                        n *= 2
                    if kind == 0:
                        P.dma("sp", s5tab[0][:, ms, :], Tre[:], reads=[Tre], writes=[s5tabB])
                        P.dma("sp", s5tab[1][:, ms, :], Tim[:], reads=[Tim], writes=[s5tabB])
                    else:
                        fre = bc(s5fre[:, ms].unsqueeze(2), [128, 8, TP])
                        fim = bc(s5fim[:, ms].unsqueeze(2), [128, 8, TP])
                        P.tt(Fre, Fre[:], Tre, Tre[:], s5fre, fre, ALU.mult)
                        P.tt(Tt, Tt[:], Tim, Tim[:], s5fim, fim, ALU.mult)
                        P.tt(Fre, Fre[:], Fre, Fre[:], Tt, Tt[:], ALU.subtract)
                        P.tt(Fim, Fim[:], Tre, Tre[:], s5fim, fim, ALU.mult)
                        P.tt(Tt, Tt[:], Tim, Tim[:], s5fre, fre, ALU.mult)
                        P.tt(Fim, Fim[:], Fim, Fim[:], Tt, Tt[:], ALU.add)
                        P.dma("sp", s5tab[2][:, ms, :], Fre[:], reads=[Fre], writes=[s5tabB])
                        P.dma("sp", s5tab[3][:, ms, :], Fim[:], reads=[Fim], writes=[s5tabB])

        def ssd_state_in(n):
            P.barrier([c_stt])
            P.dma("sp", c_stt[:].rearrange("p a t -> p (a t)")[:, 0:1024].rearrange("p (c s) -> p c s", s=128), I["ssd_st"][n],
                  writes=[c_stt])
            sv = c_stt[:].rearrange("p a t -> p (a t)")[:, 0:1024].rearrange("p (c s) -> p c s", s=128)
            for half in range(2):
                ps = P.next_ps()
                for q4 in range(4):
                    P.tr(ps, ps[:, q4 * 128:(q4 + 1) * 128], c_stt, sv[:, half * 4 + q4, :], ident, ident[:])
                pv = ps[:, :].rearrange("p (c q) -> p c q", q=128)
                P.cp(He, He[:, half * 4:half * 4 + 4, 0:64], ps, pv[:, :, 0:64])
                P.cp(Ho, Ho[:, half * 4:half * 4 + 4, 64:128], ps, pv[:, :, 64:128])

        def ssd_state_out(dst):
            P.barrier([c_stt, c_es])
            sv = c_stt[:].rearrange("p a t -> p (a t)")[:, 0:1024].rearrange("p (c s) -> p c s", s=128)
            sm = c_es[:].rearrange("p a t -> p (a t)")[:, 0:1024].rearrange("p (c s) -> p c s", s=128)
            P.tt(c_es, sm, He, He[:], Ho, Ho[:], ALU.add)
            for half in range(2):
                ps = P.next_ps()
                for q4 in range(4):
                    P.tr(ps, ps[:, q4 * 128:(q4 + 1) * 128], c_es, sm[:, half * 4 + q4, :], ident, ident[:])
                P.cp(c_stt, sv[:, half * 4:half * 4 + 4, :], ps, ps[:, :].rearrange("p (c q) -> p c q", q=128), eng="act")
            P.dma("sp", dst, sv, reads=[c_stt])

        def ssd(tl):
            T, C, nch = tl.T, tl.C, tl.nch
            dT, aT, acT, wT = dtr
            P.act(dT, dT[0:16, :T], dT, dT[0:16, :T], AF.Exp, bias=ssdtb[0:16, 0:1], extra=[ssdtb])
            P.act(dT, dT[0:16, :T], dT, dT[0:16, :T], AF.Ln, bias=1.0)
            P.ts1(aT, aT[0:16, :T], dT, dT[0:16, :T], ssnegA[0:16, 0:1], ALU.mult, extra=[ssnegA])
            rm = rmask[tl.kind]
            P.scan(acT, acT[0:16, :T], rm, rm[0:16, 0:T], aT, aT[0:16, :T])
            a3 = acT[0:16, :T].rearrange("p (n c) -> p n c", c=C)
            w3 = wT[0:16, :T].rearrange("p (n c) -> p n c", c=C)
            P.tt(wT, w3, acT, bc(a3[:, :, C - 1:C], [16, nch, C]), acT, a3, ALU.subtract)
            P.act(wT, wT[0:16, :T], wT, wT[0:16, :T], AF.Exp)
            P.tt(wT, wT[0:16, :T], wT, wT[0:16, :T], dT, dT[0:16, :T], ALU.mult)
            P.act(acT, acT[0:16, :T], acT, acT[0:16, :T], AF.Exp)
            scT = t4(c_sc, C, 16)
            Ap = t4(c_Ap, C, 16)
            cin = t4(c_cin, C, 16)
            bout = c_bout[:].rearrange("p a t -> p (a t)")[0:C, :].rearrange("p (h s) -> p h s", s=128)
            xe, xo = tok(c_xe, C, 1024), tok(c_xo, C, 1024)
            btk = tok(c_btk, C, 256)
            cbm = t4(c_cbm, C, 2)
            P.memset(c_xe, xe, 0.0)
            P.memset(c_xo, xo, 0.0)
            p3 = lambda ps_, nh: ps_[0:C, 0:nh * C].rearrange("p (h c) -> p h c", c=C)
            for n in range(nch if SC >= 2 else 0):
                c0 = n * C
                cs = slice(c0, c0 + C)
                if tl.kind == "s" and SC >= 7:
                    ssd_state_in(n)
                    P.barrier([c_bout, c_cin])
                pst = P.next_ps()
                for i, src in enumerate((aT, dT, wT)):
                    P.tr(pst, pst[0:C, i * 16:(i + 1) * 16], src, src[0:16, cs], ident, ident[0:16, 0:16])
                P.cp(tk, tk[0:C, :], pst, pst[0:C, 0:48], eng="act")
                P.tt(c_Ap, Ap[0:C], tk, bc(tk[0:C, 0:16].unsqueeze(2), [C, 16, C]), SL, bc(SL[0:C, 0:C].unsqueeze(1), [C, 16, C]),
                     ALU.mult)
                for half in range(2):
                    psd = P.next_ps()
                    for hh in range(8):
                        P.mm(psd, psd[0:C, hh * C:(hh + 1) * C], c_Ap, Ap[0:C, half * 8 + hh, :], UT, UT[0:C, 0:C])
                    P.act(c_sc, scT[0:C, half * 8:half * 8 + 8, :], psd, p3(psd, 8), AF.Exp)
                pcb = P.next_ps()
                for g in range(2):
                    P.mm(pcb, pcb[0:C, g * C:(g + 1) * C], c_xbc, c_xbc[:, 8 + g, cs], c_xbc, c_xbc[:, 10 + g, cs])
                P.tt(c_cbm, cbm[0:C], pcb, p3(pcb, 2), UT, bc(UT[0:C, 0:C].unsqueeze(1), [C, 2, C]), ALU.mult)
                sc4 = scT[0:C].rearrange("p (g h) c -> p g h c", g=2)
                P.tt(c_sc, sc4, c_sc, sc4, c_cbm, bc(cbm[0:C].unsqueeze(2), [C, 2, 8, C]), ALU.mult)
                P.tt(c_sc, scT[0:C], c_sc, scT[0:C], tk, bc(tk[0:C, 16:32].unsqueeze(2), [C, 16, C]), ALU.mult)
                if SC < 3:
                    continue
                for half in range(2):
                    pt = P.next_ps()
                    for q4 in range(4):
                        P.tr(pt, pt[0:C, q4 * 128:(q4 + 1) * 128], c_xbc, c_xbc[:, half * 4 + q4, cs], ident, ident[:])
                    pv = pt[0:C, :].rearrange("p (c q) -> p c q", q=128)
                    xev = xe[:, half * 512:(half + 1) * 512].rearrange("p (c q) -> p c q", q=128)
                    xov = xo[:, half * 512:(half + 1) * 512].rearrange("p (c q) -> p c q", q=128)
                    P.cp(c_xe, xev[:, :, 0:64], pt, pv[:, :, 0:64])
                    P.cp(c_xo, xov[:, :, 64:128], pt, pv[:, :, 64:128])
                pb = P.next_ps()
                for g in range(2):
                    P.tr(pb, pb[0:C, g * 128:(g + 1) * 128], c_xbc, c_xbc[:, 8 + g, cs], ident, ident[:])
                P.cp(c_btk, btk, pb, pb[0:C, 0:256], eng="act")
                b4 = bout.rearrange("p (g h) s -> p g h s", g=2)
                P.tt(c_bout, b4, c_btk, bc(btk.rearrange("p (g s) -> p g s", g=2).unsqueeze(2), [C, 2, 8, 128]),
                     tk, bc(tk[0:C, 32:48].rearrange("p (g h) -> p g h", g=2).unsqueeze(3), [C, 2, 8, 128]), ALU.mult)
                if SC < 4:
                    continue
                xm = c_xm[:].rearrange("p a t -> p (a t)")[0:16, 0:16 * C].rearrange("p (h c) -> p h c", c=C)
                P.tt(c_xm, xm, acT, bc(acT[0:16, cs].unsqueeze(1), [16, 16, C]),
                     ident, bc(ident[0:16, 0:16].unsqueeze(2), [16, 16, C]), ALU.mult)
                for g in range(2):
                    pse = P.next_ps()
                    for hh in range(8):
                        P.mm(pse, pse[:, hh * C:(hh + 1) * C], ones, ones[0:16, :], c_xm, xm[:, g * 8 + hh, :])
                    pev = pse[:, 0:8 * C].rearrange("p (h c) -> p h c", c=C)
                    P.tt(c_cin, cin[:, g * 8:g * 8 + 8, :], c_xbc, bc(c_xbc[:, 10 + g, cs].unsqueeze(1), [128, 8, C]), pse, pev,
                         ALU.mult)
                    P.cp(dcl, dcl[:, g * 8:g * 8 + 8].unsqueeze(2), pse, pev[:, :, C - 1:C])
                if SC < 5:
                    continue
                py = P.next_ps()
                for c in range(8):
                    o_ap = py[:, c * C:(c + 1) * C]
                    P.mm(py, o_ap, c_xe, xe[:, c * 128:(c + 1) * 128], c_sc, scT[0:C, 2 * c, :], start=True, stop=False)
                    P.mm(py, o_ap, c_xo, xo[:, c * 128:(c + 1) * 128], c_sc, scT[0:C, 2 * c + 1, :], start=False, stop=False)
                    P.mm(py, o_ap, He, He[:, c, :], c_cin, cin[:, 2 * c, :], start=False, stop=False)
                    P.mm(py, o_ap, Ho, Ho[:, c, :], c_cin, cin[:, 2 * c + 1, :], start=False, stop=True)
                P.cp(c_y, c_y[:, :, cs], py, py[:, 0:8 * C].rearrange("p (a c) -> p a c", c=C), eng="act")
                if SC < 6:
                    continue
                for half in range(2):
                    pss = P.next_ps()
                    for cc in range(4):
                        c = half * 4 + cc
                        P.mm(pss, pss[:, cc * 128:cc * 128 + 64], c_bout, bout[:, 2 * c, :], c_xe, xe[:, c * 128:c * 128 + 64])
                        P.mm(pss, pss[:, cc * 128 + 64:(cc + 1) * 128], c_bout, bout[:, 2 * c + 1, :],
                             c_xo, xo[:, c * 128 + 64:(c + 1) * 128])
                    for cc in range(4):
                        c = half * 4 + cc
                        P.stt(He, He[:, c, 0:64], He, He[:, c, 0:64], dcl[:, 2 * c:2 * c + 1], pss, pss[:, cc * 128:cc * 128 + 64],
                              ALU.mult, ALU.add, extra=[dcl])
                        P.stt(Ho, Ho[:, c, 64:128], Ho, Ho[:, c, 64:128], dcl[:, 2 * c + 1:2 * c + 2],
                              pss, pss[:, cc * 128 + 64:(cc + 1) * 128], ALU.mult, ALU.add, extra=[dcl])
                if SC < 7:
                    continue
                if tl.kind == "s":
                    ssd_state_out(O["ssd_o"][1 + n])
                    P.barrier([c_bout, c_cin, c_sc, c_Ap])
                elif tl.last and n == nch - 1:
                    ssd_state_out(O["ssd_o"][0])
            if SC < 8:
                return
            for c in range(8):
                P.stt(c_y, c_y[:, c, :T], c_xbc, c_xbc[:, c, :T], ssDcol[:, c:c + 1], c_y, c_y[:, c, :T], ALU.mult, ALU.add,
                      extra=[ssDcol])
            P.tt(c_y, c_y[:, :, :T], c_y, c_y[:, :, :T], c_z, c_z[:, :, :T], ALU.mult)
            P.barrier([c_es])
            P.act(c_es, c_es[:, :, :T], c_y, c_y[:, :, :T], AF.Square)
            ps = P.next_ps()
            for g in range(2):
                for cc in range(4):
                    P.mm(ps, ps[:, g * T:(g + 1) * T], ones, ones[:], c_es, c_es[:, g * 4 + cc, :T], start=(cc == 0), stop=(cc == 3))
            rsv = c_es[:, 0:2, :T]
            P.act(c_es, rsv, ps, ps[:, 0:2 * T].rearrange("p (g t) -> p g t", t=T), AF.Ln, bias=EPS, scale=1.0 / 512)
            P.act(c_es, rsv, c_es, rsv, AF.Exp, scale=-0.5)
            y4 = c_y[:, :, :T].rearrange("p (g c) t -> p g c t", g=2)
            P.tt(c_y, y4, c_y, y4, c_es, bc(rsv.unsqueeze(2), [128, 2, 4, T]), ALU.mult)
            for c in range(8):
                P.ts1(hT, hT[:, c, :T], c_y, c_y[:, c, :T], ssng[:, c:c + 1], ALU.mult, extra=[ssng])

        def s5(tl):
            T, L, ns = tl.T, tl.L, tl.nseq
            P.barrier(s_tab + [s_bp, s_z, s_x, s_yd, s_tmp, s_x0, s_so])
            x0v = s_x0[:].rearrange("p a t -> p (a t)")[:, 0:2 * 32 * NSQ].rearrange("p (k m s) -> p k m s", k=2, s=NSQ)
            sov = s_so[:].rearrange("p a t -> p (a t)")[:, 0:32 * NSQ * 2].rearrange("p (m s k) -> p m s k", s=NSQ, k=2)
            if tl.kind == "s":
                P.dma("sp", x0v, I["s5_x0"], writes=[s_x0])
            onesrow = bc(ones[:, 0:1], [128, T])
            TL = L
            for c in range(8):
                wb = next_wb()
                wv = wb[:, 0:16 * 128].rearrange("p (k m q) -> p k m q", k=4, q=128)
                P.dma("pool", wv, I["s5w"][c], writes=[wb])
                pyr = P.next_ps()
                pyi = P.next_ps()
                for mm_ in range(4):
                    m = 4 * c + mm_
                    tb = s_tab[m % 2]
                    tv = tb[:].rearrange("p a t -> p (a t)")[:, 0:4 * TL].rearrange("p (k t) -> p k t", k=4)
                    P.dma("sp", tv, s5tab[:, :, m, 0:TL].rearrange("k p t -> p k t"), reads=[s5tabB], writes=[tb])

                    def tab(k):
                        return bc(tv[:, k, :].unsqueeze(1), [128, ns, L])

                    pb = P.next_ps()
                    P.mm(pb, pb[:, 0:T], wb, wv[:, 0, mm_, :], c_u, c_u[:, c, :T])
                    P.mm(pb, pb[:, T:2 * T], wb, wv[:, 1, mm_, :], c_u, c_u[:, c, :T])
                    bur = pb[:, 0:T].rearrange("p (s l) -> p s l", l=L)
                    bui = pb[:, T:2 * T].rearrange("p (s l) -> p s l", l=L)
                    bp = s_bp[:, :, :T].rearrange("p k (s l) -> p k s l", l=L)
                    tm = s_tmp[:, :, :T].rearrange("p k (s l) -> p k s l", l=L)
                    P.tt(s_bp, bp[:, 0], pb, bur, tb, tab(2), ALU.mult)
                    P.tt(s_tmp, tm[:, 0], pb, bui, tb, tab(3), ALU.mult)
                    P.tt(s_bp, bp[:, 0], s_bp, bp[:, 0], s_tmp, tm[:, 0], ALU.subtract)
                    P.tt(s_bp, bp[:, 1], pb, bui, tb, tab(2), ALU.mult)
                    P.tt(s_tmp, tm[:, 1], pb, bur, tb, tab(3), ALU.mult)
                    P.tt(s_bp, bp[:, 1], s_bp, bp[:, 1], s_tmp, tm[:, 1], ALU.add)
                    if tl.kind == "s":
                        for k in range(2):
                            P.tt(s_bp, bp[:, k, :, 0:1], s_bp, bp[:, k, :, 0:1], s_x0, x0v[:, k, m, :].unsqueeze(2), ALU.add)
                        for k in range(2):
                            P.scan(s_z, s_z[:, k, :T], rmask["s"], rmask["s"][:, 0:T], s_bp, s_bp[:, k, :T])
                    else:
                        for k in range(2):
                            P.op("dve", lambda E, k=k, m=m: E.tensor_tensor_scan(
                                s_z[:, k, :T], onesrow, s_bp[:, k, :T], s5st[:, m, k:k + 1], ALU.mult, ALU.add),
                                reads=[ones, s_bp, s5st], writes=[s_z])
                    zv = s_z[:, :, :T].rearrange("p k (s l) -> p k s l", l=L)
                    xv = s_x[:, :, :T].rearrange("p k (s l) -> p k s l", l=L)
                    P.tt(s_tmp, tm[:, 0], s_z, zv[:, 1], tb, tab(1), ALU.mult)
                    P.tt(s_tmp, tm[:, 1], s_z, zv[:, 0], tb, tab(0), ALU.mult)
                    P.tt(s_x, xv[:, 0], s_tmp, tm[:, 1], s_tmp, tm[:, 0], ALU.subtract)
                    P.tt(s_tmp, tm[:, 0], s_z, zv[:, 0], tb, tab(1), ALU.mult)
                    P.tt(s_tmp, tm[:, 1], s_z, zv[:, 1], tb, tab(0), ALU.mult)
                    P.tt(s_x, xv[:, 1], s_tmp, tm[:, 1], s_tmp, tm[:, 0], ALU.add)
                    if tl.kind == "s":
                        P.cp(s_so, sov[:, m].rearrange("p s k -> p k s").unsqueeze(3), s_x, xv[:, :, :, L - 1:L].bitcast(F32))
                    else:
                        P.cp(s5st, s5st[:, m, :].unsqueeze(2), s_x, s_x[:, :, T - 1:T].bitcast(F32))
                    P.mm(pyr, pyr[:, :T], wb, wv[:, 2, mm_, :], s_x, s_x[:, 0, :T], start=(mm_ == 0), stop=(mm_ == 3))
                    P.mm(pyi, pyi[:, :T], wb, wv[:, 3, mm_, :], s_x, s_x[:, 1, :T], start=(mm_ == 0), stop=(mm_ == 3))
                P.cp(s_tmp, s_tmp[:, 0, :T], pyi, pyi[:, :T], eng="act")
                P.tt(s_yd, s_yd[:, c, :T], pyr, pyr[:, :T], s_tmp, s_tmp[:, 0, :T], ALU.subtract)
                P.stt(s_yd, s_yd[:, c, :T], c_u, c_u[:, c, :T].bitcast(F32), s5Dcol[:, c:c + 1], s_yd, s_yd[:, c, :T],
                      ALU.mult, ALU.add, extra=[s5Dcol])
            if tl.kind == "s":
                P.dma("sp", O["s5_o"][:, :, 1:NS1, :], sov, reads=[s_so])
            elif tl.last:
                P.dma("sp", O["s5_o"][:, :, 0:1, :], s5st[:].unsqueeze(2), reads=[s5st])
            P.barrier([s_z5])
            P.act(s_z5, s_z5[:, :, :T], s_yd, s_yd[:, :, :T], AF.Gelu)

            def cons_glu(j, mrows, ps):
                P.act(s_tmp, s_tmp[:, 0, :T], ps, ps[:, :T], AF.Sigmoid, bias=glub[:, j:j + 1], extra=[glub])
                P.tt(hT, hT[:, 8 + j, :T], s_z5, s_z5[:, j, :T].bitcast(F32), s_tmp, s_tmp[:, 0, :T], ALU.mult)

            proj_fm(tl, "glu_w", 0, 1024, cons_glu, rhs=s_z5, nk=8)

        def mixer_cd(tl):
            T = tl.T
            W = "w_in_cd"
            P.barrier(CD_BUFS)
            PJ = int(os.environ.get("KDEV_PJ", "15"))
            if PJ & 1:
                proj_fm(tl, W, 0, 1024, lambda j, m, ps: P.act(c_z, c_z[:, j, :T], ps, ps[:, :T], AF.Silu))
            if PJ & 2:
                proj_fm(tl, W, 1024, 1536, lambda j, m, ps: conv_chunk(
                    tl, ps, sscw, sscb, j, 4, sscst, I["ssd_cst"], O["ssd_cst_o"], c_xbc, c_xbc[:, j, :T], AF.Silu))
            if PJ & 4:
                proj_fm(tl, W, 2560, 16, lambda j, m, ps: P.cp(dtr[0], dtr[0][0:16, :T], ps, ps[0:16, :T]))
            if PJ & 8:
                proj_fm(tl, W, 2576, 1024, lambda j, m, ps: P.cp(c_u, c_u[:, j, :T], ps, ps[:, :T], eng="act"))
            if CDCUT >= 3:
                ssd(tl)
            if CDCUT >= 4:
                s5(tl)
            if CDCUT < 4:
                return
            P.barrier([yacc])
            proj_fm(tl, "w_out_cd", 0, D, lambda j, m, ps: P.cp(yacc, yacc[:, j, :T], ps, ps[:, :T], eng="act"))
            resid_add(tl, 1, 32, yacc)

        def final_out(tl):
            T = tl.T
            P.barrier([yacc])
            rms_stats(tl)
            for c in range(KC):
                P.act(yacc, yacc[:, c, :T], yacc, yacc[:, c, :T], AF.Copy, scale=gfin[:, c:c + 1], extra=[gfin])
            if tl.kind == "p":
                P.dma("sp", ypv[:, :, tl.idx * TP:(tl.idx + 1) * TP], yacc[:, :, :T], reads=[yacc])
            else:
                P.dma("sp", ysv, yacc[:, :, :T], reads=[yacc])

        if STAGE >= 3:
            s5_setup()
        tiles = [Tile("p", i) for i in range(NPT)] + [Tile("s", 0)]
        xpv = I["xp"].rearrange("(c p) t -> p c t", p=128)
        xsv = I["xs"].rearrange("(c p) t -> p c t", p=128)
        ypv = O["yp"].rearrange("(c p) t -> p c t", p=128)
        ysv = O["ys"].rearrange("(c p) t -> p c t", p=128)
        for tl in tiles:
            if tl.kind == "p":
                P.dma("sp", xT[:, :, :tl.T], xpv[:, :, tl.idx * TP:(tl.idx + 1) * TP], writes=[xT])
            else:
                P.dma("sp", xT[:, :, :tl.T], xsv, writes=[xT])
            for l in range(2):
                norm_mod(tl, l, 0, 16)
                if l == 0 and STAGE >= 2:
                    mixer_ab(tl)
                if l == 1 and STAGE >= 3 and CDCUT >= 2:
                    mixer_cd(tl)
                norm_mod(tl, l, 48, 64)
                ffn(tl, l)
                resid_add(tl, l, 80, yacc)
            final_out(tl)
        P.finish()
    return nc


def _fm(v):
    v = np.asarray(v, np.float32)
    return np.ascontiguousarray(v.reshape(-1, 128).T)


def _consts():
    i = np.arange(128)
    c = {}
    c["ident"] = np.eye(128, dtype=np.float32)
    c["UT"] = (i[None, :] >= i[:, None]).astype(np.float32)
    c["nSU"] = -(i[None, :] > i[:, None]).astype(np.float32)
    c["SL"] = (i[:, None] > i[None, :]).astype(np.float32)
    rp = np.ones((128, TP), np.float32)
    rp[:, ::64] = 0.0
    rs = np.ones((128, TS), np.float32)
    rs[:, ::LS] = 0.0
    c["rmask_p"], c["rmask_s"] = rp, rs
    es = np.zeros((8, 8, 128), np.float32)
    for h in range(8):
        es[h, h, :] = 1.0
    c["esel"] = es
    return c


def make_in_maps(inp):
    maps = []
    cst = _consts()
    assert WS == 256
    wt = {}
    wt["w_ada"] = np.stack([tile_cols_all(inp["w_ada"][l]) for l in range(2)])
    wt["w_ffn_up"] = np.stack([tile_cols_all(inp["w_ffn_up"][l]) for l in range(2)])
    wt["w_ffn_down"] = np.ascontiguousarray(inp["w_ffn_down"].reshape(2, DFF // 256, 2, 128, D).transpose(0, 1, 3, 2, 4))
    wt["w_in_ab"] = tile_weight(inp["w_in_ab"][0], "w_in_ab")
    wt["w_out_ab"] = tile_weight(inp["w_out_ab"][0], "w_out_ab")
    wt["w_in_cd"] = tile_weight(inp["w_in_cd"][0], "w_in_cd")
    wt["w_out_cd"] = tile_weight(inp["w_out_cd"][0], "w_out_cd")
    wt["glu_w"] = tile_weight(inp["s5_glu_w"][0], "glu_w")
    Bre, Bim, Cre, Cim = (inp[k][0] for k in ("s5_B_re", "s5_B_im", "s5_C_re", "s5_C_im"))
    cst_s5w = np.zeros((8, 128, 4, 4, 128), np.float32)
    for c in range(8):
        for mm_ in range(4):
            for g2 in range(2):
                gl = 2 * mm_ + g2
                g = 8 * c + gl
                cst_s5w[c, gl * 16:(gl + 1) * 16, 0, mm_, g2 * 64:(g2 + 1) * 64] = Bre[g].T
                cst_s5w[c, gl * 16:(gl + 1) * 16, 1, mm_, g2 * 64:(g2 + 1) * 64] = Bim[g].T
                cst_s5w[c, g2 * 64:(g2 + 1) * 64, 2, mm_, gl * 16:(gl + 1) * 16] = Cre[g].T
                cst_s5w[c, g2 * 64:(g2 + 1) * 64, 3, mm_, gl * 16:(gl + 1) * 16] = Cim[g].T
    for c in range(NCORES):
        b = c // 2
        sl = slice(NSQ * c, NSQ * (c + 1))
        m = dict(cst)
        m["xp"] = inp["x_prompt"][b, :SEQ].T
        m["xs"] = inp["x_sample"][sl].reshape(TS, D).T
        m["cT"] = np.concatenate([inp["c_prompt"][b:b + 1], inp["c_sample"][sl]], 0).T
        m.update(wt)
        m["b_adaT"] = np.stack([_fm(inp["b_ada"][l]) for l in range(2)])
        m["g_mixT"] = np.stack([_fm(inp["g_mix"][l]) for l in range(2)])
        m["g_ffnT"] = np.stack([_fm(inp["g_ffn"][l]) for l in range(2)])
        m["g_finT"] = _fm(inp["g_final"])
        m["ffn_cw"] = inp["ffn_conv_w"].reshape(2, 3, NFF, 128).transpose(0, 3, 2, 1)
        m["ffn_cb"] = inp["ffn_conv_b"].reshape(2, NFF, 128).transpose(0, 2, 1)
        m["ffn_st"] = inp["state_ffn_conv"][:, sl].reshape(2, NSQ, 2, NFF, 128).transpose(0, 4, 3, 1, 2)
        m["gla_w2"] = inp["gla_w2"][0]
        m["gla_b2T"] = _fm(inp["gla_b2"][0])
        m["gla_ngT"] = _fm(inp["gla_norm_g"][0])
        m["gla_st"] = inp["state_gla"][0, sl].transpose(0, 2, 1, 3)
        m["gdn_cw"] = inp["gdn_conv_w"][0].reshape(4, 24, 128).transpose(2, 1, 0)
        m["gdn_cb"] = inp["gdn_conv_b"][0].reshape(24, 128).T
        m["gdn_cst"] = inp["state_gdn_conv"][0, sl].reshape(NSQ, 3, 24, 128).transpose(3, 2, 0, 1)
        m["gdn_Alog"] = inp["gdn_A_log"][0].reshape(8, 1)
        m["gdn_dtb"] = inp["gdn_dt_bias"][0].reshape(8, 1)
        m["gdn_ngT"] = inp["gdn_norm_g"][0].reshape(128, 1)
        m["gdn_st"] = inp["state_gdn"][0, sl].transpose(0, 2, 1, 3)
        m["ssd_cw"] = inp["ssd_conv_w"][0].reshape(4, 12, 128).transpose(2, 1, 0)
        m["ssd_cb"] = inp["ssd_conv_b"][0].reshape(12, 128).T
        m["ssd_cst"] = inp["state_ssd_conv"][0, sl].reshape(NSQ, 3, 12, 128).transpose(3, 2, 0, 1)
        m["ssd_Alog"] = inp["ssd_A_log"][0].reshape(16, 1)
        m["ssd_dtb"] = inp["ssd_dt_bias"][0].reshape(16, 1)
        m["ssd_Dcol"] = np.repeat(inp["ssd_D"][0], 64).reshape(8, 128).T
        m["ssd_ngT"] = _fm(inp["ssd_norm_g"][0])
        m["ssd_st"] = inp["state_ssd"][0, sl].reshape(NSQ, 8, 128, 128).transpose(0, 2, 1, 3)
        modes = lambda a: a.reshape(32, 128).T
        m["s5_are"] = modes(inp["s5_A_re"][0])
        m["s5_aim"] = modes(inp["s5_A_im"][0])
        m["s5_ldt"] = modes(np.repeat(inp["s5_log_dt"][0][:, None], 64, axis=1))
        m["s5w"] = cst_s5w
        m["s5_Dcol"] = _fm(inp["s5_D"][0])
        m["s5_x0"] = np.stack([inp["state_s5_re"][0, sl].reshape(NSQ, 32, 128).transpose(2, 1, 0),
                               inp["state_s5_im"][0, sl].reshape(NSQ, 32, 128).transpose(2, 1, 0)], 1)
        m["glu_bT"] = _fm(inp["s5_glu_b"][0])
        maps.append({k: np.ascontiguousarray(v, dtype=np.float32) for k, v in m.items()})
    return maps


_NC_CACHE = {}


def run_device(inp):
    if "nc" not in _NC_CACHE:
        _NC_CACHE["nc"] = build_program()
    nc = _NC_CACHE["nc"]
    maps = make_in_maps(inp)
    if RUNCORES < NCORES:
        res = run_bass_kernel_spmd(nc, maps[:RUNCORES], core_ids=list(range(RUNCORES)))
        return [res.results[min(c, RUNCORES - 1)] for c in range(NCORES)]
    res = run_bass_kernel_spmd(nc, maps, core_ids=list(range(NCORES)))
    return res.results


def assemble(R):
    B, DB = 4, 128
    y_p = np.stack([R[2 * b]["yp"].T for b in range(B)])
    y_s = np.concatenate([R[c]["ys"].T.reshape(NSQ, LS, D) for c in range(NCORES)], 0)

    def ffn_un(a):
        return a.transpose(0, 3, 4, 2, 1).reshape(2, a.shape[3], 2, 2 * DFF)

    ffn_p = np.concatenate([ffn_un(R[2 * b]["ffn_st_o"][:, :, :, 0:1]) for b in range(B)], 1)
    ffn_s = np.concatenate([ffn_un(R[c]["ffn_st_o"][:, :, :, 1:]) for c in range(NCORES)], 1)

    def st_un(a):
        return a.transpose(0, 2, 1, 3)[None]

    gla_p = np.concatenate([st_un(R[2 * b]["gla_o"][0:1]) for b in range(B)], 1)
    gla_s = np.concatenate([st_un(R[c]["gla_o"][1:]) for c in range(NCORES)], 1)
    gdn_p = np.concatenate([st_un(R[2 * b]["gdn_o"][0:1]) for b in range(B)], 1)
    gdn_s = np.concatenate([st_un(R[c]["gdn_o"][1:]) for c in range(NCORES)], 1)

    def cv_un(a, nchn):
        return a.transpose(2, 3, 1, 0).reshape(1, a.shape[2], 3, nchn * 128)

    gdc_p = np.concatenate([cv_un(R[2 * b]["gdn_cst_o"][:, :, 0:1], 24) for b in range(B)], 1)
    gdc_s = np.concatenate([cv_un(R[c]["gdn_cst_o"][:, :, 1:], 24) for c in range(NCORES)], 1)
    def ssd_un(a):
        return a.transpose(0, 2, 1, 3).reshape(1, a.shape[0], 16, 64, 128)

    ssd_p = np.concatenate([ssd_un(R[2 * b]["ssd_o"][0:1]) for b in range(B)], 1)
    ssd_s = np.concatenate([ssd_un(R[c]["ssd_o"][1:]) for c in range(NCORES)], 1)
    ssc_p = np.concatenate([cv_un(R[2 * b]["ssd_cst_o"][:, :, 0:1], 12) for b in range(B)], 1)
    ssc_s = np.concatenate([cv_un(R[c]["ssd_cst_o"][:, :, 1:], 12) for c in range(NCORES)], 1)

    def s5_un(a, k):
        a = a[:, :, :, k]
        return a.transpose(2, 1, 0).reshape(1, a.shape[2], 64, 64)

    s5 = [np.concatenate([s5_un(R[2 * b]["s5_o"][:, :, 0:1], k) for b in range(B)], 1) for k in range(2)]
    s5s = [np.concatenate([s5_un(R[c]["s5_o"][:, :, 1:], k) for c in range(NCORES)], 1) for k in range(2)]
    out = (y_p, y_s, gla_p, gla_s, gdn_p, gdn_s, gdc_p, gdc_s,
           ssd_p, ssd_s, ssc_p, ssc_s, s5[0], s5s[0], s5[1], s5s[1], ffn_p, ffn_s)
    return tuple(np.ascontiguousarray(o, dtype=np.float32) for o in out)


def kernel(**inputs):
    inp = {k: np.asarray(v) for k, v in inputs.items()}
    R = run_device(inp)
    return assemble(R)
```
